# Optimizing a Trainium2 kernel written in Bass

```python
import math
import jax
import jax.numpy as jnp
from jax import lax
import numpy as np

D_MODEL = 1024
BATCH = 8
SEQ = 2048
DEPTH = 2
DEC_BATCH = 128
DEC_SEQ = 1
PAST_LEN = 16384
PAGE_SIZE = 128

N_META = 16
D_FF = ((8 * D_MODEL // 3 + 255) // 256) * 256
NORM_EPS = 1e-6
FFN_RES_SCALE = 0.5
S5_WIDTH = 3 * D_MODEL // 8
S5_GROUP = 16
S5_GROUPS = S5_WIDTH // S5_GROUP
S5_STATE = 64
RWKV_HEAD = 64
RWKV_WIDTH = 3 * D_MODEL // 8
RWKV_HEADS = RWKV_WIDTH // RWKV_HEAD
DECAY_RANK = 64
ICL_RANK = 64
GATE_RANK = 128
RWKV_PROJ = 3 * RWKV_WIDTH + DECAY_RANK + ICL_RANK + GATE_RANK
RWKV_LN_EPS = 64e-5
LRU_WIDTH = D_MODEL // 4
LRU_BLOCKS = 4
LRU_BLOCK = LRU_WIDTH // LRU_BLOCKS
CONV_WIDTH = 4
LRU_C = 8.0
D_MIX = S5_WIDTH + RWKV_WIDTH + LRU_WIDTH
D_IN = S5_WIDTH + RWKV_PROJ + 2 * LRU_WIDTH

kernel_name = 'hybrid_s5_rwkv7_rglru_decode_step'


def rmsnorm(x, g):
    xf = x.astype(jnp.float32)
    y = xf * lax.rsqrt(jnp.mean(xf * xf, axis=-1, keepdims=True) + NORM_EPS)
    return (y * g.astype(jnp.float32)).astype(x.dtype)


def swiglu(x, w_gate, w_up, w_down):
    return (jax.nn.silu(x @ w_gate) * (x @ w_up)) @ w_down


def s5_group_mixer(u, h0_re, h0_im, lam_re, lam_im, log_dt, b_re, b_im, c_re, c_im, d_skip, glu_w, glu_b):
    f32 = jnp.float32
    bsz, L, _ = u.shape
    uf = u.astype(f32)
    ug = uf.reshape(bsz, L, S5_GROUPS, S5_GROUP)
    lr = lam_re.astype(f32)
    li = lam_im.astype(f32)
    dt = jnp.exp(log_dt.astype(f32))[:, None]
    mag = jnp.exp(lr * dt)
    ab_re = mag * jnp.cos(li * dt)
    ab_im = mag * jnp.sin(li * dt)
    den = lr * lr + li * li
    f_re = ((ab_re - 1.0) * lr + ab_im * li) / den
    f_im = (ab_im * lr - (ab_re - 1.0) * li) / den
    br = b_re.astype(f32)
    bi = b_im.astype(f32)
    bb_re = f_re[..., None] * br - f_im[..., None] * bi
    bb_im = f_re[..., None] * bi + f_im[..., None] * br
    x_re = jnp.einsum('gnc,blgc->blgn', bb_re, ug)
    x_im = jnp.einsum('gnc,blgc->blgn', bb_im, ug)
    h0r = h0_re.astype(f32)
    h0i = h0_im.astype(f32)
    x_re = x_re.at[:, 0].add(ab_re * h0r - ab_im * h0i)
    x_im = x_im.at[:, 0].add(ab_re * h0i + ab_im * h0r)
    a_re = jnp.broadcast_to(ab_re, (1, L) + ab_re.shape)
    a_im = jnp.broadcast_to(ab_im, (1, L) + ab_im.shape)

    def combine(e1, e2):
        a1r, a1i, b1r, b1i = e1
        a2r, a2i, b2r, b2i = e2
        return (a1r * a2r - a1i * a2i, a1r * a2i + a1i * a2r,
                a2r * b1r - a2i * b1i + b2r, a2r * b1i + a2i * b1r + b2i)

    _, _, h_re, h_im = lax.associative_scan(combine, (a_re, a_im, x_re, x_im), axis=1)
    y = (jnp.einsum('gcn,blgn->blgc', c_re.astype(f32), h_re)
         - jnp.einsum('gcn,blgn->blgc', c_im.astype(f32), h_im)).reshape(bsz, L, S5_WIDTH)
    y = y + d_skip.astype(f32) * uf
    z = jax.nn.gelu(y)
    out = z * jax.nn.sigmoid(z @ glu_w.astype(f32) + glu_b.astype(f32))
    return out, h_re[:, -1], h_im[:, -1]


def rwkv7_group_mixer(p, shift0, state0, mu, w0, w_up, a0, a_up, g_up, k_k, k_a, r_k, ln_w, ln_b):
    f32 = jnp.float32
    bsz, L, _ = p.shape
    pf = p.astype(f32)
    prev = jnp.concatenate([shift0.astype(f32)[:, None], pf[:, :-1]], axis=1)
    xm = pf + (prev - pf) * mu.astype(f32)
    W = RWKV_WIDTH
    r, k, v, xw, xa, xg = jnp.split(xm, [W, 2 * W, 3 * W, 3 * W + DECAY_RANK, 3 * W + DECAY_RANK + ICL_RANK], axis=-1)
    logw = -jax.nn.softplus(-(w0.astype(f32) + jnp.tanh(xw) @ w_up.astype(f32))) - 0.5
    decay = jnp.exp(-jnp.exp(logw))
    a = jax.nn.sigmoid(a0.astype(f32) + xa @ a_up.astype(f32))
    g = jax.nn.sigmoid(xg) @ g_up.astype(f32)
    hs = (bsz, L, RWKV_HEADS, RWKV_HEAD)
    kk = (k * k_k.astype(f32)).reshape(hs)
    kk = kk / jnp.maximum(jnp.sqrt(jnp.sum(kk * kk, axis=-1, keepdims=True)), 1e-12)
    k = k * (1.0 + (a - 1.0) * k_a.astype(f32))
    rh, kh, vh, wh, ah = (t.reshape(hs) for t in (r, k, v, decay, a))

    def step(S, inp):
        r_t, w_t, k_t, v_t, kk_t, a_t = inp
        sa = jnp.einsum('bhvk,bhk->bhv', S, -kk_t)
        S = (S * w_t[:, :, None, :] + sa[..., None] * (kk_t * a_t)[:, :, None, :]
             + v_t[..., None] * k_t[:, :, None, :])
        return S, jnp.einsum('bhvk,bhk->bhv', S, r_t)

    tm = lambda t: jnp.swapaxes(t, 0, 1)
    S_fin, ys = lax.scan(step, state0.astype(f32), (tm(rh), tm(wh), tm(kh), tm(vh), tm(kk), tm(ah)))
    y = tm(ys)
    mean = jnp.mean(y, axis=-1, keepdims=True)
    var = jnp.mean(jnp.square(y - mean), axis=-1, keepdims=True)
    yn = ((y - mean) * lax.rsqrt(var + RWKV_LN_EPS)).reshape(bsz, L, W) * ln_w.astype(f32) + ln_b.astype(f32)
    bonus = jnp.sum(rh * kh * r_k.astype(f32), axis=-1, keepdims=True) * vh
    out = (yn + bonus.reshape(bsz, L, W)) * g
    return out, S_fin, p[:, -1]


def rglru_group_mixer(xb, gb, conv0, h0, conv_w, conv_b, w_a, b_a, w_x, b_x, lam):
    f32 = jnp.float32
    bsz, L, _ = xb.shape
    xp = jnp.concatenate([conv0.astype(f32), xb.astype(f32)], axis=1)
    xc = conv_b.astype(f32) + xp[:, 0:L] * conv_w[0].astype(f32)
    for j in range(1, CONV_WIDTH):
        xc = xc + xp[:, j:j + L] * conv_w[j].astype(f32)
    xh = xc.reshape(bsz, L, LRU_BLOCKS, LRU_BLOCK)
    gate_a = jax.nn.sigmoid(jnp.einsum('blhi,hij->blhj', xh, w_a.astype(f32)).reshape(bsz, L, LRU_WIDTH) + b_a.astype(f32))
    gate_x = jax.nn.sigmoid(jnp.einsum('blhi,hij->blhj', xh, w_x.astype(f32)).reshape(bsz, L, LRU_WIDTH) + b_x.astype(f32))
    log_a = LRU_C * gate_a * jax.nn.log_sigmoid(lam.astype(f32))
    a = jnp.exp(log_a)
    b = jnp.sqrt(-jnp.expm1(2.0 * log_a)) * (gate_x * xc)
    b = b.at[:, 0].add(a[:, 0] * h0.astype(f32))

    def combine(e1, e2):
        a1, b1 = e1
        a2, b2 = e2
        return (a1 * a2, a2 * b1 + b2)

    _, h = lax.associative_scan(combine, (a, b), axis=1)
    out = h * jax.nn.gelu(gb.astype(f32))
    return out, h[:, -1], xp[:, -(CONV_WIDTH - 1):]


def layer_forward(x, s5_re0, s5_im0, rwkv0, shift0, lru0, conv0,
                  ffn1_norm, ffn1_w_gate, ffn1_w_up, ffn1_w_down, mix_norm, w_in,
                  s5_lambda_re, s5_lambda_im, s5_log_dt, s5_b_re, s5_b_im, s5_c_re, s5_c_im, s5_d, s5_glu_w, s5_glu_b,
                  rwkv_mu, rwkv_w0, rwkv_w_up, rwkv_a0, rwkv_a_up, rwkv_g_up, rwkv_k_k, rwkv_k_a, rwkv_r_k, rwkv_ln_w, rwkv_ln_b,
                  lru_conv_w, lru_conv_b, lru_w_a, lru_b_a, lru_w_x, lru_b_x, lru_lambda,
                  w_out, ffn2_norm, ffn2_w_gate, ffn2_w_up, ffn2_w_down):
    h = x + FFN_RES_SCALE * swiglu(rmsnorm(x, ffn1_norm), ffn1_w_gate, ffn1_w_up, ffn1_w_down)
    proj = rmsnorm(h, mix_norm) @ w_in
    u_s5, p_rwkv, x_lru, g_lru = jnp.split(
        proj, [S5_WIDTH, S5_WIDTH + RWKV_PROJ, S5_WIDTH + RWKV_PROJ + LRU_WIDTH], axis=-1)
    o_s5, s5_re, s5_im = s5_group_mixer(u_s5, s5_re0, s5_im0, s5_lambda_re, s5_lambda_im, s5_log_dt,
                                        s5_b_re, s5_b_im, s5_c_re, s5_c_im, s5_d, s5_glu_w, s5_glu_b)
    o_rwkv, rwkv_s, shift = rwkv7_group_mixer(p_rwkv, shift0, rwkv0, rwkv_mu, rwkv_w0, rwkv_w_up, rwkv_a0,
                                              rwkv_a_up, rwkv_g_up, rwkv_k_k, rwkv_k_a, rwkv_r_k, rwkv_ln_w, rwkv_ln_b)
    o_lru, lru_h, conv = rglru_group_mixer(x_lru, g_lru, conv0, lru0, lru_conv_w, lru_conv_b,
                                           lru_w_a, lru_b_a, lru_w_x, lru_b_x, lru_lambda)
    mix = jnp.concatenate([o_s5, o_rwkv, o_lru], axis=-1).astype(x.dtype) @ w_out
    h = h + mix
    h = h + FFN_RES_SCALE * swiglu(rmsnorm(h, ffn2_norm), ffn2_w_gate, ffn2_w_up, ffn2_w_down)
    return h, s5_re, s5_im, rwkv_s, shift, lru_h, conv


def setup_inputs(seed: int = 0) -> dict:
    key = jax.random.key(seed)
    ks = iter(jax.random.split(key, 64))
    f32 = jnp.float32

    def nrm(shape, scale):
        return scale * jax.random.normal(next(ks), shape, f32)

    def unif(shape, lo, hi):
        return jax.random.uniform(next(ks), shape, f32, lo, hi)

    L = DEPTH
    lam_im0 = jnp.pi * jnp.arange(S5_STATE, dtype=f32)
    s_lru = unif((L, LRU_WIDTH), 0.9, 0.999) ** (1.0 / LRU_C)
    return {
        'x_prompt': nrm((BATCH, SEQ, D_MODEL), 1.0),
        'x_sample': nrm((DEC_BATCH, DEC_SEQ, D_MODEL), 1.0),
        'state_s5_re': nrm((L, DEC_BATCH, S5_GROUPS, S5_STATE), 0.5),
        'state_s5_im': nrm((L, DEC_BATCH, S5_GROUPS, S5_STATE), 0.5),
        'state_rwkv': nrm((L, DEC_BATCH, RWKV_HEADS, RWKV_HEAD, RWKV_HEAD), 0.2),
        'state_rwkv_shift': nrm((L, DEC_BATCH, RWKV_PROJ), 1.0),
        'state_lru': nrm((L, DEC_BATCH, LRU_WIDTH), 0.5),
        'state_lru_conv': nrm((L, DEC_BATCH, CONV_WIDTH - 1, LRU_WIDTH), 1.0),
        'meta_tokens': nrm((N_META, D_MODEL), 1.0),
        'ffn1_norm': 1.0 + nrm((L, D_MODEL), 0.02),
        'ffn1_w_gate': nrm((L, D_MODEL, D_FF), D_MODEL ** -0.5),
        'ffn1_w_up': nrm((L, D_MODEL, D_FF), D_MODEL ** -0.5),
        'ffn1_w_down': nrm((L, D_FF, D_MODEL), D_FF ** -0.5),
        'mix_norm': 1.0 + nrm((L, D_MODEL), 0.02),
        'w_in': nrm((L, D_MODEL, D_IN), D_MODEL ** -0.5),
        's5_lambda_re': -0.5 + nrm((L, S5_GROUPS, S5_STATE), 0.01),
        's5_lambda_im': lam_im0 + nrm((L, S5_GROUPS, S5_STATE), 0.01),
        's5_log_dt': unif((L, S5_GROUPS), math.log(1e-3), math.log(1e-1)),
        's5_b_re': nrm((L, S5_GROUPS, S5_STATE, S5_GROUP), (2 * S5_GROUP) ** -0.5),
        's5_b_im': nrm((L, S5_GROUPS, S5_STATE, S5_GROUP), (2 * S5_GROUP) ** -0.5),
        's5_c_re': nrm((L, S5_GROUPS, S5_GROUP, S5_STATE), (2 * S5_STATE) ** -0.5),
        's5_c_im': nrm((L, S5_GROUPS, S5_GROUP, S5_STATE), (2 * S5_STATE) ** -0.5),
        's5_d': nrm((L, S5_WIDTH), 0.5),
        's5_glu_w': nrm((L, S5_WIDTH, S5_WIDTH), S5_WIDTH ** -0.5),
        's5_glu_b': nrm((L, S5_WIDTH), 0.01),
        'rwkv_mu': unif((L, RWKV_PROJ), 0.0, 1.0),
        'rwkv_w0': unif((L, RWKV_WIDTH), -6.0, -1.0),
        'rwkv_w_up': nrm((L, DECAY_RANK, RWKV_WIDTH), 0.1),
        'rwkv_a0': nrm((L, RWKV_WIDTH), 0.1),
        'rwkv_a_up': nrm((L, ICL_RANK, RWKV_WIDTH), ICL_RANK ** -0.5),
        'rwkv_g_up': nrm((L, GATE_RANK, RWKV_WIDTH), GATE_RANK ** -0.5),
        'rwkv_k_k': 0.85 + nrm((L, RWKV_WIDTH), 0.02),
        'rwkv_k_a': 1.0 + nrm((L, RWKV_WIDTH), 0.02),
        'rwkv_r_k': nrm((L, RWKV_HEADS, RWKV_HEAD), 0.1),
        'rwkv_ln_w': 1.0 + nrm((L, RWKV_WIDTH), 0.02),
        'rwkv_ln_b': nrm((L, RWKV_WIDTH), 0.01),
        'lru_conv_w': nrm((L, CONV_WIDTH, LRU_WIDTH), CONV_WIDTH ** -0.5),
        'lru_conv_b': nrm((L, LRU_WIDTH), 0.01),
        'lru_w_a': nrm((L, LRU_BLOCKS, LRU_BLOCK, LRU_BLOCK), LRU_BLOCK ** -0.5),
        'lru_b_a': nrm((L, LRU_WIDTH), 0.01),
        'lru_w_x': nrm((L, LRU_BLOCKS, LRU_BLOCK, LRU_BLOCK), LRU_BLOCK ** -0.5),
        'lru_b_x': nrm((L, LRU_WIDTH), 0.01),
        'lru_lambda': jnp.log(s_lru) - jnp.log1p(-s_lru),
        'w_out': nrm((L, D_MIX, D_MODEL), D_MIX ** -0.5),
        'ffn2_norm': 1.0 + nrm((L, D_MODEL), 0.02),
        'ffn2_w_gate': nrm((L, D_MODEL, D_FF), D_MODEL ** -0.5),
        'ffn2_w_up': nrm((L, D_MODEL, D_FF), D_MODEL ** -0.5),
        'ffn2_w_down': nrm((L, D_FF, D_MODEL), D_FF ** -0.5),
        'final_norm': 1.0 + nrm((D_MODEL,), 0.02),
    }


def reference(x_prompt, x_sample, state_s5_re, state_s5_im, state_rwkv, state_rwkv_shift, state_lru, state_lru_conv,
              meta_tokens, ffn1_norm, ffn1_w_gate, ffn1_w_up, ffn1_w_down, mix_norm, w_in,
              s5_lambda_re, s5_lambda_im, s5_log_dt, s5_b_re, s5_b_im, s5_c_re, s5_c_im, s5_d, s5_glu_w, s5_glu_b,
              rwkv_mu, rwkv_w0, rwkv_w_up, rwkv_a0, rwkv_a_up, rwkv_g_up, rwkv_k_k, rwkv_k_a, rwkv_r_k, rwkv_ln_w, rwkv_ln_b,
              lru_conv_w, lru_conv_b, lru_w_a, lru_b_a, lru_w_x, lru_b_x, lru_lambda,
              w_out, ffn2_norm, ffn2_w_gate, ffn2_w_up, ffn2_w_down, final_norm):
    f32 = jnp.float32
    bp = x_prompt.shape[0]
    meta = jnp.broadcast_to(meta_tokens.astype(x_prompt.dtype)[None], (bp, N_META, D_MODEL))
    xp = jnp.concatenate([meta, x_prompt], axis=1)
    xs = x_sample
    p_list = []
    s_list = []
    for l in range(DEPTH):
        lp = (ffn1_norm[l], ffn1_w_gate[l], ffn1_w_up[l], ffn1_w_down[l], mix_norm[l], w_in[l],
              s5_lambda_re[l], s5_lambda_im[l], s5_log_dt[l], s5_b_re[l], s5_b_im[l], s5_c_re[l], s5_c_im[l],
              s5_d[l], s5_glu_w[l], s5_glu_b[l],
              rwkv_mu[l], rwkv_w0[l], rwkv_w_up[l], rwkv_a0[l], rwkv_a_up[l], rwkv_g_up[l], rwkv_k_k[l],
              rwkv_k_a[l], rwkv_r_k[l], rwkv_ln_w[l], rwkv_ln_b[l],
              lru_conv_w[l], lru_conv_b[l], lru_w_a[l], lru_b_a[l], lru_w_x[l], lru_b_x[l], lru_lambda[l],
              w_out[l], ffn2_norm[l], ffn2_w_gate[l], ffn2_w_up[l], ffn2_w_down[l])
        xp, *new_p = layer_forward(
            xp,
            jnp.zeros((bp, S5_GROUPS, S5_STATE), f32), jnp.zeros((bp, S5_GROUPS, S5_STATE), f32),
            jnp.zeros((bp, RWKV_HEADS, RWKV_HEAD, RWKV_HEAD), f32), jnp.zeros((bp, RWKV_PROJ), f32),
            jnp.zeros((bp, LRU_WIDTH), f32), jnp.zeros((bp, CONV_WIDTH - 1, LRU_WIDTH), f32),
            *lp)
        xs, *new_s = layer_forward(
            xs, state_s5_re[l], state_s5_im[l], state_rwkv[l], state_rwkv_shift[l],
            state_lru[l], state_lru_conv[l], *lp)
        p_list.append(new_p)
        s_list.append(new_s)
    p_new = [jnp.stack([st[i] for st in p_list], axis=0) for i in range(6)]
    s_new = [jnp.stack([st[i] for st in s_list], axis=0) for i in range(6)]
    y_prompt = rmsnorm(xp, final_norm)[:, N_META:]
    y_sample = rmsnorm(xs, final_norm)
    return (y_prompt, y_sample,
            p_new[0], p_new[1], p_new[2], p_new[3], p_new[4], p_new[5],
            s_new[0], s_new[1], s_new[2], s_new[3], s_new[4], s_new[5])
```

```python
import numpy as np
import os
_RWSTOP = int(os.environ.get('RW_STOP', '99'))
_NOSAMP = int(os.environ.get('NO_SAMP', '0'))
from contextlib import ExitStack
import concourse.bass as bass
import concourse.mybir as mybir
from concourse.bass_utils import run_bass_kernel_spmd

F32 = mybir.dt.float32
BF16 = mybir.dt.bfloat16
ALU = mybir.AluOpType
AF = mybir.ActivationFunctionType
AX = mybir.AxisListType

NCORES = 8
D = 1024
KC = 8
DFF = 2816
NJ = 22
DIN = 2304
L = 2
NCOL = 2080
SLOT = 3072
NSLOT = 9
GRP = 4
EPS = 1e-6
TILES = [(0, 32)] + [(32 + 512 * i, 512) for i in range(4)]


class Em:
    ENG = ('pe', 'dve', 'act', 'pool', 'sp')

    def __init__(self, nc, es):
        self.nc = nc
        self.es = es
        self.e = dict(pe=nc.tensor, dve=nc.vector, act=nc.scalar, pool=nc.gpsimd, sp=nc.sync)
        self.sem = {k: es.enter_context(nc.semaphore("s_" + k)) for k in ('pe', 'dve', 'act', 'pool')}
        self.cnt = {k: 0 for k in self.sem}
        self.waited = {k: {} for k in self.ENG}
        self.buf = {}
        self.dsem = {}
        self.ninstr = 0
        self.alias = {}

    def _handle(self, sk):
        return self.sem[sk] if sk in self.sem else self.dsem[sk][0]

    def _wait(self, eng, ev):
        if ev is None:
            return
        sk, val = ev
        if sk == 'pe' and eng == 'pe':
            return
        if self.waited[eng].get(sk, 0) >= val:
            return
        self.waited[eng][sk] = val
        self.e[eng].wait_ge(self._handle(sk), val)

    def deps(self, eng, reads, writes):
        reads = [self.alias.get(k, k) for k in reads]
        writes = [self.alias.get(k, k) for k in writes]
        for r in reads:
            b = self.buf.get(r)
            if b:
                self._wait(eng, b[0])
        for w in writes:
            b = self.buf.get(w)
            if b:
                self._wait(eng, b[0])
                for ev in b[1].values():
                    self._wait(eng, ev)

    def record(self, ev, reads, writes):
        reads = [self.alias.get(k, k) for k in reads]
        writes = [self.alias.get(k, k) for k in writes]
        for r in reads:
            b = self.buf.setdefault(r, [None, {}])
            b[1][ev[0]] = ev
        for w in writes:
            self.buf[w] = [ev, {}]

    def op(self, eng, fn, reads=(), writes=(), inc=True):
        psr = [k for k in reads if k.startswith('ps')]
        if psr:
            reads = [k for k in reads if not k.startswith('ps')]
            writes = list(writes) + psr
        self.deps(eng, reads, writes)
        ins = fn(self.e[eng])
        self.ninstr += 1
        if inc:
            self.cnt[eng] += 1
            ins.then_inc(self.sem[eng], 1)
            ev = (eng, self.cnt[eng])
        else:
            ev = (eng, self.cnt[eng] + 1)
        self.record(ev, reads, writes)
        return ins

    def dma(self, q, out, in_, name, reads=(), writes=(), **kw):
        if name not in self.dsem:
            self.dsem[name] = [self.es.enter_context(self.nc.semaphore("d_" + name)), 0]
        self.deps(q, reads, writes)
        d = self.dsem[name]
        d[1] += 16
        self.e[q].dma_start(out=out, in_=in_, **kw).then_inc(d[0], 16)
        self.ninstr += 1
        self.record((name, d[1]), reads, writes)

    def dma_batch(self, q, items, name):
        if name not in self.dsem:
            self.dsem[name] = [self.es.enter_context(self.nc.semaphore("d_" + name)), 0]
        d = self.dsem[name]
        for (out, in_, reads, writes) in items:
            self.deps(q, reads, writes)
        final = d[1] + 16 * len(items)
        for (out, in_, reads, writes) in items:
            d[1] += 16
            self.e[q].dma_start(out=out, in_=in_).then_inc(d[0], 16)
            self.ninstr += 1
            self.record((name, final), reads, writes)

    def barrier(self):
        for eng in self.ENG:
            for name, d in self.dsem.items():
                self._wait(eng, (name, d[1]))
            for k in self.sem:
                if self.cnt[k]:
                    self._wait(eng, (k, self.cnt[k]))

    def finish(self, eng='sp'):
        for name, d in self.dsem.items():
            self._wait(eng, (name, d[1]))
        for k in self.sem:
            if self.cnt[k]:
                self._wait(eng, (k, self.cnt[k]))


PI2 = 6.283185307179586
NB = 64
NSEQ = 17


def build(stage=99, PT=2048, mixers=('lru', 's5', 'rwkv')):
    NCOL_ = 32 + PT
    tiles = [(0, 32)] + [(32 + 512 * i, min(512, PT - 512 * i)) for i in range((PT + 511) // 512)]
    NT = len(tiles)
    nc = bass.Bass("TRN2", target_bir_lowering=False)
    es = ExitStack()
    with es:
        es.enter_context(nc.allow_non_contiguous_dma(reason="small strided state io"))
        em = Em(nc, es)

        def din(name, shape, dt=F32):
            return nc.dram_tensor(name, list(shape), dt, kind="ExternalInput").ap()

        def dout(name, shape, dt=F32):
            return nc.dram_tensor(name, list(shape), dt, kind="ExternalOutput").ap()

        def sbuf(stack, name, shape, dt=F32):
            return stack.enter_context(nc.sbuf_tensor(name, list(shape), dt))

        xT = din("xT", [128, KC, NCOL_])
        wffn = din("wffn", [L, 2, NJ, 128, SLOT])
        w_in = din("w_in", [L, 128, KC * DIN])
        w_out = din("w_out", [L, 128, KC * D])
        norms = din("norms", [128, 7, KC])
        consts = din("consts", [128, 400])
        cmask = din("cmask", [64, 4, 64])
        s5A = din("s5A", [L, 128, 12, 3])
        s5B = din("s5B", [L, 128, 3, 64, 3])
        s5bT = din("s5bT", [L, 128, 3, 64, 2])
        s5cT = din("s5cT", [L, 128, 12, 16, 2])
        s5v = din("s5v", [L, 128, 3, 2])
        s5glu = din("s5glu", [L, 128, 3, 384])
        rwp = din("rwp", [L, 128, 3, 8])
        rwmu = din("rwmu", [L, 128, 11])
        rwup = din("rwup", [L, 128, 2, 384])
        lrp = din("lrp", [L, 128, 2, 8])
        lrw = din("lrw", [L, 128, 2, 2, 128])
        i_s5 = din("i_s5", [L, 16, 128, 12, 2])
        i_rwkv = din("i_rwkv", [L, 16, 128, 3, 64])
        i_shift = din("i_shift", [L, 16, 128, 11])
        i_lru = din("i_lru", [L, 16, 128, 2])
        i_conv = din("i_conv", [L, 16, 128, 2, 3])
        yT = dout("yT", [128, KC, NCOL_])
        o_s5 = dout("o_s5", [L, NSEQ, 128, 12, 2])
        o_rwkv = dout("o_rwkv", [L, NSEQ, 128, 3, 64])
        o_shift = dout("o_shift", [L, NSEQ, 128, 11])
        o_lru = dout("o_lru", [L, NSEQ, 128, 2])
        o_conv = dout("o_conv", [L, NSEQ, 128, 2, 3])

        xres = sbuf(es, "xres", [128, KC, NCOL_])
        ring = sbuf(es, "ring", [128, NSLOT * SLOT], BF16)
        normw = sbuf(es, "normw", [128, 7, KC])
        ones_bf = sbuf(es, "ones_bf", [128, 128], BF16)
        cst = sbuf(es, "cst", [128, 400])
        ps = [es.enter_context(nc.psum_tensor("ps%d" % i, [128, 512], F32)) for i in range(8)]
        ident = cst[:, 0:128]
        bones = cst[:, 128:256]
        iota1 = cst[:, 256:320]
        mask8 = cst[:, 320:328]
        maskg2 = cst[:, 328:330]
        identb = cst[:, 330:394]

        em.op('pool', lambda e: e.memset(ones_bf[:], 1.0), writes=['ones_bf'])
        em.dma('sp', normw[:], norms, 'ld_c', writes=['normw'])
        em.dma('sp', cst[:], consts, 'ld_c2', writes=['cst'])
        em.dma_batch('sp', [(xres[:, kc, :], xT[:, kc, :], [], ['xres']) for kc in range(KC)], 'ld_x')

        def T(eng, out, a, b, op, R, W):
            em.op(eng, lambda e: e.tensor_tensor(out=out, in0=a, in1=b, op=op), reads=R, writes=W)

        def S(eng, out, a, s1, s2, op0, op1, R, W):
            em.op(eng, lambda e: e.tensor_scalar(out=out, in0=a, scalar1=s1, scalar2=s2, op0=op0, op1=op1), reads=R, writes=W)

        def STT(out, a, sc, b, op0, op1, R, W):
            em.op('dve', lambda e: e.scalar_tensor_tensor(out=out, in0=a, scalar=sc, in1=b, op0=op0, op1=op1), reads=R, writes=W)

        def A(out, in_, func, R, W, bias=0.0, scale=1.0):
            em.op('act', lambda e: e.activation(out=out, in_=in_, func=func, bias=bias, scale=scale), reads=R, writes=W)

        def CP(eng, out, in_, R, W):
            if eng == 'act':
                em.op('act', lambda e: e.copy(out=out, in_=in_), reads=R, writes=W)
            else:
                em.op(eng, lambda e: e.tensor_copy(out=out, in_=in_), reads=R, writes=W)

        def MM(out, lhsT, rhs, R, W, start=True, stop=True):
            em.op('pe', lambda e: e.matmul(out, lhsT=lhsT, rhs=rhs, start=start, stop=stop), reads=R, writes=W, inc=stop)

        def TR(out, in_, idn, R, W):
            em.op('pe', lambda e: e.transpose(out=out, in_=in_, identity=idn), reads=R + ['cst'], writes=W)

        bank = {'A': 0, 'B': 0}

        def nb(pool='B'):
            b = bank[pool]
            bank[pool] = (bank[pool] + 1) % 4
            return b + (0 if pool == 'A' else 4)

        def load_slot(s, src):
            dst = ring[:, s * SLOT:(s + 1) * SLOT]
            em.dma('pool', dst.rearrange("p (a b) -> p a b", b=1024), src.rearrange("p (a b) -> p a b", b=1024),
                   'ring%d' % s, writes=['ring%d' % s])

        def ffn(l, f):
            fs = ExitStack()
            with fs:
                xn = sbuf(fs, "xn%d%d" % (l, f), [128, KC, NCOL_], BF16)
                hbuf = sbuf(fs, "hbuf%d%d" % (l, f), [128, 2, GRP, 512], BF16)
                sq = sbuf(fs, "sq%d%d" % (l, f), [128, KC, 512], BF16)
                rstd = sbuf(fs, "rstd%d%d" % (l, f), [128, 512])
                silu = sbuf(fs, "silu%d%d" % (l, f), [128, 2, 512])
                nidx = (0 if f == 0 else 4) + l
                for ti in range(NT):
                    c0, n = tiles[ti]
                    A(sq[:, :, :n], xres[:, :, c0:c0 + n], AF.Square, ['xres'], ['sq'])
                    for kc in range(KC):
                        MM(ps[7][:, :n], ones_bf[:], sq[:, kc, :n], ['ones_bf', 'sq'], ['ps7'], start=(kc == 0), stop=(kc == KC - 1))
                    S('dve', rstd[:, :n], ps[7][:, :n], 1.0 / D, EPS, ALU.mult, ALU.add, ['ps7'], ['rstd'])
                    A(rstd[:, :n], rstd[:, :n], AF.Sqrt, ['rstd'], ['rstd'])
                    em.op('dve', lambda e: e.reciprocal(out=rstd[:, :n], in_=rstd[:, :n]), reads=['rstd'], writes=['rstd'])
                    for kc in range(KC):
                        STT(xn[:, kc, c0:c0 + n], xres[:, kc, c0:c0 + n], normw[:, nidx, kc:kc + 1], rstd[:, :n],
                            ALU.mult, ALU.mult, ['xres', 'rstd', 'normw'], ['xn%d' % ti])
                groups = [list(range(g, min(g + GRP, NJ))) for g in range(0, NJ, GRP)]
                slot_of = {}
                nxt = [0]

                def issue_group(gi):
                    for j in groups[gi]:
                        s_ = nxt[0] % 8
                        nxt[0] += 1
                        slot_of[j] = s_
                        load_slot(s_, wffn[l, f, j])

                issue_group(0)
                for gi, grp in enumerate(groups):
                    if gi + 1 < len(groups):
                        issue_group(gi + 1)

                    def gu(ti):
                        c0, n = tiles[ti]
                        hb = ti % 2
                        for ji, j in enumerate(grp):
                            s_ = slot_of[j]
                            for which in (0, 1):
                                b = 2 * (ji % 2) + which
                                for kc in range(KC):
                                    o = s_ * SLOT + which * 1024 + kc * 128
                                    MM(ps[b][:, :n], ring[:, o:o + 128], xn[:, kc, c0:c0 + n], ['ring%d' % s_, 'xn%d' % ti],
                                       ['ps%d' % b], start=(kc == 0), stop=(kc == KC - 1))
                            b = 2 * (ji % 2)
                            A(silu[:, ji % 2, :n], ps[b][:, :n], AF.Silu, ['ps%d' % b], ['silu%d' % (ji % 2)])
                            T('dve', hbuf[:, hb, ji, :n], silu[:, ji % 2, :n], ps[b + 1][:, :n], ALU.mult,
                              ['silu%d' % (ji % 2), 'ps%d' % (b + 1)], ['h%d_%d' % (hb, ji)])

                    def down(ti):
                        c0, n = tiles[ti]
                        hb = ti % 2
                        for mc in range(KC):
                            b = 4 + mc % 3
                            for ji, j in enumerate(grp):
                                o = slot_of[j] * SLOT + 2048 + mc * 128
                                MM(ps[b][:, :n], ring[:, o:o + 128], hbuf[:, hb, ji, :n], ['ring%d' % slot_of[j], 'h%d_%d' % (hb, ji)],
                                   ['ps%d' % b], start=(ji == 0), stop=(ji == len(grp) - 1))
                            STT(xres[:, mc, c0:c0 + n], ps[b][:, :n], 0.5, xres[:, mc, c0:c0 + n], ALU.mult, ALU.add,
                                ['ps%d' % b, 'xres'], ['xres'])

                    gu(0)
                    for ti in range(NT):
                        if ti + 1 < NT:
                            gu(ti + 1)
                        down(ti)
                em.barrier()

        def mixer_phase(l):
            bs = ExitStack()
            with bs:
                def sb(name, shape, dt=F32):
                    return sbuf(bs, name + "_%d" % l, shape, dt)
                for i in range(3):
                    em.dma('pool', ring[:, i * 6144:(i + 1) * 6144].rearrange("p (a b) -> p a b", b=2048),
                           w_in[l][:, i * 6144:(i + 1) * 6144].rearrange("p (a b) -> p a b", b=2048), 'win%d' % i,
                           writes=['ring%d' % (2 * i), 'ring%d' % (2 * i + 1)])
                em.dma('pool', ring[:, 18432:26624].rearrange("p (a b) -> p a b", b=2048),
                       w_out[l].rearrange("p (a b) -> p a b", b=2048), 'wout', writes=['ring6', 'ring7', 'ring8'])
                RIN = ['ring%d' % i for i in range(6)]
                ROUT = ['ring6', 'ring7', 'ring8']

                xnb = sb("xnb", [128, KC, NB], BF16)
                sqb = sb("sqb", [128, KC, NB], BF16)
                rstb = sb("rstb", [128, NB])
                pj = sb("pj", [128, 18, 3 + NB])
                tmpc = sb("tmpc", [128, 18, 3])
                omix = sb("omix", [128, KC, NB], BF16)
                g1_def = g2_def = None
                lg1 = sb("lg1", [128, 2, NB])
                lg2 = sb("lg2", [128, 2, NB])
                s5st = sb("s5st", [128, 16, 12, 2])
                lst = sb("lst", [128, 16, 2])
                cst16 = sb("cst16", [128, 16, 2, 3])
                sh16 = sb("sh16", [128, 16, 11])
                h0v = lambda ri: s5st[:, :, :, ri].rearrange("p b g -> p g b")
                lh0v = lst[:].rearrange("p b c -> p c b")
                convv = lambda j: cst16[:, :, :, j].rearrange("p b c -> p c b")
                prevv = sh16[:].rearrange("p b c -> p c b")
                cm = sb("cm", [64, 4, 64])
                em.dma('sp', cm[:], cmask, 'ld_cm', writes=['cm'])
                MSU, MU, MSL, I6 = cm[:, 0, :], cm[:, 1, :], cm[:, 2, :], cm[:, 3, :]

                def gelu(dst, src, k, n, R, W, g1=None, g2_=None, kk_='g'):
                    g1 = g1 if g1 is not None else g1_def
                    g2_ = g2_ if g2_ is not None else g2_def
                    k1, k2 = kk_ + '1', kk_ + '2'
                    A(g1[:, :k, :n], src, AF.Square, R, [k1])
                    S('dve', g1[:, :k, :n], g1[:, :k, :n], 0.044715, 1.0, ALU.mult, ALU.add, [k1], [k1])
                    T('dve', g1[:, :k, :n], g1[:, :k, :n], src, ALU.mult, [k1] + R, [k1])
                    A(g2_[:, :k, :n], g1[:, :k, :n], AF.Sigmoid, [k1], [k2], scale=1.5957691216057308)
                    T('dve', dst, g2_[:, :k, :n], src, ALU.mult, [k2] + R, W)

                lp = sb("lp", [128, 2, 8])
                lw = sb("lw", [128, 2, 2, 128])
                lcl = sb("lcl", [128, 2])
                em.dma('sp', lp[:], lrp[l], 'ld_lp', writes=['lp'])
                em.dma('sp', lw[:], lrw[l], 'ld_lw', writes=['lw'])
                A(lcl[:], lp[:, :, 7], AF.Sigmoid, ['lp'], ['lcl'])
                A(lcl[:], lcl[:], AF.Ln, ['lcl'], ['lcl'])
                S('dve', lcl[:], lcl[:], 8.0, None, ALU.mult, ALU.bypass, ['lcl'], ['lcl'])
                lru_h = sb("lru_h", [128, 2])
                lxc = sb("lxc", [128, 2, NB])
                lga = sb("lga", [128, 2, NB])
                lgx = sb("lgx", [128, 2, NB])
                la = sb("la", [128, 2, NB])
                lb = sb("lb", [128, 2, NB])
                lh = sb("lh", [128, 2, NB])

                def lru_tile(n, samp=False):
                    gelu(lg1[:, :, :n], pj[:, 16:18, 3:3 + n], 2, n, ['pj'], ['lg1'], lg1, lg2, 'lg')
                    yield
                    for c in range(2):
                        S('dve', lxc[:, c, :n], pj[:, 14 + c, 3:3 + n], lp[:, c, 3:4], lp[:, c, 4:5], ALU.mult, ALU.add,
                          ['pj', 'lp'], ['lxc'])
                        yield
                        for j in range(3):
                            STT(lxc[:, c, :n], (convv(j)[:, c, :] if samp else pj[:, 14 + c, j:j + n]), lp[:, c, j:j + 1], lxc[:, c, :n],
                                ALU.mult, ALU.add, ['pj', 'lp', 'lxc', 'cst16'], ['lxc'])
                        yield
                    yield
                    b = nb()
                    yield
                    for c in range(2):
                        MM(ps[b][:, c * 64:c * 64 + n], lw[:, c, 0, :], lxc[:, c, :n], ['lw', 'lxc'], ['ps%d' % b])
                        yield
                        MM(ps[b][:, 128 + c * 64:128 + c * 64 + n], lw[:, c, 1, :], lxc[:, c, :n], ['lw', 'lxc'], ['ps%d' % b])
                        yield
                    yield
                    for c in range(2):
                        A(lga[:, c, :n], ps[b][:, c * 64:c * 64 + n], AF.Sigmoid, ['ps%d' % b, 'lp'], ['lga'], bias=lp[:, c, 5:6])
                        yield
                        A(lgx[:, c, :n], ps[b][:, 128 + c * 64:128 + c * 64 + n], AF.Sigmoid, ['ps%d' % b, 'lp'], ['lgx'], bias=lp[:, c, 6:7])
                        yield
                        A(la[:, c, :n], lga[:, c, :n], AF.Exp, ['lga', 'lcl'], ['la'], scale=lcl[:, c:c + 1])
                        yield
                    yield
                    T('dve', lb[:, :, :n], la[:, :, :n], la[:, :, :n], ALU.mult, ['la'], ['lb'])
                    yield
                    S('dve', lb[:, :, :n], lb[:, :, :n], -1.0, 1.0, ALU.mult, ALU.add, ['lb'], ['lb'])
                    yield
                    A(lb[:, :, :n], lb[:, :, :n], AF.Sqrt, ['lb'], ['lb'])
                    yield
                    T('dve', lb[:, :, :n], lb[:, :, :n], lgx[:, :, :n], ALU.mult, ['lb', 'lgx'], ['lb'])
                    yield
                    T('dve', lb[:, :, :n], lb[:, :, :n], lxc[:, :, :n], ALU.mult, ['lb', 'lxc'], ['lb'])
                    yield
                    if samp:
                        T('dve', lh[:, :, :n], la[:, :, :n], lh0v, ALU.mult, ['la', 'lst'], ['lh'])
                        T('dve', lh[:, :, :n], lh[:, :, :n], lb[:, :, :n], ALU.add, ['lh', 'lb'], ['lh'])
                        CP('dve', lh0v, lh[:, :, :n], ['lh'], ['lst'])
                        CP('dve', convv(0), convv(1), ['cst16', 'lxc'], ['cst16'])
                        CP('dve', convv(1), convv(2), ['cst16'], ['cst16'])
                        CP('dve', convv(2), pj[:, 14:16, 3:3 + n], ['pj'], ['cst16'])
                    else:
                        for c in range(2):
                            em.op('dve', lambda e, c=c: e.tensor_tensor_scan(out=lh[:, c, :n], data0=la[:, c, :n], data1=lb[:, c, :n],
                                                                          initial=lru_h[:, c:c + 1], op0=ALU.mult, op1=ALU.add),
                                  reads=['la', 'lb', 'lru_h'], writes=['lh'])
                        CP('dve', lru_h[:], lh[:, :, n - 1], ['lh'], ['lru_h'])
                    yield
                    T('dve', omix[:, 6:8, :n], lh[:, :, :n], lg1[:, :, :n], ALU.mult, ['lh', 'lg1'], ['omix_l'])
                    yield

                hst = sb("hst", [128, 12, 2])
                if 's5' in mixers:
                    sv = sb("sv", [128, 3, 2])
                    sglu = sb("sglu", [128, 3, 384], BF16)
                    szb = sb("szb", [128, 3, NB], BF16)
                    lB = sb("lB", [128, 2, 12, 128], BF16)
                    lC = sb("lC", [128, 2, 12, 128], BF16)
                    rmag = sb("rmag", [128, 12])
                    Et = sb("Et", [128, 2, 12, NB])
                    abar = sb("abar", [128, 12, 2])
                    ss = ExitStack()
                    ss.__enter__()
                    sA = sbuf(ss, "sA_%d" % l, [128, 12, 3])
                    sBp = sbuf(ss, "sBp_%d" % l, [128, 3, 64, 3])
                    sbT = sbuf(ss, "sbT_%d" % l, [128, 3, 64, 2])
                    scT = sbuf(ss, "scT_%d" % l, [128, 12, 16, 2])
                    em.dma('sp', sA[:], s5A[l], 'ld_s5a', writes=['sA'])
                    em.dma('sp', sBp[:], s5B[l], 'ld_s5b', writes=['sBp'])
                    em.dma('sp', sbT[:], s5bT[l], 'ld_s5c', writes=['sbT'])
                    em.dma('sp', scT[:], s5cT[l], 'ld_s5d', writes=['scT'])
                    em.dma('sp', sv[:], s5v[l], 'ld_s5e', writes=['sv'])
                    em.dma('pool', sglu[:], s5glu[l], 'ld_s5f', writes=['sglu'])
                    wk = sbuf(ss, "wk_%d" % l, [128, 7, 64])
                    wk7 = sbuf(ss, "wk7_%d" % l, [128, 768])
                    wk8 = sbuf(ss, "wk8_%d" % l, [128, 768])
                    wki = sbuf(ss, "wki_%d" % l, [128, 768], mybir.dt.int32)

                    def rr(x, m, k, R, W):
                        tmp_ = wk[:, k, :m] if m <= 64 else wk8[:, :m]
                        CP('dve', wki[:, :m], x, R, ['wki'])
                        CP('dve', tmp_, wki[:, :m], ['wki'], ['wk%d' % k])
                        T('dve', x, x, tmp_, ALU.subtract, R + ['wk%d' % k], W)
                        S('dve', tmp_, x, 0.5, None, ALU.is_gt, ALU.bypass, W, ['wk%d' % k])
                        T('dve', x, x, tmp_, ALU.subtract, W + ['wk%d' % k], W)
                        S('dve', tmp_, x, -0.5, None, ALU.is_lt, ALU.bypass, W, ['wk%d' % k])
                        T('dve', x, x, tmp_, ALU.add, W + ['wk%d' % k], W)

                    def disc(lr, li, ldt, m, pre):
                        A(wk[:, 4, :m], ldt, AF.Exp, [pre], ['wk4'])
                        T('dve', wk[:, 0, :m], lr, wk[:, 4, :m], ALU.mult, [pre, 'wk4'], ['wk0'])
                        A(wk[:, 0, :m], wk[:, 0, :m], AF.Exp, ['wk0'], ['wk0'])
                        T('dve', wk[:, 3, :m], li, wk[:, 4, :m], ALU.mult, [pre, 'wk4'], ['wk3'])
                        S('dve', wk[:, 3, :m], wk[:, 3, :m], 1.0 / PI2, None, ALU.mult, ALU.bypass, ['wk3'], ['wk3'])
                        rr(wk[:, 3, :m], m, 5, ['wk3'], ['wk3'])
                        A(wk[:, 2, :m], wk[:, 3, :m], AF.Sin, ['wk3'], ['wk2'], scale=PI2 * 0.999999)
                        S('dve', wk[:, 6, :m], wk[:, 3, :m], 0.25, None, ALU.add, ALU.bypass, ['wk3'], ['wk6'])
                        rr(wk[:, 6, :m], m, 5, ['wk6'], ['wk6'])
                        A(wk[:, 1, :m], wk[:, 6, :m], AF.Sin, ['wk6'], ['wk1'], scale=PI2 * 0.999999)
                        T('dve', wk[:, 1, :m], wk[:, 1, :m], wk[:, 0, :m], ALU.mult, ['wk1', 'wk0'], ['wk1'])
                        T('dve', wk[:, 2, :m], wk[:, 2, :m], wk[:, 0, :m], ALU.mult, ['wk2', 'wk0'], ['wk2'])

                    disc(sA[:, :, 0], sA[:, :, 1], sA[:, :, 2], 12, 'sA')
                    CP('dve', abar[:, :, 0], wk[:, 1, :12], ['wk1'], ['abar'])
                    CP('dve', abar[:, :, 1], wk[:, 2, :12], ['wk2'], ['abar'])
                    CP('dve', rmag[:], wk[:, 0, :12], ['wk0'], ['Rt'])
                    wkE = wk7[:, :].rearrange("p (a b) -> p a b", b=NB)
                    T('dve', wkE, wk[:, 3, :12].unsqueeze(2).to_broadcast([128, 12, NB]),
                      iota1.unsqueeze(1).to_broadcast([128, 12, NB]), ALU.mult, ['wk3', 'cst'], ['wk7'])
                    rr(wk7[:, :], 768, 5, ['wk7'], ['wk7'])
                    A(Et[:, 1, :, :], wkE, AF.Sin, ['wk7'], ['Et'], scale=-PI2 * 0.999999)
                    S('dve', wk7[:, :], wk7[:, :], 0.25, None, ALU.add, ALU.bypass, ['wk7'], ['wk7'])
                    rr(wk7[:, :], 768, 5, ['wk7'], ['wk7'])
                    A(Et[:, 0, :, :], wkE, AF.Sin, ['wk7'], ['Et'], scale=PI2 * 0.999999)
                    for kc in range(3):
                        lr, li = sBp[:, kc, :, 0], sBp[:, kc, :, 1]
                        disc(lr, li, sBp[:, kc, :, 2], 64, 'sBp')
                        m = 64
                        T('dve', wk[:, 4, :m], lr, lr, ALU.mult, ['sBp'], ['wk4'])
                        T('dve', wk[:, 5, :m], li, li, ALU.mult, ['sBp'], ['wk5'])
                        T('dve', wk[:, 4, :m], wk[:, 4, :m], wk[:, 5, :m], ALU.add, ['wk4', 'wk5'], ['wk4'])
                        em.op('dve', lambda e: e.reciprocal(out=wk[:, 4, :64], in_=wk[:, 4, :64]), reads=['wk4'], writes=['wk4'])
                        S('dve', wk[:, 1, :m], wk[:, 1, :m], -1.0, None, ALU.add, ALU.bypass, ['wk1'], ['wk1'])
                        T('dve', wk[:, 5, :m], wk[:, 1, :m], lr, ALU.mult, ['wk1', 'sBp'], ['wk5'])
                        T('dve', wk[:, 6, :m], wk[:, 2, :m], li, ALU.mult, ['wk2', 'sBp'], ['wk6'])
                        T('dve', wk[:, 5, :m], wk[:, 5, :m], wk[:, 6, :m], ALU.add, ['wk5', 'wk6'], ['wk5'])
                        T('dve', wk[:, 5, :m], wk[:, 5, :m], wk[:, 4, :m], ALU.mult, ['wk5', 'wk4'], ['wk5'])
                        T('dve', wk[:, 6, :m], wk[:, 2, :m], lr, ALU.mult, ['wk2', 'sBp'], ['wk6'])
                        T('dve', wk[:, 0, :m], wk[:, 1, :m], li, ALU.mult, ['wk1', 'sBp'], ['wk0'])
                        T('dve', wk[:, 6, :m], wk[:, 6, :m], wk[:, 0, :m], ALU.subtract, ['wk6', 'wk0'], ['wk6'])
                        T('dve', wk[:, 6, :m], wk[:, 6, :m], wk[:, 4, :m], ALU.mult, ['wk6', 'wk4'], ['wk6'])
                        bre, bim = sbT[:, kc, :, 0], sbT[:, kc, :, 1]
                        T('dve', wk[:, 0, :m], wk[:, 5, :m], bre, ALU.mult, ['wk5', 'sbT'], ['wk0'])
                        T('dve', wk[:, 1, :m], wk[:, 6, :m], bim, ALU.mult, ['wk6', 'sbT'], ['wk1'])
                        T('dve', wk[:, 0, :m], wk[:, 0, :m], wk[:, 1, :m], ALU.subtract, ['wk0', 'wk1'], ['wk0'])
                        T('dve', wk[:, 1, :m], wk[:, 5, :m], bim, ALU.mult, ['wk5', 'sbT'], ['wk1'])
                        T('dve', wk[:, 2, :m], wk[:, 6, :m], bre, ALU.mult, ['wk6', 'sbT'], ['wk2'])
                        T('dve', wk[:, 1, :m], wk[:, 1, :m], wk[:, 2, :m], ALU.add, ['wk1', 'wk2'], ['wk1'])
                        for q in range(4):
                            for gg in range(2):
                                for ri in range(2):
                                    S('dve', lB[:, ri, 4 * kc + q, 64 * gg:64 * gg + 64], wk[:, ri, :64],
                                      mask8[:, 2 * q + gg:2 * q + gg + 1], None, ALU.mult, ALU.bypass, ['wk%d' % ri, 'cst'], ['lB'])
                    em.op('pool', lambda e: e.memset(lC[:], 0.0), writes=['lC'])
                    for gh in range(12):
                        q = gh % 4
                        for gg in range(2):
                            S('dve', lC[:, 0, gh, 32 * q + 16 * gg:32 * q + 16 * gg + 16], scT[:, gh, :, 0], maskg2[:, gg:gg + 1], None,
                              ALU.mult, ALU.bypass, ['scT', 'cst'], ['lC'])
                            S('dve', lC[:, 1, gh, 32 * q + 16 * gg:32 * q + 16 * gg + 16], scT[:, gh, :, 1], maskg2[:, gg:gg + 1], -1.0,
                              ALU.mult, ALU.mult, ['scT', 'cst'], ['lC'])
                    em.barrier()
                    ss.close()
                    sx = sb("sx", [128, 2, 12, NB])
                    sg = sb("sg", [128, 2, 12, NB])
                    sh = sx
                    em.alias['sh'] = 'sx'
                    shb = sb("shb", [128, 2, 12, NB], BF16)
                    ubf = sb("ubf", [128, 3, NB], BF16)
                    st1 = sb("st1", [128, 12, NB])
                    sy = sb("sy", [128, 3, NB])
                    sz = sb("sz", [128, 3, NB])

                def s5_tile(n, samp=False):
                    CP('act', ubf[:, :, :n], pj[:, 0:3, 3:3 + n], ['pj'], ['ubf'])
                    yield
                    bre = [nb('A'), nb('A')]
                    yield
                    bim = [nb('A'), nb('A')]
                    yield
                    for ri, bb in ((0, bre), (1, bim)):
                        for gh in range(12):
                            b = bb[gh // 8]
                            o = (gh % 8) * 64
                            MM(ps[b][:, o:o + n], lB[:, ri, gh, :], ubf[:, gh // 4, :n], ['lB', 'ubf'], ['ps%d' % b])
                    yield
                    if samp:
                        for half, (g0, g1_) in enumerate(((0, 8), (8, 12))):
                            k = g1_ - g0
                            pr = ps[bre[half]][:, 0:k * 64].rearrange("p (a b) -> p a b", b=64)[:, :, :n]
                            pi = ps[bim[half]][:, 0:k * 64].rearrange("p (a b) -> p a b", b=64)[:, :, :n]
                            ar = abar[:, g0:g1_, 0].unsqueeze(2).to_broadcast([128, k, n])
                            ai = abar[:, g0:g1_, 1].unsqueeze(2).to_broadcast([128, k, n])
                            h0r, h0i = h0v(0)[:, g0:g1_, :], h0v(1)[:, g0:g1_, :]
                            T('dve', sg[:, 0, g0:g1_, :n], h0r, ar, ALU.mult, ['s5st', 'abar'], ['sg'])
                            T('dve', st1[:, g0:g1_, :n], h0i, ai, ALU.mult, ['s5st', 'abar'], ['st1'])
                            T('dve', sg[:, 0, g0:g1_, :n], sg[:, 0, g0:g1_, :n], st1[:, g0:g1_, :n], ALU.subtract, ['sg', 'st1'], ['sg'])
                            T('dve', sx[:, 0, g0:g1_, :n], sg[:, 0, g0:g1_, :n], pr, ALU.add, ['sg', 'ps%d' % bre[half]], ['sx'])
                            T('dve', sg[:, 1, g0:g1_, :n], h0i, ar, ALU.mult, ['s5st', 'abar'], ['sg'])
                            T('dve', st1[:, g0:g1_, :n], h0r, ai, ALU.mult, ['s5st', 'abar'], ['st1'])
                            T('dve', sg[:, 1, g0:g1_, :n], sg[:, 1, g0:g1_, :n], st1[:, g0:g1_, :n], ALU.add, ['sg', 'st1'], ['sg'])
                            T('dve', sx[:, 1, g0:g1_, :n], sg[:, 1, g0:g1_, :n], pi, ALU.add, ['sg', 'ps%d' % bim[half]], ['sx'])
                        CP('dve', h0v(0), sx[:, 0, :, :n], ['sx'], ['s5st'])
                        CP('dve', h0v(1), sx[:, 1, :, :n], ['sx'], ['s5st'])
                    else:
                        for half, (g0, g1_) in enumerate(((0, 8), (8, 12))):
                            k = g1_ - g0
                            pr = ps[bre[half]][:, 0:k * 64].rearrange("p (a b) -> p a b", b=64)[:, :, :n]
                            pi = ps[bim[half]][:, 0:k * 64].rearrange("p (a b) -> p a b", b=64)[:, :, :n]
                            Rr, Ri = ['ps%d' % bre[half], 'Et'], ['ps%d' % bim[half], 'Et']
                            er, ei = Et[:, 0, g0:g1_, :n], Et[:, 1, g0:g1_, :n]
                            T('dve', sx[:, 0, g0:g1_, :n], pr, er, ALU.mult, Rr, ['sx'])
                            T('dve', st1[:, g0:g1_, :n], pi, ei, ALU.mult, Ri, ['st1'])
                            T('dve', sx[:, 0, g0:g1_, :n], sx[:, 0, g0:g1_, :n], st1[:, g0:g1_, :n], ALU.subtract, ['sx', 'st1'], ['sx'])
                            T('dve', sx[:, 1, g0:g1_, :n], pr, ei, ALU.mult, Rr, ['sx'])
                            T('dve', st1[:, g0:g1_, :n], pi, er, ALU.mult, Ri, ['st1'])
                            T('dve', sx[:, 1, g0:g1_, :n], sx[:, 1, g0:g1_, :n], st1[:, g0:g1_, :n], ALU.add, ['sx', 'st1'], ['sx'])
                        for ri in range(2):
                            for gh in range(12):
                                em.op('dve', lambda e, ri=ri, gh=gh: e.tensor_tensor_scan(
                                    out=sg[:, ri, gh, :n], data0=rmag[:, gh:gh + 1].to_broadcast([128, n]), data1=sx[:, ri, gh, :n],
                                    initial=hst[:, gh, ri:ri + 1], op0=ALU.mult, op1=ALU.add),
                                    reads=['Rt', 'sx', 'hst'], writes=['sg'])
                        er, ei = Et[:, 0, :, :n], Et[:, 1, :, :n]
                        T('dve', sh[:, 0, :, :n], sg[:, 0, :, :n], er, ALU.mult, ['sg', 'Et'], ['sh'])
                        T('dve', st1[:, :, :n], sg[:, 1, :, :n], ei, ALU.mult, ['sg', 'Et'], ['st1'])
                        T('dve', sh[:, 0, :, :n], sh[:, 0, :, :n], st1[:, :, :n], ALU.add, ['sh', 'st1'], ['sh'])
                        T('dve', sh[:, 1, :, :n], sg[:, 1, :, :n], er, ALU.mult, ['sg', 'Et'], ['sh'])
                        T('dve', st1[:, :, :n], sg[:, 0, :, :n], ei, ALU.mult, ['sg', 'Et'], ['st1'])
                        T('dve', sh[:, 1, :, :n], sh[:, 1, :, :n], st1[:, :, :n], ALU.subtract, ['sh', 'st1'], ['sh'])
                        CP('dve', hst[:, :, 0], sh[:, 0, :, n - 1], ['sh'], ['hst'])
                        CP('dve', hst[:, :, 1], sh[:, 1, :, n - 1], ['sh'], ['hst'])
                    yield
                    CP('act', shb[:, :, :, :n], sh[:, :, :, :n], ['sh'], ['shb'])
                    yield
                    b = nb('A')
                    yield
                    for kc in range(3):
                        for q in range(4):
                            gh = 4 * kc + q
                            MM(ps[b][:, kc * 64:kc * 64 + n], lC[:, 0, gh, :], shb[:, 0, gh, :n], ['lC', 'shb'], ['ps%d' % b],
                               start=(q == 0), stop=False)
                            MM(ps[b][:, kc * 64:kc * 64 + n], lC[:, 1, gh, :], shb[:, 1, gh, :n], ['lC', 'shb'], ['ps%d' % b],
                               start=False, stop=(q == 3))
                    yield
                    for kc in range(3):
                        STT(sy[:, kc, :n], pj[:, kc, 3:3 + n], sv[:, kc, 0:1], ps[b][:, kc * 64:kc * 64 + n], ALU.mult, ALU.add,
                            ['pj', 'sv', 'ps%d' % b], ['sy'])
                    yield
                    gelu(sz[:, :, :n], sy[:, :, :n], 3, n, ['sy'], ['sz'], st1[:, 0:3, :], st1[:, 3:6, :], 'st')
                    yield
                    CP('act', szb[:, :, :n], sz[:, :, :n], ['sz'], ['szb'])
                    yield
                    b = nb('A')
                    yield
                    for k2 in range(3):
                        for kc in range(3):
                            MM(ps[b][:, k2 * 64:k2 * 64 + n], sglu[:, kc, k2 * 128:(k2 + 1) * 128], szb[:, kc, :n], ['sglu', 'szb'],
                               ['ps%d' % b], start=(kc == 0), stop=(kc == 2))
                    yield
                    for k2 in range(3):
                        A(sy[:, k2, :n], ps[b][:, k2 * 64:k2 * 64 + n], AF.Sigmoid, ['ps%d' % b, 'sv'], ['sy'], bias=sv[:, k2, 1:2])
                    yield
                    T('dve', omix[:, 0:3, :n], sz[:, :, :n], sy[:, :, :n], ALU.mult, ['sz', 'sy'], ['omix_s'])
                    yield

                S0T = sb("S0T", [128, 3, 64])
                if 'rwkv' in mixers:
                    rp = sb("rp", [128, 3, 8])
                    rmu = sb("rmu", [128, 11])
                    rup = sb("rup", [128, 2, 384])
                    em.dma('sp', rp[:], rwp[l], 'ld_r1', writes=['rp'])
                    em.dma('sp', rmu[:], rwmu[l], 'ld_r2', writes=['rmu'])
                    em.dma('sp', rup[:], rwup[l], 'ld_r3', writes=['rup'])
                    ones64 = sb("ones64", [128, NB])
                    em.op('pool', lambda e: e.memset(ones64[:], 1.0), writes=['ones64'])
                    xm = sb("xm", [128, 11, NB])
                    early_ = ('sgw', 'a', 'kk', 'kp', 'b', 'lg', 'gam', 'gex', 'gin', 't1')
                    own_ = ('g', 'bon', 'rt0', 'rt1', 'kt0', 'kt1', 'kti', 'nbt')
                    rarr = {}
                    for i_, nm in enumerate(early_):
                        if 's5' in mixers:
                            base_ = sx if i_ < 8 else sg
                            j_ = i_ % 8
                            rarr[nm] = base_[:, j_ // 4, 3 * (j_ % 4):3 * (j_ % 4) + 3, :]
                            em.alias[nm] = 'sx' if i_ < 8 else 'sg'
                        else:
                            rarr[nm] = sb("r_" + nm, [128, 3, NB])
                    for nm in own_:
                        rarr[nm] = sb("r_" + nm, [128, 3, NB])
                    rarr['y'], rarr['d'], rarr['t2'] = rarr['kti'], rarr['nbt'], rarr['kt0']
                    em.alias.update({'y': 'kti', 'd': 'nbt', 't2': 'kt0'})
                    tw = sb("tw", [128, NB])
                    tw2 = sb("tw2", [128, NB])
                    twb = sb("twb", [128, NB])
                    em.op('pool', lambda e: e.memset(tw[:], 0.0), writes=['tw'])
                    em.op('pool', lambda e: e.memset(twb[:], 0.0), writes=['twb'])
                    gC = sb("gC", [128, 3])
                    tmj_all = sb("tmj_all", [128, 8 * 384])
                    TMN = ('V', 'K', 'B', 'AkkT', 'N', 'M', 'P', 'X')
                    TMK = ['tm_' + x_ for x_ in TMN]
                    tmj = {nm: tmj_all[:, i_ * 384:(i_ + 1) * 384] for i_, nm in enumerate(TMN)}

                    def zero_tmj():
                        for nm_ in ('V', 'K', 'B', 'AkkT', 'N', 'M', 'P', 'X'):
                            em.op('pool', lambda e, nm_=nm_: e.memset(tmj[nm_][:], 0.0), writes=['tm_' + nm_])
                    tmj['R'], tmj['U'], tmj['Y'] = tmj['N'], tmj['M'], tmj['P']
                    tmj['ArkT'], tmj['ArbT'] = tmj['AkkT'], tmj['X']
                    em.alias.update({'tm_R': 'tm_N', 'tm_U': 'tm_M', 'tm_Y': 'tm_P', 'tm_ArkT': 'tm_AkkT', 'tm_ArbT': 'tm_X'})
                    stmp = sb("stmp", [128, 64])

                def rwkv_tile(n, samp=False):
                    C = n
                    yield
                    ra = rarr
                    yield
                    RP = ['rp']
                    yield
                    T('dve', xm[:, :, :n], (prevv if samp else pj[:, 3:14, 2:2 + n]), pj[:, 3:14, 3:3 + n], ALU.subtract, ['pj', 'sh16'], ['xm'])
                    yield
                    for c in range(11):
                        STT(xm[:, c, :n], xm[:, c, :n], rmu[:, c:c + 1], pj[:, 3 + c, 3:3 + n], ALU.mult, ALU.add, ['xm', 'rmu', 'pj'], ['xm'])
                    yield
                    r_, k_, v_ = xm[:, 0:3, :n], xm[:, 3:6, :n], xm[:, 6:9, :n]
                    yield
                    A(tw[0:64, :n], xm[0:64, 9, :n], AF.Tanh, ['xm'], ['tw'])
                    yield
                    CP('act', twb[64:128, :n], xm[64:128, 9, :n], ['xm'], ['twb'])
                    yield
                    for c in range(3):
                        S('dve', ra['kk'][:, c, :n], xm[:, 3 + c, :n], rp[:, c, 2:3], None, ALU.mult, ALU.bypass, ['xm'] + RP, ['kk'])
                    yield
                    T('dve', ra['t1'][:, :, :n], ra['kk'][:, :, :n], ra['kk'][:, :, :n], ALU.mult, ['kk'], ['t1'])
                    yield
                    b0 = nb()
                    yield
                    for c in range(3):
                        MM(ps[b0][:, c * 64:c * 64 + n], rup[:, 0, c * 128:(c + 1) * 128], tw[:, :n], ['rup', 'tw'], ['ps%d' % b0])
                        yield
                        MM(ps[b0][:, 192 + c * 64:192 + c * 64 + n], rup[:, 0, c * 128:(c + 1) * 128], twb[:, :n],
                           ['rup', 'twb'], ['ps%d' % b0])
                        yield
                    yield
                    b2 = nb()
                    yield
                    for c in range(3):
                        MM(ps[b2][:, c * 64:c * 64 + n], bones, ra['t1'][:, c, :n], ['cst', 't1'], ['ps%d' % b2])
                    yield
                    for c in range(3):
                        A(ra['sgw'][:, c, :n], ps[b0][:, c * 64:c * 64 + n], AF.Sigmoid, ['ps%d' % b0] + RP, ['sgw'], bias=rp[:, c, 0:1])
                    yield
                    for c in range(3):
                        A(ra['a'][:, c, :n], ps[b0][:, 192 + c * 64:192 + c * 64 + n], AF.Sigmoid, ['ps%d' % b0] + RP, ['a'], bias=rp[:, c, 1:2])
                    yield
                    A(tw2[:, :n], xm[:, 10, :n], AF.Sigmoid, ['xm'], ['tw2'])
                    yield
                    S('dve', ra['sgw'][:, :, :n], ra['sgw'][:, :, :n], -0.6065306597126334, None, ALU.mult, ALU.bypass, ['sgw'], ['sgw'])
                    yield
                    if samp:
                        CP('dve', ra['lg'][:, :, :n], ra['sgw'][:, :, :n], ['sgw'], ['lg'])
                    else:
                        for c in range(3):
                            em.op('dve', lambda e, c=c: e.tensor_tensor_scan(out=ra['lg'][:, c, :n], data0=ones64[:, :n], data1=ra['sgw'][:, c, :n],
                                                                          initial=0.0, op0=ALU.mult, op1=ALU.add),
                                  reads=['ones64', 'sgw'], writes=['lg'])
                    yield
                    b1 = nb()
                    yield
                    for c in range(3):
                        MM(ps[b1][:, c * 64:c * 64 + n], rup[:, 1, c * 128:(c + 1) * 128], tw2[:, :n], ['rup', 'tw2'], ['ps%d' % b1])
                    yield
                    A(ra['gam'][:, :, :n], ra['lg'][:, :, :n], AF.Exp, ['lg'], ['gam'])
                    yield
                    A(ra['gin'][:, :, :n], ra['lg'][:, :, :n], AF.Exp, ['lg'], ['gin'], scale=-1.0)
                    yield
                    T('dve', ra['gex'][:, :, :n], ra['lg'][:, :, :n], ra['sgw'][:, :, :n], ALU.subtract, ['lg', 'sgw'], ['gex'])
                    yield
                    A(ra['gex'][:, :, :n], ra['gex'][:, :, :n], AF.Exp, ['gex'], ['gex'])
                    yield
                    for c in range(3):
                        S('dve', ra['kp'][:, c, :n], ra['a'][:, c, :n], -1.0, rp[:, c, 3:4], ALU.add, ALU.mult, ['a'] + RP, ['kp'])
                    yield
                    STT(ra['kp'][:, :, :n], ra['kp'][:, :, :n], 1.0, k_, ALU.add, ALU.mult, ['kp', 'xm'], ['kp'])
                    yield
                    for c in range(3):
                        STT(ra['bon'][:, c, :n], xm[:, c, :n], rp[:, c, 4:5], ra['kp'][:, c, :n], ALU.mult, ALU.mult, ['xm', 'kp'] + RP, ['bon'])
                    yield
                    b3 = nb()
                    yield
                    for c in range(3):
                        MM(ps[b3][:, c * 64:c * 64 + n], bones, ra['bon'][:, c, :n], ['cst', 'bon'], ['ps%d' % b3])
                    yield
                    CP('act', ra['g'][:, :, :n], ps[b1][:, 0:192].rearrange("p (a b) -> p a b", b=64)[:, :, :n], ['ps%d' % b1], ['g'])
                    yield
                    p3 = ps[b2][:, 0:192].rearrange("p (a b) -> p a b", b=64)[:, :, :n]
                    yield
                    A(ra['t1'][:, :, :n], p3, AF.Sqrt, ['ps%d' % b2], ['t1'])
                    yield
                    S('dve', ra['t1'][:, :, :n], ra['t1'][:, :, :n], 1e-12, None, ALU.max, ALU.bypass, ['t1'], ['t1'])
                    yield
                    em.op('dve', lambda e: e.reciprocal(out=ra['t1'][:, :, :n], in_=ra['t1'][:, :, :n]), reads=['t1'], writes=['t1'])
                    yield
                    T('dve', ra['kk'][:, :, :n], ra['kk'][:, :, :n], ra['t1'][:, :, :n], ALU.mult, ['kk', 't1'], ['kk'])
                    yield
                    T('dve', ra['b'][:, :, :n], ra['kk'][:, :, :n], ra['a'][:, :, :n], ALU.mult, ['kk', 'a'], ['b'])
                    yield
                    T('dve', ra['bon'][:, :, :n], ps[b3][:, 0:192].rearrange("p (a b) -> p a b", b=64)[:, :, :n], v_, ALU.mult,
                      ['ps%d' % b3, 'xm'], ['bon'])
                    yield
                    CP('dve', gC[:], ra['gam'][:, :, n - 1], ['gam'], ['gC'])
                    yield
                    for hp in range(2):
                        STT(ra['rt%d' % hp][:, :, :n], r_, maskg2[:, hp:hp + 1], ra['gam'][:, :, :n], ALU.mult, ALU.mult,
                            ['xm', 'gam', 'cst'], ['rt%d' % hp])
                        yield
                        STT(ra['kt%d' % hp][:, :, :n], ra['kk'][:, :, :n], maskg2[:, hp:hp + 1], ra['gex'][:, :, :n], ALU.mult, ALU.mult,
                            ['kk', 'gex', 'cst'], ['kt%d' % hp])
                        yield
                    yield
                    T('dve', ra['kti'][:, :, :n], ra['kp'][:, :, :n], ra['gin'][:, :, :n], ALU.mult, ['kp', 'gin'], ['kti'])
                    yield
                    STT(ra['nbt'][:, :, :n], ra['b'][:, :, :n], -1.0, ra['gin'][:, :, :n], ALU.mult, ALU.mult, ['b', 'gin'], ['nbt'])
                    yield
                    if not samp:
                        yield 'PHASE'

                    def rw_direct():
                        Sb = tmj_all[:, 0:1024].rearrange("p (b v) -> p b v", v=64)
                        yield
                        tb = tmj_all[:, 1024:2048].rearrange("p (b v) -> p b v", v=64)
                        yield

                        def bc(ap2, lo=0, hi=16):
                            return ap2[:, lo:hi].unsqueeze(2).to_broadcast([128, hi - lo, 64])

                        def idb(k_):
                            return identb.unsqueeze(1).to_broadcast([128, k_, 64])

                        def bmm(src):
                            bb_ = (nb(), nb())
                            for hf in range(2):
                                MM(ps[bb_[hf]][:, :], bones, src[:, 8 * hf:8 * hf + 8, :], ['cst'] + TMK, ['ps%d' % bb_[hf]])
                            return bb_

                        def pview(b_):
                            return ps[b_][:, :].rearrange("p (b v) -> p b v", v=64)
                        for c in range(3):
                            em.dma('sp', Sb, i_rwkv[l, :, :, c, :].rearrange("b p v -> p b v"), 'ld_sb', writes=TMK)
                            yield
                            kap, w_, kp_, bq_ = ra['kk'][:, c, :n], ra['gam'][:, c, :n], ra['kp'][:, c, :n], ra['b'][:, c, :n]
                            yield
                            rq_, vq_ = xm[:, c, :n], xm[:, 6 + c, :n]
                            yield
                            T('dve', tb, Sb, bc(kap), ALU.mult, TMK + ['kk'], TMK)
                            yield
                            bb_ = bmm(tb)
                            yield
                            T('dve', Sb, Sb, bc(w_), ALU.mult, TMK + ['gam'], TMK)
                            yield
                            for hf in range(2):
                                T('dve', tb[:, 8 * hf:8 * hf + 8, :], pview(bb_[hf]), bc(bq_, 8 * hf, 8 * hf + 8), ALU.mult,
                                  ['ps%d' % bb_[hf], 'b'], TMK)
                            yield
                            T('dve', Sb, Sb, tb, ALU.subtract, TMK, TMK)
                            yield
                            T('dve', tb, idb(16), bc(vq_), ALU.mult, ['cst', 'xm'], TMK)
                            yield
                            bb_ = bmm(tb)
                            yield
                            for hf in range(2):
                                T('dve', tb[:, 8 * hf:8 * hf + 8, :], pview(bb_[hf]), bc(kp_, 8 * hf, 8 * hf + 8), ALU.mult,
                                  ['ps%d' % bb_[hf], 'kp'], TMK)
                            yield
                            T('dve', Sb, Sb, tb, ALU.add, TMK, TMK)
                            yield
                            T('dve', tb, Sb, bc(rq_), ALU.mult, TMK + ['xm'], TMK)
                            yield
                            bb_ = bmm(tb)
                            yield
                            for hf in range(2):
                                T('dve', tb[:, 8 * hf:8 * hf + 8, :], pview(bb_[hf]), idb(8), ALU.mult, ['ps%d' % bb_[hf], 'cst'], TMK)
                            yield
                            em.op('dve', lambda e, c=c: e.tensor_reduce(out=ra['y'][:, c, :n], in_=tb, axis=AX.X, op=ALU.add),
                                  reads=TMK, writes=['y'])
                            yield
                            em.dma('sp', o_rwkv[l, 1:17, :, c, :].rearrange("b p v -> p b v"), Sb, 'st_sb', reads=TMK)
                            yield
                        yield

                    def rw_chunk():
                        if _RWSTOP <= 2:
                            return
                        yield
                        for nm, src, RR in (('V', xm[:, 6:9, :n], ['xm']), ('K', ra['kti'][:, :, :n], ['kti']), ('B', ra['nbt'][:, :, :n], ['nbt'])):
                            bt = nb()
                            yield
                            for c in range(3):
                                TR(ps[bt][:C, c * 128:(c + 1) * 128], src[:, c, :], ident, RR, ['ps%d' % bt])
                            yield
                            CP('act', tmj[nm][:C, :], ps[bt][:C, 0:384], ['ps%d' % bt], ['tm_' + nm])
                            yield
                        yield
                        if _RWSTOP <= 3:
                            return
                        yield
                        def hv(ap, hp, w):
                            return ap.rearrange("p (c h k) -> p h c k", h=2, k=64)[:, hp, :, :w]

                        def pv(b, w):
                            return ps[b][:C, 0:192].rearrange("p (a b) -> p a b", b=64)[:, :, :w]

                        def amat(lk, rk, RR, dst, mask):
                            bb2 = [nb(), nb()]
                            for h in range(6):
                                c, hp = h // 2, h % 2
                                ln_ = lk + str(hp) if lk in ('kt', 'rt') else lk
                                rn_ = rk + str(hp) if rk in ('kt', 'rt') else rk
                                MM(ps[bb2[hp]][:C, c * 64:c * 64 + C], ra[ln_][:, c, :n], ra[rn_][:, c, :n],
                                   [ln_, rn_], ['ps%d' % bb2[hp]])
                            for hp in range(2):
                                T('dve', hv(tmj[dst][:C, :], hp, C), pv(bb2[hp], C), mask[:C, :C].unsqueeze(1).to_broadcast([C, 3, C]),
                                  ALU.mult, ['ps%d' % bb2[hp], 'cm'], ['tm_' + dst])
                        v3 = lambda ap: ap.rearrange("p (a b) -> p a b", b=64)[:, :, :C]
                        yield
                        amat('kti', 'kt', ['kti', 'kt'], 'AkkT', MSU)
                        yield
                        amat('nbt', 'kt', ['nbt', 'kt'], 'M', MSU)
                        yield
                        T('dve', v3(tmj['X'][:C, :]), v3(tmj['M'][:C, :]), I6[:C, :C].unsqueeze(1).to_broadcast([C, 6, C]), ALU.add, ['tm_M', 'cm'], ['tm_X'])
                        yield
                        amat('kt', 'nbt', ['nbt', 'kt'], 'N', MSL)
                        yield
                        if _RWSTOP <= 4:
                            return
                        yield
                        nr = 0
                        yield
                        while (1 << (nr + 1)) < C:
                            nr += 1
                        yield
                        def xstep():
                            bx = nb()
                            for h in range(6):
                                sl = slice(h * 64, h * 64 + C)
                                MM(ps[bx][:C, sl], tmj['P'][:, sl], tmj['X'][:, sl], ['tm_P', 'tm_X'], ['ps%d' % bx])
                            CP('dve', v3(tmj['X'][:C, :]), v3(ps[bx][:C, 0:384]), ['ps%d' % bx], ['tm_X'])
                        for i in range(1, nr + 1):
                            bn_, bm_ = nb(), nb()
                            yield
                            last = (i == nr)
                            yield
                            for h in range(6):
                                sl = slice(h * 64, h * 64 + C)
                                MM(ps[bn_][:C, sl], tmj['M'][:, sl], tmj['N'][:, sl], ['tm_M', 'tm_N'], ['ps%d' % bn_])
                                if not last:
                                    MM(ps[bm_][:C, sl], tmj['N'][:, sl], tmj['M'][:, sl], ['tm_M', 'tm_N'], ['ps%d' % bm_])
                            yield
                            if not last:
                                CP('act', v3(tmj['N'][:C, :]), v3(ps[bn_][:C, 0:384]), ['ps%d' % bn_], ['tm_N'])
                                CP('act', v3(tmj['M'][:C, :]), v3(ps[bm_][:C, 0:384]), ['ps%d' % bm_], ['tm_M'])
                            yield
                            if i > 1:
                                xstep()
                            yield
                            T('dve', v3(tmj['P'][:C, :]), v3(ps[bn_][:C, 0:384]), I6[:C, :C].unsqueeze(1).to_broadcast([C, 6, C]), ALU.add, ['ps%d' % bn_, 'cm'], ['tm_P'])
                            yield
                        yield
                        if nr >= 1:
                            xstep()
                        yield
                        if _RWSTOP <= 5:
                            return
                        yield
                        br = [nb(), nb()]
                        yield
                        for h in range(6):
                            c, hp = h // 2, h % 2
                            yield
                            pr_ = slice(64 * hp, 64 * hp + 64)
                            yield
                            MM(ps[br[hp]][:C, c * 64:(c + 1) * 64], ra['kt%d' % hp][:, c, :n], S0T[:, c, :], ['kt%d' % hp, 'S0T'], ['ps%d' % br[hp]], start=True, stop=False)
                            yield
                            MM(ps[br[hp]][:C, c * 64:(c + 1) * 64], tmj['AkkT'][:, h * 64:h * 64 + C], tmj['V'][:, h * 64:(h + 1) * 64],
                               ['tm_AkkT', 'tm_V'], ['ps%d' % br[hp]], start=False, stop=True)
                            yield
                        yield
                        for hp in range(2):
                            CP('act', hv(tmj['R'][:C, :], hp, 64), pv(br[hp], 64), ['ps%d' % br[hp]], ['tm_R'])
                        yield
                        bu = nb()
                        yield
                        for h in range(6):
                            MM(ps[bu][:C, h * 64:(h + 1) * 64], tmj['X'][:, h * 64:h * 64 + C], tmj['R'][:, h * 64:(h + 1) * 64],
                               ['tm_X', 'tm_R'], ['ps%d' % bu])
                        yield
                        CP('act', tmj['U'][:C, :], ps[bu][:C, 0:384], ['ps%d' % bu], ['tm_U'])
                        yield
                        if _RWSTOP <= 6:
                            return
                        yield
                        amat('kti', 'rt', ['kti', 'rt'], 'ArkT', MU)
                        yield
                        amat('nbt', 'rt', ['nbt', 'rt'], 'ArbT', MU)
                        yield
                        by = [nb(), nb()]
                        yield
                        for h in range(6):
                            c, hp = h // 2, h % 2
                            yield
                            pr_ = slice(64 * hp, 64 * hp + 64)
                            yield
                            hs = slice(h * 64, (h + 1) * 64)
                            yield
                            cs = slice(c * 64, (c + 1) * 64)
                            yield
                            MM(ps[by[hp]][:C, cs], ra['rt%d' % hp][:, c, :n], S0T[:, c, :], ['rt%d' % hp, 'S0T'], ['ps%d' % by[hp]], start=True, stop=False)
                            yield
                            MM(ps[by[hp]][:C, cs], tmj['ArkT'][:, h * 64:h * 64 + C], tmj['V'][:, hs], ['tm_ArkT', 'tm_V'], ['ps%d' % by[hp]], start=False, stop=False)
                            yield
                            MM(ps[by[hp]][:C, cs], tmj['ArbT'][:, h * 64:h * 64 + C], tmj['U'][:, hs], ['tm_ArbT', 'tm_U'], ['ps%d' % by[hp]], start=False, stop=True)
                            yield
                        yield
                        for hp in range(2):
                            CP('act', hv(tmj['Y'][:C, :], hp, 64), pv(by[hp], 64), ['ps%d' % by[hp]], ['tm_Y'])
                        yield
                        if _RWSTOP <= 7:
                            return
                        yield
                        for c in range(3):
                            bs_ = nb()
                            yield
                            for hp in range(2):
                                hs = slice((2 * c + hp) * 64, (2 * c + hp + 1) * 64)
                                MM(ps[bs_][:, hp * 64:(hp + 1) * 64], tmj['K'][:, c * 128:(c + 1) * 128], tmj['V'][:, hs], ['tm_K', 'tm_V'],
                                   ['ps%d' % bs_], start=True, stop=False)
                                MM(ps[bs_][:, hp * 64:(hp + 1) * 64], tmj['B'][:, c * 128:(c + 1) * 128], tmj['U'][:, hs], ['tm_B', 'tm_U'],
                                   ['ps%d' % bs_], start=False, stop=True)
                            yield
                            for hp in range(2):
                                pr_ = slice(64 * hp, 64 * hp + 64)
                                T('dve', stmp[pr_, :], ps[bs_][pr_, hp * 64:(hp + 1) * 64], S0T[pr_, c, :], ALU.add, ['ps%d' % bs_, 'S0T'], ['stmp'])
                                S('dve', S0T[pr_, c, :], stmp[pr_, :], gC[pr_, c:c + 1], None, ALU.mult, ALU.bypass, ['stmp', 'gC'], ['S0T'])
                            yield
                        yield
                        if _RWSTOP <= 8:
                            return
                        yield
                        bt = nb()
                        yield
                        for c in range(3):
                            TR(ps[bt][:, c * 128:(c + 1) * 128], tmj['Y'][:, c * 128:(c + 1) * 128], ident, ['tm_Y'], ['ps%d' % bt])
                        yield
                        CP('act', ra['y'][:, :, :n], ps[bt][:, 0:384].rearrange("p (a b) -> p a b", b=128)[:, :, :n], ['ps%d' % bt], ['y'])
                        yield
                    if samp:
                        yield from rw_direct()
                    else:
                        yield from rw_chunk()
                    bm1 = nb()
                    yield
                    for c in range(3):
                        MM(ps[bm1][:, c * 64:c * 64 + n], bones, ra['y'][:, c, :n], ['cst', 'y'], ['ps%d' % bm1])
                    yield
                    STT(ra['d'][:, :, :n], ps[bm1][:, 0:192].rearrange("p (a b) -> p a b", b=64)[:, :, :n], -1.0 / 64, ra['y'][:, :, :n],
                        ALU.mult, ALU.add, ['ps%d' % bm1, 'y'], ['d'])
                    yield
                    T('dve', ra['t2'][:, :, :n], ra['d'][:, :, :n], ra['d'][:, :, :n], ALU.mult, ['d'], ['t2'])
                    yield
                    bm2 = nb()
                    yield
                    for c in range(3):
                        MM(ps[bm2][:, c * 64:c * 64 + n], bones, ra['t2'][:, c, :n], ['cst', 't2'], ['ps%d' % bm2])
                    yield
                    S('dve', ra['t2'][:, :, :n], ps[bm2][:, 0:192].rearrange("p (a b) -> p a b", b=64)[:, :, :n], 1.0 / 64, 64e-5,
                      ALU.mult, ALU.add, ['ps%d' % bm2], ['t2'])
                    yield
                    A(ra['t2'][:, :, :n], ra['t2'][:, :, :n], AF.Sqrt, ['t2'], ['t2'])
                    yield
                    em.op('dve', lambda e: e.reciprocal(out=ra['t2'][:, :, :n], in_=ra['t2'][:, :, :n]), reads=['t2'], writes=['t2'])
                    yield
                    T('dve', ra['d'][:, :, :n], ra['d'][:, :, :n], ra['t2'][:, :, :n], ALU.mult, ['d', 't2'], ['d'])
                    yield
                    for c in range(3):
                        S('dve', ra['d'][:, c, :n], ra['d'][:, c, :n], rp[:, c, 5:6], rp[:, c, 6:7], ALU.mult, ALU.add, ['d'] + RP, ['d'])
                    yield
                    T('dve', ra['d'][:, :, :n], ra['d'][:, :, :n], ra['bon'][:, :, :n], ALU.add, ['d', 'bon'], ['d'])
                    yield
                    T('dve', omix[:, 3:6, :n], ra['d'][:, :, :n], ra['g'][:, :, :n], ALU.mult, ['d', 'g'], ['omix_r'])
                    yield

                def do_tile(c0, n, samp=False):
                    A(sqb[:, :, :n], xres[:, :, c0:c0 + n], AF.Square, ['xres'], ['sqb'])
                    b = nb()
                    for kc in range(KC):
                        MM(ps[b][:, :n], ones_bf[:], sqb[:, kc, :n], ['ones_bf', 'sqb'], ['ps%d' % b], start=(kc == 0), stop=(kc == KC - 1))
                    S('dve', rstb[:, :n], ps[b][:, :n], 1.0 / D, EPS, ALU.mult, ALU.add, ['ps%d' % b], ['rstb'])
                    A(rstb[:, :n], rstb[:, :n], AF.Sqrt, ['rstb'], ['rstb'])
                    em.op('dve', lambda e: e.reciprocal(out=rstb[:, :n], in_=rstb[:, :n]), reads=['rstb'], writes=['rstb'])
                    for kc in range(KC):
                        STT(xnb[:, kc, :n], xres[:, kc, c0:c0 + n], normw[:, 2 + l, kc:kc + 1], rstb[:, :n], ALU.mult, ALU.mult,
                            ['xres', 'normw', 'rstb'], ['xnb'])
                    for f0 in (0, 8, 16):
                        b = nb()
                        nf = min(8, 18 - f0)
                        for fi in range(nf):
                            fc = f0 + fi
                            for kc in range(KC):
                                o = kc * DIN + fc * 128
                                MM(ps[b][:, fi * 64:fi * 64 + n], ring[:, o:o + 128], xnb[:, kc, :n], RIN + ['xnb'], ['ps%d' % b],
                                   start=(kc == 0), stop=(kc == KC - 1))
                        CP('act', pj[:, f0:f0 + nf, 3:3 + n], ps[b][:, 0:nf * 64].rearrange("p (a b) -> p a b", b=64)[:, :, :n],
                           ['ps%d' % b], ['pj'])
                    if len(mixers) < 3:
                        em.op('pool', lambda e: e.memset(omix[:, :, :n], 0.0), writes=['omix_s', 'omix_r', 'omix_l'])

                    def run_rr(gens):
                        while gens:
                            for g_ in list(gens):
                                try:
                                    next(g_)
                                except StopIteration:
                                    gens.remove(g_)
                    if samp or len(mixers) < 3:
                        if 's5' in mixers:
                            run_rr([s5_tile(n, samp)])
                        gens = []
                        if 'rwkv' in mixers:
                            gens.append(rwkv_tile(n, samp))
                        if 'lru' in mixers:
                            gens.append(lru_tile(n, samp))
                        run_rr(gens)
                    else:
                        gR, gL = rwkv_tile(n, samp), lru_tile(n, samp)
                        l_done = False
                        while True:
                            if next(gR) == 'PHASE':
                                break
                            if not l_done:
                                try:
                                    next(gL)
                                except StopIteration:
                                    l_done = True
                        gens = [gR, s5_tile(n, samp)] + ([] if l_done else [gL])
                        run_rr(gens)
                    b = nb()
                    for mc in range(KC):
                        for kc in range(KC):
                            o = 18432 + kc * D + mc * 128
                            MM(ps[b][:, mc * 64:mc * 64 + n], ring[:, o:o + 128], omix[:, kc, :n], ROUT + ['omix_s', 'omix_r', 'omix_l'], ['ps%d' % b],
                               start=(kc == 0), stop=(kc == KC - 1))
                    T('dve', xres[:, :, c0:c0 + n], xres[:, :, c0:c0 + n], ps[b][:, :].rearrange("p (a b) -> p a b", b=64)[:, :, :n], ALU.add,
                      ['xres', 'ps%d' % b], ['xres'])
                    if not samp:
                        CP('dve', tmpc[:], pj[:, :, n:n + 3], ['pj'], ['tmpc'])
                        CP('dve', pj[:, :, 0:3], tmpc[:], ['tmpc'], ['pj'])

                def store_state(si):
                    em.dma('sp', o_s5[l, si], hst[:], 'st_a', reads=['hst'])
                    em.dma('sp', o_rwkv[l, si], S0T[:], 'st_b', reads=['S0T'])
                    em.dma('sp', o_shift[l, si], pj[:, 3:14, 2], 'st_c', reads=['pj'])
                    em.dma('sp', o_lru[l, si], lru_h[:], 'st_d', reads=['lru_h'])
                    em.dma('sp', o_conv[l, si], pj[:, 14:16, 0:3], 'st_e', reads=['pj'])

                em.op('pool', lambda e: e.memset(pj[:], 0.0), writes=['pj'])
                em.op('pool', lambda e: e.memset(hst[:], 0.0), writes=['hst'])
                em.op('pool', lambda e: e.memset(S0T[:], 0.0), writes=['S0T'])
                em.op('pool', lambda e: e.memset(lru_h[:], 0.0), writes=['lru_h'])
                if 'rwkv' in mixers:
                    zero_tmj()
                do_tile(0, 16)
                for i in range(PT // NB):
                    do_tile(32 + NB * i, NB)
                store_state(0)
                if not _NOSAMP:
                    em.dma('sp', s5st[:], i_s5[l].rearrange("b p g r -> p b g r"), 'ld_a', writes=['s5st'])
                    em.dma('sp', sh16[:], i_shift[l].rearrange("b p c -> p b c"), 'ld_cc', writes=['sh16'])
                    em.dma('sp', lst[:], i_lru[l].rearrange("b p c -> p b c"), 'ld_d', writes=['lst'])
                    em.dma('sp', cst16[:], i_conv[l].rearrange("b p c j -> p b c j"), 'ld_e', writes=['cst16'])
                    do_tile(16, 16, True)
                    CP('dve', prevv, pj[:, 3:14, 3:19], ['pj'], ['sh16'])
                    em.dma('sp', o_s5[l, 1:17].rearrange("b p g r -> p b g r"), s5st[:], 'st_a', reads=['s5st'])
                    em.dma('sp', o_shift[l, 1:17].rearrange("b p c -> p b c"), sh16[:], 'st_c', reads=['sh16'])
                    em.dma('sp', o_lru[l, 1:17].rearrange("b p c -> p b c"), lst[:], 'st_d', reads=['lst'])
                    em.dma('sp', o_conv[l, 1:17].rearrange("b p c j -> p b c j"), cst16[:], 'st_e', reads=['cst16'])
                em.barrier()

        for l in range(L):
            if stage >= 1:
                ffn(l, 0)
            if stage >= 2 or stage == -2:
                mixer_phase(l)
            if stage >= 1:
                ffn(l, 1)

        fs = ExitStack()
        with fs:
            sq = sbuf(fs, "sqF", [128, KC, 512], BF16)
            rstd = sbuf(fs, "rstdF", [128, 512])
            for ti in range(NT):
                c0, n = tiles[ti]
                A(sq[:, :, :n], xres[:, :, c0:c0 + n], AF.Square, ['xres'], ['sq'])
                for kc in range(KC):
                    MM(ps[7][:, :n], ones_bf[:], sq[:, kc, :n], ['ones_bf', 'sq'], ['ps7'], start=(kc == 0), stop=(kc == KC - 1))
                S('dve', rstd[:, :n], ps[7][:, :n], 1.0 / D, EPS, ALU.mult, ALU.add, ['ps7'], ['rstd'])
                A(rstd[:, :n], rstd[:, :n], AF.Sqrt, ['rstd'], ['rstd'])
                em.op('dve', lambda e: e.reciprocal(out=rstd[:, :n], in_=rstd[:, :n]), reads=['rstd'], writes=['rstd'])
                for kc in range(KC):
                    STT(xres[:, kc, c0:c0 + n], xres[:, kc, c0:c0 + n], normw[:, 6, kc:kc + 1], rstd[:, :n], ALU.mult, ALU.mult,
                        ['xres', 'rstd', 'normw'], ['xres'])
                    em.dma('sp', yT[:, kc, c0:c0 + n], xres[:, kc, c0:c0 + n], 'st_y%d' % kc, reads=['xres'])
            em.finish('sp')
        print("instructions:", em.ninstr, {k: v for k, v in em.cnt.items()})
    return nc


def prep_common(inp):
    f = np.float32
    g = lambda k: np.asarray(inp[k], f)
    c = {}
    wf = np.empty((L, 2, NJ, 128, SLOT), f)
    for l in range(L):
        for fi, pre in enumerate(("ffn1", "ffn2")):
            wg = g(pre + "_w_gate")[l].reshape(KC, 128, NJ, 128)
            wu = g(pre + "_w_up")[l].reshape(KC, 128, NJ, 128)
            wd = g(pre + "_w_down")[l].reshape(NJ, 128, D)
            wf[l, fi, :, :, 0:1024] = wg.transpose(2, 1, 0, 3).reshape(NJ, 128, 1024)
            wf[l, fi, :, :, 1024:2048] = wu.transpose(2, 1, 0, 3).reshape(NJ, 128, 1024)
            wf[l, fi, :, :, 2048:3072] = wd
    c["wffn"] = wf
    c["w_in"] = np.ascontiguousarray(g("w_in").reshape(L, KC, 128, DIN).transpose(0, 2, 1, 3)).reshape(L, 128, KC * DIN)
    c["w_out"] = np.ascontiguousarray(g("w_out").reshape(L, KC, 128, D).transpose(0, 2, 1, 3)).reshape(L, 128, KC * D)
    nv = [g("ffn1_norm")[0], g("ffn1_norm")[1], g("mix_norm")[0], g("mix_norm")[1], g("ffn2_norm")[0], g("ffn2_norm")[1], g("final_norm")]
    c["norms"] = np.ascontiguousarray(np.stack([v.reshape(KC, 128) for v in nv], 0).transpose(2, 0, 1))
    p = np.arange(128)
    cs = np.zeros((128, 400), f)
    cs[:, 0:128] = np.eye(128)
    cs[:, 128:256] = (p[:, None] // 64 == p[None, :] // 64)
    cs[:, 256:320] = np.arange(1, 65)[None, :]
    for q in range(4):
        for gg in range(2):
            cs[:, 320 + 2 * q + gg] = (p >= 32 * q + 16 * gg) & (p < 32 * q + 16 * gg + 16)
    for gg in range(2):
        cs[:, 328 + gg] = (p // 64 == gg)
    cs[:, 330:394] = (p[:, None] % 64 == np.arange(64)[None, :])
    c["consts"] = cs
    s_ = np.arange(64)
    m4 = np.stack([s_[:, None] < s_[None, :], s_[:, None] <= s_[None, :], s_[:, None] > s_[None, :], s_[:, None] == s_[None, :]], 0).astype(f)
    c["cmask"] = np.ascontiguousarray(m4.transpose(1, 0, 2))
    lr, li, ldt = g("s5_lambda_re"), g("s5_lambda_im"), g("s5_log_dt")
    tri = np.stack([lr, li, np.broadcast_to(ldt[:, :, None], lr.shape)], -1)
    c["s5A"] = np.ascontiguousarray(tri.reshape(L, 12, 2, 64, 3).transpose(0, 2, 3, 1, 4).reshape(L, 128, 12, 3))
    tB = np.broadcast_to(tri.reshape(L, 3, 8, 1, 64, 3), (L, 3, 8, 16, 64, 3))
    c["s5B"] = np.ascontiguousarray(tB.transpose(0, 2, 3, 1, 4, 5).reshape(L, 128, 3, 64, 3))
    bb = np.stack([g("s5_b_re"), g("s5_b_im")], -1)
    c["s5bT"] = np.ascontiguousarray(bb.reshape(L, 3, 8, 64, 16, 2).transpose(0, 2, 4, 1, 3, 5).reshape(L, 128, 3, 64, 2))
    cc = np.stack([g("s5_c_re"), g("s5_c_im")], -1)
    c["s5cT"] = np.ascontiguousarray(cc.reshape(L, 12, 2, 16, 64, 2).transpose(0, 2, 4, 1, 3, 5).reshape(L, 128, 12, 16, 2))
    c["s5v"] = np.ascontiguousarray(np.stack([g("s5_d"), g("s5_glu_b")], -1).reshape(L, 3, 128, 2).transpose(0, 2, 1, 3))
    c["s5glu"] = np.ascontiguousarray(g("s5_glu_w").reshape(L, 3, 128, 384).transpose(0, 2, 1, 3))
    z384 = np.zeros((L, 384), f)
    rw = np.stack([g("rwkv_w0"), g("rwkv_a0"), g("rwkv_k_k"), g("rwkv_k_a"), g("rwkv_r_k").reshape(L, 384), g("rwkv_ln_w"), g("rwkv_ln_b"), z384], -1)
    c["rwp"] = np.ascontiguousarray(rw.reshape(L, 3, 128, 8).transpose(0, 2, 1, 3))
    c["rwmu"] = np.ascontiguousarray(g("rwkv_mu").reshape(L, 11, 128).transpose(0, 2, 1))
    c["rwup"] = np.ascontiguousarray(np.stack([np.concatenate([g("rwkv_w_up"), g("rwkv_a_up")], 1), g("rwkv_g_up")], 2))
    lp_ = np.concatenate([g("lru_conv_w").transpose(0, 2, 1), g("lru_conv_b")[..., None], g("lru_b_a")[..., None], g("lru_b_x")[..., None],
                          g("lru_lambda")[..., None]], -1)
    c["lrp"] = np.ascontiguousarray(lp_.reshape(L, 2, 128, 8).transpose(0, 2, 1, 3))
    lw_ = np.zeros((L, 128, 2, 2, 128), f)
    for wi, nm in enumerate(("lru_w_a", "lru_w_x")):
        w = g(nm)
        for cb in range(2):
            for bq in range(2):
                lw_[:, 64 * bq:64 * bq + 64, cb, wi, 64 * bq:64 * bq + 64] = w[:, 2 * cb + bq]
    c["lrw"] = lw_
    return c


def prep_core(inp, ci, PT=2048):
    f = np.float32
    g = lambda k: np.asarray(inp[k], f)
    meta = g("meta_tokens")
    xs = g("x_sample")[16 * ci:16 * ci + 16, 0]
    xp = g("x_prompt")[ci][:PT]
    cols = np.concatenate([meta, xs, xp], 0)
    m = {"xT": np.ascontiguousarray(cols.T.reshape(KC, 128, 32 + PT).transpose(1, 0, 2))}
    sl = slice(16 * ci, 16 * ci + 16)
    st = np.stack([g("state_s5_re")[:, sl], g("state_s5_im")[:, sl]], -1)
    m["i_s5"] = np.ascontiguousarray(st.reshape(L, 16, 12, 2, 64, 2).transpose(0, 1, 3, 4, 2, 5).reshape(L, 16, 128, 12, 2))
    rs = g("state_rwkv")[:, sl]
    m["i_rwkv"] = np.ascontiguousarray(rs.reshape(L, 16, 3, 2, 64, 64).transpose(0, 1, 3, 5, 2, 4).reshape(L, 16, 128, 3, 64))
    m["i_shift"] = np.ascontiguousarray(g("state_rwkv_shift")[:, sl].reshape(L, 16, 11, 128).transpose(0, 1, 3, 2))
    m["i_lru"] = np.ascontiguousarray(g("state_lru")[:, sl].reshape(L, 16, 2, 128).transpose(0, 1, 3, 2))
    m["i_conv"] = np.ascontiguousarray(g("state_lru_conv")[:, sl].reshape(L, 16, 3, 2, 128).transpose(0, 1, 4, 3, 2))
    return m


def unpack_core(r):
    o = {}
    s5 = r["o_s5"].reshape(L, NSEQ, 2, 64, 12, 2).transpose(0, 1, 4, 2, 3, 5).reshape(L, NSEQ, 24, 64, 2)
    o["s5_re"], o["s5_im"] = s5[..., 0], s5[..., 1]
    o["rwkv"] = r["o_rwkv"].reshape(L, NSEQ, 2, 64, 3, 64).transpose(0, 1, 4, 2, 5, 3).reshape(L, NSEQ, 6, 64, 64)
    o["shift"] = r["o_shift"].transpose(0, 1, 3, 2).reshape(L, NSEQ, 1408)
    o["lru"] = r["o_lru"].transpose(0, 1, 3, 2).reshape(L, NSEQ, 256)
    o["conv"] = r["o_conv"].transpose(0, 1, 4, 3, 2).reshape(L, NSEQ, 3, 256)
    return o


_NC_CACHE = {}


def kernel(**inp):
    if "nc" not in _NC_CACHE:
        _NC_CACHE["nc"] = build()
    nc = _NC_CACHE["nc"]
    common = prep_common(inp)
    in_maps = []
    for ci in range(NCORES):
        m = dict(common)
        m.update(prep_core(inp, ci))
        in_maps.append(m)
    res = run_bass_kernel_spmd(nc, in_maps, core_ids=list(range(NCORES)))
    outs = res.results
    f = np.float32
    y_prompt = np.empty((8, 2048, D), f)
    y_sample = np.empty((128, 1, D), f)
    keys = ("s5_re", "s5_im", "rwkv", "shift", "lru", "conv")
    shp = {"s5_re": (24, 64), "s5_im": (24, 64), "rwkv": (6, 64, 64), "shift": (1408,), "lru": (256,), "conv": (3, 256)}
    P = {k: np.empty((L, 8) + shp[k], f) for k in keys}
    Sx = {k: np.empty((L, 128) + shp[k], f) for k in keys}
    for ci in range(NCORES):
        y = outs[ci]["yT"].transpose(1, 0, 2).reshape(D, 2080).T
        y_prompt[ci] = y[32:]
        y_sample[16 * ci:16 * ci + 16, 0] = y[16:32]
        o = unpack_core(outs[ci])
        for k in keys:
            P[k][:, ci] = o[k][:, 0]
            Sx[k][:, 16 * ci:16 * ci + 16] = o[k][:, 1:]
    return (y_prompt, y_sample) + tuple(P[k] for k in keys) + tuple(Sx[k] for k in keys)
```

```python
import numpy as np
import os
_RWSTOP = int(os.environ.get('RW_STOP', '99'))
_NOSAMP = int(os.environ.get('NO_SAMP', '0'))
from contextlib import ExitStack
import concourse.bass as bass
import concourse.mybir as mybir
from concourse.bass_utils import run_bass_kernel_spmd

F32 = mybir.dt.float32
BF16 = mybir.dt.bfloat16
ALU = mybir.AluOpType
AF = mybir.ActivationFunctionType
AX = mybir.AxisListType

NCORES = 8
D = 1024
KC = 8
DFF = 2816
NJ = 22
DIN = 2304
L = 2
NCOL = 2080
SLOT = 3072
NSLOT = 9
GRP = 4
EPS = 1e-6
TILES = [(0, 32)] + [(32 + 512 * i, 512) for i in range(4)]


class Em:
    ENG = ('pe', 'dve', 'act', 'pool', 'sp')

    def __init__(self, nc, es):
        self.nc = nc
        self.es = es
        self.e = dict(pe=nc.tensor, dve=nc.vector, act=nc.scalar, pool=nc.gpsimd, sp=nc.sync)
        self.sem = {k: es.enter_context(nc.semaphore("s_" + k)) for k in ('pe', 'dve', 'act', 'pool')}
        self.cnt = {k: 0 for k in self.sem}
        self.waited = {k: {} for k in self.ENG}
        self.buf = {}
        self.dsem = {}
        self.ninstr = 0
        self.alias = {}

    def _handle(self, sk):
        return self.sem[sk] if sk in self.sem else self.dsem[sk][0]

    def _wait(self, eng, ev):
        if ev is None:
            return
        sk, val = ev
        if sk == 'pe' and eng == 'pe':
            return
        if self.waited[eng].get(sk, 0) >= val:
            return
        self.waited[eng][sk] = val
        self.e[eng].wait_ge(self._handle(sk), val)

    def deps(self, eng, reads, writes):
        reads = [self.alias.get(k, k) for k in reads]
        writes = [self.alias.get(k, k) for k in writes]
        for r in reads:
            b = self.buf.get(r)
            if b:
                self._wait(eng, b[0])
        for w in writes:
            b = self.buf.get(w)
            if b:
                self._wait(eng, b[0])
                for ev in b[1].values():
                    self._wait(eng, ev)

    def record(self, ev, reads, writes):
        reads = [self.alias.get(k, k) for k in reads]
        writes = [self.alias.get(k, k) for k in writes]
        for r in reads:
            b = self.buf.setdefault(r, [None, {}])
            b[1][ev[0]] = ev
        for w in writes:
            self.buf[w] = [ev, {}]

    def op(self, eng, fn, reads=(), writes=(), inc=True):
        psr = [k for k in reads if k.startswith('ps')]
        if psr:
            reads = [k for k in reads if not k.startswith('ps')]
            writes = list(writes) + psr
        self.deps(eng, reads, writes)
        ins = fn(self.e[eng])
        self.ninstr += 1
        if inc:
            self.cnt[eng] += 1
            ins.then_inc(self.sem[eng], 1)
            ev = (eng, self.cnt[eng])
        else:
            ev = (eng, self.cnt[eng] + 1)
        self.record(ev, reads, writes)
        return ins

    def dma(self, q, out, in_, name, reads=(), writes=(), **kw):
        if name not in self.dsem:
            self.dsem[name] = [self.es.enter_context(self.nc.semaphore("d_" + name)), 0]
        self.deps(q, reads, writes)
        d = self.dsem[name]
        d[1] += 16
        self.e[q].dma_start(out=out, in_=in_, **kw).then_inc(d[0], 16)
        self.ninstr += 1
        self.record((name, d[1]), reads, writes)

    def dma_batch(self, q, items, name):
        if name not in self.dsem:
            self.dsem[name] = [self.es.enter_context(self.nc.semaphore("d_" + name)), 0]
        d = self.dsem[name]
        for (out, in_, reads, writes) in items:
            self.deps(q, reads, writes)
        final = d[1] + 16 * len(items)
        for (out, in_, reads, writes) in items:
            d[1] += 16
            self.e[q].dma_start(out=out, in_=in_).then_inc(d[0], 16)
            self.ninstr += 1
            self.record((name, final), reads, writes)

    def barrier(self):
        for eng in self.ENG:
            for name, d in self.dsem.items():
                self._wait(eng, (name, d[1]))
            for k in self.sem:
                if self.cnt[k]:
                    self._wait(eng, (k, self.cnt[k]))

    def finish(self, eng='sp'):
        for name, d in self.dsem.items():
            self._wait(eng, (name, d[1]))
        for k in self.sem:
            if self.cnt[k]:
                self._wait(eng, (k, self.cnt[k]))


PI2 = 6.283185307179586
NB = 64
NSEQ = 17


def build(stage=99, PT=2048, mixers=('lru', 's5', 'rwkv')):
    NCOL_ = 32 + PT
    tiles = [(0, 32)] + [(32 + 512 * i, min(512, PT - 512 * i)) for i in range((PT + 511) // 512)]
    NT = len(tiles)
    nc = bass.Bass("TRN2", target_bir_lowering=False)
    es = ExitStack()
    with es:
        es.enter_context(nc.allow_non_contiguous_dma(reason="small strided state io"))
        em = Em(nc, es)

        def din(name, shape, dt=F32):
            return nc.dram_tensor(name, list(shape), dt, kind="ExternalInput").ap()

        def dout(name, shape, dt=F32):
            return nc.dram_tensor(name, list(shape), dt, kind="ExternalOutput").ap()

        def sbuf(stack, name, shape, dt=F32):
            return stack.enter_context(nc.sbuf_tensor(name, list(shape), dt))

        xT = din("xT", [128, KC, NCOL_])
        wffn = din("wffn", [L, 2, NJ, 128, SLOT])
        w_in = din("w_in", [L, 128, KC * DIN])
        w_out = din("w_out", [L, 128, KC * D])
        norms = din("norms", [128, 7, KC])
        consts = din("consts", [128, 400])
        cmask = din("cmask", [64, 4, 64])
        s5A = din("s5A", [L, 128, 12, 3])
        s5B = din("s5B", [L, 128, 3, 64, 3])
        s5bT = din("s5bT", [L, 128, 3, 64, 2])
        s5cT = din("s5cT", [L, 128, 12, 16, 2])
        s5v = din("s5v", [L, 128, 3, 2])
        s5glu = din("s5glu", [L, 128, 3, 384])
        rwp = din("rwp", [L, 128, 3, 8])
        rwmu = din("rwmu", [L, 128, 11])
        rwup = din("rwup", [L, 128, 2, 384])
        lrp = din("lrp", [L, 128, 2, 8])
        lrw = din("lrw", [L, 128, 2, 2, 128])
        i_s5 = din("i_s5", [L, 16, 128, 12, 2])
        i_rwkv = din("i_rwkv", [L, 16, 128, 3, 64])
        i_shift = din("i_shift", [L, 16, 128, 11])
        i_lru = din("i_lru", [L, 16, 128, 2])
        i_conv = din("i_conv", [L, 16, 128, 2, 3])
        yT = dout("yT", [128, KC, NCOL_])
        o_s5 = dout("o_s5", [L, NSEQ, 128, 12, 2])
        o_rwkv = dout("o_rwkv", [L, NSEQ, 128, 3, 64])
        o_shift = dout("o_shift", [L, NSEQ, 128, 11])
        o_lru = dout("o_lru", [L, NSEQ, 128, 2])
        o_conv = dout("o_conv", [L, NSEQ, 128, 2, 3])

        xres = sbuf(es, "xres", [128, KC, NCOL_])
        ring = sbuf(es, "ring", [128, NSLOT * SLOT], BF16)
        normw = sbuf(es, "normw", [128, 7, KC])
        ones_bf = sbuf(es, "ones_bf", [128, 128], BF16)
        cst = sbuf(es, "cst", [128, 400])
        ps = [es.enter_context(nc.psum_tensor("ps%d" % i, [128, 512], F32)) for i in range(8)]
        ident = cst[:, 0:128]
        bones = cst[:, 128:256]
        iota1 = cst[:, 256:320]
        mask8 = cst[:, 320:328]
        maskg2 = cst[:, 328:330]
        identb = cst[:, 330:394]

        em.op('pool', lambda e: e.memset(ones_bf[:], 1.0), writes=['ones_bf'])
        em.dma('sp', normw[:], norms, 'ld_c', writes=['normw'])
        em.dma('sp', cst[:], consts, 'ld_c2', writes=['cst'])
        em.dma_batch('sp', [(xres[:, kc, :], xT[:, kc, :], [], ['xres']) for kc in range(KC)], 'ld_x')

        def T(eng, out, a, b, op, R, W):
            em.op(eng, lambda e: e.tensor_tensor(out=out, in0=a, in1=b, op=op), reads=R, writes=W)

        def S(eng, out, a, s1, s2, op0, op1, R, W):
            em.op(eng, lambda e: e.tensor_scalar(out=out, in0=a, scalar1=s1, scalar2=s2, op0=op0, op1=op1), reads=R, writes=W)

        def STT(out, a, sc, b, op0, op1, R, W):
            em.op('dve', lambda e: e.scalar_tensor_tensor(out=out, in0=a, scalar=sc, in1=b, op0=op0, op1=op1), reads=R, writes=W)

        def A(out, in_, func, R, W, bias=0.0, scale=1.0):
            em.op('act', lambda e: e.activation(out=out, in_=in_, func=func, bias=bias, scale=scale), reads=R, writes=W)

        def CP(eng, out, in_, R, W):
            if eng == 'act':
                em.op('act', lambda e: e.copy(out=out, in_=in_), reads=R, writes=W)
            else:
                em.op(eng, lambda e: e.tensor_copy(out=out, in_=in_), reads=R, writes=W)

        def MM(out, lhsT, rhs, R, W, start=True, stop=True):
            em.op('pe', lambda e: e.matmul(out, lhsT=lhsT, rhs=rhs, start=start, stop=stop), reads=R, writes=W, inc=stop)

        def TR(out, in_, idn, R, W):
            em.op('pe', lambda e: e.transpose(out=out, in_=in_, identity=idn), reads=R + ['cst'], writes=W)

        bank = {'A': 0, 'B': 0}

        def nb(pool='B'):
            b = bank[pool]
            bank[pool] = (bank[pool] + 1) % 4
            return b + (0 if pool == 'A' else 4)

        def load_slot(s, src):
            dst = ring[:, s * SLOT:(s + 1) * SLOT]
            em.dma('pool', dst.rearrange("p (a b) -> p a b", b=1024), src.rearrange("p (a b) -> p a b", b=1024),
                   'ring%d' % s, writes=['ring%d' % s])

        def ffn(l, f):
            fs = ExitStack()
            with fs:
                xn = sbuf(fs, "xn%d%d" % (l, f), [128, KC, NCOL_], BF16)
                hbuf = sbuf(fs, "hbuf%d%d" % (l, f), [128, 2, GRP, 512], BF16)
                sq = sbuf(fs, "sq%d%d" % (l, f), [128, KC, 512], BF16)
                rstd = sbuf(fs, "rstd%d%d" % (l, f), [128, 512])
                silu = sbuf(fs, "silu%d%d" % (l, f), [128, 2, 512])
                nidx = (0 if f == 0 else 4) + l
                for ti in range(NT):
                    c0, n = tiles[ti]
                    A(sq[:, :, :n], xres[:, :, c0:c0 + n], AF.Square, ['xres'], ['sq'])
                    for kc in range(KC):
                        MM(ps[7][:, :n], ones_bf[:], sq[:, kc, :n], ['ones_bf', 'sq'], ['ps7'], start=(kc == 0), stop=(kc == KC - 1))
                    S('dve', rstd[:, :n], ps[7][:, :n], 1.0 / D, EPS, ALU.mult, ALU.add, ['ps7'], ['rstd'])
                    A(rstd[:, :n], rstd[:, :n], AF.Sqrt, ['rstd'], ['rstd'])
                    em.op('dve', lambda e: e.reciprocal(out=rstd[:, :n], in_=rstd[:, :n]), reads=['rstd'], writes=['rstd'])
                    for kc in range(KC):
                        STT(xn[:, kc, c0:c0 + n], xres[:, kc, c0:c0 + n], normw[:, nidx, kc:kc + 1], rstd[:, :n],
                            ALU.mult, ALU.mult, ['xres', 'rstd', 'normw'], ['xn%d' % ti])
                groups = [list(range(g, min(g + GRP, NJ))) for g in range(0, NJ, GRP)]
                slot_of = {}
                nxt = [0]

                def issue_group(gi):
                    for j in groups[gi]:
                        s_ = nxt[0] % 8
                        nxt[0] += 1
                        slot_of[j] = s_
                        load_slot(s_, wffn[l, f, j])

                issue_group(0)
                for gi, grp in enumerate(groups):
                    if gi + 1 < len(groups):
                        issue_group(gi + 1)

                    def gu(ti):
                        c0, n = tiles[ti]
                        hb = ti % 2
                        for ji, j in enumerate(grp):
                            s_ = slot_of[j]
                            for which in (0, 1):
                                b = 2 * (ji % 2) + which
                                for kc in range(KC):
                                    o = s_ * SLOT + which * 1024 + kc * 128
                                    MM(ps[b][:, :n], ring[:, o:o + 128], xn[:, kc, c0:c0 + n], ['ring%d' % s_, 'xn%d' % ti],
                                       ['ps%d' % b], start=(kc == 0), stop=(kc == KC - 1))
                            b = 2 * (ji % 2)
                            A(silu[:, ji % 2, :n], ps[b][:, :n], AF.Silu, ['ps%d' % b], ['silu%d' % (ji % 2)])
                            T('dve', hbuf[:, hb, ji, :n], silu[:, ji % 2, :n], ps[b + 1][:, :n], ALU.mult,
                              ['silu%d' % (ji % 2), 'ps%d' % (b + 1)], ['h%d_%d' % (hb, ji)])

                    def down(ti):
                        c0, n = tiles[ti]
                        hb = ti % 2
                        for mc in range(KC):
                            b = 4 + mc % 3
                            for ji, j in enumerate(grp):
                                o = slot_of[j] * SLOT + 2048 + mc * 128
                                MM(ps[b][:, :n], ring[:, o:o + 128], hbuf[:, hb, ji, :n], ['ring%d' % slot_of[j], 'h%d_%d' % (hb, ji)],
                                   ['ps%d' % b], start=(ji == 0), stop=(ji == len(grp) - 1))
                            STT(xres[:, mc, c0:c0 + n], ps[b][:, :n], 0.5, xres[:, mc, c0:c0 + n], ALU.mult, ALU.add,
                                ['ps%d' % b, 'xres'], ['xres'])

                    gu(0)
                    for ti in range(NT):
                        if ti + 1 < NT:
                            gu(ti + 1)
                        down(ti)
                em.barrier()

        def mixer_phase(l):
            bs = ExitStack()
            with bs:
                def sb(name, shape, dt=F32):
                    return sbuf(bs, name + "_%d" % l, shape, dt)
                for i in range(3):
                    em.dma('pool', ring[:, i * 6144:(i + 1) * 6144].rearrange("p (a b) -> p a b", b=2048),
                           w_in[l][:, i * 6144:(i + 1) * 6144].rearrange("p (a b) -> p a b", b=2048), 'win%d' % i,
                           writes=['ring%d' % (2 * i), 'ring%d' % (2 * i + 1)])
                em.dma('pool', ring[:, 18432:26624].rearrange("p (a b) -> p a b", b=2048),
                       w_out[l].rearrange("p (a b) -> p a b", b=2048), 'wout', writes=['ring6', 'ring7', 'ring8'])
                RIN = ['ring%d' % i for i in range(6)]
                ROUT = ['ring6', 'ring7', 'ring8']

                xnb = sb("xnb", [128, KC, NB], BF16)
                sqb = sb("sqb", [128, KC, NB], BF16)
                rstb = sb("rstb", [128, NB])
                pj = sb("pj", [128, 18, 3 + NB])
                tmpc = sb("tmpc", [128, 18, 3])
                omix = sb("omix", [128, KC, NB], BF16)
                g1_def = g2_def = None
                lg1 = sb("lg1", [128, 2, NB])
                lg2 = sb("lg2", [128, 2, NB])
                s5st = sb("s5st", [128, 16, 12, 2])
                lst = sb("lst", [128, 16, 2])
                cst16 = sb("cst16", [128, 16, 2, 3])
                sh16 = sb("sh16", [128, 16, 11])
                h0v = lambda ri: s5st[:, :, :, ri].rearrange("p b g -> p g b")
                lh0v = lst[:].rearrange("p b c -> p c b")
                convv = lambda j: cst16[:, :, :, j].rearrange("p b c -> p c b")
                prevv = sh16[:].rearrange("p b c -> p c b")
                cm = sb("cm", [64, 4, 64])
                em.dma('sp', cm[:], cmask, 'ld_cm', writes=['cm'])
                MSU, MU, MSL, I6 = cm[:, 0, :], cm[:, 1, :], cm[:, 2, :], cm[:, 3, :]

                def gelu(dst, src, k, n, R, W, g1=None, g2_=None, kk_='g'):
                    g1 = g1 if g1 is not None else g1_def
                    g2_ = g2_ if g2_ is not None else g2_def
                    k1, k2 = kk_ + '1', kk_ + '2'
                    A(g1[:, :k, :n], src, AF.Square, R, [k1])
                    S('dve', g1[:, :k, :n], g1[:, :k, :n], 0.044715, 1.0, ALU.mult, ALU.add, [k1], [k1])
                    T('dve', g1[:, :k, :n], g1[:, :k, :n], src, ALU.mult, [k1] + R, [k1])
                    A(g2_[:, :k, :n], g1[:, :k, :n], AF.Sigmoid, [k1], [k2], scale=1.5957691216057308)
                    T('dve', dst, g2_[:, :k, :n], src, ALU.mult, [k2] + R, W)

                lp = sb("lp", [128, 2, 8])
                lw = sb("lw", [128, 2, 2, 128])
                lcl = sb("lcl", [128, 2])
                em.dma('sp', lp[:], lrp[l], 'ld_lp', writes=['lp'])
                em.dma('sp', lw[:], lrw[l], 'ld_lw', writes=['lw'])
                A(lcl[:], lp[:, :, 7], AF.Sigmoid, ['lp'], ['lcl'])
                A(lcl[:], lcl[:], AF.Ln, ['lcl'], ['lcl'])
                S('dve', lcl[:], lcl[:], 8.0, None, ALU.mult, ALU.bypass, ['lcl'], ['lcl'])
                lru_h = sb("lru_h", [128, 2])
                lxc = sb("lxc", [128, 2, NB])
                lga = sb("lga", [128, 2, NB])
                lgx = sb("lgx", [128, 2, NB])
                la = sb("la", [128, 2, NB])
                lb = sb("lb", [128, 2, NB])
                lh = sb("lh", [128, 2, NB])

                def lru_tile(n, samp=False):
                    gelu(lg1[:, :, :n], pj[:, 16:18, 3:3 + n], 2, n, ['pj'], ['lg1'], lg1, lg2, 'lg')
                    yield
                    for c in range(2):
                        S('dve', lxc[:, c, :n], pj[:, 14 + c, 3:3 + n], lp[:, c, 3:4], lp[:, c, 4:5], ALU.mult, ALU.add,
                          ['pj', 'lp'], ['lxc'])
                        yield
                        for j in range(3):
                            STT(lxc[:, c, :n], (convv(j)[:, c, :] if samp else pj[:, 14 + c, j:j + n]), lp[:, c, j:j + 1], lxc[:, c, :n],
                                ALU.mult, ALU.add, ['pj', 'lp', 'lxc', 'cst16'], ['lxc'])
                        yield
                    yield
                    b = nb()
                    yield
                    for c in range(2):
                        MM(ps[b][:, c * 64:c * 64 + n], lw[:, c, 0, :], lxc[:, c, :n], ['lw', 'lxc'], ['ps%d' % b])
                        yield
                        MM(ps[b][:, 128 + c * 64:128 + c * 64 + n], lw[:, c, 1, :], lxc[:, c, :n], ['lw', 'lxc'], ['ps%d' % b])
                        yield
                    yield
                    for c in range(2):
                        A(lga[:, c, :n], ps[b][:, c * 64:c * 64 + n], AF.Sigmoid, ['ps%d' % b, 'lp'], ['lga'], bias=lp[:, c, 5:6])
                        yield
                        A(lgx[:, c, :n], ps[b][:, 128 + c * 64:128 + c * 64 + n], AF.Sigmoid, ['ps%d' % b, 'lp'], ['lgx'], bias=lp[:, c, 6:7])
                        yield
                        A(la[:, c, :n], lga[:, c, :n], AF.Exp, ['lga', 'lcl'], ['la'], scale=lcl[:, c:c + 1])
                        yield
                    yield
                    T('dve', lb[:, :, :n], la[:, :, :n], la[:, :, :n], ALU.mult, ['la'], ['lb'])
                    yield
                    S('dve', lb[:, :, :n], lb[:, :, :n], -1.0, 1.0, ALU.mult, ALU.add, ['lb'], ['lb'])
                    yield
                    A(lb[:, :, :n], lb[:, :, :n], AF.Sqrt, ['lb'], ['lb'])
                    yield
                    T('dve', lb[:, :, :n], lb[:, :, :n], lgx[:, :, :n], ALU.mult, ['lb', 'lgx'], ['lb'])
                    yield
                    T('dve', lb[:, :, :n], lb[:, :, :n], lxc[:, :, :n], ALU.mult, ['lb', 'lxc'], ['lb'])
                    yield
                    if samp:
                        T('dve', lh[:, :, :n], la[:, :, :n], lh0v, ALU.mult, ['la', 'lst'], ['lh'])
                        T('dve', lh[:, :, :n], lh[:, :, :n], lb[:, :, :n], ALU.add, ['lh', 'lb'], ['lh'])
                        CP('dve', lh0v, lh[:, :, :n], ['lh'], ['lst'])
                        CP('dve', convv(0), convv(1), ['cst16', 'lxc'], ['cst16'])
                        CP('dve', convv(1), convv(2), ['cst16'], ['cst16'])
                        CP('dve', convv(2), pj[:, 14:16, 3:3 + n], ['pj'], ['cst16'])
                    else:
                        for c in range(2):
                            em.op('dve', lambda e, c=c: e.tensor_tensor_scan(out=lh[:, c, :n], data0=la[:, c, :n], data1=lb[:, c, :n],
                                                                          initial=lru_h[:, c:c + 1], op0=ALU.mult, op1=ALU.add),
                                  reads=['la', 'lb', 'lru_h'], writes=['lh'])
                        CP('dve', lru_h[:], lh[:, :, n - 1], ['lh'], ['lru_h'])
                    yield
                    T('dve', omix[:, 6:8, :n], lh[:, :, :n], lg1[:, :, :n], ALU.mult, ['lh', 'lg1'], ['omix_l'])
                    yield

                hst = sb("hst", [128, 12, 2])
                if 's5' in mixers:
                    sv = sb("sv", [128, 3, 2])
                    sglu = sb("sglu", [128, 3, 384], BF16)
                    szb = sb("szb", [128, 3, NB], BF16)
                    lB = sb("lB", [128, 2, 12, 128], BF16)
                    lC = sb("lC", [128, 2, 12, 128], BF16)
                    rmag = sb("rmag", [128, 12])
                    Et = sb("Et", [128, 2, 12, NB])
                    abar = sb("abar", [128, 12, 2])
                    ss = ExitStack()
                    ss.__enter__()
                    sA = sbuf(ss, "sA_%d" % l, [128, 12, 3])
                    sBp = sbuf(ss, "sBp_%d" % l, [128, 3, 64, 3])
                    sbT = sbuf(ss, "sbT_%d" % l, [128, 3, 64, 2])
                    scT = sbuf(ss, "scT_%d" % l, [128, 12, 16, 2])
                    em.dma('sp', sA[:], s5A[l], 'ld_s5a', writes=['sA'])
                    em.dma('sp', sBp[:], s5B[l], 'ld_s5b', writes=['sBp'])
                    em.dma('sp', sbT[:], s5bT[l], 'ld_s5c', writes=['sbT'])
                    em.dma('sp', scT[:], s5cT[l], 'ld_s5d', writes=['scT'])
                    em.dma('sp', sv[:], s5v[l], 'ld_s5e', writes=['sv'])
                    em.dma('pool', sglu[:], s5glu[l], 'ld_s5f', writes=['sglu'])
                    wk = sbuf(ss, "wk_%d" % l, [128, 7, 64])
                    wk7 = sbuf(ss, "wk7_%d" % l, [128, 768])
                    wk8 = sbuf(ss, "wk8_%d" % l, [128, 768])
                    wki = sbuf(ss, "wki_%d" % l, [128, 768], mybir.dt.int32)

                    def rr(x, m, k, R, W):
                        tmp_ = wk[:, k, :m] if m <= 64 else wk8[:, :m]
                        CP('dve', wki[:, :m], x, R, ['wki'])
                        CP('dve', tmp_, wki[:, :m], ['wki'], ['wk%d' % k])
                        T('dve', x, x, tmp_, ALU.subtract, R + ['wk%d' % k], W)
                        S('dve', tmp_, x, 0.5, None, ALU.is_gt, ALU.bypass, W, ['wk%d' % k])
                        T('dve', x, x, tmp_, ALU.subtract, W + ['wk%d' % k], W)
                        S('dve', tmp_, x, -0.5, None, ALU.is_lt, ALU.bypass, W, ['wk%d' % k])
                        T('dve', x, x, tmp_, ALU.add, W + ['wk%d' % k], W)

                    def disc(lr, li, ldt, m, pre):
                        A(wk[:, 4, :m], ldt, AF.Exp, [pre], ['wk4'])
                        T('dve', wk[:, 0, :m], lr, wk[:, 4, :m], ALU.mult, [pre, 'wk4'], ['wk0'])
                        A(wk[:, 0, :m], wk[:, 0, :m], AF.Exp, ['wk0'], ['wk0'])
                        T('dve', wk[:, 3, :m], li, wk[:, 4, :m], ALU.mult, [pre, 'wk4'], ['wk3'])
                        S('dve', wk[:, 3, :m], wk[:, 3, :m], 1.0 / PI2, None, ALU.mult, ALU.bypass, ['wk3'], ['wk3'])
                        rr(wk[:, 3, :m], m, 5, ['wk3'], ['wk3'])
                        A(wk[:, 2, :m], wk[:, 3, :m], AF.Sin, ['wk3'], ['wk2'], scale=PI2 * 0.999999)
                        S('dve', wk[:, 6, :m], wk[:, 3, :m], 0.25, None, ALU.add, ALU.bypass, ['wk3'], ['wk6'])
                        rr(wk[:, 6, :m], m, 5, ['wk6'], ['wk6'])
                        A(wk[:, 1, :m], wk[:, 6, :m], AF.Sin, ['wk6'], ['wk1'], scale=PI2 * 0.999999)
                        T('dve', wk[:, 1, :m], wk[:, 1, :m], wk[:, 0, :m], ALU.mult, ['wk1', 'wk0'], ['wk1'])
                        T('dve', wk[:, 2, :m], wk[:, 2, :m], wk[:, 0, :m], ALU.mult, ['wk2', 'wk0'], ['wk2'])

                    disc(sA[:, :, 0], sA[:, :, 1], sA[:, :, 2], 12, 'sA')
                    CP('dve', abar[:, :, 0], wk[:, 1, :12], ['wk1'], ['abar'])
                    CP('dve', abar[:, :, 1], wk[:, 2, :12], ['wk2'], ['abar'])
                    CP('dve', rmag[:], wk[:, 0, :12], ['wk0'], ['Rt'])
                    wkE = wk7[:, :].rearrange("p (a b) -> p a b", b=NB)
                    T('dve', wkE, wk[:, 3, :12].unsqueeze(2).to_broadcast([128, 12, NB]),
                      iota1.unsqueeze(1).to_broadcast([128, 12, NB]), ALU.mult, ['wk3', 'cst'], ['wk7'])
                    rr(wk7[:, :], 768, 5, ['wk7'], ['wk7'])
                    A(Et[:, 1, :, :], wkE, AF.Sin, ['wk7'], ['Et'], scale=-PI2 * 0.999999)
                    S('dve', wk7[:, :], wk7[:, :], 0.25, None, ALU.add, ALU.bypass, ['wk7'], ['wk7'])
                    rr(wk7[:, :], 768, 5, ['wk7'], ['wk7'])
                    A(Et[:, 0, :, :], wkE, AF.Sin, ['wk7'], ['Et'], scale=PI2 * 0.999999)
                    for kc in range(3):
                        lr, li = sBp[:, kc, :, 0], sBp[:, kc, :, 1]
                        disc(lr, li, sBp[:, kc, :, 2], 64, 'sBp')
                        m = 64
                        T('dve', wk[:, 4, :m], lr, lr, ALU.mult, ['sBp'], ['wk4'])
                        T('dve', wk[:, 5, :m], li, li, ALU.mult, ['sBp'], ['wk5'])
                        T('dve', wk[:, 4, :m], wk[:, 4, :m], wk[:, 5, :m], ALU.add, ['wk4', 'wk5'], ['wk4'])
                        em.op('dve', lambda e: e.reciprocal(out=wk[:, 4, :64], in_=wk[:, 4, :64]), reads=['wk4'], writes=['wk4'])
                        S('dve', wk[:, 1, :m], wk[:, 1, :m], -1.0, None, ALU.add, ALU.bypass, ['wk1'], ['wk1'])
                        T('dve', wk[:, 5, :m], wk[:, 1, :m], lr, ALU.mult, ['wk1', 'sBp'], ['wk5'])
                        T('dve', wk[:, 6, :m], wk[:, 2, :m], li, ALU.mult, ['wk2', 'sBp'], ['wk6'])
                        T('dve', wk[:, 5, :m], wk[:, 5, :m], wk[:, 6, :m], ALU.add, ['wk5', 'wk6'], ['wk5'])
                        T('dve', wk[:, 5, :m], wk[:, 5, :m], wk[:, 4, :m], ALU.mult, ['wk5', 'wk4'], ['wk5'])
                        T('dve', wk[:, 6, :m], wk[:, 2, :m], lr, ALU.mult, ['wk2', 'sBp'], ['wk6'])
                        T('dve', wk[:, 0, :m], wk[:, 1, :m], li, ALU.mult, ['wk1', 'sBp'], ['wk0'])
                        T('dve', wk[:, 6, :m], wk[:, 6, :m], wk[:, 0, :m], ALU.subtract, ['wk6', 'wk0'], ['wk6'])
                        T('dve', wk[:, 6, :m], wk[:, 6, :m], wk[:, 4, :m], ALU.mult, ['wk6', 'wk4'], ['wk6'])
                        bre, bim = sbT[:, kc, :, 0], sbT[:, kc, :, 1]
                        T('dve', wk[:, 0, :m], wk[:, 5, :m], bre, ALU.mult, ['wk5', 'sbT'], ['wk0'])
                        T('dve', wk[:, 1, :m], wk[:, 6, :m], bim, ALU.mult, ['wk6', 'sbT'], ['wk1'])
                        T('dve', wk[:, 0, :m], wk[:, 0, :m], wk[:, 1, :m], ALU.subtract, ['wk0', 'wk1'], ['wk0'])
                        T('dve', wk[:, 1, :m], wk[:, 5, :m], bim, ALU.mult, ['wk5', 'sbT'], ['wk1'])
                        T('dve', wk[:, 2, :m], wk[:, 6, :m], bre, ALU.mult, ['wk6', 'sbT'], ['wk2'])
                        T('dve', wk[:, 1, :m], wk[:, 1, :m], wk[:, 2, :m], ALU.add, ['wk1', 'wk2'], ['wk1'])
                        for q in range(4):
                            for gg in range(2):
                                for ri in range(2):
                                    S('dve', lB[:, ri, 4 * kc + q, 64 * gg:64 * gg + 64], wk[:, ri, :64],
                                      mask8[:, 2 * q + gg:2 * q + gg + 1], None, ALU.mult, ALU.bypass, ['wk%d' % ri, 'cst'], ['lB'])
                    em.op('pool', lambda e: e.memset(lC[:], 0.0), writes=['lC'])
                    for gh in range(12):
                        q = gh % 4
                        for gg in range(2):
                            S('dve', lC[:, 0, gh, 32 * q + 16 * gg:32 * q + 16 * gg + 16], scT[:, gh, :, 0], maskg2[:, gg:gg + 1], None,
                              ALU.mult, ALU.bypass, ['scT', 'cst'], ['lC'])
                            S('dve', lC[:, 1, gh, 32 * q + 16 * gg:32 * q + 16 * gg + 16], scT[:, gh, :, 1], maskg2[:, gg:gg + 1], -1.0,
                              ALU.mult, ALU.mult, ['scT', 'cst'], ['lC'])
                    em.barrier()
                    ss.close()
                    sx = sb("sx", [128, 2, 12, NB])
                    sg = sb("sg", [128, 2, 12, NB])
                    sh = sx
                    em.alias['sh'] = 'sx'
                    shb = sb("shb", [128, 2, 12, NB], BF16)
                    ubf = sb("ubf", [128, 3, NB], BF16)
                    st1 = sb("st1", [128, 12, NB])
                    sy = sb("sy", [128, 3, NB])
                    sz = sb("sz", [128, 3, NB])

                def s5_tile(n, samp=False):
                    CP('act', ubf[:, :, :n], pj[:, 0:3, 3:3 + n], ['pj'], ['ubf'])
                    yield
                    bre = [nb('A'), nb('A')]
                    yield
                    bim = [nb('A'), nb('A')]
                    yield
                    for ri, bb in ((0, bre), (1, bim)):
                        for gh in range(12):
                            b = bb[gh // 8]
                            o = (gh % 8) * 64
                            MM(ps[b][:, o:o + n], lB[:, ri, gh, :], ubf[:, gh // 4, :n], ['lB', 'ubf'], ['ps%d' % b])
                    yield
                    if samp:
                        for half, (g0, g1_) in enumerate(((0, 8), (8, 12))):
                            k = g1_ - g0
                            pr = ps[bre[half]][:, 0:k * 64].rearrange("p (a b) -> p a b", b=64)[:, :, :n]
                            pi = ps[bim[half]][:, 0:k * 64].rearrange("p (a b) -> p a b", b=64)[:, :, :n]
                            ar = abar[:, g0:g1_, 0].unsqueeze(2).to_broadcast([128, k, n])
                            ai = abar[:, g0:g1_, 1].unsqueeze(2).to_broadcast([128, k, n])
                            h0r, h0i = h0v(0)[:, g0:g1_, :], h0v(1)[:, g0:g1_, :]
                            T('dve', sg[:, 0, g0:g1_, :n], h0r, ar, ALU.mult, ['s5st', 'abar'], ['sg'])
                            T('dve', st1[:, g0:g1_, :n], h0i, ai, ALU.mult, ['s5st', 'abar'], ['st1'])
                            T('dve', sg[:, 0, g0:g1_, :n], sg[:, 0, g0:g1_, :n], st1[:, g0:g1_, :n], ALU.subtract, ['sg', 'st1'], ['sg'])
                            T('dve', sx[:, 0, g0:g1_, :n], sg[:, 0, g0:g1_, :n], pr, ALU.add, ['sg', 'ps%d' % bre[half]], ['sx'])
                            T('dve', sg[:, 1, g0:g1_, :n], h0i, ar, ALU.mult, ['s5st', 'abar'], ['sg'])
                            T('dve', st1[:, g0:g1_, :n], h0r, ai, ALU.mult, ['s5st', 'abar'], ['st1'])
                            T('dve', sg[:, 1, g0:g1_, :n], sg[:, 1, g0:g1_, :n], st1[:, g0:g1_, :n], ALU.add, ['sg', 'st1'], ['sg'])
                            T('dve', sx[:, 1, g0:g1_, :n], sg[:, 1, g0:g1_, :n], pi, ALU.add, ['sg', 'ps%d' % bim[half]], ['sx'])
                        CP('dve', h0v(0), sx[:, 0, :, :n], ['sx'], ['s5st'])
                        CP('dve', h0v(1), sx[:, 1, :, :n], ['sx'], ['s5st'])
                    else:
                        for half, (g0, g1_) in enumerate(((0, 8), (8, 12))):
                            k = g1_ - g0
                            pr = ps[bre[half]][:, 0:k * 64].rearrange("p (a b) -> p a b", b=64)[:, :, :n]
                            pi = ps[bim[half]][:, 0:k * 64].rearrange("p (a b) -> p a b", b=64)[:, :, :n]
                            Rr, Ri = ['ps%d' % bre[half], 'Et'], ['ps%d' % bim[half], 'Et']
                            er, ei = Et[:, 0, g0:g1_, :n], Et[:, 1, g0:g1_, :n]
                            T('dve', sx[:, 0, g0:g1_, :n], pr, er, ALU.mult, Rr, ['sx'])
                            T('dve', st1[:, g0:g1_, :n], pi, ei, ALU.mult, Ri, ['st1'])
                            T('dve', sx[:, 0, g0:g1_, :n], sx[:, 0, g0:g1_, :n], st1[:, g0:g1_, :n], ALU.subtract, ['sx', 'st1'], ['sx'])
                            T('dve', sx[:, 1, g0:g1_, :n], pr, ei, ALU.mult, Rr, ['sx'])
                            T('dve', st1[:, g0:g1_, :n], pi, er, ALU.mult, Ri, ['st1'])
                            T('dve', sx[:, 1, g0:g1_, :n], sx[:, 1, g0:g1_, :n], st1[:, g0:g1_, :n], ALU.add, ['sx', 'st1'], ['sx'])
                        for ri in range(2):
                            for gh in range(12):
                                em.op('dve', lambda e, ri=ri, gh=gh: e.tensor_tensor_scan(
                                    out=sg[:, ri, gh, :n], data0=rmag[:, gh:gh + 1].to_broadcast([128, n]), data1=sx[:, ri, gh, :n],
                                    initial=hst[:, gh, ri:ri + 1], op0=ALU.mult, op1=ALU.add),
                                    reads=['Rt', 'sx', 'hst'], writes=['sg'])
                        er, ei = Et[:, 0, :, :n], Et[:, 1, :, :n]
                        T('dve', sh[:, 0, :, :n], sg[:, 0, :, :n], er, ALU.mult, ['sg', 'Et'], ['sh'])
                        T('dve', st1[:, :, :n], sg[:, 1, :, :n], ei, ALU.mult, ['sg', 'Et'], ['st1'])
                        T('dve', sh[:, 0, :, :n], sh[:, 0, :, :n], st1[:, :, :n], ALU.add, ['sh', 'st1'], ['sh'])
                        T('dve', sh[:, 1, :, :n], sg[:, 1, :, :n], er, ALU.mult, ['sg', 'Et'], ['sh'])
                        T('dve', st1[:, :, :n], sg[:, 0, :, :n], ei, ALU.mult, ['sg', 'Et'], ['st1'])
                        T('dve', sh[:, 1, :, :n], sh[:, 1, :, :n], st1[:, :, :n], ALU.subtract, ['sh', 'st1'], ['sh'])
                        CP('dve', hst[:, :, 0], sh[:, 0, :, n - 1], ['sh'], ['hst'])
                        CP('dve', hst[:, :, 1], sh[:, 1, :, n - 1], ['sh'], ['hst'])
                    yield
                    CP('act', shb[:, :, :, :n], sh[:, :, :, :n], ['sh'], ['shb'])
                    yield
                    b = nb('A')
                    yield
                    for kc in range(3):
                        for q in range(4):
                            gh = 4 * kc + q
                            MM(ps[b][:, kc * 64:kc * 64 + n], lC[:, 0, gh, :], shb[:, 0, gh, :n], ['lC', 'shb'], ['ps%d' % b],
                               start=(q == 0), stop=False)
                            MM(ps[b][:, kc * 64:kc * 64 + n], lC[:, 1, gh, :], shb[:, 1, gh, :n], ['lC', 'shb'], ['ps%d' % b],
                               start=False, stop=(q == 3))
                    yield
                    for kc in range(3):
                        STT(sy[:, kc, :n], pj[:, kc, 3:3 + n], sv[:, kc, 0:1], ps[b][:, kc * 64:kc * 64 + n], ALU.mult, ALU.add,
                            ['pj', 'sv', 'ps%d' % b], ['sy'])
                    yield
                    gelu(sz[:, :, :n], sy[:, :, :n], 3, n, ['sy'], ['sz'], st1[:, 0:3, :], st1[:, 3:6, :], 'st')
                    yield
                    CP('act', szb[:, :, :n], sz[:, :, :n], ['sz'], ['szb'])
                    yield
                    b = nb('A')
                    yield
                    for k2 in range(3):
                        for kc in range(3):
                            MM(ps[b][:, k2 * 64:k2 * 64 + n], sglu[:, kc, k2 * 128:(k2 + 1) * 128], szb[:, kc, :n], ['sglu', 'szb'],
                               ['ps%d' % b], start=(kc == 0), stop=(kc == 2))
                    yield
                    for k2 in range(3):
                        A(sy[:, k2, :n], ps[b][:, k2 * 64:k2 * 64 + n], AF.Sigmoid, ['ps%d' % b, 'sv'], ['sy'], bias=sv[:, k2, 1:2])
                    yield
                    T('dve', omix[:, 0:3, :n], sz[:, :, :n], sy[:, :, :n], ALU.mult, ['sz', 'sy'], ['omix_s'])
                    yield

                S0T = sb("S0T", [128, 3, 64])
                if 'rwkv' in mixers:
                    rp = sb("rp", [128, 3, 8])
                    rmu = sb("rmu", [128, 11])
                    rup = sb("rup", [128, 2, 384])
                    em.dma('sp', rp[:], rwp[l], 'ld_r1', writes=['rp'])
                    em.dma('sp', rmu[:], rwmu[l], 'ld_r2', writes=['rmu'])
                    em.dma('sp', rup[:], rwup[l], 'ld_r3', writes=['rup'])
                    ones64 = sb("ones64", [128, NB])
                    em.op('pool', lambda e: e.memset(ones64[:], 1.0), writes=['ones64'])
                    xm = sb("xm", [128, 11, NB])
                    early_ = ('sgw', 'a', 'kk', 'kp', 'b', 'lg', 'gam', 'gex', 'gin', 't1')
                    own_ = ('g', 'bon', 'rt0', 'rt1', 'kt0', 'kt1', 'kti', 'nbt')
                    rarr = {}
                    for i_, nm in enumerate(early_):
                        if 's5' in mixers:
                            base_ = sx if i_ < 8 else sg
                            j_ = i_ % 8
                            rarr[nm] = base_[:, j_ // 4, 3 * (j_ % 4):3 * (j_ % 4) + 3, :]
                            em.alias[nm] = 'sx' if i_ < 8 else 'sg'
                        else:
                            rarr[nm] = sb("r_" + nm, [128, 3, NB])
                    for nm in own_:
                        rarr[nm] = sb("r_" + nm, [128, 3, NB])
                    rarr['y'], rarr['d'], rarr['t2'] = rarr['kti'], rarr['nbt'], rarr['kt0']
                    em.alias.update({'y': 'kti', 'd': 'nbt', 't2': 'kt0'})
                    tw = sb("tw", [128, NB])
                    tw2 = sb("tw2", [128, NB])
                    twb = sb("twb", [128, NB])
                    em.op('pool', lambda e: e.memset(tw[:], 0.0), writes=['tw'])
                    em.op('pool', lambda e: e.memset(twb[:], 0.0), writes=['twb'])
                    gC = sb("gC", [128, 3])
                    tmj_all = sb("tmj_all", [128, 8 * 384])
                    TMN = ('V', 'K', 'B', 'AkkT', 'N', 'M', 'P', 'X')
                    TMK = ['tm_' + x_ for x_ in TMN]
                    tmj = {nm: tmj_all[:, i_ * 384:(i_ + 1) * 384] for i_, nm in enumerate(TMN)}

                    def zero_tmj():
                        for nm_ in ('V', 'K', 'B', 'AkkT', 'N', 'M', 'P', 'X'):
                            em.op('pool', lambda e, nm_=nm_: e.memset(tmj[nm_][:], 0.0), writes=['tm_' + nm_])
                    tmj['R'], tmj['U'], tmj['Y'] = tmj['N'], tmj['M'], tmj['P']
                    tmj['ArkT'], tmj['ArbT'] = tmj['AkkT'], tmj['X']
                    em.alias.update({'tm_R': 'tm_N', 'tm_U': 'tm_M', 'tm_Y': 'tm_P', 'tm_ArkT': 'tm_AkkT', 'tm_ArbT': 'tm_X'})
                    stmp = sb("stmp", [128, 64])

                def rwkv_tile(n, samp=False):
                    C = n
                    yield
                    ra = rarr
                    yield
                    RP = ['rp']
                    yield
                    T('dve', xm[:, :, :n], (prevv if samp else pj[:, 3:14, 2:2 + n]), pj[:, 3:14, 3:3 + n], ALU.subtract, ['pj', 'sh16'], ['xm'])
                    yield
                    T('dve', xm[:, :, :n], xm[:, :, :n], rmu[:, :].unsqueeze(2).to_broadcast([128, 11, n]), ALU.mult, ['xm', 'rmu'], ['xm'])
                    yield
                    T('dve', xm[:, :, :n], xm[:, :, :n], pj[:, 3:14, 3:3 + n], ALU.add, ['xm', 'pj'], ['xm'])
                    yield
                    r_, k_, v_ = xm[:, 0:3, :n], xm[:, 3:6, :n], xm[:, 6:9, :n]
                    yield
                    A(tw[0:64, :n], xm[0:64, 9, :n], AF.Tanh, ['xm'], ['tw'])
                    yield
                    CP('act', twb[64:128, :n], xm[64:128, 9, :n], ['xm'], ['twb'])
                    yield
                    T('dve', ra['kk'][:, :, :n], k_, rp[:, :, 2].unsqueeze(2).to_broadcast([128, 3, n]), ALU.mult, ['xm'] + RP, ['kk'])
                    yield
                    T('dve', ra['t1'][:, :, :n], ra['kk'][:, :, :n], ra['kk'][:, :, :n], ALU.mult, ['kk'], ['t1'])
                    yield
                    b0 = nb()
                    yield
                    for c in range(3):
                        MM(ps[b0][:, c * 64:c * 64 + n], rup[:, 0, c * 128:(c + 1) * 128], tw[:, :n], ['rup', 'tw'], ['ps%d' % b0])
                        yield
                        MM(ps[b0][:, 192 + c * 64:192 + c * 64 + n], rup[:, 0, c * 128:(c + 1) * 128], twb[:, :n],
                           ['rup', 'twb'], ['ps%d' % b0])
                        yield
                    yield
                    b2 = nb()
                    yield
                    for c in range(3):
                        MM(ps[b2][:, c * 64:c * 64 + n], bones, ra['t1'][:, c, :n], ['cst', 't1'], ['ps%d' % b2])
                    yield
                    for c in range(3):
                        A(ra['sgw'][:, c, :n], ps[b0][:, c * 64:c * 64 + n], AF.Sigmoid, ['ps%d' % b0] + RP, ['sgw'], bias=rp[:, c, 0:1])
                    yield
                    for c in range(3):
                        A(ra['a'][:, c, :n], ps[b0][:, 192 + c * 64:192 + c * 64 + n], AF.Sigmoid, ['ps%d' % b0] + RP, ['a'], bias=rp[:, c, 1:2])
                    yield
                    A(tw2[:, :n], xm[:, 10, :n], AF.Sigmoid, ['xm'], ['tw2'])
                    yield
                    S('dve', ra['sgw'][:, :, :n], ra['sgw'][:, :, :n], -0.6065306597126334, None, ALU.mult, ALU.bypass, ['sgw'], ['sgw'])
                    yield
                    if samp:
                        CP('dve', ra['lg'][:, :, :n], ra['sgw'][:, :, :n], ['sgw'], ['lg'])
                    else:
                        for c in range(3):
                            em.op('dve', lambda e, c=c: e.tensor_tensor_scan(out=ra['lg'][:, c, :n], data0=ones64[:, :n], data1=ra['sgw'][:, c, :n],
                                                                          initial=0.0, op0=ALU.mult, op1=ALU.add),
                                  reads=['ones64', 'sgw'], writes=['lg'])
                    yield
                    b1 = nb()
                    yield
                    for c in range(3):
                        MM(ps[b1][:, c * 64:c * 64 + n], rup[:, 1, c * 128:(c + 1) * 128], tw2[:, :n], ['rup', 'tw2'], ['ps%d' % b1])
                    yield
                    A(ra['gam'][:, :, :n], ra['lg'][:, :, :n], AF.Exp, ['lg'], ['gam'])
                    yield
                    A(ra['gin'][:, :, :n], ra['lg'][:, :, :n], AF.Exp, ['lg'], ['gin'], scale=-1.0)
                    yield
                    T('dve', ra['gex'][:, :, :n], ra['lg'][:, :, :n], ra['sgw'][:, :, :n], ALU.subtract, ['lg', 'sgw'], ['gex'])
                    yield
                    A(ra['gex'][:, :, :n], ra['gex'][:, :, :n], AF.Exp, ['gex'], ['gex'])
                    yield
                    for c in range(3):
                        S('dve', ra['kp'][:, c, :n], ra['a'][:, c, :n], -1.0, rp[:, c, 3:4], ALU.add, ALU.mult, ['a'] + RP, ['kp'])
                    yield
                    STT(ra['kp'][:, :, :n], ra['kp'][:, :, :n], 1.0, k_, ALU.add, ALU.mult, ['kp', 'xm'], ['kp'])
                    yield
                    for c in range(3):
                        STT(ra['bon'][:, c, :n], xm[:, c, :n], rp[:, c, 4:5], ra['kp'][:, c, :n], ALU.mult, ALU.mult, ['xm', 'kp'] + RP, ['bon'])
                    yield
                    b3 = nb()
                    yield
                    for c in range(3):
                        MM(ps[b3][:, c * 64:c * 64 + n], bones, ra['bon'][:, c, :n], ['cst', 'bon'], ['ps%d' % b3])
                    yield
                    CP('act', ra['g'][:, :, :n], ps[b1][:, 0:192].rearrange("p (a b) -> p a b", b=64)[:, :, :n], ['ps%d' % b1], ['g'])
                    yield
                    p3 = ps[b2][:, 0:192].rearrange("p (a b) -> p a b", b=64)[:, :, :n]
                    yield
                    A(ra['t1'][:, :, :n], p3, AF.Sqrt, ['ps%d' % b2], ['t1'])
                    yield
                    S('dve', ra['t1'][:, :, :n], ra['t1'][:, :, :n], 1e-12, None, ALU.max, ALU.bypass, ['t1'], ['t1'])
                    yield
                    em.op('dve', lambda e: e.reciprocal(out=ra['t1'][:, :, :n], in_=ra['t1'][:, :, :n]), reads=['t1'], writes=['t1'])
                    yield
                    T('dve', ra['kk'][:, :, :n], ra['kk'][:, :, :n], ra['t1'][:, :, :n], ALU.mult, ['kk', 't1'], ['kk'])
                    yield
                    T('dve', ra['b'][:, :, :n], ra['kk'][:, :, :n], ra['a'][:, :, :n], ALU.mult, ['kk', 'a'], ['b'])
                    yield
                    T('dve', ra['bon'][:, :, :n], ps[b3][:, 0:192].rearrange("p (a b) -> p a b", b=64)[:, :, :n], v_, ALU.mult,
                      ['ps%d' % b3, 'xm'], ['bon'])
                    yield
                    CP('dve', gC[:], ra['gam'][:, :, n - 1], ['gam'], ['gC'])
                    yield
                    for hp in range(2):
                        STT(ra['rt%d' % hp][:, :, :n], r_, maskg2[:, hp:hp + 1], ra['gam'][:, :, :n], ALU.mult, ALU.mult,
                            ['xm', 'gam', 'cst'], ['rt%d' % hp])
                        yield
                        STT(ra['kt%d' % hp][:, :, :n], ra['kk'][:, :, :n], maskg2[:, hp:hp + 1], ra['gex'][:, :, :n], ALU.mult, ALU.mult,
                            ['kk', 'gex', 'cst'], ['kt%d' % hp])
                        yield
                    yield
                    T('dve', ra['kti'][:, :, :n], ra['kp'][:, :, :n], ra['gin'][:, :, :n], ALU.mult, ['kp', 'gin'], ['kti'])
                    yield
                    STT(ra['nbt'][:, :, :n], ra['b'][:, :, :n], -1.0, ra['gin'][:, :, :n], ALU.mult, ALU.mult, ['b', 'gin'], ['nbt'])
                    yield
                    if not samp:
                        yield 'PHASE'

                    def rw_direct():
                        Sb = tmj_all[:, 0:1024].rearrange("p (b v) -> p b v", v=64)
                        yield
                        tb = tmj_all[:, 1024:2048].rearrange("p (b v) -> p b v", v=64)
                        yield

                        def bc(ap2, lo=0, hi=16):
                            return ap2[:, lo:hi].unsqueeze(2).to_broadcast([128, hi - lo, 64])

                        def idb(k_):
                            return identb.unsqueeze(1).to_broadcast([128, k_, 64])

                        def bmm(src):
                            bb_ = (nb(), nb())
                            for hf in range(2):
                                MM(ps[bb_[hf]][:, :], bones, src[:, 8 * hf:8 * hf + 8, :], ['cst'] + TMK, ['ps%d' % bb_[hf]])
                            return bb_

                        def pview(b_):
                            return ps[b_][:, :].rearrange("p (b v) -> p b v", v=64)
                        for c in range(3):
                            em.dma('sp', Sb, i_rwkv[l, :, :, c, :].rearrange("b p v -> p b v"), 'ld_sb', writes=TMK)
                            yield
                            kap, w_, kp_, bq_ = ra['kk'][:, c, :n], ra['gam'][:, c, :n], ra['kp'][:, c, :n], ra['b'][:, c, :n]
                            yield
                            rq_, vq_ = xm[:, c, :n], xm[:, 6 + c, :n]
                            yield
                            T('dve', tb, Sb, bc(kap), ALU.mult, TMK + ['kk'], TMK)
                            yield
                            bb_ = bmm(tb)
                            yield
                            T('dve', Sb, Sb, bc(w_), ALU.mult, TMK + ['gam'], TMK)
                            yield
                            for hf in range(2):
                                T('dve', tb[:, 8 * hf:8 * hf + 8, :], pview(bb_[hf]), bc(bq_, 8 * hf, 8 * hf + 8), ALU.mult,
                                  ['ps%d' % bb_[hf], 'b'], TMK)
                            yield
                            T('dve', Sb, Sb, tb, ALU.subtract, TMK, TMK)
                            yield
                            T('dve', tb, idb(16), bc(vq_), ALU.mult, ['cst', 'xm'], TMK)
                            yield
                            bb_ = bmm(tb)
                            yield
                            for hf in range(2):
                                T('dve', tb[:, 8 * hf:8 * hf + 8, :], pview(bb_[hf]), bc(kp_, 8 * hf, 8 * hf + 8), ALU.mult,
                                  ['ps%d' % bb_[hf], 'kp'], TMK)
                            yield
                            T('dve', Sb, Sb, tb, ALU.add, TMK, TMK)
                            yield
                            T('dve', tb, Sb, bc(rq_), ALU.mult, TMK + ['xm'], TMK)
                            yield
                            bb_ = bmm(tb)
                            yield
                            for hf in range(2):
                                T('dve', tb[:, 8 * hf:8 * hf + 8, :], pview(bb_[hf]), idb(8), ALU.mult, ['ps%d' % bb_[hf], 'cst'], TMK)
                            yield
                            em.op('dve', lambda e, c=c: e.tensor_reduce(out=ra['y'][:, c, :n], in_=tb, axis=AX.X, op=ALU.add),
                                  reads=TMK, writes=['y'])
                            yield
                            em.dma('sp', o_rwkv[l, 1:17, :, c, :].rearrange("b p v -> p b v"), Sb, 'st_sb', reads=TMK)
                            yield
                        yield

                    def rw_chunk():
                        if _RWSTOP <= 2:
                            return
                        yield
                        for nm, src, RR in (('V', xm[:, 6:9, :n], ['xm']), ('K', ra['kti'][:, :, :n], ['kti']), ('B', ra['nbt'][:, :, :n], ['nbt'])):
                            bt = nb()
                            yield
                            for c in range(3):
                                TR(ps[bt][:C, c * 128:(c + 1) * 128], src[:, c, :], ident, RR, ['ps%d' % bt])
                            yield
                            CP('act', tmj[nm][:C, :], ps[bt][:C, 0:384], ['ps%d' % bt], ['tm_' + nm])
                            yield
                        yield
                        if _RWSTOP <= 3:
                            return
                        yield
                        def hv(ap, hp, w):
                            return ap.rearrange("p (c h k) -> p h c k", h=2, k=64)[:, hp, :, :w]

                        def pv(b, w):
                            return ps[b][:C, 0:192].rearrange("p (a b) -> p a b", b=64)[:, :, :w]

                        def amat(lk, rk, RR, dst, mask):
                            bb2 = [nb(), nb()]
                            for h in range(6):
                                c, hp = h // 2, h % 2
                                ln_ = lk + str(hp) if lk in ('kt', 'rt') else lk
                                rn_ = rk + str(hp) if rk in ('kt', 'rt') else rk
                                MM(ps[bb2[hp]][:C, c * 64:c * 64 + C], ra[ln_][:, c, :n], ra[rn_][:, c, :n],
                                   [ln_, rn_], ['ps%d' % bb2[hp]])
                            for hp in range(2):
                                T('dve', hv(tmj[dst][:C, :], hp, C), pv(bb2[hp], C), mask[:C, :C].unsqueeze(1).to_broadcast([C, 3, C]),
                                  ALU.mult, ['ps%d' % bb2[hp], 'cm'], ['tm_' + dst])
                        v3 = lambda ap: ap.rearrange("p (a b) -> p a b", b=64)[:, :, :C]
                        yield
                        amat('kti', 'kt', ['kti', 'kt'], 'AkkT', MSU)
                        yield
                        amat('nbt', 'kt', ['nbt', 'kt'], 'M', MSU)
                        yield
                        T('dve', v3(tmj['X'][:C, :]), v3(tmj['M'][:C, :]), I6[:C, :C].unsqueeze(1).to_broadcast([C, 6, C]), ALU.add, ['tm_M', 'cm'], ['tm_X'])
                        yield
                        amat('kt', 'nbt', ['nbt', 'kt'], 'N', MSL)
                        yield
                        if _RWSTOP <= 4:
                            return
                        yield
                        nr = 0
                        yield
                        while (1 << (nr + 1)) < C:
                            nr += 1
                        yield
                        def xstep():
                            bx = nb()
                            for h in range(6):
                                sl = slice(h * 64, h * 64 + C)
                                MM(ps[bx][:C, sl], tmj['P'][:, sl], tmj['X'][:, sl], ['tm_P', 'tm_X'], ['ps%d' % bx])
                            CP('dve', v3(tmj['X'][:C, :]), v3(ps[bx][:C, 0:384]), ['ps%d' % bx], ['tm_X'])
                        for i in range(1, nr + 1):
                            bn_, bm_ = nb(), nb()
                            yield
                            last = (i == nr)
                            yield
                            for h in range(6):
                                sl = slice(h * 64, h * 64 + C)
                                MM(ps[bn_][:C, sl], tmj['M'][:, sl], tmj['N'][:, sl], ['tm_M', 'tm_N'], ['ps%d' % bn_])
                                if not last:
                                    MM(ps[bm_][:C, sl], tmj['N'][:, sl], tmj['M'][:, sl], ['tm_M', 'tm_N'], ['ps%d' % bm_])
                            yield
                            if not last:
                                CP('act', v3(tmj['N'][:C, :]), v3(ps[bn_][:C, 0:384]), ['ps%d' % bn_], ['tm_N'])
                                CP('act', v3(tmj['M'][:C, :]), v3(ps[bm_][:C, 0:384]), ['ps%d' % bm_], ['tm_M'])
                            yield
                            if i > 1:
                                xstep()
                            yield
                            T('dve', v3(tmj['P'][:C, :]), v3(ps[bn_][:C, 0:384]), I6[:C, :C].unsqueeze(1).to_broadcast([C, 6, C]), ALU.add, ['ps%d' % bn_, 'cm'], ['tm_P'])
                            yield
                        yield
                        if nr >= 1:
                            xstep()
                        yield
                        if _RWSTOP <= 5:
                            return
                        yield
                        br = [nb(), nb()]
                        yield
                        for h in range(6):
                            c, hp = h // 2, h % 2
                            yield
                            pr_ = slice(64 * hp, 64 * hp + 64)
                            yield
                            MM(ps[br[hp]][:C, c * 64:(c + 1) * 64], ra['kt%d' % hp][:, c, :n], S0T[:, c, :], ['kt%d' % hp, 'S0T'], ['ps%d' % br[hp]], start=True, stop=False)
                            yield
                            MM(ps[br[hp]][:C, c * 64:(c + 1) * 64], tmj['AkkT'][:, h * 64:h * 64 + C], tmj['V'][:, h * 64:(h + 1) * 64],
                               ['tm_AkkT', 'tm_V'], ['ps%d' % br[hp]], start=False, stop=True)
                            yield
                        yield
                        for hp in range(2):
                            CP('act', hv(tmj['R'][:C, :], hp, 64), pv(br[hp], 64), ['ps%d' % br[hp]], ['tm_R'])
                        yield
                        bu = nb()
                        yield
                        for h in range(6):
                            MM(ps[bu][:C, h * 64:(h + 1) * 64], tmj['X'][:, h * 64:h * 64 + C], tmj['R'][:, h * 64:(h + 1) * 64],
                               ['tm_X', 'tm_R'], ['ps%d' % bu])
                        yield
                        CP('act', tmj['U'][:C, :], ps[bu][:C, 0:384], ['ps%d' % bu], ['tm_U'])
                        yield
                        if _RWSTOP <= 6:
                            return
                        yield
                        amat('kti', 'rt', ['kti', 'rt'], 'ArkT', MU)
                        yield
                        amat('nbt', 'rt', ['nbt', 'rt'], 'ArbT', MU)
                        yield
                        by = [nb(), nb()]
                        yield
                        for h in range(6):
                            c, hp = h // 2, h % 2
                            yield
                            pr_ = slice(64 * hp, 64 * hp + 64)
                            yield
                            hs = slice(h * 64, (h + 1) * 64)
                            yield
                            cs = slice(c * 64, (c + 1) * 64)
                            yield
                            MM(ps[by[hp]][:C, cs], ra['rt%d' % hp][:, c, :n], S0T[:, c, :], ['rt%d' % hp, 'S0T'], ['ps%d' % by[hp]], start=True, stop=False)
                            yield
                            MM(ps[by[hp]][:C, cs], tmj['ArkT'][:, h * 64:h * 64 + C], tmj['V'][:, hs], ['tm_ArkT', 'tm_V'], ['ps%d' % by[hp]], start=False, stop=False)
                            yield
                            MM(ps[by[hp]][:C, cs], tmj['ArbT'][:, h * 64:h * 64 + C], tmj['U'][:, hs], ['tm_ArbT', 'tm_U'], ['ps%d' % by[hp]], start=False, stop=True)
                            yield
                        yield
                        for hp in range(2):
                            CP('act', hv(tmj['Y'][:C, :], hp, 64), pv(by[hp], 64), ['ps%d' % by[hp]], ['tm_Y'])
                        yield
                        if _RWSTOP <= 7:
                            return
                        yield
                        for c in range(3):
                            bs_ = nb()
                            yield
                            for hp in range(2):
                                hs = slice((2 * c + hp) * 64, (2 * c + hp + 1) * 64)
                                MM(ps[bs_][:, hp * 64:(hp + 1) * 64], tmj['K'][:, c * 128:(c + 1) * 128], tmj['V'][:, hs], ['tm_K', 'tm_V'],
                                   ['ps%d' % bs_], start=True, stop=False)
                                MM(ps[bs_][:, hp * 64:(hp + 1) * 64], tmj['B'][:, c * 128:(c + 1) * 128], tmj['U'][:, hs], ['tm_B', 'tm_U'],
                                   ['ps%d' % bs_], start=False, stop=True)
                            yield
                            for hp in range(2):
                                pr_ = slice(64 * hp, 64 * hp + 64)
                                T('dve', stmp[pr_, :], ps[bs_][pr_, hp * 64:(hp + 1) * 64], S0T[pr_, c, :], ALU.add, ['ps%d' % bs_, 'S0T'], ['stmp'])
                                S('dve', S0T[pr_, c, :], stmp[pr_, :], gC[pr_, c:c + 1], None, ALU.mult, ALU.bypass, ['stmp', 'gC'], ['S0T'])
                            yield
                        yield
                        if _RWSTOP <= 8:
                            return
                        yield
                        bt = nb()
                        yield
                        for c in range(3):
                            TR(ps[bt][:, c * 128:(c + 1) * 128], tmj['Y'][:, c * 128:(c + 1) * 128], ident, ['tm_Y'], ['ps%d' % bt])
                        yield
                        CP('act', ra['y'][:, :, :n], ps[bt][:, 0:384].rearrange("p (a b) -> p a b", b=128)[:, :, :n], ['ps%d' % bt], ['y'])
                        yield
                    if samp:
                        yield from rw_direct()
                    else:
                        yield from rw_chunk()
                    bm1 = nb()
                    yield
                    for c in range(3):
                        MM(ps[bm1][:, c * 64:c * 64 + n], bones, ra['y'][:, c, :n], ['cst', 'y'], ['ps%d' % bm1])
                    yield
                    STT(ra['d'][:, :, :n], ps[bm1][:, 0:192].rearrange("p (a b) -> p a b", b=64)[:, :, :n], -1.0 / 64, ra['y'][:, :, :n],
                        ALU.mult, ALU.add, ['ps%d' % bm1, 'y'], ['d'])
                    yield
                    T('dve', ra['t2'][:, :, :n], ra['d'][:, :, :n], ra['d'][:, :, :n], ALU.mult, ['d'], ['t2'])
                    yield
                    bm2 = nb()
                    yield
                    for c in range(3):
                        MM(ps[bm2][:, c * 64:c * 64 + n], bones, ra['t2'][:, c, :n], ['cst', 't2'], ['ps%d' % bm2])
                    yield
                    S('dve', ra['t2'][:, :, :n], ps[bm2][:, 0:192].rearrange("p (a b) -> p a b", b=64)[:, :, :n], 1.0 / 64, 64e-5,
                      ALU.mult, ALU.add, ['ps%d' % bm2], ['t2'])
                    yield
                    A(ra['t2'][:, :, :n], ra['t2'][:, :, :n], AF.Sqrt, ['t2'], ['t2'])
                    yield
                    em.op('dve', lambda e: e.reciprocal(out=ra['t2'][:, :, :n], in_=ra['t2'][:, :, :n]), reads=['t2'], writes=['t2'])
                    yield
                    T('dve', ra['d'][:, :, :n], ra['d'][:, :, :n], ra['t2'][:, :, :n], ALU.mult, ['d', 't2'], ['d'])
                    yield
                    for c in range(3):
                        S('dve', ra['d'][:, c, :n], ra['d'][:, c, :n], rp[:, c, 5:6], rp[:, c, 6:7], ALU.mult, ALU.add, ['d'] + RP, ['d'])
                    yield
                    T('dve', ra['d'][:, :, :n], ra['d'][:, :, :n], ra['bon'][:, :, :n], ALU.add, ['d', 'bon'], ['d'])
                    yield
                    T('dve', omix[:, 3:6, :n], ra['d'][:, :, :n], ra['g'][:, :, :n], ALU.mult, ['d', 'g'], ['omix_r'])
                    yield

                def do_tile(c0, n, samp=False):
                    A(sqb[:, :, :n], xres[:, :, c0:c0 + n], AF.Square, ['xres'], ['sqb'])
                    b = nb()
                    for kc in range(KC):
                        MM(ps[b][:, :n], ones_bf[:], sqb[:, kc, :n], ['ones_bf', 'sqb'], ['ps%d' % b], start=(kc == 0), stop=(kc == KC - 1))
                    S('dve', rstb[:, :n], ps[b][:, :n], 1.0 / D, EPS, ALU.mult, ALU.add, ['ps%d' % b], ['rstb'])
                    A(rstb[:, :n], rstb[:, :n], AF.Sqrt, ['rstb'], ['rstb'])
                    em.op('dve', lambda e: e.reciprocal(out=rstb[:, :n], in_=rstb[:, :n]), reads=['rstb'], writes=['rstb'])
                    for kc in range(KC):
                        STT(xnb[:, kc, :n], xres[:, kc, c0:c0 + n], normw[:, 2 + l, kc:kc + 1], rstb[:, :n], ALU.mult, ALU.mult,
                            ['xres', 'normw', 'rstb'], ['xnb'])
                    for f0 in (0, 8, 16):
                        b = nb()
                        nf = min(8, 18 - f0)
                        for fi in range(nf):
                            fc = f0 + fi
                            for kc in range(KC):
                                o = kc * DIN + fc * 128
                                MM(ps[b][:, fi * 64:fi * 64 + n], ring[:, o:o + 128], xnb[:, kc, :n], RIN + ['xnb'], ['ps%d' % b],
                                   start=(kc == 0), stop=(kc == KC - 1))
                        CP('act', pj[:, f0:f0 + nf, 3:3 + n], ps[b][:, 0:nf * 64].rearrange("p (a b) -> p a b", b=64)[:, :, :n],
                           ['ps%d' % b], ['pj'])
                    if len(mixers) < 3:
                        em.op('pool', lambda e: e.memset(omix[:, :, :n], 0.0), writes=['omix_s', 'omix_r', 'omix_l'])

                    def run_rr(gens):
                        while gens:
                            for g_ in list(gens):
                                try:
                                    next(g_)
                                except StopIteration:
                                    gens.remove(g_)
                    if samp or len(mixers) < 3:
                        if 's5' in mixers:
                            run_rr([s5_tile(n, samp)])
                        gens = []
                        if 'rwkv' in mixers:
                            gens.append(rwkv_tile(n, samp))
                        if 'lru' in mixers:
                            gens.append(lru_tile(n, samp))
                        run_rr(gens)
                    else:
                        gR, gL = rwkv_tile(n, samp), lru_tile(n, samp)
                        l_done = False
                        while True:
                            if next(gR) == 'PHASE':
                                break
                            if not l_done:
                                try:
                                    next(gL)
                                except StopIteration:
                                    l_done = True
                        gens = [gR, s5_tile(n, samp)] + ([] if l_done else [gL])
                        run_rr(gens)
                    b = nb()
                    for mc in range(KC):
                        for kc in range(KC):
                            o = 18432 + kc * D + mc * 128
                            MM(ps[b][:, mc * 64:mc * 64 + n], ring[:, o:o + 128], omix[:, kc, :n], ROUT + ['omix_s', 'omix_r', 'omix_l'], ['ps%d' % b],
                               start=(kc == 0), stop=(kc == KC - 1))
                    T('dve', xres[:, :, c0:c0 + n], xres[:, :, c0:c0 + n], ps[b][:, :].rearrange("p (a b) -> p a b", b=64)[:, :, :n], ALU.add,
                      ['xres', 'ps%d' % b], ['xres'])
                    if not samp:
                        CP('dve', tmpc[:], pj[:, :, n:n + 3], ['pj'], ['tmpc'])
                        CP('dve', pj[:, :, 0:3], tmpc[:], ['tmpc'], ['pj'])

                def store_state(si):
                    em.dma('sp', o_s5[l, si], hst[:], 'st_a', reads=['hst'])
                    em.dma('sp', o_rwkv[l, si], S0T[:], 'st_b', reads=['S0T'])
                    em.dma('sp', o_shift[l, si], pj[:, 3:14, 2], 'st_c', reads=['pj'])
                    em.dma('sp', o_lru[l, si], lru_h[:], 'st_d', reads=['lru_h'])
                    em.dma('sp', o_conv[l, si], pj[:, 14:16, 0:3], 'st_e', reads=['pj'])

                em.op('pool', lambda e: e.memset(pj[:], 0.0), writes=['pj'])
                em.op('pool', lambda e: e.memset(hst[:], 0.0), writes=['hst'])
                em.op('pool', lambda e: e.memset(S0T[:], 0.0), writes=['S0T'])
                em.op('pool', lambda e: e.memset(lru_h[:], 0.0), writes=['lru_h'])
                if 'rwkv' in mixers:
                    zero_tmj()
                do_tile(0, 16)
                for i in range(PT // NB):
                    do_tile(32 + NB * i, NB)
                store_state(0)
                if not _NOSAMP:
                    em.dma('sp', s5st[:], i_s5[l].rearrange("b p g r -> p b g r"), 'ld_a', writes=['s5st'])
                    em.dma('sp', sh16[:], i_shift[l].rearrange("b p c -> p b c"), 'ld_cc', writes=['sh16'])
                    em.dma('sp', lst[:], i_lru[l].rearrange("b p c -> p b c"), 'ld_d', writes=['lst'])
                    em.dma('sp', cst16[:], i_conv[l].rearrange("b p c j -> p b c j"), 'ld_e', writes=['cst16'])
                    do_tile(16, 16, True)
                    CP('dve', prevv, pj[:, 3:14, 3:19], ['pj'], ['sh16'])
                    em.dma('sp', o_s5[l, 1:17].rearrange("b p g r -> p b g r"), s5st[:], 'st_a', reads=['s5st'])
                    em.dma('sp', o_shift[l, 1:17].rearrange("b p c -> p b c"), sh16[:], 'st_c', reads=['sh16'])
                    em.dma('sp', o_lru[l, 1:17].rearrange("b p c -> p b c"), lst[:], 'st_d', reads=['lst'])
                    em.dma('sp', o_conv[l, 1:17].rearrange("b p c j -> p b c j"), cst16[:], 'st_e', reads=['cst16'])
                em.barrier()

        for l in range(L):
            if stage >= 1:
                ffn(l, 0)
            if stage >= 2 or stage == -2:
                mixer_phase(l)
            if stage >= 1:
                ffn(l, 1)

        fs = ExitStack()
        with fs:
            sq = sbuf(fs, "sqF", [128, KC, 512], BF16)
            rstd = sbuf(fs, "rstdF", [128, 512])
            for ti in range(NT):
                c0, n = tiles[ti]
                A(sq[:, :, :n], xres[:, :, c0:c0 + n], AF.Square, ['xres'], ['sq'])
                for kc in range(KC):
                    MM(ps[7][:, :n], ones_bf[:], sq[:, kc, :n], ['ones_bf', 'sq'], ['ps7'], start=(kc == 0), stop=(kc == KC - 1))
                S('dve', rstd[:, :n], ps[7][:, :n], 1.0 / D, EPS, ALU.mult, ALU.add, ['ps7'], ['rstd'])
                A(rstd[:, :n], rstd[:, :n], AF.Sqrt, ['rstd'], ['rstd'])
                em.op('dve', lambda e: e.reciprocal(out=rstd[:, :n], in_=rstd[:, :n]), reads=['rstd'], writes=['rstd'])
                for kc in range(KC):
                    STT(xres[:, kc, c0:c0 + n], xres[:, kc, c0:c0 + n], normw[:, 6, kc:kc + 1], rstd[:, :n], ALU.mult, ALU.mult,
                        ['xres', 'rstd', 'normw'], ['xres'])
                    em.dma('sp', yT[:, kc, c0:c0 + n], xres[:, kc, c0:c0 + n], 'st_y%d' % kc, reads=['xres'])
            em.finish('sp')
        print("instructions:", em.ninstr, {k: v for k, v in em.cnt.items()})
    return nc


def prep_common(inp):
    f = np.float32
    g = lambda k: np.asarray(inp[k], f)
    c = {}
    wf = np.empty((L, 2, NJ, 128, SLOT), f)
    for l in range(L):
        for fi, pre in enumerate(("ffn1", "ffn2")):
            wg = g(pre + "_w_gate")[l].reshape(KC, 128, NJ, 128)
            wu = g(pre + "_w_up")[l].reshape(KC, 128, NJ, 128)
            wd = g(pre + "_w_down")[l].reshape(NJ, 128, D)
            wf[l, fi, :, :, 0:1024] = wg.transpose(2, 1, 0, 3).reshape(NJ, 128, 1024)
            wf[l, fi, :, :, 1024:2048] = wu.transpose(2, 1, 0, 3).reshape(NJ, 128, 1024)
            wf[l, fi, :, :, 2048:3072] = wd
    c["wffn"] = wf
    c["w_in"] = np.ascontiguousarray(g("w_in").reshape(L, KC, 128, DIN).transpose(0, 2, 1, 3)).reshape(L, 128, KC * DIN)
    c["w_out"] = np.ascontiguousarray(g("w_out").reshape(L, KC, 128, D).transpose(0, 2, 1, 3)).reshape(L, 128, KC * D)
    nv = [g("ffn1_norm")[0], g("ffn1_norm")[1], g("mix_norm")[0], g("mix_norm")[1], g("ffn2_norm")[0], g("ffn2_norm")[1], g("final_norm")]
    c["norms"] = np.ascontiguousarray(np.stack([v.reshape(KC, 128) for v in nv], 0).transpose(2, 0, 1))
    p = np.arange(128)
    cs = np.zeros((128, 400), f)
    cs[:, 0:128] = np.eye(128)
    cs[:, 128:256] = (p[:, None] // 64 == p[None, :] // 64)
    cs[:, 256:320] = np.arange(1, 65)[None, :]
    for q in range(4):
        for gg in range(2):
            cs[:, 320 + 2 * q + gg] = (p >= 32 * q + 16 * gg) & (p < 32 * q + 16 * gg + 16)
    for gg in range(2):
        cs[:, 328 + gg] = (p // 64 == gg)
    cs[:, 330:394] = (p[:, None] % 64 == np.arange(64)[None, :])
    c["consts"] = cs
    s_ = np.arange(64)
    m4 = np.stack([s_[:, None] < s_[None, :], s_[:, None] <= s_[None, :], s_[:, None] > s_[None, :], s_[:, None] == s_[None, :]], 0).astype(f)
    c["cmask"] = np.ascontiguousarray(m4.transpose(1, 0, 2))
    lr, li, ldt = g("s5_lambda_re"), g("s5_lambda_im"), g("s5_log_dt")
    tri = np.stack([lr, li, np.broadcast_to(ldt[:, :, None], lr.shape)], -1)
    c["s5A"] = np.ascontiguousarray(tri.reshape(L, 12, 2, 64, 3).transpose(0, 2, 3, 1, 4).reshape(L, 128, 12, 3))
    tB = np.broadcast_to(tri.reshape(L, 3, 8, 1, 64, 3), (L, 3, 8, 16, 64, 3))
    c["s5B"] = np.ascontiguousarray(tB.transpose(0, 2, 3, 1, 4, 5).reshape(L, 128, 3, 64, 3))
    bb = np.stack([g("s5_b_re"), g("s5_b_im")], -1)
    c["s5bT"] = np.ascontiguousarray(bb.reshape(L, 3, 8, 64, 16, 2).transpose(0, 2, 4, 1, 3, 5).reshape(L, 128, 3, 64, 2))
    cc = np.stack([g("s5_c_re"), g("s5_c_im")], -1)
    c["s5cT"] = np.ascontiguousarray(cc.reshape(L, 12, 2, 16, 64, 2).transpose(0, 2, 4, 1, 3, 5).reshape(L, 128, 12, 16, 2))
    c["s5v"] = np.ascontiguousarray(np.stack([g("s5_d"), g("s5_glu_b")], -1).reshape(L, 3, 128, 2).transpose(0, 2, 1, 3))
    c["s5glu"] = np.ascontiguousarray(g("s5_glu_w").reshape(L, 3, 128, 384).transpose(0, 2, 1, 3))
    z384 = np.zeros((L, 384), f)
    rw = np.stack([g("rwkv_w0"), g("rwkv_a0"), g("rwkv_k_k"), g("rwkv_k_a"), g("rwkv_r_k").reshape(L, 384), g("rwkv_ln_w"), g("rwkv_ln_b"), z384], -1)
    c["rwp"] = np.ascontiguousarray(rw.reshape(L, 3, 128, 8).transpose(0, 2, 1, 3))
    c["rwmu"] = np.ascontiguousarray(g("rwkv_mu").reshape(L, 11, 128).transpose(0, 2, 1))
    c["rwup"] = np.ascontiguousarray(np.stack([np.concatenate([g("rwkv_w_up"), g("rwkv_a_up")], 1), g("rwkv_g_up")], 2))
    lp_ = np.concatenate([g("lru_conv_w").transpose(0, 2, 1), g("lru_conv_b")[..., None], g("lru_b_a")[..., None], g("lru_b_x")[..., None],
                          g("lru_lambda")[..., None]], -1)
    c["lrp"] = np.ascontiguousarray(lp_.reshape(L, 2, 128, 8).transpose(0, 2, 1, 3))
    lw_ = np.zeros((L, 128, 2, 2, 128), f)
    for wi, nm in enumerate(("lru_w_a", "lru_w_x")):
        w = g(nm)
        for cb in range(2):
            for bq in range(2):
                lw_[:, 64 * bq:64 * bq + 64, cb, wi, 64 * bq:64 * bq + 64] = w[:, 2 * cb + bq]
    c["lrw"] = lw_
    return c


def prep_core(inp, ci, PT=2048):
    f = np.float32
    g = lambda k: np.asarray(inp[k], f)
    meta = g("meta_tokens")
    xs = g("x_sample")[16 * ci:16 * ci + 16, 0]
    xp = g("x_prompt")[ci][:PT]
    cols = np.concatenate([meta, xs, xp], 0)
    m = {"xT": np.ascontiguousarray(cols.T.reshape(KC, 128, 32 + PT).transpose(1, 0, 2))}
    sl = slice(16 * ci, 16 * ci + 16)
    st = np.stack([g("state_s5_re")[:, sl], g("state_s5_im")[:, sl]], -1)
    m["i_s5"] = np.ascontiguousarray(st.reshape(L, 16, 12, 2, 64, 2).transpose(0, 1, 3, 4, 2, 5).reshape(L, 16, 128, 12, 2))
    rs = g("state_rwkv")[:, sl]
    m["i_rwkv"] = np.ascontiguousarray(rs.reshape(L, 16, 3, 2, 64, 64).transpose(0, 1, 3, 5, 2, 4).reshape(L, 16, 128, 3, 64))
    m["i_shift"] = np.ascontiguousarray(g("state_rwkv_shift")[:, sl].reshape(L, 16, 11, 128).transpose(0, 1, 3, 2))
    m["i_lru"] = np.ascontiguousarray(g("state_lru")[:, sl].reshape(L, 16, 2, 128).transpose(0, 1, 3, 2))
    m["i_conv"] = np.ascontiguousarray(g("state_lru_conv")[:, sl].reshape(L, 16, 3, 2, 128).transpose(0, 1, 4, 3, 2))
    return m


def unpack_core(r):
    o = {}
    s5 = r["o_s5"].reshape(L, NSEQ, 2, 64, 12, 2).transpose(0, 1, 4, 2, 3, 5).reshape(L, NSEQ, 24, 64, 2)
    o["s5_re"], o["s5_im"] = s5[..., 0], s5[..., 1]
    o["rwkv"] = r["o_rwkv"].reshape(L, NSEQ, 2, 64, 3, 64).transpose(0, 1, 4, 2, 5, 3).reshape(L, NSEQ, 6, 64, 64)
    o["shift"] = r["o_shift"].transpose(0, 1, 3, 2).reshape(L, NSEQ, 1408)
    o["lru"] = r["o_lru"].transpose(0, 1, 3, 2).reshape(L, NSEQ, 256)
    o["conv"] = r["o_conv"].transpose(0, 1, 4, 3, 2).reshape(L, NSEQ, 3, 256)
    return o


_NC_CACHE = {}


def kernel(**inp):
    if "nc" not in _NC_CACHE:
        _NC_CACHE["nc"] = build()
    nc = _NC_CACHE["nc"]
    common = prep_common(inp)
    in_maps = []
    for ci in range(NCORES):
        m = dict(common)
        m.update(prep_core(inp, ci))
        in_maps.append(m)
    res = run_bass_kernel_spmd(nc, in_maps, core_ids=list(range(NCORES)))
    outs = res.results
    f = np.float32
    y_prompt = np.empty((8, 2048, D), f)
    y_sample = np.empty((128, 1, D), f)
    keys = ("s5_re", "s5_im", "rwkv", "shift", "lru", "conv")
    shp = {"s5_re": (24, 64), "s5_im": (24, 64), "rwkv": (6, 64, 64), "shift": (1408,), "lru": (256,), "conv": (3, 256)}
    P = {k: np.empty((L, 8) + shp[k], f) for k in keys}
    Sx = {k: np.empty((L, 128) + shp[k], f) for k in keys}
    for ci in range(NCORES):
        y = outs[ci]["yT"].transpose(1, 0, 2).reshape(D, 2080).T
        y_prompt[ci] = y[32:]
        y_sample[16 * ci:16 * ci + 16, 0] = y[16:32]
        o = unpack_core(outs[ci])
        for k in keys:
            P[k][:, ci] = o[k][:, 0]
            Sx[k][:, 16 * ci:16 * ci + 16] = o[k][:, 1:]
    return (y_prompt, y_sample) + tuple(P[k] for k in keys) + tuple(Sx[k] for k in keys)
```

```python
import numpy as np
import os
_RWSTOP = int(os.environ.get('RW_STOP', '99'))
_NOSAMP = int(os.environ.get('NO_SAMP', '0'))
from contextlib import ExitStack
import concourse.bass as bass
import concourse.mybir as mybir
from concourse.bass_utils import run_bass_kernel_spmd

F32 = mybir.dt.float32
BF16 = mybir.dt.bfloat16
ALU = mybir.AluOpType
AF = mybir.ActivationFunctionType
AX = mybir.AxisListType

NCORES = 8
D = 1024
KC = 8
DFF = 2816
NJ = 22
DIN = 2304
L = 2
NCOL = 2080
SLOT = 3072
NSLOT = 9
GRP = 4
EPS = 1e-6
TILES = [(0, 32)] + [(32 + 512 * i, 512) for i in range(4)]


class Em:
    ENG = ('pe', 'dve', 'act', 'pool', 'sp')

    def __init__(self, nc, es):
        self.nc = nc
        self.es = es
        self.e = dict(pe=nc.tensor, dve=nc.vector, act=nc.scalar, pool=nc.gpsimd, sp=nc.sync)
        self.sem = {k: es.enter_context(nc.semaphore("s_" + k)) for k in ('pe', 'dve', 'act', 'pool')}
        self.cnt = {k: 0 for k in self.sem}
        self.waited = {k: {} for k in self.ENG}
        self.buf = {}
        self.dsem = {}
        self.ninstr = 0
        self.alias = {}

    def _handle(self, sk):
        return self.sem[sk] if sk in self.sem else self.dsem[sk][0]

    def _wait(self, eng, ev):
        if ev is None:
            return
        sk, val = ev
        if sk == 'pe' and eng == 'pe':
            return
        if self.waited[eng].get(sk, 0) >= val:
            return
        self.waited[eng][sk] = val
        self.e[eng].wait_ge(self._handle(sk), val)

    def deps(self, eng, reads, writes):
        reads = [self.alias.get(k, k) for k in reads]
        writes = [self.alias.get(k, k) for k in writes]
        for r in reads:
            b = self.buf.get(r)
            if b:
                self._wait(eng, b[0])
        for w in writes:
            b = self.buf.get(w)
            if b:
                self._wait(eng, b[0])
                for ev in b[1].values():
                    self._wait(eng, ev)

    def record(self, ev, reads, writes):
        reads = [self.alias.get(k, k) for k in reads]
        writes = [self.alias.get(k, k) for k in writes]
        for r in reads:
            b = self.buf.setdefault(r, [None, {}])
            b[1][ev[0]] = ev
        for w in writes:
            self.buf[w] = [ev, {}]

    def op(self, eng, fn, reads=(), writes=(), inc=True):
        psr = [k for k in reads if k.startswith('ps')]
        if psr:
            reads = [k for k in reads if not k.startswith('ps')]
            writes = list(writes) + psr
        self.deps(eng, reads, writes)
        ins = fn(self.e[eng])
        self.ninstr += 1
        if inc:
            self.cnt[eng] += 1
            ins.then_inc(self.sem[eng], 1)
            ev = (eng, self.cnt[eng])
        else:
            ev = (eng, self.cnt[eng] + 1)
        self.record(ev, reads, writes)
        return ins

    def dma(self, q, out, in_, name, reads=(), writes=(), **kw):
        if name not in self.dsem:
            self.dsem[name] = [self.es.enter_context(self.nc.semaphore("d_" + name)), 0]
        self.deps(q, reads, writes)
        d = self.dsem[name]
        d[1] += 16
        self.e[q].dma_start(out=out, in_=in_, **kw).then_inc(d[0], 16)
        self.ninstr += 1
        self.record((name, d[1]), reads, writes)

    def dma_batch(self, q, items, name):
        if name not in self.dsem:
            self.dsem[name] = [self.es.enter_context(self.nc.semaphore("d_" + name)), 0]
        d = self.dsem[name]
        for (out, in_, reads, writes) in items:
            self.deps(q, reads, writes)
        final = d[1] + 16 * len(items)
        for (out, in_, reads, writes) in items:
            d[1] += 16
            self.e[q].dma_start(out=out, in_=in_).then_inc(d[0], 16)
            self.ninstr += 1
            self.record((name, final), reads, writes)

    def barrier(self):
        for eng in self.ENG:
            for name, d in self.dsem.items():
                self._wait(eng, (name, d[1]))
            for k in self.sem:
                if self.cnt[k]:
                    self._wait(eng, (k, self.cnt[k]))

    def finish(self, eng='sp'):
        for name, d in self.dsem.items():
            self._wait(eng, (name, d[1]))
        for k in self.sem:
            if self.cnt[k]:
                self._wait(eng, (k, self.cnt[k]))


PI2 = 6.283185307179586
NB = 64
NSEQ = 17


def build(stage=99, PT=2048, mixers=('lru', 's5', 'rwkv')):
    NCOL_ = 32 + PT
    tiles = [(0, 32)] + [(32 + 512 * i, min(512, PT - 512 * i)) for i in range((PT + 511) // 512)]
    NT = len(tiles)
    nc = bass.Bass("TRN2", target_bir_lowering=False)
    es = ExitStack()
    with es:
        es.enter_context(nc.allow_non_contiguous_dma(reason="small strided state io"))
        em = Em(nc, es)

        def din(name, shape, dt=F32):
            return nc.dram_tensor(name, list(shape), dt, kind="ExternalInput").ap()

        def dout(name, shape, dt=F32):
            return nc.dram_tensor(name, list(shape), dt, kind="ExternalOutput").ap()

        def sbuf(stack, name, shape, dt=F32):
            return stack.enter_context(nc.sbuf_tensor(name, list(shape), dt))

        xT = din("xT", [128, KC, NCOL_])
        wffn = din("wffn", [L, 2, NJ, 128, SLOT])
        w_in = din("w_in", [L, 128, KC * DIN])
        w_out = din("w_out", [L, 128, KC * D])
        norms = din("norms", [128, 7, KC])
        consts = din("consts", [128, 400])
        cmask = din("cmask", [64, 4, 64])
        s5A = din("s5A", [L, 128, 12, 3])
        s5B = din("s5B", [L, 128, 3, 64, 3])
        s5bT = din("s5bT", [L, 128, 3, 64, 2])
        s5cT = din("s5cT", [L, 128, 12, 16, 2])
        s5v = din("s5v", [L, 128, 3, 2])
        s5glu = din("s5glu", [L, 128, 3, 384])
        rwp = din("rwp", [L, 128, 3, 8])
        rwmu = din("rwmu", [L, 128, 11])
        rwup = din("rwup", [L, 128, 2, 384])
        lrp = din("lrp", [L, 128, 2, 8])
        lrw = din("lrw", [L, 128, 2, 2, 128])
        i_s5 = din("i_s5", [L, 16, 128, 12, 2])
        i_rwkv = din("i_rwkv", [L, 16, 128, 3, 64])
        i_shift = din("i_shift", [L, 16, 128, 11])
        i_lru = din("i_lru", [L, 16, 128, 2])
        i_conv = din("i_conv", [L, 16, 128, 2, 3])
        yT = dout("yT", [128, KC, NCOL_])
        o_s5 = dout("o_s5", [L, NSEQ, 128, 12, 2])
        o_rwkv = dout("o_rwkv", [L, NSEQ, 128, 3, 64])
        o_shift = dout("o_shift", [L, NSEQ, 128, 11])
        o_lru = dout("o_lru", [L, NSEQ, 128, 2])
        o_conv = dout("o_conv", [L, NSEQ, 128, 2, 3])

        xres = sbuf(es, "xres", [128, KC, NCOL_])
        ring = sbuf(es, "ring", [128, NSLOT * SLOT], BF16)
        normw = sbuf(es, "normw", [128, 7, KC])
        ones_bf = sbuf(es, "ones_bf", [128, 128], BF16)
        cst = sbuf(es, "cst", [128, 400])
        ps = [es.enter_context(nc.psum_tensor("ps%d" % i, [128, 512], F32)) for i in range(8)]
        ident = cst[:, 0:128]
        bones = cst[:, 128:256]
        iota1 = cst[:, 256:320]
        mask8 = cst[:, 320:328]
        maskg2 = cst[:, 328:330]
        identb = cst[:, 330:394]

        em.op('pool', lambda e: e.memset(ones_bf[:], 1.0), writes=['ones_bf'])
        em.dma('sp', normw[:], norms, 'ld_c', writes=['normw'])
        em.dma('sp', cst[:], consts, 'ld_c2', writes=['cst'])
        em.dma_batch('sp', [(xres[:, kc, :], xT[:, kc, :], [], ['xres']) for kc in range(KC)], 'ld_x')

        def T(eng, out, a, b, op, R, W):
            em.op(eng, lambda e: e.tensor_tensor(out=out, in0=a, in1=b, op=op), reads=R, writes=W)

        def S(eng, out, a, s1, s2, op0, op1, R, W):
            em.op(eng, lambda e: e.tensor_scalar(out=out, in0=a, scalar1=s1, scalar2=s2, op0=op0, op1=op1), reads=R, writes=W)

        def STT(out, a, sc, b, op0, op1, R, W):
            em.op('dve', lambda e: e.scalar_tensor_tensor(out=out, in0=a, scalar=sc, in1=b, op0=op0, op1=op1), reads=R, writes=W)

        def A(out, in_, func, R, W, bias=0.0, scale=1.0):
            em.op('act', lambda e: e.activation(out=out, in_=in_, func=func, bias=bias, scale=scale), reads=R, writes=W)

        def CP(eng, out, in_, R, W):
            if eng == 'act':
                em.op('act', lambda e: e.copy(out=out, in_=in_), reads=R, writes=W)
            else:
                em.op(eng, lambda e: e.tensor_copy(out=out, in_=in_), reads=R, writes=W)

        def MM(out, lhsT, rhs, R, W, start=True, stop=True):
            em.op('pe', lambda e: e.matmul(out, lhsT=lhsT, rhs=rhs, start=start, stop=stop), reads=R, writes=W, inc=stop)

        def TR(out, in_, idn, R, W):
            em.op('pe', lambda e: e.transpose(out=out, in_=in_, identity=idn), reads=R + ['cst'], writes=W)

        bank = {'A': 0, 'B': 0}

        def nb(pool='B'):
            b = bank[pool]
            bank[pool] = (bank[pool] + 1) % 4
            return b + (0 if pool == 'A' else 4)

        def load_slot(s, src):
            dst = ring[:, s * SLOT:(s + 1) * SLOT]
            em.dma('pool', dst.rearrange("p (a b) -> p a b", b=1024), src.rearrange("p (a b) -> p a b", b=1024),
                   'ring%d' % s, writes=['ring%d' % s])

        def ffn(l, f):
            fs = ExitStack()
            with fs:
                xn = sbuf(fs, "xn%d%d" % (l, f), [128, KC, NCOL_], BF16)
                hbuf = sbuf(fs, "hbuf%d%d" % (l, f), [128, 2, GRP, 512], BF16)
                sq = sbuf(fs, "sq%d%d" % (l, f), [128, KC, 512], BF16)
                rstd = sbuf(fs, "rstd%d%d" % (l, f), [128, 512])
                silu = sbuf(fs, "silu%d%d" % (l, f), [128, 2, 512])
                nidx = (0 if f == 0 else 4) + l
                for ti in range(NT):
                    c0, n = tiles[ti]
                    A(sq[:, :, :n], xres[:, :, c0:c0 + n], AF.Square, ['xres'], ['sq'])
                    for kc in range(KC):
                        MM(ps[7][:, :n], ones_bf[:], sq[:, kc, :n], ['ones_bf', 'sq'], ['ps7'], start=(kc == 0), stop=(kc == KC - 1))
                    S('dve', rstd[:, :n], ps[7][:, :n], 1.0 / D, EPS, ALU.mult, ALU.add, ['ps7'], ['rstd'])
                    A(rstd[:, :n], rstd[:, :n], AF.Sqrt, ['rstd'], ['rstd'])
                    em.op('dve', lambda e: e.reciprocal(out=rstd[:, :n], in_=rstd[:, :n]), reads=['rstd'], writes=['rstd'])
                    for kc in range(KC):
                        STT(xn[:, kc, c0:c0 + n], xres[:, kc, c0:c0 + n], normw[:, nidx, kc:kc + 1], rstd[:, :n],
                            ALU.mult, ALU.mult, ['xres', 'rstd', 'normw'], ['xn%d' % ti])
                groups = [list(range(g, min(g + GRP, NJ))) for g in range(0, NJ, GRP)]
                slot_of = {}
                nxt = [0]

                def issue_group(gi):
                    for j in groups[gi]:
                        s_ = nxt[0] % 8
                        nxt[0] += 1
                        slot_of[j] = s_
                        load_slot(s_, wffn[l, f, j])

                issue_group(0)
                for gi, grp in enumerate(groups):
                    if gi + 1 < len(groups):
                        issue_group(gi + 1)

                    def gu(ti):
                        c0, n = tiles[ti]
                        hb = ti % 2
                        for ji, j in enumerate(grp):
                            s_ = slot_of[j]
                            for which in (0, 1):
                                b = 2 * (ji % 2) + which
                                for kc in range(KC):
                                    o = s_ * SLOT + which * 1024 + kc * 128
                                    MM(ps[b][:, :n], ring[:, o:o + 128], xn[:, kc, c0:c0 + n], ['ring%d' % s_, 'xn%d' % ti],
                                       ['ps%d' % b], start=(kc == 0), stop=(kc == KC - 1))
                            b = 2 * (ji % 2)
                            A(silu[:, ji % 2, :n], ps[b][:, :n], AF.Silu, ['ps%d' % b], ['silu%d' % (ji % 2)])
                            T('dve', hbuf[:, hb, ji, :n], silu[:, ji % 2, :n], ps[b + 1][:, :n], ALU.mult,
                              ['silu%d' % (ji % 2), 'ps%d' % (b + 1)], ['h%d_%d' % (hb, ji)])

                    def down(ti):
                        c0, n = tiles[ti]
                        hb = ti % 2
                        for mc in range(KC):
                            b = 4 + mc % 3
                            for ji, j in enumerate(grp):
                                o = slot_of[j] * SLOT + 2048 + mc * 128
                                MM(ps[b][:, :n], ring[:, o:o + 128], hbuf[:, hb, ji, :n], ['ring%d' % slot_of[j], 'h%d_%d' % (hb, ji)],
                                   ['ps%d' % b], start=(ji == 0), stop=(ji == len(grp) - 1))
                            STT(xres[:, mc, c0:c0 + n], ps[b][:, :n], 0.5, xres[:, mc, c0:c0 + n], ALU.mult, ALU.add,
                                ['ps%d' % b, 'xres'], ['xres'])

                    gu(0)
                    for ti in range(NT):
                        if ti + 1 < NT:
                            gu(ti + 1)
                        down(ti)
                em.barrier()

        def mixer_phase(l):
            bs = ExitStack()
            with bs:
                def sb(name, shape, dt=F32):
                    return sbuf(bs, name + "_%d" % l, shape, dt)
                for i in range(3):
                    em.dma('pool', ring[:, i * 6144:(i + 1) * 6144].rearrange("p (a b) -> p a b", b=2048),
                           w_in[l][:, i * 6144:(i + 1) * 6144].rearrange("p (a b) -> p a b", b=2048), 'win%d' % i,
                           writes=['ring%d' % (2 * i), 'ring%d' % (2 * i + 1)])
                em.dma('pool', ring[:, 18432:26624].rearrange("p (a b) -> p a b", b=2048),
                       w_out[l].rearrange("p (a b) -> p a b", b=2048), 'wout', writes=['ring6', 'ring7', 'ring8'])
                RIN = ['ring%d' % i for i in range(6)]
                ROUT = ['ring6', 'ring7', 'ring8']

                xnb = sb("xnb", [128, KC, NB], BF16)
                sqb = sb("sqb", [128, KC, NB], BF16)
                rstb = sb("rstb", [128, NB])
                pj = sb("pj", [128, 18, 3 + NB])
                tmpc = sb("tmpc", [128, 18, 3])
                omix = sb("omix", [128, KC, NB], BF16)
                g1_def = g2_def = None
                lg1 = sb("lg1", [128, 2, NB])
                lg2 = sb("lg2", [128, 2, NB])
                s5st = sb("s5st", [128, 16, 12, 2])
                lst = sb("lst", [128, 16, 2])
                cst16 = sb("cst16", [128, 16, 2, 3])
                sh16 = sb("sh16", [128, 16, 11])
                h0v = lambda ri: s5st[:, :, :, ri].rearrange("p b g -> p g b")
                lh0v = lst[:].rearrange("p b c -> p c b")
                convv = lambda j: cst16[:, :, :, j].rearrange("p b c -> p c b")
                prevv = sh16[:].rearrange("p b c -> p c b")
                cm = sb("cm", [64, 4, 64])
                em.dma('sp', cm[:], cmask, 'ld_cm', writes=['cm'])
                MSU, MU, MSL, I6 = cm[:, 0, :], cm[:, 1, :], cm[:, 2, :], cm[:, 3, :]

                def gelu(dst, src, k, n, R, W, g1=None, g2_=None, kk_='g'):
                    g1 = g1 if g1 is not None else g1_def
                    g2_ = g2_ if g2_ is not None else g2_def
                    k1, k2 = kk_ + '1', kk_ + '2'
                    A(g1[:, :k, :n], src, AF.Square, R, [k1])
                    S('dve', g1[:, :k, :n], g1[:, :k, :n], 0.044715, 1.0, ALU.mult, ALU.add, [k1], [k1])
                    T('dve', g1[:, :k, :n], g1[:, :k, :n], src, ALU.mult, [k1] + R, [k1])
                    A(g2_[:, :k, :n], g1[:, :k, :n], AF.Sigmoid, [k1], [k2], scale=1.5957691216057308)
                    T('dve', dst, g2_[:, :k, :n], src, ALU.mult, [k2] + R, W)

                lp = sb("lp", [128, 2, 8])
                lw = sb("lw", [128, 2, 2, 128])
                lcl = sb("lcl", [128, 2])
                em.dma('sp', lp[:], lrp[l], 'ld_lp', writes=['lp'])
                em.dma('sp', lw[:], lrw[l], 'ld_lw', writes=['lw'])
                A(lcl[:], lp[:, :, 7], AF.Sigmoid, ['lp'], ['lcl'])
                A(lcl[:], lcl[:], AF.Ln, ['lcl'], ['lcl'])
                S('dve', lcl[:], lcl[:], 8.0, None, ALU.mult, ALU.bypass, ['lcl'], ['lcl'])
                lru_h = sb("lru_h", [128, 2])
                lxc = sb("lxc", [128, 2, NB])
                lga = sb("lga", [128, 2, NB])
                lgx = sb("lgx", [128, 2, NB])
                la = sb("la", [128, 2, NB])
                lb = sb("lb", [128, 2, NB])
                lh = sb("lh", [128, 2, NB])

                def lru_tile(n, samp=False):
                    gelu(lg1[:, :, :n], pj[:, 16:18, 3:3 + n], 2, n, ['pj'], ['lg1'], lg1, lg2, 'lg')
                    yield
                    for c in range(2):
                        S('dve', lxc[:, c, :n], pj[:, 14 + c, 3:3 + n], lp[:, c, 3:4], lp[:, c, 4:5], ALU.mult, ALU.add,
                          ['pj', 'lp'], ['lxc'])
                        yield
                        for j in range(3):
                            STT(lxc[:, c, :n], (convv(j)[:, c, :] if samp else pj[:, 14 + c, j:j + n]), lp[:, c, j:j + 1], lxc[:, c, :n],
                                ALU.mult, ALU.add, ['pj', 'lp', 'lxc', 'cst16'], ['lxc'])
                        yield
                    yield
                    b = nb()
                    yield
                    for c in range(2):
                        MM(ps[b][:, c * 64:c * 64 + n], lw[:, c, 0, :], lxc[:, c, :n], ['lw', 'lxc'], ['ps%d' % b])
                        yield
                        MM(ps[b][:, 128 + c * 64:128 + c * 64 + n], lw[:, c, 1, :], lxc[:, c, :n], ['lw', 'lxc'], ['ps%d' % b])
                        yield
                    yield
                    for c in range(2):
                        A(lga[:, c, :n], ps[b][:, c * 64:c * 64 + n], AF.Sigmoid, ['ps%d' % b, 'lp'], ['lga'], bias=lp[:, c, 5:6])
                        yield
                        A(lgx[:, c, :n], ps[b][:, 128 + c * 64:128 + c * 64 + n], AF.Sigmoid, ['ps%d' % b, 'lp'], ['lgx'], bias=lp[:, c, 6:7])
                        yield
                        A(la[:, c, :n], lga[:, c, :n], AF.Exp, ['lga', 'lcl'], ['la'], scale=lcl[:, c:c + 1])
                        yield
                    yield
                    T('dve', lb[:, :, :n], la[:, :, :n], la[:, :, :n], ALU.mult, ['la'], ['lb'])
                    yield
                    S('dve', lb[:, :, :n], lb[:, :, :n], -1.0, 1.0, ALU.mult, ALU.add, ['lb'], ['lb'])
                    yield
                    A(lb[:, :, :n], lb[:, :, :n], AF.Sqrt, ['lb'], ['lb'])
                    yield
                    T('dve', lb[:, :, :n], lb[:, :, :n], lgx[:, :, :n], ALU.mult, ['lb', 'lgx'], ['lb'])
                    yield
                    T('dve', lb[:, :, :n], lb[:, :, :n], lxc[:, :, :n], ALU.mult, ['lb', 'lxc'], ['lb'])
                    yield
                    if samp:
                        T('dve', lh[:, :, :n], la[:, :, :n], lh0v, ALU.mult, ['la', 'lst'], ['lh'])
                        T('dve', lh[:, :, :n], lh[:, :, :n], lb[:, :, :n], ALU.add, ['lh', 'lb'], ['lh'])
                        CP('dve', lh0v, lh[:, :, :n], ['lh'], ['lst'])
                        CP('dve', convv(0), convv(1), ['cst16', 'lxc'], ['cst16'])
                        CP('dve', convv(1), convv(2), ['cst16'], ['cst16'])
                        CP('dve', convv(2), pj[:, 14:16, 3:3 + n], ['pj'], ['cst16'])
                    else:
                        for c in range(2):
                            em.op('dve', lambda e, c=c: e.tensor_tensor_scan(out=lh[:, c, :n], data0=la[:, c, :n], data1=lb[:, c, :n],
                                                                          initial=lru_h[:, c:c + 1], op0=ALU.mult, op1=ALU.add),
                                  reads=['la', 'lb', 'lru_h'], writes=['lh'])
                        CP('dve', lru_h[:], lh[:, :, n - 1], ['lh'], ['lru_h'])
                    yield
                    T('dve', omix[:, 6:8, :n], lh[:, :, :n], lg1[:, :, :n], ALU.mult, ['lh', 'lg1'], ['omix_l'])
                    yield

                hst = sb("hst", [128, 12, 2])
                if 's5' in mixers:
                    sv = sb("sv", [128, 3, 2])
                    sglu = sb("sglu", [128, 3, 384], BF16)
                    szb = sb("szb", [128, 3, NB], BF16)
                    lB = sb("lB", [128, 2, 12, 128], BF16)
                    lC = sb("lC", [128, 2, 12, 128], BF16)
                    rmag = sb("rmag", [128, 12])
                    Et = sb("Et", [128, 2, 12, NB])
                    abar = sb("abar", [128, 12, 2])
                    ss = ExitStack()
                    ss.__enter__()
                    sA = sbuf(ss, "sA_%d" % l, [128, 12, 3])
                    sBp = sbuf(ss, "sBp_%d" % l, [128, 3, 64, 3])
                    sbT = sbuf(ss, "sbT_%d" % l, [128, 3, 64, 2])
                    scT = sbuf(ss, "scT_%d" % l, [128, 12, 16, 2])
                    em.dma('sp', sA[:], s5A[l], 'ld_s5a', writes=['sA'])
                    em.dma('sp', sBp[:], s5B[l], 'ld_s5b', writes=['sBp'])
                    em.dma('sp', sbT[:], s5bT[l], 'ld_s5c', writes=['sbT'])
                    em.dma('sp', scT[:], s5cT[l], 'ld_s5d', writes=['scT'])
                    em.dma('sp', sv[:], s5v[l], 'ld_s5e', writes=['sv'])
                    em.dma('pool', sglu[:], s5glu[l], 'ld_s5f', writes=['sglu'])
                    wk = sbuf(ss, "wk_%d" % l, [128, 7, 64])
                    wk7 = sbuf(ss, "wk7_%d" % l, [128, 768])
                    wk8 = sbuf(ss, "wk8_%d" % l, [128, 768])
                    wki = sbuf(ss, "wki_%d" % l, [128, 768], mybir.dt.int32)

                    def rr(x, m, k, R, W):
                        tmp_ = wk[:, k, :m] if m <= 64 else wk8[:, :m]
                        CP('dve', wki[:, :m], x, R, ['wki'])
                        CP('dve', tmp_, wki[:, :m], ['wki'], ['wk%d' % k])
                        T('dve', x, x, tmp_, ALU.subtract, R + ['wk%d' % k], W)
                        S('dve', tmp_, x, 0.5, None, ALU.is_gt, ALU.bypass, W, ['wk%d' % k])
                        T('dve', x, x, tmp_, ALU.subtract, W + ['wk%d' % k], W)
                        S('dve', tmp_, x, -0.5, None, ALU.is_lt, ALU.bypass, W, ['wk%d' % k])
                        T('dve', x, x, tmp_, ALU.add, W + ['wk%d' % k], W)

                    def disc(lr, li, ldt, m, pre):
                        A(wk[:, 4, :m], ldt, AF.Exp, [pre], ['wk4'])
                        T('dve', wk[:, 0, :m], lr, wk[:, 4, :m], ALU.mult, [pre, 'wk4'], ['wk0'])
                        A(wk[:, 0, :m], wk[:, 0, :m], AF.Exp, ['wk0'], ['wk0'])
                        T('dve', wk[:, 3, :m], li, wk[:, 4, :m], ALU.mult, [pre, 'wk4'], ['wk3'])
                        S('dve', wk[:, 3, :m], wk[:, 3, :m], 1.0 / PI2, None, ALU.mult, ALU.bypass, ['wk3'], ['wk3'])
                        rr(wk[:, 3, :m], m, 5, ['wk3'], ['wk3'])
                        A(wk[:, 2, :m], wk[:, 3, :m], AF.Sin, ['wk3'], ['wk2'], scale=PI2 * 0.999999)
                        S('dve', wk[:, 6, :m], wk[:, 3, :m], 0.25, None, ALU.add, ALU.bypass, ['wk3'], ['wk6'])
                        rr(wk[:, 6, :m], m, 5, ['wk6'], ['wk6'])
                        A(wk[:, 1, :m], wk[:, 6, :m], AF.Sin, ['wk6'], ['wk1'], scale=PI2 * 0.999999)
                        T('dve', wk[:, 1, :m], wk[:, 1, :m], wk[:, 0, :m], ALU.mult, ['wk1', 'wk0'], ['wk1'])
                        T('dve', wk[:, 2, :m], wk[:, 2, :m], wk[:, 0, :m], ALU.mult, ['wk2', 'wk0'], ['wk2'])

                    disc(sA[:, :, 0], sA[:, :, 1], sA[:, :, 2], 12, 'sA')
                    CP('dve', abar[:, :, 0], wk[:, 1, :12], ['wk1'], ['abar'])
                    CP('dve', abar[:, :, 1], wk[:, 2, :12], ['wk2'], ['abar'])
                    CP('dve', rmag[:], wk[:, 0, :12], ['wk0'], ['Rt'])
                    wkE = wk7[:, :].rearrange("p (a b) -> p a b", b=NB)
                    T('dve', wkE, wk[:, 3, :12].unsqueeze(2).to_broadcast([128, 12, NB]),
                      iota1.unsqueeze(1).to_broadcast([128, 12, NB]), ALU.mult, ['wk3', 'cst'], ['wk7'])
                    rr(wk7[:, :], 768, 5, ['wk7'], ['wk7'])
                    A(Et[:, 1, :, :], wkE, AF.Sin, ['wk7'], ['Et'], scale=-PI2 * 0.999999)
                    S('dve', wk7[:, :], wk7[:, :], 0.25, None, ALU.add, ALU.bypass, ['wk7'], ['wk7'])
                    rr(wk7[:, :], 768, 5, ['wk7'], ['wk7'])
                    A(Et[:, 0, :, :], wkE, AF.Sin, ['wk7'], ['Et'], scale=PI2 * 0.999999)
                    for kc in range(3):
                        lr, li = sBp[:, kc, :, 0], sBp[:, kc, :, 1]
                        disc(lr, li, sBp[:, kc, :, 2], 64, 'sBp')
                        m = 64
                        T('dve', wk[:, 4, :m], lr, lr, ALU.mult, ['sBp'], ['wk4'])
                        T('dve', wk[:, 5, :m], li, li, ALU.mult, ['sBp'], ['wk5'])
                        T('dve', wk[:, 4, :m], wk[:, 4, :m], wk[:, 5, :m], ALU.add, ['wk4', 'wk5'], ['wk4'])
                        em.op('dve', lambda e: e.reciprocal(out=wk[:, 4, :64], in_=wk[:, 4, :64]), reads=['wk4'], writes=['wk4'])
                        S('dve', wk[:, 1, :m], wk[:, 1, :m], -1.0, None, ALU.add, ALU.bypass, ['wk1'], ['wk1'])
                        T('dve', wk[:, 5, :m], wk[:, 1, :m], lr, ALU.mult, ['wk1', 'sBp'], ['wk5'])
                        T('dve', wk[:, 6, :m], wk[:, 2, :m], li, ALU.mult, ['wk2', 'sBp'], ['wk6'])
                        T('dve', wk[:, 5, :m], wk[:, 5, :m], wk[:, 6, :m], ALU.add, ['wk5', 'wk6'], ['wk5'])
                        T('dve', wk[:, 5, :m], wk[:, 5, :m], wk[:, 4, :m], ALU.mult, ['wk5', 'wk4'], ['wk5'])
                        T('dve', wk[:, 6, :m], wk[:, 2, :m], lr, ALU.mult, ['wk2', 'sBp'], ['wk6'])
                        T('dve', wk[:, 0, :m], wk[:, 1, :m], li, ALU.mult, ['wk1', 'sBp'], ['wk0'])
                        T('dve', wk[:, 6, :m], wk[:, 6, :m], wk[:, 0, :m], ALU.subtract, ['wk6', 'wk0'], ['wk6'])
                        T('dve', wk[:, 6, :m], wk[:, 6, :m], wk[:, 4, :m], ALU.mult, ['wk6', 'wk4'], ['wk6'])
                        bre, bim = sbT[:, kc, :, 0], sbT[:, kc, :, 1]
                        T('dve', wk[:, 0, :m], wk[:, 5, :m], bre, ALU.mult, ['wk5', 'sbT'], ['wk0'])
                        T('dve', wk[:, 1, :m], wk[:, 6, :m], bim, ALU.mult, ['wk6', 'sbT'], ['wk1'])
                        T('dve', wk[:, 0, :m], wk[:, 0, :m], wk[:, 1, :m], ALU.subtract, ['wk0', 'wk1'], ['wk0'])
                        T('dve', wk[:, 1, :m], wk[:, 5, :m], bim, ALU.mult, ['wk5', 'sbT'], ['wk1'])
                        T('dve', wk[:, 2, :m], wk[:, 6, :m], bre, ALU.mult, ['wk6', 'sbT'], ['wk2'])
                        T('dve', wk[:, 1, :m], wk[:, 1, :m], wk[:, 2, :m], ALU.add, ['wk1', 'wk2'], ['wk1'])
                        for q in range(4):
                            for gg in range(2):
                                for ri in range(2):
                                    S('dve', lB[:, ri, 4 * kc + q, 64 * gg:64 * gg + 64], wk[:, ri, :64],
                                      mask8[:, 2 * q + gg:2 * q + gg + 1], None, ALU.mult, ALU.bypass, ['wk%d' % ri, 'cst'], ['lB'])
                    em.op('pool', lambda e: e.memset(lC[:], 0.0), writes=['lC'])
                    for gh in range(12):
                        q = gh % 4
                        for gg in range(2):
                            S('dve', lC[:, 0, gh, 32 * q + 16 * gg:32 * q + 16 * gg + 16], scT[:, gh, :, 0], maskg2[:, gg:gg + 1], None,
                              ALU.mult, ALU.bypass, ['scT', 'cst'], ['lC'])
                            S('dve', lC[:, 1, gh, 32 * q + 16 * gg:32 * q + 16 * gg + 16], scT[:, gh, :, 1], maskg2[:, gg:gg + 1], -1.0,
                              ALU.mult, ALU.mult, ['scT', 'cst'], ['lC'])
                    em.barrier()
                    ss.close()
                    sx = sb("sx", [128, 2, 12, NB])
                    sg = sb("sg", [128, 2, 12, NB])
                    sh = sx
                    em.alias['sh'] = 'sx'
                    shb = sb("shb", [128, 2, 12, NB], BF16)
                    ubf = sb("ubf", [128, 3, NB], BF16)
                    st1 = sb("st1", [128, 12, NB])
                    sy = sb("sy", [128, 3, NB])
                    sz = sb("sz", [128, 3, NB])

                def s5_tile(n, samp=False):
                    CP('act', ubf[:, :, :n], pj[:, 0:3, 3:3 + n], ['pj'], ['ubf'])
                    yield
                    bre = [nb('A'), nb('A')]
                    yield
                    bim = [nb('A'), nb('A')]
                    yield
                    for ri, bb in ((0, bre), (1, bim)):
                        for gh in range(12):
                            b = bb[gh // 8]
                            o = (gh % 8) * 64
                            MM(ps[b][:, o:o + n], lB[:, ri, gh, :], ubf[:, gh // 4, :n], ['lB', 'ubf'], ['ps%d' % b])
                    yield
                    if samp:
                        for half, (g0, g1_) in enumerate(((0, 8), (8, 12))):
                            k = g1_ - g0
                            pr = ps[bre[half]][:, 0:k * 64].rearrange("p (a b) -> p a b", b=64)[:, :, :n]
                            pi = ps[bim[half]][:, 0:k * 64].rearrange("p (a b) -> p a b", b=64)[:, :, :n]
                            ar = abar[:, g0:g1_, 0].unsqueeze(2).to_broadcast([128, k, n])
                            ai = abar[:, g0:g1_, 1].unsqueeze(2).to_broadcast([128, k, n])
                            h0r, h0i = h0v(0)[:, g0:g1_, :], h0v(1)[:, g0:g1_, :]
                            T('dve', sg[:, 0, g0:g1_, :n], h0r, ar, ALU.mult, ['s5st', 'abar'], ['sg'])
                            T('dve', st1[:, g0:g1_, :n], h0i, ai, ALU.mult, ['s5st', 'abar'], ['st1'])
                            T('dve', sg[:, 0, g0:g1_, :n], sg[:, 0, g0:g1_, :n], st1[:, g0:g1_, :n], ALU.subtract, ['sg', 'st1'], ['sg'])
                            T('dve', sx[:, 0, g0:g1_, :n], sg[:, 0, g0:g1_, :n], pr, ALU.add, ['sg', 'ps%d' % bre[half]], ['sx'])
                            T('dve', sg[:, 1, g0:g1_, :n], h0i, ar, ALU.mult, ['s5st', 'abar'], ['sg'])
                            T('dve', st1[:, g0:g1_, :n], h0r, ai, ALU.mult, ['s5st', 'abar'], ['st1'])
                            T('dve', sg[:, 1, g0:g1_, :n], sg[:, 1, g0:g1_, :n], st1[:, g0:g1_, :n], ALU.add, ['sg', 'st1'], ['sg'])
                            T('dve', sx[:, 1, g0:g1_, :n], sg[:, 1, g0:g1_, :n], pi, ALU.add, ['sg', 'ps%d' % bim[half]], ['sx'])
                        CP('dve', h0v(0), sx[:, 0, :, :n], ['sx'], ['s5st'])
                        CP('dve', h0v(1), sx[:, 1, :, :n], ['sx'], ['s5st'])
                    else:
                        for half, (g0, g1_) in enumerate(((0, 8), (8, 12))):
                            k = g1_ - g0
                            pr = ps[bre[half]][:, 0:k * 64].rearrange("p (a b) -> p a b", b=64)[:, :, :n]
                            pi = ps[bim[half]][:, 0:k * 64].rearrange("p (a b) -> p a b", b=64)[:, :, :n]
                            Rr, Ri = ['ps%d' % bre[half], 'Et'], ['ps%d' % bim[half], 'Et']
                            er, ei = Et[:, 0, g0:g1_, :n], Et[:, 1, g0:g1_, :n]
                            T('dve', sx[:, 0, g0:g1_, :n], pr, er, ALU.mult, Rr, ['sx'])
                            T('dve', st1[:, g0:g1_, :n], pi, ei, ALU.mult, Ri, ['st1'])
                            T('dve', sx[:, 0, g0:g1_, :n], sx[:, 0, g0:g1_, :n], st1[:, g0:g1_, :n], ALU.subtract, ['sx', 'st1'], ['sx'])
                            T('dve', sx[:, 1, g0:g1_, :n], pr, ei, ALU.mult, Rr, ['sx'])
                            T('dve', st1[:, g0:g1_, :n], pi, er, ALU.mult, Ri, ['st1'])
                            T('dve', sx[:, 1, g0:g1_, :n], sx[:, 1, g0:g1_, :n], st1[:, g0:g1_, :n], ALU.add, ['sx', 'st1'], ['sx'])
                        for ri in range(2):
                            for gh in range(12):
                                em.op('dve', lambda e, ri=ri, gh=gh: e.tensor_tensor_scan(
                                    out=sg[:, ri, gh, :n], data0=rmag[:, gh:gh + 1].to_broadcast([128, n]), data1=sx[:, ri, gh, :n],
                                    initial=hst[:, gh, ri:ri + 1], op0=ALU.mult, op1=ALU.add),
                                    reads=['Rt', 'sx', 'hst'], writes=['sg'])
                        er, ei = Et[:, 0, :, :n], Et[:, 1, :, :n]
                        T('dve', sh[:, 0, :, :n], sg[:, 0, :, :n], er, ALU.mult, ['sg', 'Et'], ['sh'])
                        T('dve', st1[:, :, :n], sg[:, 1, :, :n], ei, ALU.mult, ['sg', 'Et'], ['st1'])
                        T('dve', sh[:, 0, :, :n], sh[:, 0, :, :n], st1[:, :, :n], ALU.add, ['sh', 'st1'], ['sh'])
                        T('dve', sh[:, 1, :, :n], sg[:, 1, :, :n], er, ALU.mult, ['sg', 'Et'], ['sh'])
                        T('dve', st1[:, :, :n], sg[:, 0, :, :n], ei, ALU.mult, ['sg', 'Et'], ['st1'])
                        T('dve', sh[:, 1, :, :n], sh[:, 1, :, :n], st1[:, :, :n], ALU.subtract, ['sh', 'st1'], ['sh'])
                        CP('dve', hst[:, :, 0], sh[:, 0, :, n - 1], ['sh'], ['hst'])
                        CP('dve', hst[:, :, 1], sh[:, 1, :, n - 1], ['sh'], ['hst'])
                    yield
                    CP('act', shb[:, :, :, :n], sh[:, :, :, :n], ['sh'], ['shb'])
                    yield
                    b = nb('A')
                    yield
                    for kc in range(3):
                        for q in range(4):
                            gh = 4 * kc + q
                            MM(ps[b][:, kc * 64:kc * 64 + n], lC[:, 0, gh, :], shb[:, 0, gh, :n], ['lC', 'shb'], ['ps%d' % b],
                               start=(q == 0), stop=False)
                            MM(ps[b][:, kc * 64:kc * 64 + n], lC[:, 1, gh, :], shb[:, 1, gh, :n], ['lC', 'shb'], ['ps%d' % b],
                               start=False, stop=(q == 3))
                    yield
                    for kc in range(3):
                        STT(sy[:, kc, :n], pj[:, kc, 3:3 + n], sv[:, kc, 0:1], ps[b][:, kc * 64:kc * 64 + n], ALU.mult, ALU.add,
                            ['pj', 'sv', 'ps%d' % b], ['sy'])
                    yield
                    gelu(sz[:, :, :n], sy[:, :, :n], 3, n, ['sy'], ['sz'], st1[:, 0:3, :], st1[:, 3:6, :], 'st')
                    yield
                    CP('act', szb[:, :, :n], sz[:, :, :n], ['sz'], ['szb'])
                    yield
                    b = nb('A')
                    yield
                    for k2 in range(3):
                        for kc in range(3):
                            MM(ps[b][:, k2 * 64:k2 * 64 + n], sglu[:, kc, k2 * 128:(k2 + 1) * 128], szb[:, kc, :n], ['sglu', 'szb'],
                               ['ps%d' % b], start=(kc == 0), stop=(kc == 2))
                    yield
                    for k2 in range(3):
                        A(sy[:, k2, :n], ps[b][:, k2 * 64:k2 * 64 + n], AF.Sigmoid, ['ps%d' % b, 'sv'], ['sy'], bias=sv[:, k2, 1:2])
                    yield
                    T('dve', omix[:, 0:3, :n], sz[:, :, :n], sy[:, :, :n], ALU.mult, ['sz', 'sy'], ['omix_s'])
                    yield

                S0T = sb("S0T", [128, 3, 64])
                if 'rwkv' in mixers:
                    rp = sb("rp", [128, 3, 8])
                    rmu = sb("rmu", [128, 11])
                    rup = sb("rup", [128, 2, 384])
                    em.dma('sp', rp[:], rwp[l], 'ld_r1', writes=['rp'])
                    em.dma('sp', rmu[:], rwmu[l], 'ld_r2', writes=['rmu'])
                    em.dma('sp', rup[:], rwup[l], 'ld_r3', writes=['rup'])
                    ones64 = sb("ones64", [128, NB])
                    em.op('pool', lambda e: e.memset(ones64[:], 1.0), writes=['ones64'])
                    xm = sb("xm", [128, 11, NB])
                    early_ = ('sgw', 'a', 'kk', 'kp', 'b', 'lg', 'gam', 'gex', 'gin', 't1')
                    own_ = ('g', 'bon', 'rt0', 'rt1', 'kt0', 'kt1', 'kti', 'nbt')
                    rarr = {}
                    for i_, nm in enumerate(early_):
                        if 's5' in mixers:
                            base_ = sx if i_ < 8 else sg
                            j_ = i_ % 8
                            rarr[nm] = base_[:, j_ // 4, 3 * (j_ % 4):3 * (j_ % 4) + 3, :]
                            em.alias[nm] = 'sx' if i_ < 8 else 'sg'
                        else:
                            rarr[nm] = sb("r_" + nm, [128, 3, NB])
                    for nm in own_:
                        rarr[nm] = sb("r_" + nm, [128, 3, NB])
                    rarr['y'], rarr['d'], rarr['t2'] = rarr['kti'], rarr['nbt'], rarr['kt0']
                    em.alias.update({'y': 'kti', 'd': 'nbt', 't2': 'kt0'})
                    tw = sb("tw", [128, NB])
                    tw2 = sb("tw2", [128, NB])
                    twb = sb("twb", [128, NB])
                    em.op('pool', lambda e: e.memset(tw[:], 0.0), writes=['tw'])
                    em.op('pool', lambda e: e.memset(twb[:], 0.0), writes=['twb'])
                    gC = sb("gC", [128, 3])
                    tmj_all = sb("tmj_all", [128, 8 * 384])
                    TMN = ('V', 'K', 'B', 'AkkT', 'N', 'M', 'P', 'X')
                    TMK = ['tm_' + x_ for x_ in TMN]
                    tmj = {nm: tmj_all[:, i_ * 384:(i_ + 1) * 384] for i_, nm in enumerate(TMN)}

                    def zero_tmj():
                        for nm_ in ('V', 'K', 'B', 'AkkT', 'N', 'M', 'P', 'X'):
                            em.op('pool', lambda e, nm_=nm_: e.memset(tmj[nm_][:], 0.0), writes=['tm_' + nm_])
                    tmj['R'], tmj['U'], tmj['Y'] = tmj['N'], tmj['M'], tmj['P']
                    tmj['ArkT'], tmj['ArbT'] = tmj['AkkT'], tmj['X']
                    em.alias.update({'tm_R': 'tm_N', 'tm_U': 'tm_M', 'tm_Y': 'tm_P', 'tm_ArkT': 'tm_AkkT', 'tm_ArbT': 'tm_X'})
                    stmp = sb("stmp", [128, 64])

                def rwkv_tile(n, samp=False):
                    C = n
                    yield
                    ra = rarr
                    yield
                    RP = ['rp']
                    yield
                    T('dve', xm[:, :, :n], (prevv if samp else pj[:, 3:14, 2:2 + n]), pj[:, 3:14, 3:3 + n], ALU.subtract, ['pj', 'sh16'], ['xm'])
                    yield
                    T('dve', xm[:, :, :n], xm[:, :, :n], rmu[:, :].unsqueeze(2).to_broadcast([128, 11, n]), ALU.mult, ['xm', 'rmu'], ['xm'])
                    yield
                    T('dve', xm[:, :, :n], xm[:, :, :n], pj[:, 3:14, 3:3 + n], ALU.add, ['xm', 'pj'], ['xm'])
                    yield
                    r_, k_, v_ = xm[:, 0:3, :n], xm[:, 3:6, :n], xm[:, 6:9, :n]
                    yield
                    A(tw[0:64, :n], xm[0:64, 9, :n], AF.Tanh, ['xm'], ['tw'])
                    yield
                    CP('act', twb[64:128, :n], xm[64:128, 9, :n], ['xm'], ['twb'])
                    yield
                    T('dve', ra['kk'][:, :, :n], k_, rp[:, :, 2].unsqueeze(2).to_broadcast([128, 3, n]), ALU.mult, ['xm'] + RP, ['kk'])
                    yield
                    T('dve', ra['t1'][:, :, :n], ra['kk'][:, :, :n], ra['kk'][:, :, :n], ALU.mult, ['kk'], ['t1'])
                    yield
                    b0 = nb()
                    yield
                    for c in range(3):
                        MM(ps[b0][:, c * 64:c * 64 + n], rup[:, 0, c * 128:(c + 1) * 128], tw[:, :n], ['rup', 'tw'], ['ps%d' % b0])
                        yield
                        MM(ps[b0][:, 192 + c * 64:192 + c * 64 + n], rup[:, 0, c * 128:(c + 1) * 128], twb[:, :n],
                           ['rup', 'twb'], ['ps%d' % b0])
                        yield
                    yield
                    b2 = nb()
                    yield
                    for c in range(3):
                        MM(ps[b2][:, c * 64:c * 64 + n], bones, ra['t1'][:, c, :n], ['cst', 't1'], ['ps%d' % b2])
                    yield
                    for c in range(3):
                        A(ra['sgw'][:, c, :n], ps[b0][:, c * 64:c * 64 + n], AF.Sigmoid, ['ps%d' % b0] + RP, ['sgw'], bias=rp[:, c, 0:1])
                    yield
                    for c in range(3):
                        A(ra['a'][:, c, :n], ps[b0][:, 192 + c * 64:192 + c * 64 + n], AF.Sigmoid, ['ps%d' % b0] + RP, ['a'], bias=rp[:, c, 1:2])
                    yield
                    A(tw2[:, :n], xm[:, 10, :n], AF.Sigmoid, ['xm'], ['tw2'])
                    yield
                    S('dve', ra['sgw'][:, :, :n], ra['sgw'][:, :, :n], -0.6065306597126334, None, ALU.mult, ALU.bypass, ['sgw'], ['sgw'])
                    yield
                    if samp:
                        CP('dve', ra['lg'][:, :, :n], ra['sgw'][:, :, :n], ['sgw'], ['lg'])
                    else:
                        for c in range(3):
                            em.op('dve', lambda e, c=c: e.tensor_tensor_scan(out=ra['lg'][:, c, :n], data0=ones64[:, :n], data1=ra['sgw'][:, c, :n],
                                                                          initial=0.0, op0=ALU.mult, op1=ALU.add),
                                  reads=['ones64', 'sgw'], writes=['lg'])
                    yield
                    b1 = nb()
                    yield
                    for c in range(3):
                        MM(ps[b1][:, c * 64:c * 64 + n], rup[:, 1, c * 128:(c + 1) * 128], tw2[:, :n], ['rup', 'tw2'], ['ps%d' % b1])
                    yield
                    A(ra['gam'][:, :, :n], ra['lg'][:, :, :n], AF.Exp, ['lg'], ['gam'])
                    yield
                    A(ra['gin'][:, :, :n], ra['lg'][:, :, :n], AF.Exp, ['lg'], ['gin'], scale=-1.0)
                    yield
                    T('dve', ra['gex'][:, :, :n], ra['lg'][:, :, :n], ra['sgw'][:, :, :n], ALU.subtract, ['lg', 'sgw'], ['gex'])
                    yield
                    A(ra['gex'][:, :, :n], ra['gex'][:, :, :n], AF.Exp, ['gex'], ['gex'])
                    yield
                    for c in range(3):
                        S('dve', ra['kp'][:, c, :n], ra['a'][:, c, :n], -1.0, rp[:, c, 3:4], ALU.add, ALU.mult, ['a'] + RP, ['kp'])
                    yield
                    STT(ra['kp'][:, :, :n], ra['kp'][:, :, :n], 1.0, k_, ALU.add, ALU.mult, ['kp', 'xm'], ['kp'])
                    yield
                    for c in range(3):
                        STT(ra['bon'][:, c, :n], xm[:, c, :n], rp[:, c, 4:5], ra['kp'][:, c, :n], ALU.mult, ALU.mult, ['xm', 'kp'] + RP, ['bon'])
                    yield
                    b3 = nb()
                    yield
                    for c in range(3):
                        MM(ps[b3][:, c * 64:c * 64 + n], bones, ra['bon'][:, c, :n], ['cst', 'bon'], ['ps%d' % b3])
                    yield
                    CP('act', ra['g'][:, :, :n], ps[b1][:, 0:192].rearrange("p (a b) -> p a b", b=64)[:, :, :n], ['ps%d' % b1], ['g'])
                    yield
                    p3 = ps[b2][:, 0:192].rearrange("p (a b) -> p a b", b=64)[:, :, :n]
                    yield
                    A(ra['t1'][:, :, :n], p3, AF.Sqrt, ['ps%d' % b2], ['t1'])
                    yield
                    S('dve', ra['t1'][:, :, :n], ra['t1'][:, :, :n], 1e-12, None, ALU.max, ALU.bypass, ['t1'], ['t1'])
                    yield
                    em.op('dve', lambda e: e.reciprocal(out=ra['t1'][:, :, :n], in_=ra['t1'][:, :, :n]), reads=['t1'], writes=['t1'])
                    yield
                    T('dve', ra['kk'][:, :, :n], ra['kk'][:, :, :n], ra['t1'][:, :, :n], ALU.mult, ['kk', 't1'], ['kk'])
                    yield
                    T('dve', ra['b'][:, :, :n], ra['kk'][:, :, :n], ra['a'][:, :, :n], ALU.mult, ['kk', 'a'], ['b'])
                    yield
                    T('dve', ra['bon'][:, :, :n], ps[b3][:, 0:192].rearrange("p (a b) -> p a b", b=64)[:, :, :n], v_, ALU.mult,
                      ['ps%d' % b3, 'xm'], ['bon'])
                    yield
                    CP('dve', gC[:], ra['gam'][:, :, n - 1], ['gam'], ['gC'])
                    yield
                    for hp in range(2):
                        STT(ra['rt%d' % hp][:, :, :n], r_, maskg2[:, hp:hp + 1], ra['gam'][:, :, :n], ALU.mult, ALU.mult,
                            ['xm', 'gam', 'cst'], ['rt%d' % hp])
                        yield
                        STT(ra['kt%d' % hp][:, :, :n], ra['kk'][:, :, :n], maskg2[:, hp:hp + 1], ra['gex'][:, :, :n], ALU.mult, ALU.mult,
                            ['kk', 'gex', 'cst'], ['kt%d' % hp])
                        yield
                    yield
                    T('dve', ra['kti'][:, :, :n], ra['kp'][:, :, :n], ra['gin'][:, :, :n], ALU.mult, ['kp', 'gin'], ['kti'])
                    yield
                    STT(ra['nbt'][:, :, :n], ra['b'][:, :, :n], -1.0, ra['gin'][:, :, :n], ALU.mult, ALU.mult, ['b', 'gin'], ['nbt'])
                    yield
                    if not samp:
                        yield 'PHASE'

                    def rw_direct():
                        Sb = tmj_all[:, 0:1024].rearrange("p (b v) -> p b v", v=64)
                        yield
                        tb = tmj_all[:, 1024:2048].rearrange("p (b v) -> p b v", v=64)
                        yield

                        def bc(ap2, lo=0, hi=16):
                            return ap2[:, lo:hi].unsqueeze(2).to_broadcast([128, hi - lo, 64])

                        def idb(k_):
                            return identb.unsqueeze(1).to_broadcast([128, k_, 64])

                        def bmm(src):
                            bb_ = (nb(), nb())
                            for hf in range(2):
                                MM(ps[bb_[hf]][:, :], bones, src[:, 8 * hf:8 * hf + 8, :], ['cst'] + TMK, ['ps%d' % bb_[hf]])
                            return bb_

                        def pview(b_):
                            return ps[b_][:, :].rearrange("p (b v) -> p b v", v=64)
                        for c in range(3):
                            em.dma('sp', Sb, i_rwkv[l, :, :, c, :].rearrange("b p v -> p b v"), 'ld_sb', writes=TMK)
                            yield
                            kap, w_, kp_, bq_ = ra['kk'][:, c, :n], ra['gam'][:, c, :n], ra['kp'][:, c, :n], ra['b'][:, c, :n]
                            yield
                            rq_, vq_ = xm[:, c, :n], xm[:, 6 + c, :n]
                            yield
                            T('dve', tb, Sb, bc(kap), ALU.mult, TMK + ['kk'], TMK)
                            yield
                            bb_ = bmm(tb)
                            yield
                            T('dve', Sb, Sb, bc(w_), ALU.mult, TMK + ['gam'], TMK)
                            yield
                            for hf in range(2):
                                T('dve', tb[:, 8 * hf:8 * hf + 8, :], pview(bb_[hf]), bc(bq_, 8 * hf, 8 * hf + 8), ALU.mult,
                                  ['ps%d' % bb_[hf], 'b'], TMK)
                            yield
                            T('dve', Sb, Sb, tb, ALU.subtract, TMK, TMK)
                            yield
                            T('dve', tb, idb(16), bc(vq_), ALU.mult, ['cst', 'xm'], TMK)
                            yield
                            bb_ = bmm(tb)
                            yield
                            for hf in range(2):
                                T('dve', tb[:, 8 * hf:8 * hf + 8, :], pview(bb_[hf]), bc(kp_, 8 * hf, 8 * hf + 8), ALU.mult,
                                  ['ps%d' % bb_[hf], 'kp'], TMK)
                            yield
                            T('dve', Sb, Sb, tb, ALU.add, TMK, TMK)
                            yield
                            T('dve', tb, Sb, bc(rq_), ALU.mult, TMK + ['xm'], TMK)
                            yield
                            bb_ = bmm(tb)
                            yield
                            for hf in range(2):
                                T('dve', tb[:, 8 * hf:8 * hf + 8, :], pview(bb_[hf]), idb(8), ALU.mult, ['ps%d' % bb_[hf], 'cst'], TMK)
                            yield
                            em.op('dve', lambda e, c=c: e.tensor_reduce(out=ra['y'][:, c, :n], in_=tb, axis=AX.X, op=ALU.add),
                                  reads=TMK, writes=['y'])
                            yield
                            em.dma('sp', o_rwkv[l, 1:17, :, c, :].rearrange("b p v -> p b v"), Sb, 'st_sb', reads=TMK)
                            yield
                        yield

                    def rw_chunk():
                        if _RWSTOP <= 2:
                            return
                        yield
                        for nm, src, RR in (('V', xm[:, 6:9, :n], ['xm']), ('K', ra['kti'][:, :, :n], ['kti']), ('B', ra['nbt'][:, :, :n], ['nbt'])):
                            bt = nb()
                            yield
                            for c in range(3):
                                TR(ps[bt][:C, c * 128:(c + 1) * 128], src[:, c, :], ident, RR, ['ps%d' % bt])
                            yield
                            CP('act', tmj[nm][:C, :], ps[bt][:C, 0:384], ['ps%d' % bt], ['tm_' + nm])
                            yield
                        yield
                        if _RWSTOP <= 3:
                            return
                        yield
                        def hv(ap, hp, w):
                            return ap.rearrange("p (c h k) -> p h c k", h=2, k=64)[:, hp, :, :w]

                        def pv(b, w):
                            return ps[b][:C, 0:192].rearrange("p (a b) -> p a b", b=64)[:, :, :w]

                        def amat(lk, rk, RR, dst, mask):
                            bb2 = [nb(), nb()]
                            for h in range(6):
                                c, hp = h // 2, h % 2
                                ln_ = lk + str(hp) if lk in ('kt', 'rt') else lk
                                rn_ = rk + str(hp) if rk in ('kt', 'rt') else rk
                                MM(ps[bb2[hp]][:C, c * 64:c * 64 + C], ra[ln_][:, c, :n], ra[rn_][:, c, :n],
                                   [ln_, rn_], ['ps%d' % bb2[hp]])
                            for hp in range(2):
                                T('dve', hv(tmj[dst][:C, :], hp, C), pv(bb2[hp], C), mask[:C, :C].unsqueeze(1).to_broadcast([C, 3, C]),
                                  ALU.mult, ['ps%d' % bb2[hp], 'cm'], ['tm_' + dst])
                        v3 = lambda ap: ap.rearrange("p (a b) -> p a b", b=64)[:, :, :C]
                        yield
                        amat('kti', 'kt', ['kti', 'kt'], 'AkkT', MSU)
                        yield
                        amat('nbt', 'kt', ['nbt', 'kt'], 'M', MSU)
                        yield
                        T('dve', v3(tmj['X'][:C, :]), v3(tmj['M'][:C, :]), I6[:C, :C].unsqueeze(1).to_broadcast([C, 6, C]), ALU.add, ['tm_M', 'cm'], ['tm_X'])
                        yield
                        amat('kt', 'nbt', ['nbt', 'kt'], 'N', MSL)
                        yield
                        if _RWSTOP <= 4:
                            return
                        yield
                        nr = 0
                        yield
                        while (1 << (nr + 1)) < C:
                            nr += 1
                        yield
                        def xstep():
                            bx = nb()
                            for h in range(6):
                                sl = slice(h * 64, h * 64 + C)
                                MM(ps[bx][:C, sl], tmj['P'][:, sl], tmj['X'][:, sl], ['tm_P', 'tm_X'], ['ps%d' % bx])
                            CP('dve', v3(tmj['X'][:C, :]), v3(ps[bx][:C, 0:384]), ['ps%d' % bx], ['tm_X'])
                        for i in range(1, nr + 1):
                            bn_, bm_ = nb(), nb()
                            yield
                            last = (i == nr)
                            yield
                            for h in range(6):
                                sl = slice(h * 64, h * 64 + C)
                                MM(ps[bn_][:C, sl], tmj['M'][:, sl], tmj['N'][:, sl], ['tm_M', 'tm_N'], ['ps%d' % bn_])
                                if not last:
                                    MM(ps[bm_][:C, sl], tmj['N'][:, sl], tmj['M'][:, sl], ['tm_M', 'tm_N'], ['ps%d' % bm_])
                            yield
                            if not last:
                                CP('act', v3(tmj['N'][:C, :]), v3(ps[bn_][:C, 0:384]), ['ps%d' % bn_], ['tm_N'])
                                CP('act', v3(tmj['M'][:C, :]), v3(ps[bm_][:C, 0:384]), ['ps%d' % bm_], ['tm_M'])
                            yield
                            if i > 1:
                                xstep()
                            yield
                            T('dve', v3(tmj['P'][:C, :]), v3(ps[bn_][:C, 0:384]), I6[:C, :C].unsqueeze(1).to_broadcast([C, 6, C]), ALU.add, ['ps%d' % bn_, 'cm'], ['tm_P'])
                            yield
                        yield
                        if nr >= 1:
                            xstep()
                        yield
                        if _RWSTOP <= 5:
                            return
                        yield
                        br = [nb(), nb()]
                        yield
                        for h in range(6):
                            c, hp = h // 2, h % 2
                            yield
                            pr_ = slice(64 * hp, 64 * hp + 64)
                            yield
                            MM(ps[br[hp]][:C, c * 64:(c + 1) * 64], ra['kt%d' % hp][:, c, :n], S0T[:, c, :], ['kt%d' % hp, 'S0T'], ['ps%d' % br[hp]], start=True, stop=False)
                            yield
                            MM(ps[br[hp]][:C, c * 64:(c + 1) * 64], tmj['AkkT'][:, h * 64:h * 64 + C], tmj['V'][:, h * 64:(h + 1) * 64],
                               ['tm_AkkT', 'tm_V'], ['ps%d' % br[hp]], start=False, stop=True)
                            yield
                        yield
                        for hp in range(2):
                            CP('act', hv(tmj['R'][:C, :], hp, 64), pv(br[hp], 64), ['ps%d' % br[hp]], ['tm_R'])
                        yield
                        bu = nb()
                        yield
                        for h in range(6):
                            MM(ps[bu][:C, h * 64:(h + 1) * 64], tmj['X'][:, h * 64:h * 64 + C], tmj['R'][:, h * 64:(h + 1) * 64],
                               ['tm_X', 'tm_R'], ['ps%d' % bu])
                        yield
                        CP('act', tmj['U'][:C, :], ps[bu][:C, 0:384], ['ps%d' % bu], ['tm_U'])
                        yield
                        if _RWSTOP <= 6:
                            return
                        yield
                        amat('kti', 'rt', ['kti', 'rt'], 'ArkT', MU)
                        yield
                        amat('nbt', 'rt', ['nbt', 'rt'], 'ArbT', MU)
                        yield
                        by = [nb(), nb()]
                        yield
                        for h in range(6):
                            c, hp = h // 2, h % 2
                            yield
                            pr_ = slice(64 * hp, 64 * hp + 64)
                            yield
                            hs = slice(h * 64, (h + 1) * 64)
                            yield
                            cs = slice(c * 64, (c + 1) * 64)
                            yield
                            MM(ps[by[hp]][:C, cs], ra['rt%d' % hp][:, c, :n], S0T[:, c, :], ['rt%d' % hp, 'S0T'], ['ps%d' % by[hp]], start=True, stop=False)
                            yield
                            MM(ps[by[hp]][:C, cs], tmj['ArkT'][:, h * 64:h * 64 + C], tmj['V'][:, hs], ['tm_ArkT', 'tm_V'], ['ps%d' % by[hp]], start=False, stop=False)
                            yield
                            MM(ps[by[hp]][:C, cs], tmj['ArbT'][:, h * 64:h * 64 + C], tmj['U'][:, hs], ['tm_ArbT', 'tm_U'], ['ps%d' % by[hp]], start=False, stop=True)
                            yield
                        yield
                        for hp in range(2):
                            CP('act', hv(tmj['Y'][:C, :], hp, 64), pv(by[hp], 64), ['ps%d' % by[hp]], ['tm_Y'])
                        yield
                        if _RWSTOP <= 7:
                            return
                        yield
                        for c in range(3):
                            bs_ = nb()
                            yield
                            for hp in range(2):
                                hs = slice((2 * c + hp) * 64, (2 * c + hp + 1) * 64)
                                MM(ps[bs_][:, hp * 64:(hp + 1) * 64], tmj['K'][:, c * 128:(c + 1) * 128], tmj['V'][:, hs], ['tm_K', 'tm_V'],
                                   ['ps%d' % bs_], start=True, stop=False)
                                MM(ps[bs_][:, hp * 64:(hp + 1) * 64], tmj['B'][:, c * 128:(c + 1) * 128], tmj['U'][:, hs], ['tm_B', 'tm_U'],
                                   ['ps%d' % bs_], start=False, stop=True)
                            yield
                            for hp in range(2):
                                pr_ = slice(64 * hp, 64 * hp + 64)
                                T('dve', stmp[pr_, :], ps[bs_][pr_, hp * 64:(hp + 1) * 64], S0T[pr_, c, :], ALU.add, ['ps%d' % bs_, 'S0T'], ['stmp'])
                                S('dve', S0T[pr_, c, :], stmp[pr_, :], gC[pr_, c:c + 1], None, ALU.mult, ALU.bypass, ['stmp', 'gC'], ['S0T'])
                            yield
                        yield
                        if _RWSTOP <= 8:
                            return
                        yield
                        bt = nb()
                        yield
                        for c in range(3):
                            TR(ps[bt][:, c * 128:(c + 1) * 128], tmj['Y'][:, c * 128:(c + 1) * 128], ident, ['tm_Y'], ['ps%d' % bt])
                        yield
                        CP('act', ra['y'][:, :, :n], ps[bt][:, 0:384].rearrange("p (a b) -> p a b", b=128)[:, :, :n], ['ps%d' % bt], ['y'])
                        yield
                    if samp:
                        yield from rw_direct()
                    else:
                        yield from rw_chunk()
                    bm1 = nb()
                    yield
                    for c in range(3):
                        MM(ps[bm1][:, c * 64:c * 64 + n], bones, ra['y'][:, c, :n], ['cst', 'y'], ['ps%d' % bm1])
                    yield
                    STT(ra['d'][:, :, :n], ps[bm1][:, 0:192].rearrange("p (a b) -> p a b", b=64)[:, :, :n], -1.0 / 64, ra['y'][:, :, :n],
                        ALU.mult, ALU.add, ['ps%d' % bm1, 'y'], ['d'])
                    yield
                    T('dve', ra['t2'][:, :, :n], ra['d'][:, :, :n], ra['d'][:, :, :n], ALU.mult, ['d'], ['t2'])
                    yield
                    bm2 = nb()
                    yield
                    for c in range(3):
                        MM(ps[bm2][:, c * 64:c * 64 + n], bones, ra['t2'][:, c, :n], ['cst', 't2'], ['ps%d' % bm2])
                    yield
                    S('dve', ra['t2'][:, :, :n], ps[bm2][:, 0:192].rearrange("p (a b) -> p a b", b=64)[:, :, :n], 1.0 / 64, 64e-5,
                      ALU.mult, ALU.add, ['ps%d' % bm2], ['t2'])
                    yield
                    A(ra['t2'][:, :, :n], ra['t2'][:, :, :n], AF.Sqrt, ['t2'], ['t2'])
                    yield
                    em.op('dve', lambda e: e.reciprocal(out=ra['t2'][:, :, :n], in_=ra['t2'][:, :, :n]), reads=['t2'], writes=['t2'])
                    yield
                    T('dve', ra['d'][:, :, :n], ra['d'][:, :, :n], ra['t2'][:, :, :n], ALU.mult, ['d', 't2'], ['d'])
                    yield
                    for c in range(3):
                        S('dve', ra['d'][:, c, :n], ra['d'][:, c, :n], rp[:, c, 5:6], rp[:, c, 6:7], ALU.mult, ALU.add, ['d'] + RP, ['d'])
                    yield
                    T('dve', ra['d'][:, :, :n], ra['d'][:, :, :n], ra['bon'][:, :, :n], ALU.add, ['d', 'bon'], ['d'])
                    yield
                    T('dve', omix[:, 3:6, :n], ra['d'][:, :, :n], ra['g'][:, :, :n], ALU.mult, ['d', 'g'], ['omix_r'])
                    yield

                def do_tile(c0, n, samp=False):
                    A(sqb[:, :, :n], xres[:, :, c0:c0 + n], AF.Square, ['xres'], ['sqb'])
                    b = nb()
                    for kc in range(KC):
                        MM(ps[b][:, :n], ones_bf[:], sqb[:, kc, :n], ['ones_bf', 'sqb'], ['ps%d' % b], start=(kc == 0), stop=(kc == KC - 1))
                    S('dve', rstb[:, :n], ps[b][:, :n], 1.0 / D, EPS, ALU.mult, ALU.add, ['ps%d' % b], ['rstb'])
                    A(rstb[:, :n], rstb[:, :n], AF.Sqrt, ['rstb'], ['rstb'])
                    em.op('dve', lambda e: e.reciprocal(out=rstb[:, :n], in_=rstb[:, :n]), reads=['rstb'], writes=['rstb'])
                    if 's5' in mixers:
                        T('dve', st1[:, 0:KC, :n], xres[:, :, c0:c0 + n], normw[:, 2 + l, :].unsqueeze(2).to_broadcast([128, KC, n]), ALU.mult,
                          ['xres', 'normw'], ['st1'])
                        T('dve', xnb[:, :, :n], st1[:, 0:KC, :n], rstb[:, :n].unsqueeze(1).to_broadcast([128, KC, n]), ALU.mult,
                          ['st1', 'rstb'], ['xnb'])
                    else:
                        for kc in range(KC):
                            STT(xnb[:, kc, :n], xres[:, kc, c0:c0 + n], normw[:, 2 + l, kc:kc + 1], rstb[:, :n], ALU.mult, ALU.mult,
                                ['xres', 'normw', 'rstb'], ['xnb'])
                    for f0 in (0, 8, 16):
                        b = nb()
                        nf = min(8, 18 - f0)
                        for fi in range(nf):
                            fc = f0 + fi
                            for kc in range(KC):
                                o = kc * DIN + fc * 128
                                MM(ps[b][:, fi * 64:fi * 64 + n], ring[:, o:o + 128], xnb[:, kc, :n], RIN + ['xnb'], ['ps%d' % b],
                                   start=(kc == 0), stop=(kc == KC - 1))
                        CP('act', pj[:, f0:f0 + nf, 3:3 + n], ps[b][:, 0:nf * 64].rearrange("p (a b) -> p a b", b=64)[:, :, :n],
                           ['ps%d' % b], ['pj'])
                    if len(mixers) < 3:
                        em.op('pool', lambda e: e.memset(omix[:, :, :n], 0.0), writes=['omix_s', 'omix_r', 'omix_l'])

                    def run_rr(gens):
                        while gens:
                            for g_ in list(gens):
                                try:
                                    next(g_)
                                except StopIteration:
                                    gens.remove(g_)
                    if samp or len(mixers) < 3:
                        if 's5' in mixers:
                            run_rr([s5_tile(n, samp)])
                        gens = []
                        if 'rwkv' in mixers:
                            gens.append(rwkv_tile(n, samp))
                        if 'lru' in mixers:
                            gens.append(lru_tile(n, samp))
                        run_rr(gens)
                    else:
                        gR, gL = rwkv_tile(n, samp), lru_tile(n, samp)
                        l_done = False
                        while True:
                            if next(gR) == 'PHASE':
                                break
                            if not l_done:
                                try:
                                    next(gL)
                                except StopIteration:
                                    l_done = True
                        gens = [gR, s5_tile(n, samp)] + ([] if l_done else [gL])
                        run_rr(gens)
                    b = nb()
                    for mc in range(KC):
                        for kc in range(KC):
                            o = 18432 + kc * D + mc * 128
                            MM(ps[b][:, mc * 64:mc * 64 + n], ring[:, o:o + 128], omix[:, kc, :n], ROUT + ['omix_s', 'omix_r', 'omix_l'], ['ps%d' % b],
                               start=(kc == 0), stop=(kc == KC - 1))
                    T('dve', xres[:, :, c0:c0 + n], xres[:, :, c0:c0 + n], ps[b][:, :].rearrange("p (a b) -> p a b", b=64)[:, :, :n], ALU.add,
                      ['xres', 'ps%d' % b], ['xres'])
                    if not samp:
                        CP('dve', tmpc[:], pj[:, :, n:n + 3], ['pj'], ['tmpc'])
                        CP('dve', pj[:, :, 0:3], tmpc[:], ['tmpc'], ['pj'])

                def store_state(si):
                    em.dma('sp', o_s5[l, si], hst[:], 'st_a', reads=['hst'])
                    em.dma('sp', o_rwkv[l, si], S0T[:], 'st_b', reads=['S0T'])
                    em.dma('sp', o_shift[l, si], pj[:, 3:14, 2], 'st_c', reads=['pj'])
                    em.dma('sp', o_lru[l, si], lru_h[:], 'st_d', reads=['lru_h'])
                    em.dma('sp', o_conv[l, si], pj[:, 14:16, 0:3], 'st_e', reads=['pj'])

                em.op('pool', lambda e: e.memset(pj[:], 0.0), writes=['pj'])
                em.op('pool', lambda e: e.memset(hst[:], 0.0), writes=['hst'])
                em.op('pool', lambda e: e.memset(S0T[:], 0.0), writes=['S0T'])
                em.op('pool', lambda e: e.memset(lru_h[:], 0.0), writes=['lru_h'])
                if 'rwkv' in mixers:
                    zero_tmj()
                do_tile(0, 16)
                for i in range(PT // NB):
                    do_tile(32 + NB * i, NB)
                store_state(0)
                if not _NOSAMP:
                    em.dma('sp', s5st[:], i_s5[l].rearrange("b p g r -> p b g r"), 'ld_a', writes=['s5st'])
                    em.dma('sp', sh16[:], i_shift[l].rearrange("b p c -> p b c"), 'ld_cc', writes=['sh16'])
                    em.dma('sp', lst[:], i_lru[l].rearrange("b p c -> p b c"), 'ld_d', writes=['lst'])
                    em.dma('sp', cst16[:], i_conv[l].rearrange("b p c j -> p b c j"), 'ld_e', writes=['cst16'])
                    do_tile(16, 16, True)
                    CP('dve', prevv, pj[:, 3:14, 3:19], ['pj'], ['sh16'])
                    em.dma('sp', o_s5[l, 1:17].rearrange("b p g r -> p b g r"), s5st[:], 'st_a', reads=['s5st'])
                    em.dma('sp', o_shift[l, 1:17].rearrange("b p c -> p b c"), sh16[:], 'st_c', reads=['sh16'])
                    em.dma('sp', o_lru[l, 1:17].rearrange("b p c -> p b c"), lst[:], 'st_d', reads=['lst'])
                    em.dma('sp', o_conv[l, 1:17].rearrange("b p c j -> p b c j"), cst16[:], 'st_e', reads=['cst16'])
                em.barrier()

        for l in range(L):
            if stage >= 1:
                ffn(l, 0)
            if stage >= 2 or stage == -2:
                mixer_phase(l)
            if stage >= 1:
                ffn(l, 1)

        fs = ExitStack()
        with fs:
            sq = sbuf(fs, "sqF", [128, KC, 512], BF16)
            rstd = sbuf(fs, "rstdF", [128, 512])
            for ti in range(NT):
                c0, n = tiles[ti]
                A(sq[:, :, :n], xres[:, :, c0:c0 + n], AF.Square, ['xres'], ['sq'])
                for kc in range(KC):
                    MM(ps[7][:, :n], ones_bf[:], sq[:, kc, :n], ['ones_bf', 'sq'], ['ps7'], start=(kc == 0), stop=(kc == KC - 1))
                S('dve', rstd[:, :n], ps[7][:, :n], 1.0 / D, EPS, ALU.mult, ALU.add, ['ps7'], ['rstd'])
                A(rstd[:, :n], rstd[:, :n], AF.Sqrt, ['rstd'], ['rstd'])
                em.op('dve', lambda e: e.reciprocal(out=rstd[:, :n], in_=rstd[:, :n]), reads=['rstd'], writes=['rstd'])
                for kc in range(KC):
                    STT(xres[:, kc, c0:c0 + n], xres[:, kc, c0:c0 + n], normw[:, 6, kc:kc + 1], rstd[:, :n], ALU.mult, ALU.mult,
                        ['xres', 'rstd', 'normw'], ['xres'])
                    em.dma('sp', yT[:, kc, c0:c0 + n], xres[:, kc, c0:c0 + n], 'st_y%d' % kc, reads=['xres'])
            em.finish('sp')
        print("instructions:", em.ninstr, {k: v for k, v in em.cnt.items()})
    return nc


def prep_common(inp):
    f = np.float32
    g = lambda k: np.asarray(inp[k], f)
    c = {}
    wf = np.empty((L, 2, NJ, 128, SLOT), f)
    for l in range(L):
        for fi, pre in enumerate(("ffn1", "ffn2")):
            wg = g(pre + "_w_gate")[l].reshape(KC, 128, NJ, 128)
            wu = g(pre + "_w_up")[l].reshape(KC, 128, NJ, 128)
            wd = g(pre + "_w_down")[l].reshape(NJ, 128, D)
            wf[l, fi, :, :, 0:1024] = wg.transpose(2, 1, 0, 3).reshape(NJ, 128, 1024)
            wf[l, fi, :, :, 1024:2048] = wu.transpose(2, 1, 0, 3).reshape(NJ, 128, 1024)
            wf[l, fi, :, :, 2048:3072] = wd
    c["wffn"] = wf
    c["w_in"] = np.ascontiguousarray(g("w_in").reshape(L, KC, 128, DIN).transpose(0, 2, 1, 3)).reshape(L, 128, KC * DIN)
    c["w_out"] = np.ascontiguousarray(g("w_out").reshape(L, KC, 128, D).transpose(0, 2, 1, 3)).reshape(L, 128, KC * D)
    nv = [g("ffn1_norm")[0], g("ffn1_norm")[1], g("mix_norm")[0], g("mix_norm")[1], g("ffn2_norm")[0], g("ffn2_norm")[1], g("final_norm")]
    c["norms"] = np.ascontiguousarray(np.stack([v.reshape(KC, 128) for v in nv], 0).transpose(2, 0, 1))
    p = np.arange(128)
    cs = np.zeros((128, 400), f)
    cs[:, 0:128] = np.eye(128)
    cs[:, 128:256] = (p[:, None] // 64 == p[None, :] // 64)
    cs[:, 256:320] = np.arange(1, 65)[None, :]
    for q in range(4):
        for gg in range(2):
            cs[:, 320 + 2 * q + gg] = (p >= 32 * q + 16 * gg) & (p < 32 * q + 16 * gg + 16)
    for gg in range(2):
        cs[:, 328 + gg] = (p // 64 == gg)
    cs[:, 330:394] = (p[:, None] % 64 == np.arange(64)[None, :])
    c["consts"] = cs
    s_ = np.arange(64)
    m4 = np.stack([s_[:, None] < s_[None, :], s_[:, None] <= s_[None, :], s_[:, None] > s_[None, :], s_[:, None] == s_[None, :]], 0).astype(f)
    c["cmask"] = np.ascontiguousarray(m4.transpose(1, 0, 2))
    lr, li, ldt = g("s5_lambda_re"), g("s5_lambda_im"), g("s5_log_dt")
    tri = np.stack([lr, li, np.broadcast_to(ldt[:, :, None], lr.shape)], -1)
    c["s5A"] = np.ascontiguousarray(tri.reshape(L, 12, 2, 64, 3).transpose(0, 2, 3, 1, 4).reshape(L, 128, 12, 3))
    tB = np.broadcast_to(tri.reshape(L, 3, 8, 1, 64, 3), (L, 3, 8, 16, 64, 3))
    c["s5B"] = np.ascontiguousarray(tB.transpose(0, 2, 3, 1, 4, 5).reshape(L, 128, 3, 64, 3))
    bb = np.stack([g("s5_b_re"), g("s5_b_im")], -1)
    c["s5bT"] = np.ascontiguousarray(bb.reshape(L, 3, 8, 64, 16, 2).transpose(0, 2, 4, 1, 3, 5).reshape(L, 128, 3, 64, 2))
    cc = np.stack([g("s5_c_re"), g("s5_c_im")], -1)
    c["s5cT"] = np.ascontiguousarray(cc.reshape(L, 12, 2, 16, 64, 2).transpose(0, 2, 4, 1, 3, 5).reshape(L, 128, 12, 16, 2))
    c["s5v"] = np.ascontiguousarray(np.stack([g("s5_d"), g("s5_glu_b")], -1).reshape(L, 3, 128, 2).transpose(0, 2, 1, 3))
    c["s5glu"] = np.ascontiguousarray(g("s5_glu_w").reshape(L, 3, 128, 384).transpose(0, 2, 1, 3))
    z384 = np.zeros((L, 384), f)
    rw = np.stack([g("rwkv_w0"), g("rwkv_a0"), g("rwkv_k_k"), g("rwkv_k_a"), g("rwkv_r_k").reshape(L, 384), g("rwkv_ln_w"), g("rwkv_ln_b"), z384], -1)
    c["rwp"] = np.ascontiguousarray(rw.reshape(L, 3, 128, 8).transpose(0, 2, 1, 3))
    c["rwmu"] = np.ascontiguousarray(g("rwkv_mu").reshape(L, 11, 128).transpose(0, 2, 1))
    c["rwup"] = np.ascontiguousarray(np.stack([np.concatenate([g("rwkv_w_up"), g("rwkv_a_up")], 1), g("rwkv_g_up")], 2))
    lp_ = np.concatenate([g("lru_conv_w").transpose(0, 2, 1), g("lru_conv_b")[..., None], g("lru_b_a")[..., None], g("lru_b_x")[..., None],
                          g("lru_lambda")[..., None]], -1)
    c["lrp"] = np.ascontiguousarray(lp_.reshape(L, 2, 128, 8).transpose(0, 2, 1, 3))
    lw_ = np.zeros((L, 128, 2, 2, 128), f)
    for wi, nm in enumerate(("lru_w_a", "lru_w_x")):
        w = g(nm)
        for cb in range(2):
            for bq in range(2):
                lw_[:, 64 * bq:64 * bq + 64, cb, wi, 64 * bq:64 * bq + 64] = w[:, 2 * cb + bq]
    c["lrw"] = lw_
    return c


def prep_core(inp, ci, PT=2048):
    f = np.float32
    g = lambda k: np.asarray(inp[k], f)
    meta = g("meta_tokens")
    xs = g("x_sample")[16 * ci:16 * ci + 16, 0]
    xp = g("x_prompt")[ci][:PT]
    cols = np.concatenate([meta, xs, xp], 0)
    m = {"xT": np.ascontiguousarray(cols.T.reshape(KC, 128, 32 + PT).transpose(1, 0, 2))}
    sl = slice(16 * ci, 16 * ci + 16)
    st = np.stack([g("state_s5_re")[:, sl], g("state_s5_im")[:, sl]], -1)
    m["i_s5"] = np.ascontiguousarray(st.reshape(L, 16, 12, 2, 64, 2).transpose(0, 1, 3, 4, 2, 5).reshape(L, 16, 128, 12, 2))
    rs = g("state_rwkv")[:, sl]
    m["i_rwkv"] = np.ascontiguousarray(rs.reshape(L, 16, 3, 2, 64, 64).transpose(0, 1, 3, 5, 2, 4).reshape(L, 16, 128, 3, 64))
    m["i_shift"] = np.ascontiguousarray(g("state_rwkv_shift")[:, sl].reshape(L, 16, 11, 128).transpose(0, 1, 3, 2))
    m["i_lru"] = np.ascontiguousarray(g("state_lru")[:, sl].reshape(L, 16, 2, 128).transpose(0, 1, 3, 2))
    m["i_conv"] = np.ascontiguousarray(g("state_lru_conv")[:, sl].reshape(L, 16, 3, 2, 128).transpose(0, 1, 4, 3, 2))
    return m


def unpack_core(r):
    o = {}
    s5 = r["o_s5"].reshape(L, NSEQ, 2, 64, 12, 2).transpose(0, 1, 4, 2, 3, 5).reshape(L, NSEQ, 24, 64, 2)
    o["s5_re"], o["s5_im"] = s5[..., 0], s5[..., 1]
    o["rwkv"] = r["o_rwkv"].reshape(L, NSEQ, 2, 64, 3, 64).transpose(0, 1, 4, 2, 5, 3).reshape(L, NSEQ, 6, 64, 64)
    o["shift"] = r["o_shift"].transpose(0, 1, 3, 2).reshape(L, NSEQ, 1408)
    o["lru"] = r["o_lru"].transpose(0, 1, 3, 2).reshape(L, NSEQ, 256)
    o["conv"] = r["o_conv"].transpose(0, 1, 4, 3, 2).reshape(L, NSEQ, 3, 256)
    return o


_NC_CACHE = {}


def kernel(**inp):
    if "nc" not in _NC_CACHE:
        _NC_CACHE["nc"] = build()
    nc = _NC_CACHE["nc"]
    common = prep_common(inp)
    in_maps = []
    for ci in range(NCORES):
        m = dict(common)
        m.update(prep_core(inp, ci))
        in_maps.append(m)
    res = run_bass_kernel_spmd(nc, in_maps, core_ids=list(range(NCORES)))
    outs = res.results
    f = np.float32
    y_prompt = np.empty((8, 2048, D), f)
    y_sample = np.empty((128, 1, D), f)
    keys = ("s5_re", "s5_im", "rwkv", "shift", "lru", "conv")
    shp = {"s5_re": (24, 64), "s5_im": (24, 64), "rwkv": (6, 64, 64), "shift": (1408,), "lru": (256,), "conv": (3, 256)}
    P = {k: np.empty((L, 8) + shp[k], f) for k in keys}
    Sx = {k: np.empty((L, 128) + shp[k], f) for k in keys}
    for ci in range(NCORES):
        y = outs[ci]["yT"].transpose(1, 0, 2).reshape(D, 2080).T
        y_prompt[ci] = y[32:]
        y_sample[16 * ci:16 * ci + 16, 0] = y[16:32]
        o = unpack_core(outs[ci])
        for k in keys:
            P[k][:, ci] = o[k][:, 0]
            Sx[k][:, 16 * ci:16 * ci + 16] = o[k][:, 1:]
    return (y_prompt, y_sample) + tuple(P[k] for k in keys) + tuple(Sx[k] for k in keys)
```

```python
import numpy as np
import os
_RWSTOP = int(os.environ.get('RW_STOP', '99'))
_NOSAMP = int(os.environ.get('NO_SAMP', '0'))
from contextlib import ExitStack
import concourse.bass as bass
import concourse.mybir as mybir
from concourse.bass_utils import run_bass_kernel_spmd

F32 = mybir.dt.float32
BF16 = mybir.dt.bfloat16
ALU = mybir.AluOpType
AF = mybir.ActivationFunctionType
AX = mybir.AxisListType

NCORES = 8
D = 1024
KC = 8
DFF = 2816
NJ = 22
DIN = 2304
L = 2
NCOL = 2080
SLOT = 3072
NSLOT = 9
GRP = 4
EPS = 1e-6
TILES = [(0, 32)] + [(32 + 512 * i, 512) for i in range(4)]


class Em:
    ENG = ('pe', 'dve', 'act', 'pool', 'sp')

    def __init__(self, nc, es):
        self.nc = nc
        self.es = es
        self.e = dict(pe=nc.tensor, dve=nc.vector, act=nc.scalar, pool=nc.gpsimd, sp=nc.sync)
        self.sem = {k: es.enter_context(nc.semaphore("s_" + k)) for k in ('pe', 'dve', 'act', 'pool')}
        self.cnt = {k: 0 for k in self.sem}
        self.waited = {k: {} for k in self.ENG}
        self.buf = {}
        self.dsem = {}
        self.ninstr = 0
        self.alias = {}

    def _handle(self, sk):
        return self.sem[sk] if sk in self.sem else self.dsem[sk][0]

    def _wait(self, eng, ev):
        if ev is None:
            return
        sk, val = ev
        if sk == 'pe' and eng == 'pe':
            return
        if self.waited[eng].get(sk, 0) >= val:
            return
        self.waited[eng][sk] = val
        self.e[eng].wait_ge(self._handle(sk), val)

    def deps(self, eng, reads, writes):
        reads = [self.alias.get(k, k) for k in reads]
        writes = [self.alias.get(k, k) for k in writes]
        for r in reads:
            b = self.buf.get(r)
            if b:
                self._wait(eng, b[0])
        for w in writes:
            b = self.buf.get(w)
            if b:
                self._wait(eng, b[0])
                for ev in b[1].values():
                    self._wait(eng, ev)

    def record(self, ev, reads, writes):
        reads = [self.alias.get(k, k) for k in reads]
        writes = [self.alias.get(k, k) for k in writes]
        for r in reads:
            b = self.buf.setdefault(r, [None, {}])
            b[1][ev[0]] = ev
        for w in writes:
            self.buf[w] = [ev, {}]

    def op(self, eng, fn, reads=(), writes=(), inc=True):
        psr = [k for k in reads if k.startswith('ps')]
        if psr:
            reads = [k for k in reads if not k.startswith('ps')]
            writes = list(writes) + psr
        self.deps(eng, reads, writes)
        ins = fn(self.e[eng])
        self.ninstr += 1
        if inc:
            self.cnt[eng] += 1
            ins.then_inc(self.sem[eng], 1)
            ev = (eng, self.cnt[eng])
        else:
            ev = (eng, self.cnt[eng] + 1)
        self.record(ev, reads, writes)
        return ins

    def dma(self, q, out, in_, name, reads=(), writes=(), **kw):
        if name not in self.dsem:
            self.dsem[name] = [self.es.enter_context(self.nc.semaphore("d_" + name)), 0]
        self.deps(q, reads, writes)
        d = self.dsem[name]
        d[1] += 16
        self.e[q].dma_start(out=out, in_=in_, **kw).then_inc(d[0], 16)
        self.ninstr += 1
        self.record((name, d[1]), reads, writes)

    def dma_batch(self, q, items, name):
        if name not in self.dsem:
            self.dsem[name] = [self.es.enter_context(self.nc.semaphore("d_" + name)), 0]
        d = self.dsem[name]
        for (out, in_, reads, writes) in items:
            self.deps(q, reads, writes)
        final = d[1] + 16 * len(items)
        for (out, in_, reads, writes) in items:
            d[1] += 16
            self.e[q].dma_start(out=out, in_=in_).then_inc(d[0], 16)
            self.ninstr += 1
            self.record((name, final), reads, writes)

    def barrier(self):
        for eng in self.ENG:
            for name, d in self.dsem.items():
                self._wait(eng, (name, d[1]))
            for k in self.sem:
                if self.cnt[k]:
                    self._wait(eng, (k, self.cnt[k]))

    def finish(self, eng='sp'):
        for name, d in self.dsem.items():
            self._wait(eng, (name, d[1]))
        for k in self.sem:
            if self.cnt[k]:
                self._wait(eng, (k, self.cnt[k]))


PI2 = 6.283185307179586
NB = 64
NSEQ = 17


def build(stage=99, PT=2048, mixers=('lru', 's5', 'rwkv')):
    NCOL_ = 32 + PT
    tiles = [(0, 32)] + [(32 + 512 * i, min(512, PT - 512 * i)) for i in range((PT + 511) // 512)]
    NT = len(tiles)
    nc = bass.Bass("TRN2", target_bir_lowering=False)
    es = ExitStack()
    with es:
        es.enter_context(nc.allow_non_contiguous_dma(reason="small strided state io"))
        em = Em(nc, es)

        def din(name, shape, dt=F32):
            return nc.dram_tensor(name, list(shape), dt, kind="ExternalInput").ap()

        def dout(name, shape, dt=F32):
            return nc.dram_tensor(name, list(shape), dt, kind="ExternalOutput").ap()

        def sbuf(stack, name, shape, dt=F32):
            return stack.enter_context(nc.sbuf_tensor(name, list(shape), dt))

        xT = din("xT", [128, KC, NCOL_])
        wffn = din("wffn", [L, 2, NJ, 128, SLOT])
        w_in = din("w_in", [L, 128, KC * DIN])
        w_out = din("w_out", [L, 128, KC * D])
        norms = din("norms", [128, 7, KC])
        consts = din("consts", [128, 400])
        cmask = din("cmask", [64, 4, 64])
        s5A = din("s5A", [L, 128, 12, 3])
        s5B = din("s5B", [L, 128, 3, 64, 3])
        s5bT = din("s5bT", [L, 128, 3, 64, 2])
        s5cT = din("s5cT", [L, 128, 12, 16, 2])
        s5v = din("s5v", [L, 128, 3, 2])
        s5glu = din("s5glu", [L, 128, 3, 384])
        rwp = din("rwp", [L, 128, 3, 8])
        rwmu = din("rwmu", [L, 128, 11])
        rwup = din("rwup", [L, 128, 2, 384])
        lrp = din("lrp", [L, 128, 2, 8])
        lrw = din("lrw", [L, 128, 2, 2, 128])
        i_s5 = din("i_s5", [L, 16, 128, 12, 2])
        i_rwkv = din("i_rwkv", [L, 16, 128, 3, 64])
        i_shift = din("i_shift", [L, 16, 128, 11])
        i_lru = din("i_lru", [L, 16, 128, 2])
        i_conv = din("i_conv", [L, 16, 128, 2, 3])
        yT = dout("yT", [128, KC, NCOL_])
        o_s5 = dout("o_s5", [L, NSEQ, 128, 12, 2])
        o_rwkv = dout("o_rwkv", [L, NSEQ, 128, 3, 64])
        o_shift = dout("o_shift", [L, NSEQ, 128, 11])
        o_lru = dout("o_lru", [L, NSEQ, 128, 2])
        o_conv = dout("o_conv", [L, NSEQ, 128, 2, 3])

        xres = sbuf(es, "xres", [128, KC, NCOL_])
        ring = sbuf(es, "ring", [128, NSLOT * SLOT], BF16)
        normw = sbuf(es, "normw", [128, 7, KC])
        ones_bf = sbuf(es, "ones_bf", [128, 128], BF16)
        cst = sbuf(es, "cst", [128, 400])
        ps = [es.enter_context(nc.psum_tensor("ps%d" % i, [128, 512], F32)) for i in range(8)]
        ident = cst[:, 0:128]
        bones = cst[:, 128:256]
        iota1 = cst[:, 256:320]
        mask8 = cst[:, 320:328]
        maskg2 = cst[:, 328:330]
        identb = cst[:, 330:394]

        em.op('pool', lambda e: e.memset(ones_bf[:], 1.0), writes=['ones_bf'])
        em.dma('sp', normw[:], norms, 'ld_c', writes=['normw'])
        em.dma('sp', cst[:], consts, 'ld_c2', writes=['cst'])
        em.dma_batch('sp', [(xres[:, kc, :], xT[:, kc, :], [], ['xres']) for kc in range(KC)], 'ld_x')

        def T(eng, out, a, b, op, R, W):
            em.op(eng, lambda e: e.tensor_tensor(out=out, in0=a, in1=b, op=op), reads=R, writes=W)

        def S(eng, out, a, s1, s2, op0, op1, R, W):
            em.op(eng, lambda e: e.tensor_scalar(out=out, in0=a, scalar1=s1, scalar2=s2, op0=op0, op1=op1), reads=R, writes=W)

        def STT(out, a, sc, b, op0, op1, R, W):
            em.op('dve', lambda e: e.scalar_tensor_tensor(out=out, in0=a, scalar=sc, in1=b, op0=op0, op1=op1), reads=R, writes=W)

        def A(out, in_, func, R, W, bias=0.0, scale=1.0):
            em.op('act', lambda e: e.activation(out=out, in_=in_, func=func, bias=bias, scale=scale), reads=R, writes=W)

        def CP(eng, out, in_, R, W):
            if eng == 'act':
                em.op('act', lambda e: e.copy(out=out, in_=in_), reads=R, writes=W)
            else:
                em.op(eng, lambda e: e.tensor_copy(out=out, in_=in_), reads=R, writes=W)

        def MM(out, lhsT, rhs, R, W, start=True, stop=True):
            em.op('pe', lambda e: e.matmul(out, lhsT=lhsT, rhs=rhs, start=start, stop=stop), reads=R, writes=W, inc=stop)

        def TR(out, in_, idn, R, W):
            em.op('pe', lambda e: e.transpose(out=out, in_=in_, identity=idn), reads=R + ['cst'], writes=W)

        bank = {'A': 0, 'B': 0}

        def nb(pool='B'):
            b = bank[pool]
            bank[pool] = (bank[pool] + 1) % 4
            return b + (0 if pool == 'A' else 4)

        def load_slot(s, src):
            dst = ring[:, s * SLOT:(s + 1) * SLOT]
            em.dma('pool', dst.rearrange("p (a b) -> p a b", b=1024), src.rearrange("p (a b) -> p a b", b=1024),
                   'ring%d' % s, writes=['ring%d' % s])

        def ffn(l, f):
            fs = ExitStack()
            with fs:
                xn = sbuf(fs, "xn%d%d" % (l, f), [128, KC, NCOL_], BF16)
                hbuf = sbuf(fs, "hbuf%d%d" % (l, f), [128, 2, GRP, 512], BF16)
                sq = sbuf(fs, "sq%d%d" % (l, f), [128, KC, 512], BF16)
                rstd = sbuf(fs, "rstd%d%d" % (l, f), [128, 512])
                silu = sbuf(fs, "silu%d%d" % (l, f), [128, 2, 512])
                nidx = (0 if f == 0 else 4) + l
                for ti in range(NT):
                    c0, n = tiles[ti]
                    A(sq[:, :, :n], xres[:, :, c0:c0 + n], AF.Square, ['xres'], ['sq'])
                    for kc in range(KC):
                        MM(ps[7][:, :n], ones_bf[:], sq[:, kc, :n], ['ones_bf', 'sq'], ['ps7'], start=(kc == 0), stop=(kc == KC - 1))
                    S('dve', rstd[:, :n], ps[7][:, :n], 1.0 / D, EPS, ALU.mult, ALU.add, ['ps7'], ['rstd'])
                    A(rstd[:, :n], rstd[:, :n], AF.Sqrt, ['rstd'], ['rstd'])
                    em.op('dve', lambda e: e.reciprocal(out=rstd[:, :n], in_=rstd[:, :n]), reads=['rstd'], writes=['rstd'])
                    for kc in range(KC):
                        STT(xn[:, kc, c0:c0 + n], xres[:, kc, c0:c0 + n], normw[:, nidx, kc:kc + 1], rstd[:, :n],
                            ALU.mult, ALU.mult, ['xres', 'rstd', 'normw'], ['xn%d' % ti])
                groups = [list(range(g, min(g + GRP, NJ))) for g in range(0, NJ, GRP)]
                slot_of = {}
                nxt = [0]

                def issue_group(gi):
                    for j in groups[gi]:
                        s_ = nxt[0] % 8
                        nxt[0] += 1
                        slot_of[j] = s_
                        load_slot(s_, wffn[l, f, j])

                issue_group(0)
                for gi, grp in enumerate(groups):
                    if gi + 1 < len(groups):
                        issue_group(gi + 1)

                    def gu(ti):
                        c0, n = tiles[ti]
                        hb = ti % 2
                        for ji, j in enumerate(grp):
                            s_ = slot_of[j]
                            for which in (0, 1):
                                b = 2 * (ji % 2) + which
                                for kc in range(KC):
                                    o = s_ * SLOT + which * 1024 + kc * 128
                                    MM(ps[b][:, :n], ring[:, o:o + 128], xn[:, kc, c0:c0 + n], ['ring%d' % s_, 'xn%d' % ti],
                                       ['ps%d' % b], start=(kc == 0), stop=(kc == KC - 1))
                            b = 2 * (ji % 2)
                            A(silu[:, ji % 2, :n], ps[b][:, :n], AF.Silu, ['ps%d' % b], ['silu%d' % (ji % 2)])
                            T('dve', hbuf[:, hb, ji, :n], silu[:, ji % 2, :n], ps[b + 1][:, :n], ALU.mult,
                              ['silu%d' % (ji % 2), 'ps%d' % (b + 1)], ['h%d_%d' % (hb, ji)])

                    def down(ti):
                        c0, n = tiles[ti]
                        hb = ti % 2
                        for mc in range(KC):
                            b = 4 + mc % 3
                            for ji, j in enumerate(grp):
                                o = slot_of[j] * SLOT + 2048 + mc * 128
                                MM(ps[b][:, :n], ring[:, o:o + 128], hbuf[:, hb, ji, :n], ['ring%d' % slot_of[j], 'h%d_%d' % (hb, ji)],
                                   ['ps%d' % b], start=(ji == 0), stop=(ji == len(grp) - 1))
                            STT(xres[:, mc, c0:c0 + n], ps[b][:, :n], 0.5, xres[:, mc, c0:c0 + n], ALU.mult, ALU.add,
                                ['ps%d' % b, 'xres'], ['xres'])

                    gu(0)
                    for ti in range(NT):
                        if ti + 1 < NT:
                            gu(ti + 1)
                        down(ti)
                em.barrier()

        def mixer_phase(l):
            bs = ExitStack()
            with bs:
                def sb(name, shape, dt=F32):
                    return sbuf(bs, name + "_%d" % l, shape, dt)
                for i in range(3):
                    em.dma('pool', ring[:, i * 6144:(i + 1) * 6144].rearrange("p (a b) -> p a b", b=2048),
                           w_in[l][:, i * 6144:(i + 1) * 6144].rearrange("p (a b) -> p a b", b=2048), 'win%d' % i,
                           writes=['ring%d' % (2 * i), 'ring%d' % (2 * i + 1)])
                em.dma('pool', ring[:, 18432:26624].rearrange("p (a b) -> p a b", b=2048),
                       w_out[l].rearrange("p (a b) -> p a b", b=2048), 'wout', writes=['ring6', 'ring7', 'ring8'])
                RIN = ['ring%d' % i for i in range(6)]
                ROUT = ['ring6', 'ring7', 'ring8']

                xnb = sb("xnb", [128, KC, NB], BF16)
                sqb = sb("sqb", [128, KC, NB], BF16)
                rstb = sb("rstb", [128, NB])
                pj = sb("pj", [128, 18, 3 + NB])
                tmpc = sb("tmpc", [128, 18, 3])
                omix = sb("omix", [128, KC, NB], BF16)
                g1_def = g2_def = None
                lg1 = sb("lg1", [128, 2, NB])
                lg2 = sb("lg2", [128, 2, NB])
                s5st = sb("s5st", [128, 16, 12, 2])
                lst = sb("lst", [128, 16, 2])
                cst16 = sb("cst16", [128, 16, 2, 3])
                sh16 = sb("sh16", [128, 16, 11])
                h0v = lambda ri: s5st[:, :, :, ri].rearrange("p b g -> p g b")
                lh0v = lst[:].rearrange("p b c -> p c b")
                convv = lambda j: cst16[:, :, :, j].rearrange("p b c -> p c b")
                prevv = sh16[:].rearrange("p b c -> p c b")
                cm = sb("cm", [64, 4, 64])
                em.dma('sp', cm[:], cmask, 'ld_cm', writes=['cm'])
                MSU, MU, MSL, I6 = cm[:, 0, :], cm[:, 1, :], cm[:, 2, :], cm[:, 3, :]

                def gelu(dst, src, k, n, R, W, g1=None, g2_=None, kk_='g'):
                    g1 = g1 if g1 is not None else g1_def
                    g2_ = g2_ if g2_ is not None else g2_def
                    k1, k2 = kk_ + '1', kk_ + '2'
                    A(g1[:, :k, :n], src, AF.Square, R, [k1])
                    S('dve', g1[:, :k, :n], g1[:, :k, :n], 0.044715, 1.0, ALU.mult, ALU.add, [k1], [k1])
                    T('dve', g1[:, :k, :n], g1[:, :k, :n], src, ALU.mult, [k1] + R, [k1])
                    A(g2_[:, :k, :n], g1[:, :k, :n], AF.Sigmoid, [k1], [k2], scale=1.5957691216057308)
                    T('dve', dst, g2_[:, :k, :n], src, ALU.mult, [k2] + R, W)

                lp = sb("lp", [128, 2, 8])
                lw = sb("lw", [128, 2, 2, 128])
                lcl = sb("lcl", [128, 2])
                em.dma('sp', lp[:], lrp[l], 'ld_lp', writes=['lp'])
                em.dma('sp', lw[:], lrw[l], 'ld_lw', writes=['lw'])
                A(lcl[:], lp[:, :, 7], AF.Sigmoid, ['lp'], ['lcl'])
                A(lcl[:], lcl[:], AF.Ln, ['lcl'], ['lcl'])
                S('dve', lcl[:], lcl[:], 8.0, None, ALU.mult, ALU.bypass, ['lcl'], ['lcl'])
                lru_h = sb("lru_h", [128, 2])
                lxc = sb("lxc", [128, 2, NB])
                lga = sb("lga", [128, 2, NB])
                lgx = sb("lgx", [128, 2, NB])
                la = sb("la", [128, 2, NB])
                lb = sb("lb", [128, 2, NB])
                lh = sb("lh", [128, 2, NB])

                def lru_tile(n, samp=False):
                    gelu(lg1[:, :, :n], pj[:, 16:18, 3:3 + n], 2, n, ['pj'], ['lg1'], lg1, lg2, 'lg')
                    yield
                    for c in range(2):
                        S('dve', lxc[:, c, :n], pj[:, 14 + c, 3:3 + n], lp[:, c, 3:4], lp[:, c, 4:5], ALU.mult, ALU.add,
                          ['pj', 'lp'], ['lxc'])
                        yield
                        for j in range(3):
                            STT(lxc[:, c, :n], (convv(j)[:, c, :] if samp else pj[:, 14 + c, j:j + n]), lp[:, c, j:j + 1], lxc[:, c, :n],
                                ALU.mult, ALU.add, ['pj', 'lp', 'lxc', 'cst16'], ['lxc'])
                        yield
                    yield
                    b = nb()
                    yield
                    for c in range(2):
                        MM(ps[b][:, c * 64:c * 64 + n], lw[:, c, 0, :], lxc[:, c, :n], ['lw', 'lxc'], ['ps%d' % b])
                        yield
                        MM(ps[b][:, 128 + c * 64:128 + c * 64 + n], lw[:, c, 1, :], lxc[:, c, :n], ['lw', 'lxc'], ['ps%d' % b])
                        yield
                    yield
                    for c in range(2):
                        A(lga[:, c, :n], ps[b][:, c * 64:c * 64 + n], AF.Sigmoid, ['ps%d' % b, 'lp'], ['lga'], bias=lp[:, c, 5:6])
                        yield
                        A(lgx[:, c, :n], ps[b][:, 128 + c * 64:128 + c * 64 + n], AF.Sigmoid, ['ps%d' % b, 'lp'], ['lgx'], bias=lp[:, c, 6:7])
                        yield
                        A(la[:, c, :n], lga[:, c, :n], AF.Exp, ['lga', 'lcl'], ['la'], scale=lcl[:, c:c + 1])
                        yield
                    yield
                    T('dve', lb[:, :, :n], la[:, :, :n], la[:, :, :n], ALU.mult, ['la'], ['lb'])
                    yield
                    S('dve', lb[:, :, :n], lb[:, :, :n], -1.0, 1.0, ALU.mult, ALU.add, ['lb'], ['lb'])
                    yield
                    A(lb[:, :, :n], lb[:, :, :n], AF.Sqrt, ['lb'], ['lb'])
                    yield
                    T('dve', lb[:, :, :n], lb[:, :, :n], lgx[:, :, :n], ALU.mult, ['lb', 'lgx'], ['lb'])
                    yield
                    T('dve', lb[:, :, :n], lb[:, :, :n], lxc[:, :, :n], ALU.mult, ['lb', 'lxc'], ['lb'])
                    yield
                    if samp:
                        T('dve', lh[:, :, :n], la[:, :, :n], lh0v, ALU.mult, ['la', 'lst'], ['lh'])
                        T('dve', lh[:, :, :n], lh[:, :, :n], lb[:, :, :n], ALU.add, ['lh', 'lb'], ['lh'])
                        CP('dve', lh0v, lh[:, :, :n], ['lh'], ['lst'])
                        CP('dve', convv(0), convv(1), ['cst16', 'lxc'], ['cst16'])
                        CP('dve', convv(1), convv(2), ['cst16'], ['cst16'])
                        CP('dve', convv(2), pj[:, 14:16, 3:3 + n], ['pj'], ['cst16'])
                    else:
                        for c in range(2):
                            em.op('dve', lambda e, c=c: e.tensor_tensor_scan(out=lh[:, c, :n], data0=la[:, c, :n], data1=lb[:, c, :n],
                                                                          initial=lru_h[:, c:c + 1], op0=ALU.mult, op1=ALU.add),
                                  reads=['la', 'lb', 'lru_h'], writes=['lh'])
                        CP('dve', lru_h[:], lh[:, :, n - 1], ['lh'], ['lru_h'])
                    yield
                    T('dve', omix[:, 6:8, :n], lh[:, :, :n], lg1[:, :, :n], ALU.mult, ['lh', 'lg1'], ['omix_l'])
                    yield

                hst = sb("hst", [128, 12, 2])
                if 's5' in mixers:
                    sv = sb("sv", [128, 3, 2])
                    sglu = sb("sglu", [128, 3, 384], BF16)
                    szb = sb("szb", [128, 3, NB], BF16)
                    lB = sb("lB", [128, 2, 12, 128], BF16)
                    lC = sb("lC", [128, 2, 12, 128], BF16)
                    rmag = sb("rmag", [128, 12])
                    Et = sb("Et", [128, 2, 12, NB])
                    abar = sb("abar", [128, 12, 2])
                    ss = ExitStack()
                    ss.__enter__()
                    sA = sbuf(ss, "sA_%d" % l, [128, 12, 3])
                    sBp = sbuf(ss, "sBp_%d" % l, [128, 3, 64, 3])
                    sbT = sbuf(ss, "sbT_%d" % l, [128, 3, 64, 2])
                    scT = sbuf(ss, "scT_%d" % l, [128, 12, 16, 2])
                    em.dma('sp', sA[:], s5A[l], 'ld_s5a', writes=['sA'])
                    em.dma('sp', sBp[:], s5B[l], 'ld_s5b', writes=['sBp'])
                    em.dma('sp', sbT[:], s5bT[l], 'ld_s5c', writes=['sbT'])
                    em.dma('sp', scT[:], s5cT[l], 'ld_s5d', writes=['scT'])
                    em.dma('sp', sv[:], s5v[l], 'ld_s5e', writes=['sv'])
                    em.dma('pool', sglu[:], s5glu[l], 'ld_s5f', writes=['sglu'])
                    wk = sbuf(ss, "wk_%d" % l, [128, 7, 64])
                    wk7 = sbuf(ss, "wk7_%d" % l, [128, 768])
                    wk8 = sbuf(ss, "wk8_%d" % l, [128, 768])
                    wki = sbuf(ss, "wki_%d" % l, [128, 768], mybir.dt.int32)

                    def rr(x, m, k, R, W):
                        tmp_ = wk[:, k, :m] if m <= 64 else wk8[:, :m]
                        CP('dve', wki[:, :m], x, R, ['wki'])
                        CP('dve', tmp_, wki[:, :m], ['wki'], ['wk%d' % k])
                        T('dve', x, x, tmp_, ALU.subtract, R + ['wk%d' % k], W)
                        S('dve', tmp_, x, 0.5, None, ALU.is_gt, ALU.bypass, W, ['wk%d' % k])
                        T('dve', x, x, tmp_, ALU.subtract, W + ['wk%d' % k], W)
                        S('dve', tmp_, x, -0.5, None, ALU.is_lt, ALU.bypass, W, ['wk%d' % k])
                        T('dve', x, x, tmp_, ALU.add, W + ['wk%d' % k], W)

                    def disc(lr, li, ldt, m, pre):
                        A(wk[:, 4, :m], ldt, AF.Exp, [pre], ['wk4'])
                        T('dve', wk[:, 0, :m], lr, wk[:, 4, :m], ALU.mult, [pre, 'wk4'], ['wk0'])
                        A(wk[:, 0, :m], wk[:, 0, :m], AF.Exp, ['wk0'], ['wk0'])
                        T('dve', wk[:, 3, :m], li, wk[:, 4, :m], ALU.mult, [pre, 'wk4'], ['wk3'])
                        S('dve', wk[:, 3, :m], wk[:, 3, :m], 1.0 / PI2, None, ALU.mult, ALU.bypass, ['wk3'], ['wk3'])
                        rr(wk[:, 3, :m], m, 5, ['wk3'], ['wk3'])
                        A(wk[:, 2, :m], wk[:, 3, :m], AF.Sin, ['wk3'], ['wk2'], scale=PI2 * 0.999999)
                        S('dve', wk[:, 6, :m], wk[:, 3, :m], 0.25, None, ALU.add, ALU.bypass, ['wk3'], ['wk6'])
                        rr(wk[:, 6, :m], m, 5, ['wk6'], ['wk6'])
                        A(wk[:, 1, :m], wk[:, 6, :m], AF.Sin, ['wk6'], ['wk1'], scale=PI2 * 0.999999)
                        T('dve', wk[:, 1, :m], wk[:, 1, :m], wk[:, 0, :m], ALU.mult, ['wk1', 'wk0'], ['wk1'])
                        T('dve', wk[:, 2, :m], wk[:, 2, :m], wk[:, 0, :m], ALU.mult, ['wk2', 'wk0'], ['wk2'])

                    disc(sA[:, :, 0], sA[:, :, 1], sA[:, :, 2], 12, 'sA')
                    CP('dve', abar[:, :, 0], wk[:, 1, :12], ['wk1'], ['abar'])
                    CP('dve', abar[:, :, 1], wk[:, 2, :12], ['wk2'], ['abar'])
                    CP('dve', rmag[:], wk[:, 0, :12], ['wk0'], ['Rt'])
                    wkE = wk7[:, :].rearrange("p (a b) -> p a b", b=NB)
                    T('dve', wkE, wk[:, 3, :12].unsqueeze(2).to_broadcast([128, 12, NB]),
                      iota1.unsqueeze(1).to_broadcast([128, 12, NB]), ALU.mult, ['wk3', 'cst'], ['wk7'])
                    rr(wk7[:, :], 768, 5, ['wk7'], ['wk7'])
                    A(Et[:, 1, :, :], wkE, AF.Sin, ['wk7'], ['Et'], scale=-PI2 * 0.999999)
                    S('dve', wk7[:, :], wk7[:, :], 0.25, None, ALU.add, ALU.bypass, ['wk7'], ['wk7'])
                    rr(wk7[:, :], 768, 5, ['wk7'], ['wk7'])
                    A(Et[:, 0, :, :], wkE, AF.Sin, ['wk7'], ['Et'], scale=PI2 * 0.999999)
                    for kc in range(3):
                        lr, li = sBp[:, kc, :, 0], sBp[:, kc, :, 1]
                        disc(lr, li, sBp[:, kc, :, 2], 64, 'sBp')
                        m = 64
                        T('dve', wk[:, 4, :m], lr, lr, ALU.mult, ['sBp'], ['wk4'])
                        T('dve', wk[:, 5, :m], li, li, ALU.mult, ['sBp'], ['wk5'])
                        T('dve', wk[:, 4, :m], wk[:, 4, :m], wk[:, 5, :m], ALU.add, ['wk4', 'wk5'], ['wk4'])
                        em.op('dve', lambda e: e.reciprocal(out=wk[:, 4, :64], in_=wk[:, 4, :64]), reads=['wk4'], writes=['wk4'])
                        S('dve', wk[:, 1, :m], wk[:, 1, :m], -1.0, None, ALU.add, ALU.bypass, ['wk1'], ['wk1'])
                        T('dve', wk[:, 5, :m], wk[:, 1, :m], lr, ALU.mult, ['wk1', 'sBp'], ['wk5'])
                        T('dve', wk[:, 6, :m], wk[:, 2, :m], li, ALU.mult, ['wk2', 'sBp'], ['wk6'])
                        T('dve', wk[:, 5, :m], wk[:, 5, :m], wk[:, 6, :m], ALU.add, ['wk5', 'wk6'], ['wk5'])
                        T('dve', wk[:, 5, :m], wk[:, 5, :m], wk[:, 4, :m], ALU.mult, ['wk5', 'wk4'], ['wk5'])
                        T('dve', wk[:, 6, :m], wk[:, 2, :m], lr, ALU.mult, ['wk2', 'sBp'], ['wk6'])
                        T('dve', wk[:, 0, :m], wk[:, 1, :m], li, ALU.mult, ['wk1', 'sBp'], ['wk0'])
                        T('dve', wk[:, 6, :m], wk[:, 6, :m], wk[:, 0, :m], ALU.subtract, ['wk6', 'wk0'], ['wk6'])
                        T('dve', wk[:, 6, :m], wk[:, 6, :m], wk[:, 4, :m], ALU.mult, ['wk6', 'wk4'], ['wk6'])
                        bre, bim = sbT[:, kc, :, 0], sbT[:, kc, :, 1]
                        T('dve', wk[:, 0, :m], wk[:, 5, :m], bre, ALU.mult, ['wk5', 'sbT'], ['wk0'])
                        T('dve', wk[:, 1, :m], wk[:, 6, :m], bim, ALU.mult, ['wk6', 'sbT'], ['wk1'])
                        T('dve', wk[:, 0, :m], wk[:, 0, :m], wk[:, 1, :m], ALU.subtract, ['wk0', 'wk1'], ['wk0'])
                        T('dve', wk[:, 1, :m], wk[:, 5, :m], bim, ALU.mult, ['wk5', 'sbT'], ['wk1'])
                        T('dve', wk[:, 2, :m], wk[:, 6, :m], bre, ALU.mult, ['wk6', 'sbT'], ['wk2'])
                        T('dve', wk[:, 1, :m], wk[:, 1, :m], wk[:, 2, :m], ALU.add, ['wk1', 'wk2'], ['wk1'])
                        for q in range(4):
                            for gg in range(2):
                                for ri in range(2):
                                    S('dve', lB[:, ri, 4 * kc + q, 64 * gg:64 * gg + 64], wk[:, ri, :64],
                                      mask8[:, 2 * q + gg:2 * q + gg + 1], None, ALU.mult, ALU.bypass, ['wk%d' % ri, 'cst'], ['lB'])
                    em.op('pool', lambda e: e.memset(lC[:], 0.0), writes=['lC'])
                    for gh in range(12):
                        q = gh % 4
                        for gg in range(2):
                            S('dve', lC[:, 0, gh, 32 * q + 16 * gg:32 * q + 16 * gg + 16], scT[:, gh, :, 0], maskg2[:, gg:gg + 1], None,
                              ALU.mult, ALU.bypass, ['scT', 'cst'], ['lC'])
                            S('dve', lC[:, 1, gh, 32 * q + 16 * gg:32 * q + 16 * gg + 16], scT[:, gh, :, 1], maskg2[:, gg:gg + 1], -1.0,
                              ALU.mult, ALU.mult, ['scT', 'cst'], ['lC'])
                    em.barrier()
                    ss.close()
                    sx = sb("sx", [128, 2, 12, NB])
                    sg = sb("sg", [128, 2, 12, NB])
                    sh = sx
                    em.alias['sh'] = 'sx'
                    shb = sb("shb", [128, 2, 12, NB], BF16)
                    ubf = sb("ubf", [128, 3, NB], BF16)
                    st1 = sb("st1", [128, 12, NB])
                    sy = sb("sy", [128, 3, NB])
                    sz = sb("sz", [128, 3, NB])

                def s5_tile(n, samp=False):
                    CP('act', ubf[:, :, :n], pj[:, 0:3, 3:3 + n], ['pj'], ['ubf'])
                    yield
                    bre = [nb('A'), nb('A')]
                    yield
                    bim = [nb('A'), nb('A')]
                    yield
                    for ri, bb in ((0, bre), (1, bim)):
                        for gh in range(12):
                            b = bb[gh // 8]
                            o = (gh % 8) * 64
                            MM(ps[b][:, o:o + n], lB[:, ri, gh, :], ubf[:, gh // 4, :n], ['lB', 'ubf'], ['ps%d' % b])
                    yield
                    if samp:
                        for half, (g0, g1_) in enumerate(((0, 8), (8, 12))):
                            k = g1_ - g0
                            pr = ps[bre[half]][:, 0:k * 64].rearrange("p (a b) -> p a b", b=64)[:, :, :n]
                            pi = ps[bim[half]][:, 0:k * 64].rearrange("p (a b) -> p a b", b=64)[:, :, :n]
                            ar = abar[:, g0:g1_, 0].unsqueeze(2).to_broadcast([128, k, n])
                            ai = abar[:, g0:g1_, 1].unsqueeze(2).to_broadcast([128, k, n])
                            h0r, h0i = h0v(0)[:, g0:g1_, :], h0v(1)[:, g0:g1_, :]
                            T('dve', sg[:, 0, g0:g1_, :n], h0r, ar, ALU.mult, ['s5st', 'abar'], ['sg'])
                            T('dve', st1[:, g0:g1_, :n], h0i, ai, ALU.mult, ['s5st', 'abar'], ['st1'])
                            T('dve', sg[:, 0, g0:g1_, :n], sg[:, 0, g0:g1_, :n], st1[:, g0:g1_, :n], ALU.subtract, ['sg', 'st1'], ['sg'])
                            T('dve', sx[:, 0, g0:g1_, :n], sg[:, 0, g0:g1_, :n], pr, ALU.add, ['sg', 'ps%d' % bre[half]], ['sx'])
                            T('dve', sg[:, 1, g0:g1_, :n], h0i, ar, ALU.mult, ['s5st', 'abar'], ['sg'])
                            T('dve', st1[:, g0:g1_, :n], h0r, ai, ALU.mult, ['s5st', 'abar'], ['st1'])
                            T('dve', sg[:, 1, g0:g1_, :n], sg[:, 1, g0:g1_, :n], st1[:, g0:g1_, :n], ALU.add, ['sg', 'st1'], ['sg'])
                            T('dve', sx[:, 1, g0:g1_, :n], sg[:, 1, g0:g1_, :n], pi, ALU.add, ['sg', 'ps%d' % bim[half]], ['sx'])
                        CP('dve', h0v(0), sx[:, 0, :, :n], ['sx'], ['s5st'])
                        CP('dve', h0v(1), sx[:, 1, :, :n], ['sx'], ['s5st'])
                    else:
                        for half, (g0, g1_) in enumerate(((0, 8), (8, 12))):
                            k = g1_ - g0
                            pr = ps[bre[half]][:, 0:k * 64].rearrange("p (a b) -> p a b", b=64)[:, :, :n]
                            pi = ps[bim[half]][:, 0:k * 64].rearrange("p (a b) -> p a b", b=64)[:, :, :n]
                            Rr, Ri = ['ps%d' % bre[half], 'Et'], ['ps%d' % bim[half], 'Et']
                            er, ei = Et[:, 0, g0:g1_, :n], Et[:, 1, g0:g1_, :n]
                            T('dve', sx[:, 0, g0:g1_, :n], pr, er, ALU.mult, Rr, ['sx'])
                            T('dve', st1[:, g0:g1_, :n], pi, ei, ALU.mult, Ri, ['st1'])
                            T('dve', sx[:, 0, g0:g1_, :n], sx[:, 0, g0:g1_, :n], st1[:, g0:g1_, :n], ALU.subtract, ['sx', 'st1'], ['sx'])
                            T('dve', sx[:, 1, g0:g1_, :n], pr, ei, ALU.mult, Rr, ['sx'])
                            T('dve', st1[:, g0:g1_, :n], pi, er, ALU.mult, Ri, ['st1'])
                            T('dve', sx[:, 1, g0:g1_, :n], sx[:, 1, g0:g1_, :n], st1[:, g0:g1_, :n], ALU.add, ['sx', 'st1'], ['sx'])
                        for ri in range(2):
                            for gh in range(12):
                                em.op('dve', lambda e, ri=ri, gh=gh: e.tensor_tensor_scan(
                                    out=sg[:, ri, gh, :n], data0=rmag[:, gh:gh + 1].to_broadcast([128, n]), data1=sx[:, ri, gh, :n],
                                    initial=hst[:, gh, ri:ri + 1], op0=ALU.mult, op1=ALU.add),
                                    reads=['Rt', 'sx', 'hst'], writes=['sg'])
                        er, ei = Et[:, 0, :, :n], Et[:, 1, :, :n]
                        T('dve', sh[:, 0, :, :n], sg[:, 0, :, :n], er, ALU.mult, ['sg', 'Et'], ['sh'])
                        T('dve', st1[:, :, :n], sg[:, 1, :, :n], ei, ALU.mult, ['sg', 'Et'], ['st1'])
                        T('dve', sh[:, 0, :, :n], sh[:, 0, :, :n], st1[:, :, :n], ALU.add, ['sh', 'st1'], ['sh'])
                        T('dve', sh[:, 1, :, :n], sg[:, 1, :, :n], er, ALU.mult, ['sg', 'Et'], ['sh'])
                        T('dve', st1[:, :, :n], sg[:, 0, :, :n], ei, ALU.mult, ['sg', 'Et'], ['st1'])
                        T('dve', sh[:, 1, :, :n], sh[:, 1, :, :n], st1[:, :, :n], ALU.subtract, ['sh', 'st1'], ['sh'])
                        CP('dve', hst[:, :, 0], sh[:, 0, :, n - 1], ['sh'], ['hst'])
                        CP('dve', hst[:, :, 1], sh[:, 1, :, n - 1], ['sh'], ['hst'])
                    yield
                    CP('act', shb[:, :, :, :n], sh[:, :, :, :n], ['sh'], ['shb'])
                    yield
                    b = nb('A')
                    yield
                    for kc in range(3):
                        for q in range(4):
                            gh = 4 * kc + q
                            MM(ps[b][:, kc * 64:kc * 64 + n], lC[:, 0, gh, :], shb[:, 0, gh, :n], ['lC', 'shb'], ['ps%d' % b],
                               start=(q == 0), stop=False)
                            MM(ps[b][:, kc * 64:kc * 64 + n], lC[:, 1, gh, :], shb[:, 1, gh, :n], ['lC', 'shb'], ['ps%d' % b],
                               start=False, stop=(q == 3))
                    yield
                    for kc in range(3):
                        STT(sy[:, kc, :n], pj[:, kc, 3:3 + n], sv[:, kc, 0:1], ps[b][:, kc * 64:kc * 64 + n], ALU.mult, ALU.add,
                            ['pj', 'sv', 'ps%d' % b], ['sy'])
                    yield
                    gelu(sz[:, :, :n], sy[:, :, :n], 3, n, ['sy'], ['sz'], st1[:, 0:3, :], st1[:, 3:6, :], 'st')
                    yield
                    CP('act', szb[:, :, :n], sz[:, :, :n], ['sz'], ['szb'])
                    yield
                    b = nb('A')
                    yield
                    for k2 in range(3):
                        for kc in range(3):
                            MM(ps[b][:, k2 * 64:k2 * 64 + n], sglu[:, kc, k2 * 128:(k2 + 1) * 128], szb[:, kc, :n], ['sglu', 'szb'],
                               ['ps%d' % b], start=(kc == 0), stop=(kc == 2))
                    yield
                    for k2 in range(3):
                        A(sy[:, k2, :n], ps[b][:, k2 * 64:k2 * 64 + n], AF.Sigmoid, ['ps%d' % b, 'sv'], ['sy'], bias=sv[:, k2, 1:2])
                    yield
                    T('dve', omix[:, 0:3, :n], sz[:, :, :n], sy[:, :, :n], ALU.mult, ['sz', 'sy'], ['omix_s'])
                    yield

                S0T = sb("S0T", [128, 3, 64])
                if 'rwkv' in mixers:
                    rp = sb("rp", [128, 3, 8])
                    rmu = sb("rmu", [128, 11])
                    rup = sb("rup", [128, 2, 384])
                    em.dma('sp', rp[:], rwp[l], 'ld_r1', writes=['rp'])
                    em.dma('sp', rmu[:], rwmu[l], 'ld_r2', writes=['rmu'])
                    em.dma('sp', rup[:], rwup[l], 'ld_r3', writes=['rup'])
                    ones64 = sb("ones64", [128, NB])
                    em.op('pool', lambda e: e.memset(ones64[:], 1.0), writes=['ones64'])
                    xm = sb("xm", [128, 11, NB])
                    early_ = ('sgw', 'a', 'kk', 'kp', 'b', 'lg', 'gam', 'gex', 'gin', 't1')
                    own_ = ('g', 'bon', 'rt0', 'rt1', 'kt0', 'kt1', 'kti', 'nbt')
                    rarr = {}
                    for i_, nm in enumerate(early_):
                        if 's5' in mixers:
                            base_ = sx if i_ < 8 else sg
                            j_ = i_ % 8
                            rarr[nm] = base_[:, j_ // 4, 3 * (j_ % 4):3 * (j_ % 4) + 3, :]
                            em.alias[nm] = 'sx' if i_ < 8 else 'sg'
                        else:
                            rarr[nm] = sb("r_" + nm, [128, 3, NB])
                    for nm in own_:
                        rarr[nm] = sb("r_" + nm, [128, 3, NB])
                    rarr['y'], rarr['d'], rarr['t2'] = rarr['kti'], rarr['nbt'], rarr['kt0']
                    em.alias.update({'y': 'kti', 'd': 'nbt', 't2': 'kt0'})
                    tw = sb("tw", [128, NB])
                    tw2 = sb("tw2", [128, NB])
                    twb = sb("twb", [128, NB])
                    em.op('pool', lambda e: e.memset(tw[:], 0.0), writes=['tw'])
                    em.op('pool', lambda e: e.memset(twb[:], 0.0), writes=['twb'])
                    gC = sb("gC", [128, 3])
                    tmj_all = sb("tmj_all", [128, 8 * 384])
                    TMN = ('V', 'K', 'B', 'AkkT', 'N', 'M', 'P', 'X')
                    TMK = ['tm_' + x_ for x_ in TMN]
                    tmj = {nm: tmj_all[:, i_ * 384:(i_ + 1) * 384] for i_, nm in enumerate(TMN)}

                    def zero_tmj():
                        for nm_ in ('V', 'K', 'B', 'AkkT', 'N', 'M', 'P', 'X'):
                            em.op('pool', lambda e, nm_=nm_: e.memset(tmj[nm_][:], 0.0), writes=['tm_' + nm_])
                    tmj['R'], tmj['U'], tmj['Y'] = tmj['N'], tmj['M'], tmj['P']
                    tmj['ArkT'], tmj['ArbT'] = tmj['AkkT'], tmj['X']
                    em.alias.update({'tm_R': 'tm_N', 'tm_U': 'tm_M', 'tm_Y': 'tm_P', 'tm_ArkT': 'tm_AkkT', 'tm_ArbT': 'tm_X'})
                    stmp = sb("stmp", [128, 64])

                def rwkv_tile(n, samp=False):
                    C = n
                    yield
                    ra = rarr
                    yield
                    RP = ['rp']
                    yield
                    T('dve', xm[:, :, :n], (prevv if samp else pj[:, 3:14, 2:2 + n]), pj[:, 3:14, 3:3 + n], ALU.subtract, ['pj', 'sh16'], ['xm'])
                    yield
                    T('dve', xm[:, :, :n], xm[:, :, :n], rmu[:, :].unsqueeze(2).to_broadcast([128, 11, n]), ALU.mult, ['xm', 'rmu'], ['xm'])
                    yield
                    T('dve', xm[:, :, :n], xm[:, :, :n], pj[:, 3:14, 3:3 + n], ALU.add, ['xm', 'pj'], ['xm'])
                    yield
                    r_, k_, v_ = xm[:, 0:3, :n], xm[:, 3:6, :n], xm[:, 6:9, :n]
                    yield
                    A(tw[0:64, :n], xm[0:64, 9, :n], AF.Tanh, ['xm'], ['tw'])
                    yield
                    CP('act', twb[64:128, :n], xm[64:128, 9, :n], ['xm'], ['twb'])
                    yield
                    T('dve', ra['kk'][:, :, :n], k_, rp[:, :, 2].unsqueeze(2).to_broadcast([128, 3, n]), ALU.mult, ['xm'] + RP, ['kk'])
                    yield
                    T('dve', ra['t1'][:, :, :n], ra['kk'][:, :, :n], ra['kk'][:, :, :n], ALU.mult, ['kk'], ['t1'])
                    yield
                    b0 = nb()
                    yield
                    for c in range(3):
                        MM(ps[b0][:, c * 64:c * 64 + n], rup[:, 0, c * 128:(c + 1) * 128], tw[:, :n], ['rup', 'tw'], ['ps%d' % b0])
                        yield
                        MM(ps[b0][:, 192 + c * 64:192 + c * 64 + n], rup[:, 0, c * 128:(c + 1) * 128], twb[:, :n],
                           ['rup', 'twb'], ['ps%d' % b0])
                        yield
                    yield
                    b2 = nb()
                    yield
                    for c in range(3):
                        MM(ps[b2][:, c * 64:c * 64 + n], bones, ra['t1'][:, c, :n], ['cst', 't1'], ['ps%d' % b2])
                    yield
                    for c in range(3):
                        A(ra['sgw'][:, c, :n], ps[b0][:, c * 64:c * 64 + n], AF.Sigmoid, ['ps%d' % b0] + RP, ['sgw'], bias=rp[:, c, 0:1])
                    yield
                    for c in range(3):
                        A(ra['a'][:, c, :n], ps[b0][:, 192 + c * 64:192 + c * 64 + n], AF.Sigmoid, ['ps%d' % b0] + RP, ['a'], bias=rp[:, c, 1:2])
                    yield
                    A(tw2[:, :n], xm[:, 10, :n], AF.Sigmoid, ['xm'], ['tw2'])
                    yield
                    S('dve', ra['sgw'][:, :, :n], ra['sgw'][:, :, :n], -0.6065306597126334, None, ALU.mult, ALU.bypass, ['sgw'], ['sgw'])
                    yield
                    if samp:
                        CP('dve', ra['lg'][:, :, :n], ra['sgw'][:, :, :n], ['sgw'], ['lg'])
                    else:
                        for c in range(3):
                            em.op('dve', lambda e, c=c: e.tensor_tensor_scan(out=ra['lg'][:, c, :n], data0=ones64[:, :n], data1=ra['sgw'][:, c, :n],
                                                                          initial=0.0, op0=ALU.mult, op1=ALU.add),
                                  reads=['ones64', 'sgw'], writes=['lg'])
                    yield
                    b1 = nb()
                    yield
                    for c in range(3):
                        MM(ps[b1][:, c * 64:c * 64 + n], rup[:, 1, c * 128:(c + 1) * 128], tw2[:, :n], ['rup', 'tw2'], ['ps%d' % b1])
                    yield
                    A(ra['gam'][:, :, :n], ra['lg'][:, :, :n], AF.Exp, ['lg'], ['gam'])
                    yield
                    A(ra['gin'][:, :, :n], ra['lg'][:, :, :n], AF.Exp, ['lg'], ['gin'], scale=-1.0)
                    yield
                    T('dve', ra['gex'][:, :, :n], ra['lg'][:, :, :n], ra['sgw'][:, :, :n], ALU.subtract, ['lg', 'sgw'], ['gex'])
                    yield
                    A(ra['gex'][:, :, :n], ra['gex'][:, :, :n], AF.Exp, ['gex'], ['gex'])
                    yield
                    STT(ra['kp'][:, :, :n], ra['a'][:, :, :n], -1.0, rp[:, :, 3].unsqueeze(2).to_broadcast([128, 3, n]), ALU.add, ALU.mult,
                        ['a'] + RP, ['kp'])
                    yield
                    STT(ra['kp'][:, :, :n], ra['kp'][:, :, :n], 1.0, k_, ALU.add, ALU.mult, ['kp', 'xm'], ['kp'])
                    yield
                    T('dve', ra['bon'][:, :, :n], r_, rp[:, :, 4].unsqueeze(2).to_broadcast([128, 3, n]), ALU.mult, ['xm'] + RP, ['bon'])
                    yield
                    T('dve', ra['bon'][:, :, :n], ra['bon'][:, :, :n], ra['kp'][:, :, :n], ALU.mult, ['bon', 'kp'], ['bon'])
                    yield
                    b3 = nb()
                    yield
                    for c in range(3):
                        MM(ps[b3][:, c * 64:c * 64 + n], bones, ra['bon'][:, c, :n], ['cst', 'bon'], ['ps%d' % b3])
                    yield
                    CP('act', ra['g'][:, :, :n], ps[b1][:, 0:192].rearrange("p (a b) -> p a b", b=64)[:, :, :n], ['ps%d' % b1], ['g'])
                    yield
                    p3 = ps[b2][:, 0:192].rearrange("p (a b) -> p a b", b=64)[:, :, :n]
                    yield
                    A(ra['t1'][:, :, :n], p3, AF.Sqrt, ['ps%d' % b2], ['t1'])
                    yield
                    S('dve', ra['t1'][:, :, :n], ra['t1'][:, :, :n], 1e-12, None, ALU.max, ALU.bypass, ['t1'], ['t1'])
                    yield
                    em.op('dve', lambda e: e.reciprocal(out=ra['t1'][:, :, :n], in_=ra['t1'][:, :, :n]), reads=['t1'], writes=['t1'])
                    yield
                    T('dve', ra['kk'][:, :, :n], ra['kk'][:, :, :n], ra['t1'][:, :, :n], ALU.mult, ['kk', 't1'], ['kk'])
                    yield
                    T('dve', ra['b'][:, :, :n], ra['kk'][:, :, :n], ra['a'][:, :, :n], ALU.mult, ['kk', 'a'], ['b'])
                    yield
                    T('dve', ra['bon'][:, :, :n], ps[b3][:, 0:192].rearrange("p (a b) -> p a b", b=64)[:, :, :n], v_, ALU.mult,
                      ['ps%d' % b3, 'xm'], ['bon'])
                    yield
                    CP('dve', gC[:], ra['gam'][:, :, n - 1], ['gam'], ['gC'])
                    yield
                    for hp in range(2):
                        STT(ra['rt%d' % hp][:, :, :n], r_, maskg2[:, hp:hp + 1], ra['gam'][:, :, :n], ALU.mult, ALU.mult,
                            ['xm', 'gam', 'cst'], ['rt%d' % hp])
                        yield
                        STT(ra['kt%d' % hp][:, :, :n], ra['kk'][:, :, :n], maskg2[:, hp:hp + 1], ra['gex'][:, :, :n], ALU.mult, ALU.mult,
                            ['kk', 'gex', 'cst'], ['kt%d' % hp])
                        yield
                    yield
                    T('dve', ra['kti'][:, :, :n], ra['kp'][:, :, :n], ra['gin'][:, :, :n], ALU.mult, ['kp', 'gin'], ['kti'])
                    yield
                    STT(ra['nbt'][:, :, :n], ra['b'][:, :, :n], -1.0, ra['gin'][:, :, :n], ALU.mult, ALU.mult, ['b', 'gin'], ['nbt'])
                    yield
                    if not samp:
                        yield 'PHASE'

                    def rw_direct():
                        Sb = tmj_all[:, 0:1024].rearrange("p (b v) -> p b v", v=64)
                        yield
                        tb = tmj_all[:, 1024:2048].rearrange("p (b v) -> p b v", v=64)
                        yield

                        def bc(ap2, lo=0, hi=16):
                            return ap2[:, lo:hi].unsqueeze(2).to_broadcast([128, hi - lo, 64])

                        def idb(k_):
                            return identb.unsqueeze(1).to_broadcast([128, k_, 64])

                        def bmm(src):
                            bb_ = (nb(), nb())
                            for hf in range(2):
                                MM(ps[bb_[hf]][:, :], bones, src[:, 8 * hf:8 * hf + 8, :], ['cst'] + TMK, ['ps%d' % bb_[hf]])
                            return bb_

                        def pview(b_):
                            return ps[b_][:, :].rearrange("p (b v) -> p b v", v=64)
                        for c in range(3):
                            em.dma('sp', Sb, i_rwkv[l, :, :, c, :].rearrange("b p v -> p b v"), 'ld_sb', writes=TMK)
                            yield
                            kap, w_, kp_, bq_ = ra['kk'][:, c, :n], ra['gam'][:, c, :n], ra['kp'][:, c, :n], ra['b'][:, c, :n]
                            yield
                            rq_, vq_ = xm[:, c, :n], xm[:, 6 + c, :n]
                            yield
                            T('dve', tb, Sb, bc(kap), ALU.mult, TMK + ['kk'], TMK)
                            yield
                            bb_ = bmm(tb)
                            yield
                            T('dve', Sb, Sb, bc(w_), ALU.mult, TMK + ['gam'], TMK)
                            yield
                            for hf in range(2):
                                T('dve', tb[:, 8 * hf:8 * hf + 8, :], pview(bb_[hf]), bc(bq_, 8 * hf, 8 * hf + 8), ALU.mult,
                                  ['ps%d' % bb_[hf], 'b'], TMK)
                            yield
                            T('dve', Sb, Sb, tb, ALU.subtract, TMK, TMK)
                            yield
                            T('dve', tb, idb(16), bc(vq_), ALU.mult, ['cst', 'xm'], TMK)
                            yield
                            bb_ = bmm(tb)
                            yield
                            for hf in range(2):
                                T('dve', tb[:, 8 * hf:8 * hf + 8, :], pview(bb_[hf]), bc(kp_, 8 * hf, 8 * hf + 8), ALU.mult,
                                  ['ps%d' % bb_[hf], 'kp'], TMK)
                            yield
                            T('dve', Sb, Sb, tb, ALU.add, TMK, TMK)
                            yield
                            T('dve', tb, Sb, bc(rq_), ALU.mult, TMK + ['xm'], TMK)
                            yield
                            bb_ = bmm(tb)
                            yield
                            for hf in range(2):
                                T('dve', tb[:, 8 * hf:8 * hf + 8, :], pview(bb_[hf]), idb(8), ALU.mult, ['ps%d' % bb_[hf], 'cst'], TMK)
                            yield
                            em.op('dve', lambda e, c=c: e.tensor_reduce(out=ra['y'][:, c, :n], in_=tb, axis=AX.X, op=ALU.add),
                                  reads=TMK, writes=['y'])
                            yield
                            em.dma('sp', o_rwkv[l, 1:17, :, c, :].rearrange("b p v -> p b v"), Sb, 'st_sb', reads=TMK)
                            yield
                        yield

                    def rw_chunk():
                        if _RWSTOP <= 2:
                            return
                        yield
                        for nm, src, RR in (('V', xm[:, 6:9, :n], ['xm']), ('K', ra['kti'][:, :, :n], ['kti']), ('B', ra['nbt'][:, :, :n], ['nbt'])):
                            bt = nb()
                            yield
                            for c in range(3):
                                TR(ps[bt][:C, c * 128:(c + 1) * 128], src[:, c, :], ident, RR, ['ps%d' % bt])
                            yield
                            CP('act', tmj[nm][:C, :], ps[bt][:C, 0:384], ['ps%d' % bt], ['tm_' + nm])
                            yield
                        yield
                        if _RWSTOP <= 3:
                            return
                        yield
                        def hv(ap, hp, w):
                            return ap.rearrange("p (c h k) -> p h c k", h=2, k=64)[:, hp, :, :w]

                        def pv(b, w):
                            return ps[b][:C, 0:192].rearrange("p (a b) -> p a b", b=64)[:, :, :w]

                        def amat(lk, rk, RR, dst, mask):
                            bb2 = [nb(), nb()]
                            for h in range(6):
                                c, hp = h // 2, h % 2
                                ln_ = lk + str(hp) if lk in ('kt', 'rt') else lk
                                rn_ = rk + str(hp) if rk in ('kt', 'rt') else rk
                                MM(ps[bb2[hp]][:C, c * 64:c * 64 + C], ra[ln_][:, c, :n], ra[rn_][:, c, :n],
                                   [ln_, rn_], ['ps%d' % bb2[hp]])
                            for hp in range(2):
                                T('dve', hv(tmj[dst][:C, :], hp, C), pv(bb2[hp], C), mask[:C, :C].unsqueeze(1).to_broadcast([C, 3, C]),
                                  ALU.mult, ['ps%d' % bb2[hp], 'cm'], ['tm_' + dst])
                        v3 = lambda ap: ap.rearrange("p (a b) -> p a b", b=64)[:, :, :C]
                        yield
                        amat('kti', 'kt', ['kti', 'kt'], 'AkkT', MSU)
                        yield
                        amat('nbt', 'kt', ['nbt', 'kt'], 'M', MSU)
                        yield
                        T('dve', v3(tmj['X'][:C, :]), v3(tmj['M'][:C, :]), I6[:C, :C].unsqueeze(1).to_broadcast([C, 6, C]), ALU.add, ['tm_M', 'cm'], ['tm_X'])
                        yield
                        amat('kt', 'nbt', ['nbt', 'kt'], 'N', MSL)
                        yield
                        if _RWSTOP <= 4:
                            return
                        yield
                        nr = 0
                        yield
                        while (1 << (nr + 1)) < C:
                            nr += 1
                        yield
                        def xstep():
                            bx = nb()
                            for h in range(6):
                                sl = slice(h * 64, h * 64 + C)
                                MM(ps[bx][:C, sl], tmj['P'][:, sl], tmj['X'][:, sl], ['tm_P', 'tm_X'], ['ps%d' % bx])
                            CP('dve', v3(tmj['X'][:C, :]), v3(ps[bx][:C, 0:384]), ['ps%d' % bx], ['tm_X'])
                        for i in range(1, nr + 1):
                            bn_, bm_ = nb(), nb()
                            yield
                            last = (i == nr)
                            yield
                            for h in range(6):
                                sl = slice(h * 64, h * 64 + C)
                                MM(ps[bn_][:C, sl], tmj['M'][:, sl], tmj['N'][:, sl], ['tm_M', 'tm_N'], ['ps%d' % bn_])
                                if not last:
                                    MM(ps[bm_][:C, sl], tmj['N'][:, sl], tmj['M'][:, sl], ['tm_M', 'tm_N'], ['ps%d' % bm_])
                            yield
                            if not last:
                                CP('act', v3(tmj['N'][:C, :]), v3(ps[bn_][:C, 0:384]), ['ps%d' % bn_], ['tm_N'])
                                CP('act', v3(tmj['M'][:C, :]), v3(ps[bm_][:C, 0:384]), ['ps%d' % bm_], ['tm_M'])
                            yield
                            if i > 1:
                                xstep()
                            yield
                            T('dve', v3(tmj['P'][:C, :]), v3(ps[bn_][:C, 0:384]), I6[:C, :C].unsqueeze(1).to_broadcast([C, 6, C]), ALU.add, ['ps%d' % bn_, 'cm'], ['tm_P'])
                            yield
                        yield
                        if nr >= 1:
                            xstep()
                        yield
                        if _RWSTOP <= 5:
                            return
                        yield
                        br = [nb(), nb()]
                        yield
                        for h in range(6):
                            c, hp = h // 2, h % 2
                            yield
                            pr_ = slice(64 * hp, 64 * hp + 64)
                            yield
                            MM(ps[br[hp]][:C, c * 64:(c + 1) * 64], ra['kt%d' % hp][:, c, :n], S0T[:, c, :], ['kt%d' % hp, 'S0T'], ['ps%d' % br[hp]], start=True, stop=False)
                            yield
                            MM(ps[br[hp]][:C, c * 64:(c + 1) * 64], tmj['AkkT'][:, h * 64:h * 64 + C], tmj['V'][:, h * 64:(h + 1) * 64],
                               ['tm_AkkT', 'tm_V'], ['ps%d' % br[hp]], start=False, stop=True)
                            yield
                        yield
                        for hp in range(2):
                            CP('act', hv(tmj['R'][:C, :], hp, 64), pv(br[hp], 64), ['ps%d' % br[hp]], ['tm_R'])
                        yield
                        bu = nb()
                        yield
                        for h in range(6):
                            MM(ps[bu][:C, h * 64:(h + 1) * 64], tmj['X'][:, h * 64:h * 64 + C], tmj['R'][:, h * 64:(h + 1) * 64],
                               ['tm_X', 'tm_R'], ['ps%d' % bu])
                        yield
                        CP('act', tmj['U'][:C, :], ps[bu][:C, 0:384], ['ps%d' % bu], ['tm_U'])
                        yield
                        if _RWSTOP <= 6:
                            return
                        yield
                        amat('kti', 'rt', ['kti', 'rt'], 'ArkT', MU)
                        yield
                        amat('nbt', 'rt', ['nbt', 'rt'], 'ArbT', MU)
                        yield
                        by = [nb(), nb()]
                        yield
                        for h in range(6):
                            c, hp = h // 2, h % 2
                            yield
                            pr_ = slice(64 * hp, 64 * hp + 64)
                            yield
                            hs = slice(h * 64, (h + 1) * 64)
                            yield
                            cs = slice(c * 64, (c + 1) * 64)
                            yield
                            MM(ps[by[hp]][:C, cs], ra['rt%d' % hp][:, c, :n], S0T[:, c, :], ['rt%d' % hp, 'S0T'], ['ps%d' % by[hp]], start=True, stop=False)
                            yield
                            MM(ps[by[hp]][:C, cs], tmj['ArkT'][:, h * 64:h * 64 + C], tmj['V'][:, hs], ['tm_ArkT', 'tm_V'], ['ps%d' % by[hp]], start=False, stop=False)
                            yield
                            MM(ps[by[hp]][:C, cs], tmj['ArbT'][:, h * 64:h * 64 + C], tmj['U'][:, hs], ['tm_ArbT', 'tm_U'], ['ps%d' % by[hp]], start=False, stop=True)
                            yield
                        yield
                        for hp in range(2):
                            CP('act', hv(tmj['Y'][:C, :], hp, 64), pv(by[hp], 64), ['ps%d' % by[hp]], ['tm_Y'])
                        yield
                        if _RWSTOP <= 7:
                            return
                        yield
                        for c in range(3):
                            bs_ = nb()
                            yield
                            for hp in range(2):
                                hs = slice((2 * c + hp) * 64, (2 * c + hp + 1) * 64)
                                MM(ps[bs_][:, hp * 64:(hp + 1) * 64], tmj['K'][:, c * 128:(c + 1) * 128], tmj['V'][:, hs], ['tm_K', 'tm_V'],
                                   ['ps%d' % bs_], start=True, stop=False)
                                MM(ps[bs_][:, hp * 64:(hp + 1) * 64], tmj['B'][:, c * 128:(c + 1) * 128], tmj['U'][:, hs], ['tm_B', 'tm_U'],
                                   ['ps%d' % bs_], start=False, stop=True)
                            yield
                            for hp in range(2):
                                pr_ = slice(64 * hp, 64 * hp + 64)
                                T('dve', stmp[pr_, :], ps[bs_][pr_, hp * 64:(hp + 1) * 64], S0T[pr_, c, :], ALU.add, ['ps%d' % bs_, 'S0T'], ['stmp'])
                                S('dve', S0T[pr_, c, :], stmp[pr_, :], gC[pr_, c:c + 1], None, ALU.mult, ALU.bypass, ['stmp', 'gC'], ['S0T'])
                            yield
                        yield
                        if _RWSTOP <= 8:
                            return
                        yield
                        bt = nb()
                        yield
                        for c in range(3):
                            TR(ps[bt][:, c * 128:(c + 1) * 128], tmj['Y'][:, c * 128:(c + 1) * 128], ident, ['tm_Y'], ['ps%d' % bt])
                        yield
                        CP('act', ra['y'][:, :, :n], ps[bt][:, 0:384].rearrange("p (a b) -> p a b", b=128)[:, :, :n], ['ps%d' % bt], ['y'])
                        yield
                    if samp:
                        yield from rw_direct()
                    else:
                        yield from rw_chunk()
                    bm1 = nb()
                    yield
                    for c in range(3):
                        MM(ps[bm1][:, c * 64:c * 64 + n], bones, ra['y'][:, c, :n], ['cst', 'y'], ['ps%d' % bm1])
                    yield
                    STT(ra['d'][:, :, :n], ps[bm1][:, 0:192].rearrange("p (a b) -> p a b", b=64)[:, :, :n], -1.0 / 64, ra['y'][:, :, :n],
                        ALU.mult, ALU.add, ['ps%d' % bm1, 'y'], ['d'])
                    yield
                    T('dve', ra['t2'][:, :, :n], ra['d'][:, :, :n], ra['d'][:, :, :n], ALU.mult, ['d'], ['t2'])
                    yield
                    bm2 = nb()
                    yield
                    for c in range(3):
                        MM(ps[bm2][:, c * 64:c * 64 + n], bones, ra['t2'][:, c, :n], ['cst', 't2'], ['ps%d' % bm2])
                    yield
                    S('dve', ra['t2'][:, :, :n], ps[bm2][:, 0:192].rearrange("p (a b) -> p a b", b=64)[:, :, :n], 1.0 / 64, 64e-5,
                      ALU.mult, ALU.add, ['ps%d' % bm2], ['t2'])
                    yield
                    A(ra['t2'][:, :, :n], ra['t2'][:, :, :n], AF.Sqrt, ['t2'], ['t2'])
                    yield
                    em.op('dve', lambda e: e.reciprocal(out=ra['t2'][:, :, :n], in_=ra['t2'][:, :, :n]), reads=['t2'], writes=['t2'])
                    yield
                    T('dve', ra['d'][:, :, :n], ra['d'][:, :, :n], ra['t2'][:, :, :n], ALU.mult, ['d', 't2'], ['d'])
                    yield
                    for c in range(3):
                        S('dve', ra['d'][:, c, :n], ra['d'][:, c, :n], rp[:, c, 5:6], rp[:, c, 6:7], ALU.mult, ALU.add, ['d'] + RP, ['d'])
                    yield
                    T('dve', ra['d'][:, :, :n], ra['d'][:, :, :n], ra['bon'][:, :, :n], ALU.add, ['d', 'bon'], ['d'])
                    yield
                    T('dve', omix[:, 3:6, :n], ra['d'][:, :, :n], ra['g'][:, :, :n], ALU.mult, ['d', 'g'], ['omix_r'])
                    yield

                def do_tile(c0, n, samp=False):
                    A(sqb[:, :, :n], xres[:, :, c0:c0 + n], AF.Square, ['xres'], ['sqb'])
                    b = nb()
                    for kc in range(KC):
                        MM(ps[b][:, :n], ones_bf[:], sqb[:, kc, :n], ['ones_bf', 'sqb'], ['ps%d' % b], start=(kc == 0), stop=(kc == KC - 1))
                    S('dve', rstb[:, :n], ps[b][:, :n], 1.0 / D, EPS, ALU.mult, ALU.add, ['ps%d' % b], ['rstb'])
                    A(rstb[:, :n], rstb[:, :n], AF.Sqrt, ['rstb'], ['rstb'])
                    em.op('dve', lambda e: e.reciprocal(out=rstb[:, :n], in_=rstb[:, :n]), reads=['rstb'], writes=['rstb'])
                    if 's5' in mixers:
                        T('dve', st1[:, 0:KC, :n], xres[:, :, c0:c0 + n], normw[:, 2 + l, :].unsqueeze(2).to_broadcast([128, KC, n]), ALU.mult,
                          ['xres', 'normw'], ['st1'])
                        T('dve', xnb[:, :, :n], st1[:, 0:KC, :n], rstb[:, :n].unsqueeze(1).to_broadcast([128, KC, n]), ALU.mult,
                          ['st1', 'rstb'], ['xnb'])
                    else:
                        for kc in range(KC):
                            STT(xnb[:, kc, :n], xres[:, kc, c0:c0 + n], normw[:, 2 + l, kc:kc + 1], rstb[:, :n], ALU.mult, ALU.mult,
                                ['xres', 'normw', 'rstb'], ['xnb'])
                    for f0 in (0, 8, 16):
                        b = nb()
                        nf = min(8, 18 - f0)
                        for fi in range(nf):
                            fc = f0 + fi
                            for kc in range(KC):
                                o = kc * DIN + fc * 128
                                MM(ps[b][:, fi * 64:fi * 64 + n], ring[:, o:o + 128], xnb[:, kc, :n], RIN + ['xnb'], ['ps%d' % b],
                                   start=(kc == 0), stop=(kc == KC - 1))
                        CP('act', pj[:, f0:f0 + nf, 3:3 + n], ps[b][:, 0:nf * 64].rearrange("p (a b) -> p a b", b=64)[:, :, :n],
                           ['ps%d' % b], ['pj'])
                    if len(mixers) < 3:
                        em.op('pool', lambda e: e.memset(omix[:, :, :n], 0.0), writes=['omix_s', 'omix_r', 'omix_l'])

                    def run_rr(gens):
                        while gens:
                            for g_ in list(gens):
                                try:
                                    next(g_)
                                except StopIteration:
                                    gens.remove(g_)
                    if samp or len(mixers) < 3:
                        if 's5' in mixers:
                            run_rr([s5_tile(n, samp)])
                        gens = []
                        if 'rwkv' in mixers:
                            gens.append(rwkv_tile(n, samp))
                        if 'lru' in mixers:
                            gens.append(lru_tile(n, samp))
                        run_rr(gens)
                    else:
                        gR, gL = rwkv_tile(n, samp), lru_tile(n, samp)
                        l_done = False
                        while True:
                            if next(gR) == 'PHASE':
                                break
                            if not l_done:
                                try:
                                    next(gL)
                                except StopIteration:
                                    l_done = True
                        gens = [gR, s5_tile(n, samp)] + ([] if l_done else [gL])
                        run_rr(gens)
                    b = nb()
                    for mc in range(KC):
                        for kc in range(KC):
                            o = 18432 + kc * D + mc * 128
                            MM(ps[b][:, mc * 64:mc * 64 + n], ring[:, o:o + 128], omix[:, kc, :n], ROUT + ['omix_s', 'omix_r', 'omix_l'], ['ps%d' % b],
                               start=(kc == 0), stop=(kc == KC - 1))
                    T('dve', xres[:, :, c0:c0 + n], xres[:, :, c0:c0 + n], ps[b][:, :].rearrange("p (a b) -> p a b", b=64)[:, :, :n], ALU.add,
                      ['xres', 'ps%d' % b], ['xres'])
                    if not samp:
                        CP('dve', tmpc[:], pj[:, :, n:n + 3], ['pj'], ['tmpc'])
                        CP('dve', pj[:, :, 0:3], tmpc[:], ['tmpc'], ['pj'])

                def store_state(si):
                    em.dma('sp', o_s5[l, si], hst[:], 'st_a', reads=['hst'])
                    em.dma('sp', o_rwkv[l, si], S0T[:], 'st_b', reads=['S0T'])
                    em.dma('sp', o_shift[l, si], pj[:, 3:14, 2], 'st_c', reads=['pj'])
                    em.dma('sp', o_lru[l, si], lru_h[:], 'st_d', reads=['lru_h'])
                    em.dma('sp', o_conv[l, si], pj[:, 14:16, 0:3], 'st_e', reads=['pj'])

                em.op('pool', lambda e: e.memset(pj[:], 0.0), writes=['pj'])
                em.op('pool', lambda e: e.memset(hst[:], 0.0), writes=['hst'])
                em.op('pool', lambda e: e.memset(S0T[:], 0.0), writes=['S0T'])
                em.op('pool', lambda e: e.memset(lru_h[:], 0.0), writes=['lru_h'])
                if 'rwkv' in mixers:
                    zero_tmj()
                do_tile(0, 16)
                for i in range(PT // NB):
                    do_tile(32 + NB * i, NB)
                store_state(0)
                if not _NOSAMP:
                    em.dma('sp', s5st[:], i_s5[l].rearrange("b p g r -> p b g r"), 'ld_a', writes=['s5st'])
                    em.dma('sp', sh16[:], i_shift[l].rearrange("b p c -> p b c"), 'ld_cc', writes=['sh16'])
                    em.dma('sp', lst[:], i_lru[l].rearrange("b p c -> p b c"), 'ld_d', writes=['lst'])
                    em.dma('sp', cst16[:], i_conv[l].rearrange("b p c j -> p b c j"), 'ld_e', writes=['cst16'])
                    do_tile(16, 16, True)
                    CP('dve', prevv, pj[:, 3:14, 3:19], ['pj'], ['sh16'])
                    em.dma('sp', o_s5[l, 1:17].rearrange("b p g r -> p b g r"), s5st[:], 'st_a', reads=['s5st'])
                    em.dma('sp', o_shift[l, 1:17].rearrange("b p c -> p b c"), sh16[:], 'st_c', reads=['sh16'])
                    em.dma('sp', o_lru[l, 1:17].rearrange("b p c -> p b c"), lst[:], 'st_d', reads=['lst'])
                    em.dma('sp', o_conv[l, 1:17].rearrange("b p c j -> p b c j"), cst16[:], 'st_e', reads=['cst16'])
                em.barrier()

        for l in range(L):
            if stage >= 1:
                ffn(l, 0)
            if stage >= 2 or stage == -2:
                mixer_phase(l)
            if stage >= 1:
                ffn(l, 1)

        fs = ExitStack()
        with fs:
            sq = sbuf(fs, "sqF", [128, KC, 512], BF16)
            rstd = sbuf(fs, "rstdF", [128, 512])
            for ti in range(NT):
                c0, n = tiles[ti]
                A(sq[:, :, :n], xres[:, :, c0:c0 + n], AF.Square, ['xres'], ['sq'])
                for kc in range(KC):
                    MM(ps[7][:, :n], ones_bf[:], sq[:, kc, :n], ['ones_bf', 'sq'], ['ps7'], start=(kc == 0), stop=(kc == KC - 1))
                S('dve', rstd[:, :n], ps[7][:, :n], 1.0 / D, EPS, ALU.mult, ALU.add, ['ps7'], ['rstd'])
                A(rstd[:, :n], rstd[:, :n], AF.Sqrt, ['rstd'], ['rstd'])
                em.op('dve', lambda e: e.reciprocal(out=rstd[:, :n], in_=rstd[:, :n]), reads=['rstd'], writes=['rstd'])
                for kc in range(KC):
                    STT(xres[:, kc, c0:c0 + n], xres[:, kc, c0:c0 + n], normw[:, 6, kc:kc + 1], rstd[:, :n], ALU.mult, ALU.mult,
                        ['xres', 'rstd', 'normw'], ['xres'])
                    em.dma('sp', yT[:, kc, c0:c0 + n], xres[:, kc, c0:c0 + n], 'st_y%d' % kc, reads=['xres'])
            em.finish('sp')
        print("instructions:", em.ninstr, {k: v for k, v in em.cnt.items()})
    return nc


def prep_common(inp):
    f = np.float32
    g = lambda k: np.asarray(inp[k], f)
    c = {}
    wf = np.empty((L, 2, NJ, 128, SLOT), f)
    for l in range(L):
        for fi, pre in enumerate(("ffn1", "ffn2")):
            wg = g(pre + "_w_gate")[l].reshape(KC, 128, NJ, 128)
            wu = g(pre + "_w_up")[l].reshape(KC, 128, NJ, 128)
            wd = g(pre + "_w_down")[l].reshape(NJ, 128, D)
            wf[l, fi, :, :, 0:1024] = wg.transpose(2, 1, 0, 3).reshape(NJ, 128, 1024)
            wf[l, fi, :, :, 1024:2048] = wu.transpose(2, 1, 0, 3).reshape(NJ, 128, 1024)
            wf[l, fi, :, :, 2048:3072] = wd
    c["wffn"] = wf
    c["w_in"] = np.ascontiguousarray(g("w_in").reshape(L, KC, 128, DIN).transpose(0, 2, 1, 3)).reshape(L, 128, KC * DIN)
    c["w_out"] = np.ascontiguousarray(g("w_out").reshape(L, KC, 128, D).transpose(0, 2, 1, 3)).reshape(L, 128, KC * D)
    nv = [g("ffn1_norm")[0], g("ffn1_norm")[1], g("mix_norm")[0], g("mix_norm")[1], g("ffn2_norm")[0], g("ffn2_norm")[1], g("final_norm")]
    c["norms"] = np.ascontiguousarray(np.stack([v.reshape(KC, 128) for v in nv], 0).transpose(2, 0, 1))
    p = np.arange(128)
    cs = np.zeros((128, 400), f)
    cs[:, 0:128] = np.eye(128)
    cs[:, 128:256] = (p[:, None] // 64 == p[None, :] // 64)
    cs[:, 256:320] = np.arange(1, 65)[None, :]
    for q in range(4):
        for gg in range(2):
            cs[:, 320 + 2 * q + gg] = (p >= 32 * q + 16 * gg) & (p < 32 * q + 16 * gg + 16)
    for gg in range(2):
        cs[:, 328 + gg] = (p // 64 == gg)
    cs[:, 330:394] = (p[:, None] % 64 == np.arange(64)[None, :])
    c["consts"] = cs
    s_ = np.arange(64)
    m4 = np.stack([s_[:, None] < s_[None, :], s_[:, None] <= s_[None, :], s_[:, None] > s_[None, :], s_[:, None] == s_[None, :]], 0).astype(f)
    c["cmask"] = np.ascontiguousarray(m4.transpose(1, 0, 2))
    lr, li, ldt = g("s5_lambda_re"), g("s5_lambda_im"), g("s5_log_dt")
    tri = np.stack([lr, li, np.broadcast_to(ldt[:, :, None], lr.shape)], -1)
    c["s5A"] = np.ascontiguousarray(tri.reshape(L, 12, 2, 64, 3).transpose(0, 2, 3, 1, 4).reshape(L, 128, 12, 3))
    tB = np.broadcast_to(tri.reshape(L, 3, 8, 1, 64, 3), (L, 3, 8, 16, 64, 3))
    c["s5B"] = np.ascontiguousarray(tB.transpose(0, 2, 3, 1, 4, 5).reshape(L, 128, 3, 64, 3))
    bb = np.stack([g("s5_b_re"), g("s5_b_im")], -1)
    c["s5bT"] = np.ascontiguousarray(bb.reshape(L, 3, 8, 64, 16, 2).transpose(0, 2, 4, 1, 3, 5).reshape(L, 128, 3, 64, 2))
    cc = np.stack([g("s5_c_re"), g("s5_c_im")], -1)
    c["s5cT"] = np.ascontiguousarray(cc.reshape(L, 12, 2, 16, 64, 2).transpose(0, 2, 4, 1, 3, 5).reshape(L, 128, 12, 16, 2))
    c["s5v"] = np.ascontiguousarray(np.stack([g("s5_d"), g("s5_glu_b")], -1).reshape(L, 3, 128, 2).transpose(0, 2, 1, 3))
    c["s5glu"] = np.ascontiguousarray(g("s5_glu_w").reshape(L, 3, 128, 384).transpose(0, 2, 1, 3))
    z384 = np.zeros((L, 384), f)
    rw = np.stack([g("rwkv_w0"), g("rwkv_a0"), g("rwkv_k_k"), g("rwkv_k_a"), g("rwkv_r_k").reshape(L, 384), g("rwkv_ln_w"), g("rwkv_ln_b"), z384], -1)
    c["rwp"] = np.ascontiguousarray(rw.reshape(L, 3, 128, 8).transpose(0, 2, 1, 3))
    c["rwmu"] = np.ascontiguousarray(g("rwkv_mu").reshape(L, 11, 128).transpose(0, 2, 1))
    c["rwup"] = np.ascontiguousarray(np.stack([np.concatenate([g("rwkv_w_up"), g("rwkv_a_up")], 1), g("rwkv_g_up")], 2))
    lp_ = np.concatenate([g("lru_conv_w").transpose(0, 2, 1), g("lru_conv_b")[..., None], g("lru_b_a")[..., None], g("lru_b_x")[..., None],
                          g("lru_lambda")[..., None]], -1)
    c["lrp"] = np.ascontiguousarray(lp_.reshape(L, 2, 128, 8).transpose(0, 2, 1, 3))
    lw_ = np.zeros((L, 128, 2, 2, 128), f)
    for wi, nm in enumerate(("lru_w_a", "lru_w_x")):
        w = g(nm)
        for cb in range(2):
            for bq in range(2):
                lw_[:, 64 * bq:64 * bq + 64, cb, wi, 64 * bq:64 * bq + 64] = w[:, 2 * cb + bq]
    c["lrw"] = lw_
    return c


def prep_core(inp, ci, PT=2048):
    f = np.float32
    g = lambda k: np.asarray(inp[k], f)
    meta = g("meta_tokens")
    xs = g("x_sample")[16 * ci:16 * ci + 16, 0]
    xp = g("x_prompt")[ci][:PT]
    cols = np.concatenate([meta, xs, xp], 0)
    m = {"xT": np.ascontiguousarray(cols.T.reshape(KC, 128, 32 + PT).transpose(1, 0, 2))}
    sl = slice(16 * ci, 16 * ci + 16)
    st = np.stack([g("state_s5_re")[:, sl], g("state_s5_im")[:, sl]], -1)
    m["i_s5"] = np.ascontiguousarray(st.reshape(L, 16, 12, 2, 64, 2).transpose(0, 1, 3, 4, 2, 5).reshape(L, 16, 128, 12, 2))
    rs = g("state_rwkv")[:, sl]
    m["i_rwkv"] = np.ascontiguousarray(rs.reshape(L, 16, 3, 2, 64, 64).transpose(0, 1, 3, 5, 2, 4).reshape(L, 16, 128, 3, 64))
    m["i_shift"] = np.ascontiguousarray(g("state_rwkv_shift")[:, sl].reshape(L, 16, 11, 128).transpose(0, 1, 3, 2))
    m["i_lru"] = np.ascontiguousarray(g("state_lru")[:, sl].reshape(L, 16, 2, 128).transpose(0, 1, 3, 2))
    m["i_conv"] = np.ascontiguousarray(g("state_lru_conv")[:, sl].reshape(L, 16, 3, 2, 128).transpose(0, 1, 4, 3, 2))
    return m


def unpack_core(r):
    o = {}
    s5 = r["o_s5"].reshape(L, NSEQ, 2, 64, 12, 2).transpose(0, 1, 4, 2, 3, 5).reshape(L, NSEQ, 24, 64, 2)
    o["s5_re"], o["s5_im"] = s5[..., 0], s5[..., 1]
    o["rwkv"] = r["o_rwkv"].reshape(L, NSEQ, 2, 64, 3, 64).transpose(0, 1, 4, 2, 5, 3).reshape(L, NSEQ, 6, 64, 64)
    o["shift"] = r["o_shift"].transpose(0, 1, 3, 2).reshape(L, NSEQ, 1408)
    o["lru"] = r["o_lru"].transpose(0, 1, 3, 2).reshape(L, NSEQ, 256)
    o["conv"] = r["o_conv"].transpose(0, 1, 4, 3, 2).reshape(L, NSEQ, 3, 256)
    return o


_NC_CACHE = {}


def kernel(**inp):
    if "nc" not in _NC_CACHE:
        _NC_CACHE["nc"] = build()
    nc = _NC_CACHE["nc"]
    common = prep_common(inp)
    in_maps = []
    for ci in range(NCORES):
        m = dict(common)
        m.update(prep_core(inp, ci))
        in_maps.append(m)
    res = run_bass_kernel_spmd(nc, in_maps, core_ids=list(range(NCORES)))
    outs = res.results
    f = np.float32
    y_prompt = np.empty((8, 2048, D), f)
    y_sample = np.empty((128, 1, D), f)
    keys = ("s5_re", "s5_im", "rwkv", "shift", "lru", "conv")
    shp = {"s5_re": (24, 64), "s5_im": (24, 64), "rwkv": (6, 64, 64), "shift": (1408,), "lru": (256,), "conv": (3, 256)}
    P = {k: np.empty((L, 8) + shp[k], f) for k in keys}
    Sx = {k: np.empty((L, 128) + shp[k], f) for k in keys}
    for ci in range(NCORES):
        y = outs[ci]["yT"].transpose(1, 0, 2).reshape(D, 2080).T
        y_prompt[ci] = y[32:]
        y_sample[16 * ci:16 * ci + 16, 0] = y[16:32]
        o = unpack_core(outs[ci])
        for k in keys:
            P[k][:, ci] = o[k][:, 0]
            Sx[k][:, 16 * ci:16 * ci + 16] = o[k][:, 1:]
    return (y_prompt, y_sample) + tuple(P[k] for k in keys) + tuple(Sx[k] for k in keys)
```

```python
import numpy as np
import os
_RWSTOP = int(os.environ.get('RW_STOP', '99'))
_NOSAMP = int(os.environ.get('NO_SAMP', '0'))
from contextlib import ExitStack
import concourse.bass as bass
import concourse.mybir as mybir
from concourse.bass_utils import run_bass_kernel_spmd

F32 = mybir.dt.float32
BF16 = mybir.dt.bfloat16
ALU = mybir.AluOpType
AF = mybir.ActivationFunctionType
AX = mybir.AxisListType

NCORES = 8
D = 1024
KC = 8
DFF = 2816
NJ = 22
DIN = 2304
L = 2
NCOL = 2080
SLOT = 3072
NSLOT = 9
GRP = 4
EPS = 1e-6
TILES = [(0, 32)] + [(32 + 512 * i, 512) for i in range(4)]


class Em:
    ENG = ('pe', 'dve', 'act', 'pool', 'sp')

    def __init__(self, nc, es):
        self.nc = nc
        self.es = es
        self.e = dict(pe=nc.tensor, dve=nc.vector, act=nc.scalar, pool=nc.gpsimd, sp=nc.sync)
        self.sem = {k: es.enter_context(nc.semaphore("s_" + k)) for k in ('pe', 'dve', 'act', 'pool')}
        self.cnt = {k: 0 for k in self.sem}
        self.waited = {k: {} for k in self.ENG}
        self.buf = {}
        self.dsem = {}
        self.ninstr = 0
        self.alias = {}

    def _handle(self, sk):
        return self.sem[sk] if sk in self.sem else self.dsem[sk][0]

    def _wait(self, eng, ev):
        if ev is None:
            return
        sk, val = ev
        if sk == 'pe' and eng == 'pe':
            return
        if self.waited[eng].get(sk, 0) >= val:
            return
        self.waited[eng][sk] = val
        self.e[eng].wait_ge(self._handle(sk), val)

    def deps(self, eng, reads, writes):
        reads = [self.alias.get(k, k) for k in reads]
        writes = [self.alias.get(k, k) for k in writes]
        for r in reads:
            b = self.buf.get(r)
            if b:
                self._wait(eng, b[0])
        for w in writes:
            b = self.buf.get(w)
            if b:
                self._wait(eng, b[0])
                for ev in b[1].values():
                    self._wait(eng, ev)

    def record(self, ev, reads, writes):
        reads = [self.alias.get(k, k) for k in reads]
        writes = [self.alias.get(k, k) for k in writes]
        for r in reads:
            b = self.buf.setdefault(r, [None, {}])
            b[1][ev[0]] = ev
        for w in writes:
            self.buf[w] = [ev, {}]

    def op(self, eng, fn, reads=(), writes=(), inc=True):
        psr = [k for k in reads if k.startswith('ps')]
        if psr:
            reads = [k for k in reads if not k.startswith('ps')]
            writes = list(writes) + psr
        self.deps(eng, reads, writes)
        ins = fn(self.e[eng])
        self.ninstr += 1
        if inc:
            self.cnt[eng] += 1
            ins.then_inc(self.sem[eng], 1)
            ev = (eng, self.cnt[eng])
        else:
            ev = (eng, self.cnt[eng] + 1)
        self.record(ev, reads, writes)
        return ins

    def dma(self, q, out, in_, name, reads=(), writes=(), **kw):
        if name not in self.dsem:
            self.dsem[name] = [self.es.enter_context(self.nc.semaphore("d_" + name)), 0]
        self.deps(q, reads, writes)
        d = self.dsem[name]
        d[1] += 16
        self.e[q].dma_start(out=out, in_=in_, **kw).then_inc(d[0], 16)
        self.ninstr += 1
        self.record((name, d[1]), reads, writes)

    def dma_batch(self, q, items, name):
        if name not in self.dsem:
            self.dsem[name] = [self.es.enter_context(self.nc.semaphore("d_" + name)), 0]
        d = self.dsem[name]
        for (out, in_, reads, writes) in items:
            self.deps(q, reads, writes)
        final = d[1] + 16 * len(items)
        for (out, in_, reads, writes) in items:
            d[1] += 16
            self.e[q].dma_start(out=out, in_=in_).then_inc(d[0], 16)
            self.ninstr += 1
            self.record((name, final), reads, writes)

    def barrier(self):
        for eng in self.ENG:
            for name, d in self.dsem.items():
                self._wait(eng, (name, d[1]))
            for k in self.sem:
                if self.cnt[k]:
                    self._wait(eng, (k, self.cnt[k]))

    def finish(self, eng='sp'):
        for name, d in self.dsem.items():
            self._wait(eng, (name, d[1]))
        for k in self.sem:
            if self.cnt[k]:
                self._wait(eng, (k, self.cnt[k]))


PI2 = 6.283185307179586
NB = 64
NSEQ = 17


def build(stage=99, PT=2048, mixers=('lru', 's5', 'rwkv')):
    NCOL_ = 32 + PT
    tiles = [(0, 32)] + [(32 + 512 * i, min(512, PT - 512 * i)) for i in range((PT + 511) // 512)]
    NT = len(tiles)
    nc = bass.Bass("TRN2", target_bir_lowering=False)
    es = ExitStack()
    with es:
        es.enter_context(nc.allow_non_contiguous_dma(reason="small strided state io"))
        em = Em(nc, es)

        def din(name, shape, dt=F32):
            return nc.dram_tensor(name, list(shape), dt, kind="ExternalInput").ap()

        def dout(name, shape, dt=F32):
            return nc.dram_tensor(name, list(shape), dt, kind="ExternalOutput").ap()

        def sbuf(stack, name, shape, dt=F32):
            return stack.enter_context(nc.sbuf_tensor(name, list(shape), dt))

        xT = din("xT", [128, KC, NCOL_])
        wffn = din("wffn", [L, 2, NJ, 128, SLOT])
        w_in = din("w_in", [L, 128, KC * DIN])
        w_out = din("w_out", [L, 128, KC * D])
        norms = din("norms", [128, 7, KC])
        consts = din("consts", [128, 400])
        cmask = din("cmask", [64, 4, 64])
        s5A = din("s5A", [L, 128, 12, 3])
        s5B = din("s5B", [L, 128, 3, 64, 3])
        s5bT = din("s5bT", [L, 128, 3, 64, 2])
        s5cT = din("s5cT", [L, 128, 12, 16, 2])
        s5v = din("s5v", [L, 128, 3, 2])
        s5glu = din("s5glu", [L, 128, 3, 384])
        rwp = din("rwp", [L, 128, 3, 8])
        rwmu = din("rwmu", [L, 128, 11])
        rwup = din("rwup", [L, 128, 2, 384])
        lrp = din("lrp", [L, 128, 2, 8])
        lrw = din("lrw", [L, 128, 2, 2, 128])
        i_s5 = din("i_s5", [L, 16, 128, 12, 2])
        i_rwkv = din("i_rwkv", [L, 16, 128, 3, 64])
        i_shift = din("i_shift", [L, 16, 128, 11])
        i_lru = din("i_lru", [L, 16, 128, 2])
        i_conv = din("i_conv", [L, 16, 128, 2, 3])
        yT = dout("yT", [128, KC, NCOL_])
        o_s5 = dout("o_s5", [L, NSEQ, 128, 12, 2])
        o_rwkv = dout("o_rwkv", [L, NSEQ, 128, 3, 64])
        o_shift = dout("o_shift", [L, NSEQ, 128, 11])
        o_lru = dout("o_lru", [L, NSEQ, 128, 2])
        o_conv = dout("o_conv", [L, NSEQ, 128, 2, 3])

        xres = sbuf(es, "xres", [128, KC, NCOL_])
        ring = sbuf(es, "ring", [128, NSLOT * SLOT], BF16)
        normw = sbuf(es, "normw", [128, 7, KC])
        ones_bf = sbuf(es, "ones_bf", [128, 128], BF16)
        cst = sbuf(es, "cst", [128, 400])
        ps = [es.enter_context(nc.psum_tensor("ps%d" % i, [128, 512], F32)) for i in range(8)]
        ident = cst[:, 0:128]
        bones = cst[:, 128:256]
        iota1 = cst[:, 256:320]
        mask8 = cst[:, 320:328]
        maskg2 = cst[:, 328:330]
        identb = cst[:, 330:394]

        em.op('pool', lambda e: e.memset(ones_bf[:], 1.0), writes=['ones_bf'])
        em.dma('sp', normw[:], norms, 'ld_c', writes=['normw'])
        em.dma('sp', cst[:], consts, 'ld_c2', writes=['cst'])
        em.dma_batch('sp', [(xres[:, kc, :], xT[:, kc, :], [], ['xres']) for kc in range(KC)], 'ld_x')

        def T(eng, out, a, b, op, R, W):
            em.op(eng, lambda e: e.tensor_tensor(out=out, in0=a, in1=b, op=op), reads=R, writes=W)

        def S(eng, out, a, s1, s2, op0, op1, R, W):
            em.op(eng, lambda e: e.tensor_scalar(out=out, in0=a, scalar1=s1, scalar2=s2, op0=op0, op1=op1), reads=R, writes=W)

        def STT(out, a, sc, b, op0, op1, R, W):
            em.op('dve', lambda e: e.scalar_tensor_tensor(out=out, in0=a, scalar=sc, in1=b, op0=op0, op1=op1), reads=R, writes=W)

        def A(out, in_, func, R, W, bias=0.0, scale=1.0):
            em.op('act', lambda e: e.activation(out=out, in_=in_, func=func, bias=bias, scale=scale), reads=R, writes=W)

        def CP(eng, out, in_, R, W):
            if eng == 'act':
                em.op('act', lambda e: e.copy(out=out, in_=in_), reads=R, writes=W)
            else:
                em.op(eng, lambda e: e.tensor_copy(out=out, in_=in_), reads=R, writes=W)

        def MM(out, lhsT, rhs, R, W, start=True, stop=True):
            em.op('pe', lambda e: e.matmul(out, lhsT=lhsT, rhs=rhs, start=start, stop=stop), reads=R, writes=W, inc=stop)

        def TR(out, in_, idn, R, W):
            em.op('pe', lambda e: e.transpose(out=out, in_=in_, identity=idn), reads=R + ['cst'], writes=W)

        bank = {'A': 0, 'B': 0}

        def nb(pool='B'):
            b = bank[pool]
            bank[pool] = (bank[pool] + 1) % 4
            return b + (0 if pool == 'A' else 4)

        def load_slot(s, src):
            dst = ring[:, s * SLOT:(s + 1) * SLOT]
            em.dma('pool', dst.rearrange("p (a b) -> p a b", b=1024), src.rearrange("p (a b) -> p a b", b=1024),
                   'ring%d' % s, writes=['ring%d' % s])

        def ffn(l, f):
            fs = ExitStack()
            with fs:
                xn = sbuf(fs, "xn%d%d" % (l, f), [128, KC, NCOL_], BF16)
                hbuf = sbuf(fs, "hbuf%d%d" % (l, f), [128, 2, GRP, 512], BF16)
                sq = sbuf(fs, "sq%d%d" % (l, f), [128, KC, 512], BF16)
                rstd = sbuf(fs, "rstd%d%d" % (l, f), [128, 512])
                silu = sbuf(fs, "silu%d%d" % (l, f), [128, 2, 512])
                nidx = (0 if f == 0 else 4) + l
                for ti in range(NT):
                    c0, n = tiles[ti]
                    A(sq[:, :, :n], xres[:, :, c0:c0 + n], AF.Square, ['xres'], ['sq'])
                    for kc in range(KC):
                        MM(ps[7][:, :n], ones_bf[:], sq[:, kc, :n], ['ones_bf', 'sq'], ['ps7'], start=(kc == 0), stop=(kc == KC - 1))
                    S('dve', rstd[:, :n], ps[7][:, :n], 1.0 / D, EPS, ALU.mult, ALU.add, ['ps7'], ['rstd'])
                    A(rstd[:, :n], rstd[:, :n], AF.Sqrt, ['rstd'], ['rstd'])
                    em.op('dve', lambda e: e.reciprocal(out=rstd[:, :n], in_=rstd[:, :n]), reads=['rstd'], writes=['rstd'])
                    for kc in range(KC):
                        STT(xn[:, kc, c0:c0 + n], xres[:, kc, c0:c0 + n], normw[:, nidx, kc:kc + 1], rstd[:, :n],
                            ALU.mult, ALU.mult, ['xres', 'rstd', 'normw'], ['xn%d' % ti])
                groups = [list(range(g, min(g + GRP, NJ))) for g in range(0, NJ, GRP)]
                slot_of = {}
                nxt = [0]

                def issue_group(gi):
                    for j in groups[gi]:
                        s_ = nxt[0] % 8
                        nxt[0] += 1
                        slot_of[j] = s_
                        load_slot(s_, wffn[l, f, j])

                issue_group(0)
                for gi, grp in enumerate(groups):
                    if gi + 1 < len(groups):
                        issue_group(gi + 1)

                    def gu(ti):
                        c0, n = tiles[ti]
                        hb = ti % 2
                        for ji, j in enumerate(grp):
                            s_ = slot_of[j]
                            for which in (0, 1):
                                b = 2 * (ji % 2) + which
                                for kc in range(KC):
                                    o = s_ * SLOT + which * 1024 + kc * 128
                                    MM(ps[b][:, :n], ring[:, o:o + 128], xn[:, kc, c0:c0 + n], ['ring%d' % s_, 'xn%d' % ti],
                                       ['ps%d' % b], start=(kc == 0), stop=(kc == KC - 1))
                            b = 2 * (ji % 2)
                            A(silu[:, ji % 2, :n], ps[b][:, :n], AF.Silu, ['ps%d' % b], ['silu%d' % (ji % 2)])
                            T('dve', hbuf[:, hb, ji, :n], silu[:, ji % 2, :n], ps[b + 1][:, :n], ALU.mult,
                              ['silu%d' % (ji % 2), 'ps%d' % (b + 1)], ['h%d_%d' % (hb, ji)])

                    def down(ti):
                        c0, n = tiles[ti]
                        hb = ti % 2
                        for mc in range(KC):
                            b = 4 + mc % 3
                            for ji, j in enumerate(grp):
                                o = slot_of[j] * SLOT + 2048 + mc * 128
                                MM(ps[b][:, :n], ring[:, o:o + 128], hbuf[:, hb, ji, :n], ['ring%d' % slot_of[j], 'h%d_%d' % (hb, ji)],
                                   ['ps%d' % b], start=(ji == 0), stop=(ji == len(grp) - 1))
                            STT(xres[:, mc, c0:c0 + n], ps[b][:, :n], 0.5, xres[:, mc, c0:c0 + n], ALU.mult, ALU.add,
                                ['ps%d' % b, 'xres'], ['xres'])

                    gu(0)
                    for ti in range(NT):
                        if ti + 1 < NT:
                            gu(ti + 1)
                        down(ti)
                em.barrier()

        def mixer_phase(l):
            bs = ExitStack()
            with bs:
                def sb(name, shape, dt=F32):
                    return sbuf(bs, name + "_%d" % l, shape, dt)
                for i in range(3):
                    em.dma('pool', ring[:, i * 6144:(i + 1) * 6144].rearrange("p (a b) -> p a b", b=2048),
                           w_in[l][:, i * 6144:(i + 1) * 6144].rearrange("p (a b) -> p a b", b=2048), 'win%d' % i,
                           writes=['ring%d' % (2 * i), 'ring%d' % (2 * i + 1)])
                em.dma('pool', ring[:, 18432:26624].rearrange("p (a b) -> p a b", b=2048),
                       w_out[l].rearrange("p (a b) -> p a b", b=2048), 'wout', writes=['ring6', 'ring7', 'ring8'])
                RIN = ['ring%d' % i for i in range(6)]
                ROUT = ['ring6', 'ring7', 'ring8']

                xnb = sb("xnb", [128, KC, NB], BF16)
                sqb = sb("sqb", [128, KC, NB], BF16)
                rstb = sb("rstb", [128, NB])
                pj = sb("pj", [128, 18, 3 + NB])
                tmpc = sb("tmpc", [128, 18, 3])
                omix = sb("omix", [128, KC, NB], BF16)
                g1_def = g2_def = None
                lg1 = sb("lg1", [128, 2, NB])
                lg2 = sb("lg2", [128, 2, NB])
                s5st = sb("s5st", [128, 16, 12, 2])
                lst = sb("lst", [128, 16, 2])
                cst16 = sb("cst16", [128, 16, 2, 3])
                sh16 = sb("sh16", [128, 16, 11])
                h0v = lambda ri: s5st[:, :, :, ri].rearrange("p b g -> p g b")
                lh0v = lst[:].rearrange("p b c -> p c b")
                convv = lambda j: cst16[:, :, :, j].rearrange("p b c -> p c b")
                prevv = sh16[:].rearrange("p b c -> p c b")
                cm = sb("cm", [64, 4, 64])
                em.dma('sp', cm[:], cmask, 'ld_cm', writes=['cm'])
                MSU, MU, MSL, I6 = cm[:, 0, :], cm[:, 1, :], cm[:, 2, :], cm[:, 3, :]

                def gelu(dst, src, k, n, R, W, g1=None, g2_=None, kk_='g'):
                    g1 = g1 if g1 is not None else g1_def
                    g2_ = g2_ if g2_ is not None else g2_def
                    k1, k2 = kk_ + '1', kk_ + '2'
                    A(g1[:, :k, :n], src, AF.Square, R, [k1])
                    S('dve', g1[:, :k, :n], g1[:, :k, :n], 0.044715, 1.0, ALU.mult, ALU.add, [k1], [k1])
                    T('dve', g1[:, :k, :n], g1[:, :k, :n], src, ALU.mult, [k1] + R, [k1])
                    A(g2_[:, :k, :n], g1[:, :k, :n], AF.Sigmoid, [k1], [k2], scale=1.5957691216057308)
                    T('dve', dst, g2_[:, :k, :n], src, ALU.mult, [k2] + R, W)

                lp = sb("lp", [128, 2, 8])
                lw = sb("lw", [128, 2, 2, 128])
                lcl = sb("lcl", [128, 2])
                em.dma('sp', lp[:], lrp[l], 'ld_lp', writes=['lp'])
                em.dma('sp', lw[:], lrw[l], 'ld_lw', writes=['lw'])
                A(lcl[:], lp[:, :, 7], AF.Sigmoid, ['lp'], ['lcl'])
                A(lcl[:], lcl[:], AF.Ln, ['lcl'], ['lcl'])
                S('dve', lcl[:], lcl[:], 8.0, None, ALU.mult, ALU.bypass, ['lcl'], ['lcl'])
                lru_h = sb("lru_h", [128, 2])
                lxc = sb("lxc", [128, 2, NB])
                lga = sb("lga", [128, 2, NB])
                lgx = sb("lgx", [128, 2, NB])
                la = sb("la", [128, 2, NB])
                lb = sb("lb", [128, 2, NB])
                lh = sb("lh", [128, 2, NB])

                def lru_tile(n, samp=False):
                    gelu(lg1[:, :, :n], pj[:, 16:18, 3:3 + n], 2, n, ['pj'], ['lg1'], lg1, lg2, 'lg')
                    yield
                    for c in range(2):
                        S('dve', lxc[:, c, :n], pj[:, 14 + c, 3:3 + n], lp[:, c, 3:4], lp[:, c, 4:5], ALU.mult, ALU.add,
                          ['pj', 'lp'], ['lxc'])
                        yield
                        for j in range(3):
                            STT(lxc[:, c, :n], (convv(j)[:, c, :] if samp else pj[:, 14 + c, j:j + n]), lp[:, c, j:j + 1], lxc[:, c, :n],
                                ALU.mult, ALU.add, ['pj', 'lp', 'lxc', 'cst16'], ['lxc'])
                        yield
                    yield
                    b = nb()
                    yield
                    for c in range(2):
                        MM(ps[b][:, c * 64:c * 64 + n], lw[:, c, 0, :], lxc[:, c, :n], ['lw', 'lxc'], ['ps%d' % b])
                        yield
                        MM(ps[b][:, 128 + c * 64:128 + c * 64 + n], lw[:, c, 1, :], lxc[:, c, :n], ['lw', 'lxc'], ['ps%d' % b])
                        yield
                    yield
                    for c in range(2):
                        A(lga[:, c, :n], ps[b][:, c * 64:c * 64 + n], AF.Sigmoid, ['ps%d' % b, 'lp'], ['lga'], bias=lp[:, c, 5:6])
                        yield
                        A(lgx[:, c, :n], ps[b][:, 128 + c * 64:128 + c * 64 + n], AF.Sigmoid, ['ps%d' % b, 'lp'], ['lgx'], bias=lp[:, c, 6:7])
                        yield
                        A(la[:, c, :n], lga[:, c, :n], AF.Exp, ['lga', 'lcl'], ['la'], scale=lcl[:, c:c + 1])
                        yield
                    yield
                    T('dve', lb[:, :, :n], la[:, :, :n], la[:, :, :n], ALU.mult, ['la'], ['lb'])
                    yield
                    S('dve', lb[:, :, :n], lb[:, :, :n], -1.0, 1.0, ALU.mult, ALU.add, ['lb'], ['lb'])
                    yield
                    A(lb[:, :, :n], lb[:, :, :n], AF.Sqrt, ['lb'], ['lb'])
                    yield
                    T('dve', lb[:, :, :n], lb[:, :, :n], lgx[:, :, :n], ALU.mult, ['lb', 'lgx'], ['lb'])
                    yield
                    T('dve', lb[:, :, :n], lb[:, :, :n], lxc[:, :, :n], ALU.mult, ['lb', 'lxc'], ['lb'])
                    yield
                    if samp:
                        T('dve', lh[:, :, :n], la[:, :, :n], lh0v, ALU.mult, ['la', 'lst'], ['lh'])
                        T('dve', lh[:, :, :n], lh[:, :, :n], lb[:, :, :n], ALU.add, ['lh', 'lb'], ['lh'])
                        CP('dve', lh0v, lh[:, :, :n], ['lh'], ['lst'])
                        CP('dve', convv(0), convv(1), ['cst16', 'lxc'], ['cst16'])
                        CP('dve', convv(1), convv(2), ['cst16'], ['cst16'])
                        CP('dve', convv(2), pj[:, 14:16, 3:3 + n], ['pj'], ['cst16'])
                    else:
                        for c in range(2):
                            em.op('dve', lambda e, c=c: e.tensor_tensor_scan(out=lh[:, c, :n], data0=la[:, c, :n], data1=lb[:, c, :n],
                                                                          initial=lru_h[:, c:c + 1], op0=ALU.mult, op1=ALU.add),
                                  reads=['la', 'lb', 'lru_h'], writes=['lh'])
                        CP('dve', lru_h[:], lh[:, :, n - 1], ['lh'], ['lru_h'])
                    yield
                    T('dve', omix[:, 6:8, :n], lh[:, :, :n], lg1[:, :, :n], ALU.mult, ['lh', 'lg1'], ['omix_l'])
                    yield

                hst = sb("hst", [128, 12, 2])
                if 's5' in mixers:
                    sv = sb("sv", [128, 3, 2])
                    sglu = sb("sglu", [128, 3, 384], BF16)
                    szb = sb("szb", [128, 3, NB], BF16)
                    lB = sb("lB", [128, 2, 12, 128], BF16)
                    lC = sb("lC", [128, 2, 12, 128], BF16)
                    rmag = sb("rmag", [128, 12])
                    Et = sb("Et", [128, 2, 12, NB])
                    abar = sb("abar", [128, 12, 2])
                    ss = ExitStack()
                    ss.__enter__()
                    sA = sbuf(ss, "sA_%d" % l, [128, 12, 3])
                    sBp = sbuf(ss, "sBp_%d" % l, [128, 3, 64, 3])
                    sbT = sbuf(ss, "sbT_%d" % l, [128, 3, 64, 2])
                    scT = sbuf(ss, "scT_%d" % l, [128, 12, 16, 2])
                    em.dma('sp', sA[:], s5A[l], 'ld_s5a', writes=['sA'])
                    em.dma('sp', sBp[:], s5B[l], 'ld_s5b', writes=['sBp'])
                    em.dma('sp', sbT[:], s5bT[l], 'ld_s5c', writes=['sbT'])
                    em.dma('sp', scT[:], s5cT[l], 'ld_s5d', writes=['scT'])
                    em.dma('sp', sv[:], s5v[l], 'ld_s5e', writes=['sv'])
                    em.dma('pool', sglu[:], s5glu[l], 'ld_s5f', writes=['sglu'])
                    wk = sbuf(ss, "wk_%d" % l, [128, 7, 64])
                    wk7 = sbuf(ss, "wk7_%d" % l, [128, 768])
                    wk8 = sbuf(ss, "wk8_%d" % l, [128, 768])
                    wki = sbuf(ss, "wki_%d" % l, [128, 768], mybir.dt.int32)

                    def rr(x, m, k, R, W):
                        tmp_ = wk[:, k, :m] if m <= 64 else wk8[:, :m]
                        CP('dve', wki[:, :m], x, R, ['wki'])
                        CP('dve', tmp_, wki[:, :m], ['wki'], ['wk%d' % k])
                        T('dve', x, x, tmp_, ALU.subtract, R + ['wk%d' % k], W)
                        S('dve', tmp_, x, 0.5, None, ALU.is_gt, ALU.bypass, W, ['wk%d' % k])
                        T('dve', x, x, tmp_, ALU.subtract, W + ['wk%d' % k], W)
                        S('dve', tmp_, x, -0.5, None, ALU.is_lt, ALU.bypass, W, ['wk%d' % k])
                        T('dve', x, x, tmp_, ALU.add, W + ['wk%d' % k], W)

                    def disc(lr, li, ldt, m, pre):
                        A(wk[:, 4, :m], ldt, AF.Exp, [pre], ['wk4'])
                        T('dve', wk[:, 0, :m], lr, wk[:, 4, :m], ALU.mult, [pre, 'wk4'], ['wk0'])
                        A(wk[:, 0, :m], wk[:, 0, :m], AF.Exp, ['wk0'], ['wk0'])
                        T('dve', wk[:, 3, :m], li, wk[:, 4, :m], ALU.mult, [pre, 'wk4'], ['wk3'])
                        S('dve', wk[:, 3, :m], wk[:, 3, :m], 1.0 / PI2, None, ALU.mult, ALU.bypass, ['wk3'], ['wk3'])
                        rr(wk[:, 3, :m], m, 5, ['wk3'], ['wk3'])
                        A(wk[:, 2, :m], wk[:, 3, :m], AF.Sin, ['wk3'], ['wk2'], scale=PI2 * 0.999999)
                        S('dve', wk[:, 6, :m], wk[:, 3, :m], 0.25, None, ALU.add, ALU.bypass, ['wk3'], ['wk6'])
                        rr(wk[:, 6, :m], m, 5, ['wk6'], ['wk6'])
                        A(wk[:, 1, :m], wk[:, 6, :m], AF.Sin, ['wk6'], ['wk1'], scale=PI2 * 0.999999)
                        T('dve', wk[:, 1, :m], wk[:, 1, :m], wk[:, 0, :m], ALU.mult, ['wk1', 'wk0'], ['wk1'])
                        T('dve', wk[:, 2, :m], wk[:, 2, :m], wk[:, 0, :m], ALU.mult, ['wk2', 'wk0'], ['wk2'])

                    disc(sA[:, :, 0], sA[:, :, 1], sA[:, :, 2], 12, 'sA')
                    CP('dve', abar[:, :, 0], wk[:, 1, :12], ['wk1'], ['abar'])
                    CP('dve', abar[:, :, 1], wk[:, 2, :12], ['wk2'], ['abar'])
                    CP('dve', rmag[:], wk[:, 0, :12], ['wk0'], ['Rt'])
                    wkE = wk7[:, :].rearrange("p (a b) -> p a b", b=NB)
                    T('dve', wkE, wk[:, 3, :12].unsqueeze(2).to_broadcast([128, 12, NB]),
                      iota1.unsqueeze(1).to_broadcast([128, 12, NB]), ALU.mult, ['wk3', 'cst'], ['wk7'])
                    rr(wk7[:, :], 768, 5, ['wk7'], ['wk7'])
                    A(Et[:, 1, :, :], wkE, AF.Sin, ['wk7'], ['Et'], scale=-PI2 * 0.999999)
                    S('dve', wk7[:, :], wk7[:, :], 0.25, None, ALU.add, ALU.bypass, ['wk7'], ['wk7'])
                    rr(wk7[:, :], 768, 5, ['wk7'], ['wk7'])
                    A(Et[:, 0, :, :], wkE, AF.Sin, ['wk7'], ['Et'], scale=PI2 * 0.999999)
                    for kc in range(3):
                        lr, li = sBp[:, kc, :, 0], sBp[:, kc, :, 1]
                        disc(lr, li, sBp[:, kc, :, 2], 64, 'sBp')
                        m = 64
                        T('dve', wk[:, 4, :m], lr, lr, ALU.mult, ['sBp'], ['wk4'])
                        T('dve', wk[:, 5, :m], li, li, ALU.mult, ['sBp'], ['wk5'])
                        T('dve', wk[:, 4, :m], wk[:, 4, :m], wk[:, 5, :m], ALU.add, ['wk4', 'wk5'], ['wk4'])
                        em.op('dve', lambda e: e.reciprocal(out=wk[:, 4, :64], in_=wk[:, 4, :64]), reads=['wk4'], writes=['wk4'])
                        S('dve', wk[:, 1, :m], wk[:, 1, :m], -1.0, None, ALU.add, ALU.bypass, ['wk1'], ['wk1'])
                        T('dve', wk[:, 5, :m], wk[:, 1, :m], lr, ALU.mult, ['wk1', 'sBp'], ['wk5'])
                        T('dve', wk[:, 6, :m], wk[:, 2, :m], li, ALU.mult, ['wk2', 'sBp'], ['wk6'])
                        T('dve', wk[:, 5, :m], wk[:, 5, :m], wk[:, 6, :m], ALU.add, ['wk5', 'wk6'], ['wk5'])
                        T('dve', wk[:, 5, :m], wk[:, 5, :m], wk[:, 4, :m], ALU.mult, ['wk5', 'wk4'], ['wk5'])
                        T('dve', wk[:, 6, :m], wk[:, 2, :m], lr, ALU.mult, ['wk2', 'sBp'], ['wk6'])
                        T('dve', wk[:, 0, :m], wk[:, 1, :m], li, ALU.mult, ['wk1', 'sBp'], ['wk0'])
                        T('dve', wk[:, 6, :m], wk[:, 6, :m], wk[:, 0, :m], ALU.subtract, ['wk6', 'wk0'], ['wk6'])
                        T('dve', wk[:, 6, :m], wk[:, 6, :m], wk[:, 4, :m], ALU.mult, ['wk6', 'wk4'], ['wk6'])
                        bre, bim = sbT[:, kc, :, 0], sbT[:, kc, :, 1]
                        T('dve', wk[:, 0, :m], wk[:, 5, :m], bre, ALU.mult, ['wk5', 'sbT'], ['wk0'])
                        T('dve', wk[:, 1, :m], wk[:, 6, :m], bim, ALU.mult, ['wk6', 'sbT'], ['wk1'])
                        T('dve', wk[:, 0, :m], wk[:, 0, :m], wk[:, 1, :m], ALU.subtract, ['wk0', 'wk1'], ['wk0'])
                        T('dve', wk[:, 1, :m], wk[:, 5, :m], bim, ALU.mult, ['wk5', 'sbT'], ['wk1'])
                        T('dve', wk[:, 2, :m], wk[:, 6, :m], bre, ALU.mult, ['wk6', 'sbT'], ['wk2'])
                        T('dve', wk[:, 1, :m], wk[:, 1, :m], wk[:, 2, :m], ALU.add, ['wk1', 'wk2'], ['wk1'])
                        for q in range(4):
                            for gg in range(2):
                                for ri in range(2):
                                    S('dve', lB[:, ri, 4 * kc + q, 64 * gg:64 * gg + 64], wk[:, ri, :64],
                                      mask8[:, 2 * q + gg:2 * q + gg + 1], None, ALU.mult, ALU.bypass, ['wk%d' % ri, 'cst'], ['lB'])
                    em.op('pool', lambda e: e.memset(lC[:], 0.0), writes=['lC'])
                    for gh in range(12):
                        q = gh % 4
                        for gg in range(2):
                            S('dve', lC[:, 0, gh, 32 * q + 16 * gg:32 * q + 16 * gg + 16], scT[:, gh, :, 0], maskg2[:, gg:gg + 1], None,
                              ALU.mult, ALU.bypass, ['scT', 'cst'], ['lC'])
                            S('dve', lC[:, 1, gh, 32 * q + 16 * gg:32 * q + 16 * gg + 16], scT[:, gh, :, 1], maskg2[:, gg:gg + 1], -1.0,
                              ALU.mult, ALU.mult, ['scT', 'cst'], ['lC'])
                    em.barrier()
                    ss.close()
                    sx = sb("sx", [128, 2, 12, NB])
                    sg = sb("sg", [128, 2, 12, NB])
                    sh = sx
                    em.alias['sh'] = 'sx'
                    shb = sb("shb", [128, 2, 12, NB], BF16)
                    ubf = sb("ubf", [128, 3, NB], BF16)
                    st1 = sb("st1", [128, 12, NB])
                    sy = sb("sy", [128, 3, NB])
                    sz = sb("sz", [128, 3, NB])

                def s5_tile(n, samp=False):
                    CP('act', ubf[:, :, :n], pj[:, 0:3, 3:3 + n], ['pj'], ['ubf'])
                    yield
                    bre = [nb('A'), nb('A')]
                    yield
                    bim = [nb('A'), nb('A')]
                    yield
                    for ri, bb in ((0, bre), (1, bim)):
                        for gh in range(12):
                            b = bb[gh // 8]
                            o = (gh % 8) * 64
                            MM(ps[b][:, o:o + n], lB[:, ri, gh, :], ubf[:, gh // 4, :n], ['lB', 'ubf'], ['ps%d' % b])
                    yield
                    if samp:
                        for half, (g0, g1_) in enumerate(((0, 8), (8, 12))):
                            k = g1_ - g0
                            pr = ps[bre[half]][:, 0:k * 64].rearrange("p (a b) -> p a b", b=64)[:, :, :n]
                            pi = ps[bim[half]][:, 0:k * 64].rearrange("p (a b) -> p a b", b=64)[:, :, :n]
                            ar = abar[:, g0:g1_, 0].unsqueeze(2).to_broadcast([128, k, n])
                            ai = abar[:, g0:g1_, 1].unsqueeze(2).to_broadcast([128, k, n])
                            h0r, h0i = h0v(0)[:, g0:g1_, :], h0v(1)[:, g0:g1_, :]
                            T('dve', sg[:, 0, g0:g1_, :n], h0r, ar, ALU.mult, ['s5st', 'abar'], ['sg'])
                            T('dve', st1[:, g0:g1_, :n], h0i, ai, ALU.mult, ['s5st', 'abar'], ['st1'])
                            T('dve', sg[:, 0, g0:g1_, :n], sg[:, 0, g0:g1_, :n], st1[:, g0:g1_, :n], ALU.subtract, ['sg', 'st1'], ['sg'])
                            T('dve', sx[:, 0, g0:g1_, :n], sg[:, 0, g0:g1_, :n], pr, ALU.add, ['sg', 'ps%d' % bre[half]], ['sx'])
                            T('dve', sg[:, 1, g0:g1_, :n], h0i, ar, ALU.mult, ['s5st', 'abar'], ['sg'])
                            T('dve', st1[:, g0:g1_, :n], h0r, ai, ALU.mult, ['s5st', 'abar'], ['st1'])
                            T('dve', sg[:, 1, g0:g1_, :n], sg[:, 1, g0:g1_, :n], st1[:, g0:g1_, :n], ALU.add, ['sg', 'st1'], ['sg'])
                            T('dve', sx[:, 1, g0:g1_, :n], sg[:, 1, g0:g1_, :n], pi, ALU.add, ['sg', 'ps%d' % bim[half]], ['sx'])
                        CP('dve', h0v(0), sx[:, 0, :, :n], ['sx'], ['s5st'])
                        CP('dve', h0v(1), sx[:, 1, :, :n], ['sx'], ['s5st'])
                    else:
                        for half, (g0, g1_) in enumerate(((0, 8), (8, 12))):
                            k = g1_ - g0
                            pr = ps[bre[half]][:, 0:k * 64].rearrange("p (a b) -> p a b", b=64)[:, :, :n]
                            pi = ps[bim[half]][:, 0:k * 64].rearrange("p (a b) -> p a b", b=64)[:, :, :n]
                            Rr, Ri = ['ps%d' % bre[half], 'Et'], ['ps%d' % bim[half], 'Et']
                            er, ei = Et[:, 0, g0:g1_, :n], Et[:, 1, g0:g1_, :n]
                            T('dve', sx[:, 0, g0:g1_, :n], pr, er, ALU.mult, Rr, ['sx'])
                            T('dve', st1[:, g0:g1_, :n], pi, ei, ALU.mult, Ri, ['st1'])
                            T('dve', sx[:, 0, g0:g1_, :n], sx[:, 0, g0:g1_, :n], st1[:, g0:g1_, :n], ALU.subtract, ['sx', 'st1'], ['sx'])
                            T('dve', sx[:, 1, g0:g1_, :n], pr, ei, ALU.mult, Rr, ['sx'])
                            T('dve', st1[:, g0:g1_, :n], pi, er, ALU.mult, Ri, ['st1'])
                            T('dve', sx[:, 1, g0:g1_, :n], sx[:, 1, g0:g1_, :n], st1[:, g0:g1_, :n], ALU.add, ['sx', 'st1'], ['sx'])
                        for ri in range(2):
                            for gh in range(12):
                                em.op('dve', lambda e, ri=ri, gh=gh: e.tensor_tensor_scan(
                                    out=sg[:, ri, gh, :n], data0=rmag[:, gh:gh + 1].to_broadcast([128, n]), data1=sx[:, ri, gh, :n],
                                    initial=hst[:, gh, ri:ri + 1], op0=ALU.mult, op1=ALU.add),
                                    reads=['Rt', 'sx', 'hst'], writes=['sg'])
                        er, ei = Et[:, 0, :, :n], Et[:, 1, :, :n]
                        T('dve', sh[:, 0, :, :n], sg[:, 0, :, :n], er, ALU.mult, ['sg', 'Et'], ['sh'])
                        T('dve', st1[:, :, :n], sg[:, 1, :, :n], ei, ALU.mult, ['sg', 'Et'], ['st1'])
                        T('dve', sh[:, 0, :, :n], sh[:, 0, :, :n], st1[:, :, :n], ALU.add, ['sh', 'st1'], ['sh'])
                        T('dve', sh[:, 1, :, :n], sg[:, 1, :, :n], er, ALU.mult, ['sg', 'Et'], ['sh'])
                        T('dve', st1[:, :, :n], sg[:, 0, :, :n], ei, ALU.mult, ['sg', 'Et'], ['st1'])
                        T('dve', sh[:, 1, :, :n], sh[:, 1, :, :n], st1[:, :, :n], ALU.subtract, ['sh', 'st1'], ['sh'])
                        CP('dve', hst[:, :, 0], sh[:, 0, :, n - 1], ['sh'], ['hst'])
                        CP('dve', hst[:, :, 1], sh[:, 1, :, n - 1], ['sh'], ['hst'])
                    yield
                    CP('act', shb[:, :, :, :n], sh[:, :, :, :n], ['sh'], ['shb'])
                    yield
                    b = nb('A')
                    yield
                    for kc in range(3):
                        for q in range(4):
                            gh = 4 * kc + q
                            MM(ps[b][:, kc * 64:kc * 64 + n], lC[:, 0, gh, :], shb[:, 0, gh, :n], ['lC', 'shb'], ['ps%d' % b],
                               start=(q == 0), stop=False)
                            MM(ps[b][:, kc * 64:kc * 64 + n], lC[:, 1, gh, :], shb[:, 1, gh, :n], ['lC', 'shb'], ['ps%d' % b],
                               start=False, stop=(q == 3))
                    yield
                    T('dve', sy[:, :, :n], pj[:, 0:3, 3:3 + n], sv[:, :, 0].unsqueeze(2).to_broadcast([128, 3, n]), ALU.mult, ['pj', 'sv'], ['sy'])
                    yield
                    T('dve', sy[:, :, :n], sy[:, :, :n], ps[b][:, 0:192].rearrange("p (a b) -> p a b", b=64)[:, :, :n], ALU.add,
                      ['sy', 'ps%d' % b], ['sy'])
                    yield
                    gelu(sz[:, :, :n], sy[:, :, :n], 3, n, ['sy'], ['sz'], st1[:, 0:3, :], st1[:, 3:6, :], 'st')
                    yield
                    CP('act', szb[:, :, :n], sz[:, :, :n], ['sz'], ['szb'])
                    yield
                    b = nb('A')
                    yield
                    for k2 in range(3):
                        for kc in range(3):
                            MM(ps[b][:, k2 * 64:k2 * 64 + n], sglu[:, kc, k2 * 128:(k2 + 1) * 128], szb[:, kc, :n], ['sglu', 'szb'],
                               ['ps%d' % b], start=(kc == 0), stop=(kc == 2))
                    yield
                    for k2 in range(3):
                        A(sy[:, k2, :n], ps[b][:, k2 * 64:k2 * 64 + n], AF.Sigmoid, ['ps%d' % b, 'sv'], ['sy'], bias=sv[:, k2, 1:2])
                    yield
                    T('dve', omix[:, 0:3, :n], sz[:, :, :n], sy[:, :, :n], ALU.mult, ['sz', 'sy'], ['omix_s'])
                    yield

                S0T = sb("S0T", [128, 3, 64])
                if 'rwkv' in mixers:
                    rp = sb("rp", [128, 3, 8])
                    rmu = sb("rmu", [128, 11])
                    rup = sb("rup", [128, 2, 384])
                    em.dma('sp', rp[:], rwp[l], 'ld_r1', writes=['rp'])
                    em.dma('sp', rmu[:], rwmu[l], 'ld_r2', writes=['rmu'])
                    em.dma('sp', rup[:], rwup[l], 'ld_r3', writes=['rup'])
                    ones64 = sb("ones64", [128, NB])
                    em.op('pool', lambda e: e.memset(ones64[:], 1.0), writes=['ones64'])
                    xm = sb("xm", [128, 11, NB])
                    early_ = ('sgw', 'a', 'kk', 'kp', 'b', 'lg', 'gam', 'gex', 'gin', 't1')
                    own_ = ('g', 'bon', 'rt0', 'rt1', 'kt0', 'kt1', 'kti', 'nbt')
                    rarr = {}
                    for i_, nm in enumerate(early_):
                        if 's5' in mixers:
                            base_ = sx if i_ < 8 else sg
                            j_ = i_ % 8
                            rarr[nm] = base_[:, j_ // 4, 3 * (j_ % 4):3 * (j_ % 4) + 3, :]
                            em.alias[nm] = 'sx' if i_ < 8 else 'sg'
                        else:
                            rarr[nm] = sb("r_" + nm, [128, 3, NB])
                    for nm in own_:
                        rarr[nm] = sb("r_" + nm, [128, 3, NB])
                    rarr['y'], rarr['d'], rarr['t2'] = rarr['kti'], rarr['nbt'], rarr['kt0']
                    em.alias.update({'y': 'kti', 'd': 'nbt', 't2': 'kt0'})
                    tw = sb("tw", [128, NB])
                    tw2 = sb("tw2", [128, NB])
                    twb = sb("twb", [128, NB])
                    em.op('pool', lambda e: e.memset(tw[:], 0.0), writes=['tw'])
                    em.op('pool', lambda e: e.memset(twb[:], 0.0), writes=['twb'])
                    gC = sb("gC", [128, 3])
                    tmj_all = sb("tmj_all", [128, 8 * 384])
                    TMN = ('V', 'K', 'B', 'AkkT', 'N', 'M', 'P', 'X')
                    TMK = ['tm_' + x_ for x_ in TMN]
                    tmj = {nm: tmj_all[:, i_ * 384:(i_ + 1) * 384] for i_, nm in enumerate(TMN)}

                    def zero_tmj():
                        for nm_ in ('V', 'K', 'B', 'AkkT', 'N', 'M', 'P', 'X'):
                            em.op('pool', lambda e, nm_=nm_: e.memset(tmj[nm_][:], 0.0), writes=['tm_' + nm_])
                    tmj['R'], tmj['U'], tmj['Y'] = tmj['N'], tmj['M'], tmj['P']
                    tmj['ArkT'], tmj['ArbT'] = tmj['AkkT'], tmj['X']
                    em.alias.update({'tm_R': 'tm_N', 'tm_U': 'tm_M', 'tm_Y': 'tm_P', 'tm_ArkT': 'tm_AkkT', 'tm_ArbT': 'tm_X'})
                    stmp = sb("stmp", [128, 64])

                def rwkv_tile(n, samp=False):
                    C = n
                    yield
                    ra = rarr
                    yield
                    RP = ['rp']
                    yield
                    T('dve', xm[:, :, :n], (prevv if samp else pj[:, 3:14, 2:2 + n]), pj[:, 3:14, 3:3 + n], ALU.subtract, ['pj', 'sh16'], ['xm'])
                    yield
                    T('dve', xm[:, :, :n], xm[:, :, :n], rmu[:, :].unsqueeze(2).to_broadcast([128, 11, n]), ALU.mult, ['xm', 'rmu'], ['xm'])
                    yield
                    T('dve', xm[:, :, :n], xm[:, :, :n], pj[:, 3:14, 3:3 + n], ALU.add, ['xm', 'pj'], ['xm'])
                    yield
                    r_, k_, v_ = xm[:, 0:3, :n], xm[:, 3:6, :n], xm[:, 6:9, :n]
                    yield
                    A(tw[0:64, :n], xm[0:64, 9, :n], AF.Tanh, ['xm'], ['tw'])
                    yield
                    CP('act', twb[64:128, :n], xm[64:128, 9, :n], ['xm'], ['twb'])
                    yield
                    T('dve', ra['kk'][:, :, :n], k_, rp[:, :, 2].unsqueeze(2).to_broadcast([128, 3, n]), ALU.mult, ['xm'] + RP, ['kk'])
                    yield
                    T('dve', ra['t1'][:, :, :n], ra['kk'][:, :, :n], ra['kk'][:, :, :n], ALU.mult, ['kk'], ['t1'])
                    yield
                    b0 = nb()
                    yield
                    for c in range(3):
                        MM(ps[b0][:, c * 64:c * 64 + n], rup[:, 0, c * 128:(c + 1) * 128], tw[:, :n], ['rup', 'tw'], ['ps%d' % b0])
                        yield
                        MM(ps[b0][:, 192 + c * 64:192 + c * 64 + n], rup[:, 0, c * 128:(c + 1) * 128], twb[:, :n],
                           ['rup', 'twb'], ['ps%d' % b0])
                        yield
                    yield
                    b2 = nb()
                    yield
                    for c in range(3):
                        MM(ps[b2][:, c * 64:c * 64 + n], bones, ra['t1'][:, c, :n], ['cst', 't1'], ['ps%d' % b2])
                    yield
                    for c in range(3):
                        A(ra['sgw'][:, c, :n], ps[b0][:, c * 64:c * 64 + n], AF.Sigmoid, ['ps%d' % b0] + RP, ['sgw'], bias=rp[:, c, 0:1])
                    yield
                    for c in range(3):
                        A(ra['a'][:, c, :n], ps[b0][:, 192 + c * 64:192 + c * 64 + n], AF.Sigmoid, ['ps%d' % b0] + RP, ['a'], bias=rp[:, c, 1:2])
                    yield
                    A(tw2[:, :n], xm[:, 10, :n], AF.Sigmoid, ['xm'], ['tw2'])
                    yield
                    S('dve', ra['sgw'][:, :, :n], ra['sgw'][:, :, :n], -0.6065306597126334, None, ALU.mult, ALU.bypass, ['sgw'], ['sgw'])
                    yield
                    if samp:
                        CP('dve', ra['lg'][:, :, :n], ra['sgw'][:, :, :n], ['sgw'], ['lg'])
                    else:
                        for c in range(3):
                            em.op('dve', lambda e, c=c: e.tensor_tensor_scan(out=ra['lg'][:, c, :n], data0=ones64[:, :n], data1=ra['sgw'][:, c, :n],
                                                                          initial=0.0, op0=ALU.mult, op1=ALU.add),
                                  reads=['ones64', 'sgw'], writes=['lg'])
                    yield
                    b1 = nb()
                    yield
                    for c in range(3):
                        MM(ps[b1][:, c * 64:c * 64 + n], rup[:, 1, c * 128:(c + 1) * 128], tw2[:, :n], ['rup', 'tw2'], ['ps%d' % b1])
                    yield
                    A(ra['gam'][:, :, :n], ra['lg'][:, :, :n], AF.Exp, ['lg'], ['gam'])
                    yield
                    A(ra['gin'][:, :, :n], ra['lg'][:, :, :n], AF.Exp, ['lg'], ['gin'], scale=-1.0)
                    yield
                    T('dve', ra['gex'][:, :, :n], ra['lg'][:, :, :n], ra['sgw'][:, :, :n], ALU.subtract, ['lg', 'sgw'], ['gex'])
                    yield
                    A(ra['gex'][:, :, :n], ra['gex'][:, :, :n], AF.Exp, ['gex'], ['gex'])
                    yield
                    STT(ra['kp'][:, :, :n], ra['a'][:, :, :n], -1.0, rp[:, :, 3].unsqueeze(2).to_broadcast([128, 3, n]), ALU.add, ALU.mult,
                        ['a'] + RP, ['kp'])
                    yield
                    STT(ra['kp'][:, :, :n], ra['kp'][:, :, :n], 1.0, k_, ALU.add, ALU.mult, ['kp', 'xm'], ['kp'])
                    yield
                    T('dve', ra['bon'][:, :, :n], r_, rp[:, :, 4].unsqueeze(2).to_broadcast([128, 3, n]), ALU.mult, ['xm'] + RP, ['bon'])
                    yield
                    T('dve', ra['bon'][:, :, :n], ra['bon'][:, :, :n], ra['kp'][:, :, :n], ALU.mult, ['bon', 'kp'], ['bon'])
                    yield
                    b3 = nb()
                    yield
                    for c in range(3):
                        MM(ps[b3][:, c * 64:c * 64 + n], bones, ra['bon'][:, c, :n], ['cst', 'bon'], ['ps%d' % b3])
                    yield
                    CP('act', ra['g'][:, :, :n], ps[b1][:, 0:192].rearrange("p (a b) -> p a b", b=64)[:, :, :n], ['ps%d' % b1], ['g'])
                    yield
                    p3 = ps[b2][:, 0:192].rearrange("p (a b) -> p a b", b=64)[:, :, :n]
                    yield
                    A(ra['t1'][:, :, :n], p3, AF.Sqrt, ['ps%d' % b2], ['t1'])
                    yield
                    S('dve', ra['t1'][:, :, :n], ra['t1'][:, :, :n], 1e-12, None, ALU.max, ALU.bypass, ['t1'], ['t1'])
                    yield
                    em.op('dve', lambda e: e.reciprocal(out=ra['t1'][:, :, :n], in_=ra['t1'][:, :, :n]), reads=['t1'], writes=['t1'])
                    yield
                    T('dve', ra['kk'][:, :, :n], ra['kk'][:, :, :n], ra['t1'][:, :, :n], ALU.mult, ['kk', 't1'], ['kk'])
                    yield
                    T('dve', ra['b'][:, :, :n], ra['kk'][:, :, :n], ra['a'][:, :, :n], ALU.mult, ['kk', 'a'], ['b'])
                    yield
                    T('dve', ra['bon'][:, :, :n], ps[b3][:, 0:192].rearrange("p (a b) -> p a b", b=64)[:, :, :n], v_, ALU.mult,
                      ['ps%d' % b3, 'xm'], ['bon'])
                    yield
                    CP('dve', gC[:], ra['gam'][:, :, n - 1], ['gam'], ['gC'])
                    yield
                    for hp in range(2):
                        STT(ra['rt%d' % hp][:, :, :n], r_, maskg2[:, hp:hp + 1], ra['gam'][:, :, :n], ALU.mult, ALU.mult,
                            ['xm', 'gam', 'cst'], ['rt%d' % hp])
                        yield
                        STT(ra['kt%d' % hp][:, :, :n], ra['kk'][:, :, :n], maskg2[:, hp:hp + 1], ra['gex'][:, :, :n], ALU.mult, ALU.mult,
                            ['kk', 'gex', 'cst'], ['kt%d' % hp])
                        yield
                    yield
                    T('dve', ra['kti'][:, :, :n], ra['kp'][:, :, :n], ra['gin'][:, :, :n], ALU.mult, ['kp', 'gin'], ['kti'])
                    yield
                    STT(ra['nbt'][:, :, :n], ra['b'][:, :, :n], -1.0, ra['gin'][:, :, :n], ALU.mult, ALU.mult, ['b', 'gin'], ['nbt'])
                    yield
                    if not samp:
                        yield 'PHASE'

                    def rw_direct():
                        Sb = tmj_all[:, 0:1024].rearrange("p (b v) -> p b v", v=64)
                        yield
                        tb = tmj_all[:, 1024:2048].rearrange("p (b v) -> p b v", v=64)
                        yield

                        def bc(ap2, lo=0, hi=16):
                            return ap2[:, lo:hi].unsqueeze(2).to_broadcast([128, hi - lo, 64])

                        def idb(k_):
                            return identb.unsqueeze(1).to_broadcast([128, k_, 64])

                        def bmm(src):
                            bb_ = (nb(), nb())
                            for hf in range(2):
                                MM(ps[bb_[hf]][:, :], bones, src[:, 8 * hf:8 * hf + 8, :], ['cst'] + TMK, ['ps%d' % bb_[hf]])
                            return bb_

                        def pview(b_):
                            return ps[b_][:, :].rearrange("p (b v) -> p b v", v=64)
                        for c in range(3):
                            em.dma('sp', Sb, i_rwkv[l, :, :, c, :].rearrange("b p v -> p b v"), 'ld_sb', writes=TMK)
                            yield
                            kap, w_, kp_, bq_ = ra['kk'][:, c, :n], ra['gam'][:, c, :n], ra['kp'][:, c, :n], ra['b'][:, c, :n]
                            yield
                            rq_, vq_ = xm[:, c, :n], xm[:, 6 + c, :n]
                            yield
                            T('dve', tb, Sb, bc(kap), ALU.mult, TMK + ['kk'], TMK)
                            yield
                            bb_ = bmm(tb)
                            yield
                            T('dve', Sb, Sb, bc(w_), ALU.mult, TMK + ['gam'], TMK)
                            yield
                            for hf in range(2):
                                T('dve', tb[:, 8 * hf:8 * hf + 8, :], pview(bb_[hf]), bc(bq_, 8 * hf, 8 * hf + 8), ALU.mult,
                                  ['ps%d' % bb_[hf], 'b'], TMK)
                            yield
                            T('dve', Sb, Sb, tb, ALU.subtract, TMK, TMK)
                            yield
                            T('dve', tb, idb(16), bc(vq_), ALU.mult, ['cst', 'xm'], TMK)
                            yield
                            bb_ = bmm(tb)
                            yield
                            for hf in range(2):
                                T('dve', tb[:, 8 * hf:8 * hf + 8, :], pview(bb_[hf]), bc(kp_, 8 * hf, 8 * hf + 8), ALU.mult,
                                  ['ps%d' % bb_[hf], 'kp'], TMK)
                            yield
                            T('dve', Sb, Sb, tb, ALU.add, TMK, TMK)
                            yield
                            T('dve', tb, Sb, bc(rq_), ALU.mult, TMK + ['xm'], TMK)
                            yield
                            bb_ = bmm(tb)
                            yield
                            for hf in range(2):
                                T('dve', tb[:, 8 * hf:8 * hf + 8, :], pview(bb_[hf]), idb(8), ALU.mult, ['ps%d' % bb_[hf], 'cst'], TMK)
                            yield
                            em.op('dve', lambda e, c=c: e.tensor_reduce(out=ra['y'][:, c, :n], in_=tb, axis=AX.X, op=ALU.add),
                                  reads=TMK, writes=['y'])
                            yield
                            em.dma('sp', o_rwkv[l, 1:17, :, c, :].rearrange("b p v -> p b v"), Sb, 'st_sb', reads=TMK)
                            yield
                        yield

                    def rw_chunk():
                        if _RWSTOP <= 2:
                            return
                        yield
                        for nm, src, RR in (('V', xm[:, 6:9, :n], ['xm']), ('K', ra['kti'][:, :, :n], ['kti']), ('B', ra['nbt'][:, :, :n], ['nbt'])):
                            bt = nb()
                            yield
                            for c in range(3):
                                TR(ps[bt][:C, c * 128:(c + 1) * 128], src[:, c, :], ident, RR, ['ps%d' % bt])
                            yield
                            CP('act', tmj[nm][:C, :], ps[bt][:C, 0:384], ['ps%d' % bt], ['tm_' + nm])
                            yield
                        yield
                        if _RWSTOP <= 3:
                            return
                        yield
                        def hv(ap, hp, w):
                            return ap.rearrange("p (c h k) -> p h c k", h=2, k=64)[:, hp, :, :w]

                        def pv(b, w):
                            return ps[b][:C, 0:192].rearrange("p (a b) -> p a b", b=64)[:, :, :w]

                        def amat(lk, rk, RR, dst, mask):
                            bb2 = [nb(), nb()]
                            for h in range(6):
                                c, hp = h // 2, h % 2
                                ln_ = lk + str(hp) if lk in ('kt', 'rt') else lk
                                rn_ = rk + str(hp) if rk in ('kt', 'rt') else rk
                                MM(ps[bb2[hp]][:C, c * 64:c * 64 + C], ra[ln_][:, c, :n], ra[rn_][:, c, :n],
                                   [ln_, rn_], ['ps%d' % bb2[hp]])
                            for hp in range(2):
                                T('dve', hv(tmj[dst][:C, :], hp, C), pv(bb2[hp], C), mask[:C, :C].unsqueeze(1).to_broadcast([C, 3, C]),
                                  ALU.mult, ['ps%d' % bb2[hp], 'cm'], ['tm_' + dst])
                        v3 = lambda ap: ap.rearrange("p (a b) -> p a b", b=64)[:, :, :C]
                        yield
                        amat('kti', 'kt', ['kti', 'kt'], 'AkkT', MSU)
                        yield
                        amat('nbt', 'kt', ['nbt', 'kt'], 'M', MSU)
                        yield
                        T('dve', v3(tmj['X'][:C, :]), v3(tmj['M'][:C, :]), I6[:C, :C].unsqueeze(1).to_broadcast([C, 6, C]), ALU.add, ['tm_M', 'cm'], ['tm_X'])
                        yield
                        amat('kt', 'nbt', ['nbt', 'kt'], 'N', MSL)
                        yield
                        if _RWSTOP <= 4:
                            return
                        yield
                        nr = 0
                        yield
                        while (1 << (nr + 1)) < C:
                            nr += 1
                        yield
                        def xstep():
                            bx = nb()
                            for h in range(6):
                                sl = slice(h * 64, h * 64 + C)
                                MM(ps[bx][:C, sl], tmj['P'][:, sl], tmj['X'][:, sl], ['tm_P', 'tm_X'], ['ps%d' % bx])
                            CP('dve', v3(tmj['X'][:C, :]), v3(ps[bx][:C, 0:384]), ['ps%d' % bx], ['tm_X'])
                        for i in range(1, nr + 1):
                            bn_, bm_ = nb(), nb()
                            yield
                            last = (i == nr)
                            yield
                            for h in range(6):
                                sl = slice(h * 64, h * 64 + C)
                                MM(ps[bn_][:C, sl], tmj['M'][:, sl], tmj['N'][:, sl], ['tm_M', 'tm_N'], ['ps%d' % bn_])
                                if not last:
                                    MM(ps[bm_][:C, sl], tmj['N'][:, sl], tmj['M'][:, sl], ['tm_M', 'tm_N'], ['ps%d' % bm_])
                            yield
                            if not last:
                                CP('act', v3(tmj['N'][:C, :]), v3(ps[bn_][:C, 0:384]), ['ps%d' % bn_], ['tm_N'])
                                CP('act', v3(tmj['M'][:C, :]), v3(ps[bm_][:C, 0:384]), ['ps%d' % bm_], ['tm_M'])
                            yield
                            if i > 1:
                                xstep()
                            yield
                            T('dve', v3(tmj['P'][:C, :]), v3(ps[bn_][:C, 0:384]), I6[:C, :C].unsqueeze(1).to_broadcast([C, 6, C]), ALU.add, ['ps%d' % bn_, 'cm'], ['tm_P'])
                            yield
                        yield
                        if nr >= 1:
                            xstep()
                        yield
                        if _RWSTOP <= 5:
                            return
                        yield
                        br = [nb(), nb()]
                        yield
                        for h in range(6):
                            c, hp = h // 2, h % 2
                            yield
                            pr_ = slice(64 * hp, 64 * hp + 64)
                            yield
                            MM(ps[br[hp]][:C, c * 64:(c + 1) * 64], ra['kt%d' % hp][:, c, :n], S0T[:, c, :], ['kt%d' % hp, 'S0T'], ['ps%d' % br[hp]], start=True, stop=False)
                            yield
                            MM(ps[br[hp]][:C, c * 64:(c + 1) * 64], tmj['AkkT'][:, h * 64:h * 64 + C], tmj['V'][:, h * 64:(h + 1) * 64],
                               ['tm_AkkT', 'tm_V'], ['ps%d' % br[hp]], start=False, stop=True)
                            yield
                        yield
                        for hp in range(2):
                            CP('act', hv(tmj['R'][:C, :], hp, 64), pv(br[hp], 64), ['ps%d' % br[hp]], ['tm_R'])
                        yield
                        bu = nb()
                        yield
                        for h in range(6):
                            MM(ps[bu][:C, h * 64:(h + 1) * 64], tmj['X'][:, h * 64:h * 64 + C], tmj['R'][:, h * 64:(h + 1) * 64],
                               ['tm_X', 'tm_R'], ['ps%d' % bu])
                        yield
                        CP('act', tmj['U'][:C, :], ps[bu][:C, 0:384], ['ps%d' % bu], ['tm_U'])
                        yield
                        if _RWSTOP <= 6:
                            return
                        yield
                        amat('kti', 'rt', ['kti', 'rt'], 'ArkT', MU)
                        yield
                        amat('nbt', 'rt', ['nbt', 'rt'], 'ArbT', MU)
                        yield
                        by = [nb(), nb()]
                        yield
                        for h in range(6):
                            c, hp = h // 2, h % 2
                            yield
                            pr_ = slice(64 * hp, 64 * hp + 64)
                            yield
                            hs = slice(h * 64, (h + 1) * 64)
                            yield
                            cs = slice(c * 64, (c + 1) * 64)
                            yield
                            MM(ps[by[hp]][:C, cs], ra['rt%d' % hp][:, c, :n], S0T[:, c, :], ['rt%d' % hp, 'S0T'], ['ps%d' % by[hp]], start=True, stop=False)
                            yield
                            MM(ps[by[hp]][:C, cs], tmj['ArkT'][:, h * 64:h * 64 + C], tmj['V'][:, hs], ['tm_ArkT', 'tm_V'], ['ps%d' % by[hp]], start=False, stop=False)
                            yield
                            MM(ps[by[hp]][:C, cs], tmj['ArbT'][:, h * 64:h * 64 + C], tmj['U'][:, hs], ['tm_ArbT', 'tm_U'], ['ps%d' % by[hp]], start=False, stop=True)
                            yield
                        yield
                        for hp in range(2):
                            CP('act', hv(tmj['Y'][:C, :], hp, 64), pv(by[hp], 64), ['ps%d' % by[hp]], ['tm_Y'])
                        yield
                        if _RWSTOP <= 7:
                            return
                        yield
                        for c in range(3):
                            bs_ = nb()
                            yield
                            for hp in range(2):
                                hs = slice((2 * c + hp) * 64, (2 * c + hp + 1) * 64)
                                MM(ps[bs_][:, hp * 64:(hp + 1) * 64], tmj['K'][:, c * 128:(c + 1) * 128], tmj['V'][:, hs], ['tm_K', 'tm_V'],
                                   ['ps%d' % bs_], start=True, stop=False)
                                MM(ps[bs_][:, hp * 64:(hp + 1) * 64], tmj['B'][:, c * 128:(c + 1) * 128], tmj['U'][:, hs], ['tm_B', 'tm_U'],
                                   ['ps%d' % bs_], start=False, stop=True)
                            yield
                            for hp in range(2):
                                pr_ = slice(64 * hp, 64 * hp + 64)
                                T('dve', stmp[pr_, :], ps[bs_][pr_, hp * 64:(hp + 1) * 64], S0T[pr_, c, :], ALU.add, ['ps%d' % bs_, 'S0T'], ['stmp'])
                                S('dve', S0T[pr_, c, :], stmp[pr_, :], gC[pr_, c:c + 1], None, ALU.mult, ALU.bypass, ['stmp', 'gC'], ['S0T'])
                            yield
                        yield
                        if _RWSTOP <= 8:
                            return
                        yield
                        bt = nb()
                        yield
                        for c in range(3):
                            TR(ps[bt][:, c * 128:(c + 1) * 128], tmj['Y'][:, c * 128:(c + 1) * 128], ident, ['tm_Y'], ['ps%d' % bt])
                        yield
                        CP('act', ra['y'][:, :, :n], ps[bt][:, 0:384].rearrange("p (a b) -> p a b", b=128)[:, :, :n], ['ps%d' % bt], ['y'])
                        yield
                    if samp:
                        yield from rw_direct()
                    else:
                        yield from rw_chunk()
                    bm1 = nb()
                    yield
                    for c in range(3):
                        MM(ps[bm1][:, c * 64:c * 64 + n], bones, ra['y'][:, c, :n], ['cst', 'y'], ['ps%d' % bm1])
                    yield
                    STT(ra['d'][:, :, :n], ps[bm1][:, 0:192].rearrange("p (a b) -> p a b", b=64)[:, :, :n], -1.0 / 64, ra['y'][:, :, :n],
                        ALU.mult, ALU.add, ['ps%d' % bm1, 'y'], ['d'])
                    yield
                    T('dve', ra['t2'][:, :, :n], ra['d'][:, :, :n], ra['d'][:, :, :n], ALU.mult, ['d'], ['t2'])
                    yield
                    bm2 = nb()
                    yield
                    for c in range(3):
                        MM(ps[bm2][:, c * 64:c * 64 + n], bones, ra['t2'][:, c, :n], ['cst', 't2'], ['ps%d' % bm2])
                    yield
                    S('dve', ra['t2'][:, :, :n], ps[bm2][:, 0:192].rearrange("p (a b) -> p a b", b=64)[:, :, :n], 1.0 / 64, 64e-5,
                      ALU.mult, ALU.add, ['ps%d' % bm2], ['t2'])
                    yield
                    A(ra['t2'][:, :, :n], ra['t2'][:, :, :n], AF.Sqrt, ['t2'], ['t2'])
                    yield
                    em.op('dve', lambda e: e.reciprocal(out=ra['t2'][:, :, :n], in_=ra['t2'][:, :, :n]), reads=['t2'], writes=['t2'])
                    yield
                    T('dve', ra['d'][:, :, :n], ra['d'][:, :, :n], ra['t2'][:, :, :n], ALU.mult, ['d', 't2'], ['d'])
                    yield
                    T('dve', ra['d'][:, :, :n], ra['d'][:, :, :n], rp[:, :, 5].unsqueeze(2).to_broadcast([128, 3, n]), ALU.mult, ['d'] + RP, ['d'])
                    yield
                    T('dve', ra['bon'][:, :, :n], ra['bon'][:, :, :n], rp[:, :, 6].unsqueeze(2).to_broadcast([128, 3, n]), ALU.add, ['bon'] + RP, ['bon'])
                    yield
                    T('dve', ra['d'][:, :, :n], ra['d'][:, :, :n], ra['bon'][:, :, :n], ALU.add, ['d', 'bon'], ['d'])
                    yield
                    T('dve', omix[:, 3:6, :n], ra['d'][:, :, :n], ra['g'][:, :, :n], ALU.mult, ['d', 'g'], ['omix_r'])
                    yield

                def do_tile(c0, n, samp=False):
                    A(sqb[:, :, :n], xres[:, :, c0:c0 + n], AF.Square, ['xres'], ['sqb'])
                    b = nb()
                    for kc in range(KC):
                        MM(ps[b][:, :n], ones_bf[:], sqb[:, kc, :n], ['ones_bf', 'sqb'], ['ps%d' % b], start=(kc == 0), stop=(kc == KC - 1))
                    S('dve', rstb[:, :n], ps[b][:, :n], 1.0 / D, EPS, ALU.mult, ALU.add, ['ps%d' % b], ['rstb'])
                    A(rstb[:, :n], rstb[:, :n], AF.Sqrt, ['rstb'], ['rstb'])
                    em.op('dve', lambda e: e.reciprocal(out=rstb[:, :n], in_=rstb[:, :n]), reads=['rstb'], writes=['rstb'])
                    if 's5' in mixers:
                        T('dve', st1[:, 0:KC, :n], xres[:, :, c0:c0 + n], normw[:, 2 + l, :].unsqueeze(2).to_broadcast([128, KC, n]), ALU.mult,
                          ['xres', 'normw'], ['st1'])
                        T('dve', xnb[:, :, :n], st1[:, 0:KC, :n], rstb[:, :n].unsqueeze(1).to_broadcast([128, KC, n]), ALU.mult,
                          ['st1', 'rstb'], ['xnb'])
                    else:
                        for kc in range(KC):
                            STT(xnb[:, kc, :n], xres[:, kc, c0:c0 + n], normw[:, 2 + l, kc:kc + 1], rstb[:, :n], ALU.mult, ALU.mult,
                                ['xres', 'normw', 'rstb'], ['xnb'])
                    for f0 in (0, 8, 16):
                        b = nb()
                        nf = min(8, 18 - f0)
                        for fi in range(nf):
                            fc = f0 + fi
                            for kc in range(KC):
                                o = kc * DIN + fc * 128
                                MM(ps[b][:, fi * 64:fi * 64 + n], ring[:, o:o + 128], xnb[:, kc, :n], RIN + ['xnb'], ['ps%d' % b],
                                   start=(kc == 0), stop=(kc == KC - 1))
                        CP('act', pj[:, f0:f0 + nf, 3:3 + n], ps[b][:, 0:nf * 64].rearrange("p (a b) -> p a b", b=64)[:, :, :n],
                           ['ps%d' % b], ['pj'])
                    if len(mixers) < 3:
                        em.op('pool', lambda e: e.memset(omix[:, :, :n], 0.0), writes=['omix_s', 'omix_r', 'omix_l'])

                    def run_rr(gens):
                        while gens:
                            for g_ in list(gens):
                                try:
                                    next(g_)
                                except StopIteration:
                                    gens.remove(g_)
                    if samp or len(mixers) < 3:
                        if 's5' in mixers:
                            run_rr([s5_tile(n, samp)])
                        gens = []
                        if 'rwkv' in mixers:
                            gens.append(rwkv_tile(n, samp))
                        if 'lru' in mixers:
                            gens.append(lru_tile(n, samp))
                        run_rr(gens)
                    else:
                        gR, gL = rwkv_tile(n, samp), lru_tile(n, samp)
                        l_done = False
                        while True:
                            if next(gR) == 'PHASE':
                                break
                            if not l_done:
                                try:
                                    next(gL)
                                except StopIteration:
                                    l_done = True
                        gens = [gR, s5_tile(n, samp)] + ([] if l_done else [gL])
                        run_rr(gens)
                    b = nb()
                    for mc in range(KC):
                        for kc in range(KC):
                            o = 18432 + kc * D + mc * 128
                            MM(ps[b][:, mc * 64:mc * 64 + n], ring[:, o:o + 128], omix[:, kc, :n], ROUT + ['omix_s', 'omix_r', 'omix_l'], ['ps%d' % b],
                               start=(kc == 0), stop=(kc == KC - 1))
                    T('dve', xres[:, :, c0:c0 + n], xres[:, :, c0:c0 + n], ps[b][:, :].rearrange("p (a b) -> p a b", b=64)[:, :, :n], ALU.add,
                      ['xres', 'ps%d' % b], ['xres'])
                    if not samp:
                        CP('dve', tmpc[:], pj[:, :, n:n + 3], ['pj'], ['tmpc'])
                        CP('dve', pj[:, :, 0:3], tmpc[:], ['tmpc'], ['pj'])

                def store_state(si):
                    em.dma('sp', o_s5[l, si], hst[:], 'st_a', reads=['hst'])
                    em.dma('sp', o_rwkv[l, si], S0T[:], 'st_b', reads=['S0T'])
                    em.dma('sp', o_shift[l, si], pj[:, 3:14, 2], 'st_c', reads=['pj'])
                    em.dma('sp', o_lru[l, si], lru_h[:], 'st_d', reads=['lru_h'])
                    em.dma('sp', o_conv[l, si], pj[:, 14:16, 0:3], 'st_e', reads=['pj'])

                em.op('pool', lambda e: e.memset(pj[:], 0.0), writes=['pj'])
                em.op('pool', lambda e: e.memset(hst[:], 0.0), writes=['hst'])
                em.op('pool', lambda e: e.memset(S0T[:], 0.0), writes=['S0T'])
                em.op('pool', lambda e: e.memset(lru_h[:], 0.0), writes=['lru_h'])
                if 'rwkv' in mixers:
                    zero_tmj()
                do_tile(0, 16)
                for i in range(PT // NB):
                    do_tile(32 + NB * i, NB)
                store_state(0)
                if not _NOSAMP:
                    em.dma('sp', s5st[:], i_s5[l].rearrange("b p g r -> p b g r"), 'ld_a', writes=['s5st'])
                    em.dma('sp', sh16[:], i_shift[l].rearrange("b p c -> p b c"), 'ld_cc', writes=['sh16'])
                    em.dma('sp', lst[:], i_lru[l].rearrange("b p c -> p b c"), 'ld_d', writes=['lst'])
                    em.dma('sp', cst16[:], i_conv[l].rearrange("b p c j -> p b c j"), 'ld_e', writes=['cst16'])
                    do_tile(16, 16, True)
                    CP('dve', prevv, pj[:, 3:14, 3:19], ['pj'], ['sh16'])
                    em.dma('sp', o_s5[l, 1:17].rearrange("b p g r -> p b g r"), s5st[:], 'st_a', reads=['s5st'])
                    em.dma('sp', o_shift[l, 1:17].rearrange("b p c -> p b c"), sh16[:], 'st_c', reads=['sh16'])
                    em.dma('sp', o_lru[l, 1:17].rearrange("b p c -> p b c"), lst[:], 'st_d', reads=['lst'])
                    em.dma('sp', o_conv[l, 1:17].rearrange("b p c j -> p b c j"), cst16[:], 'st_e', reads=['cst16'])
                em.barrier()

        for l in range(L):
            if stage >= 1:
                ffn(l, 0)
            if stage >= 2 or stage == -2:
                mixer_phase(l)
            if stage >= 1:
                ffn(l, 1)

        fs = ExitStack()
        with fs:
            sq = sbuf(fs, "sqF", [128, KC, 512], BF16)
            rstd = sbuf(fs, "rstdF", [128, 512])
            for ti in range(NT):
                c0, n = tiles[ti]
                A(sq[:, :, :n], xres[:, :, c0:c0 + n], AF.Square, ['xres'], ['sq'])
                for kc in range(KC):
                    MM(ps[7][:, :n], ones_bf[:], sq[:, kc, :n], ['ones_bf', 'sq'], ['ps7'], start=(kc == 0), stop=(kc == KC - 1))
                S('dve', rstd[:, :n], ps[7][:, :n], 1.0 / D, EPS, ALU.mult, ALU.add, ['ps7'], ['rstd'])
                A(rstd[:, :n], rstd[:, :n], AF.Sqrt, ['rstd'], ['rstd'])
                em.op('dve', lambda e: e.reciprocal(out=rstd[:, :n], in_=rstd[:, :n]), reads=['rstd'], writes=['rstd'])
                for kc in range(KC):
                    STT(xres[:, kc, c0:c0 + n], xres[:, kc, c0:c0 + n], normw[:, 6, kc:kc + 1], rstd[:, :n], ALU.mult, ALU.mult,
                        ['xres', 'rstd', 'normw'], ['xres'])
                    em.dma('sp', yT[:, kc, c0:c0 + n], xres[:, kc, c0:c0 + n], 'st_y%d' % kc, reads=['xres'])
            em.finish('sp')
        print("instructions:", em.ninstr, {k: v for k, v in em.cnt.items()})
    return nc


def prep_common(inp):
    f = np.float32
    g = lambda k: np.asarray(inp[k], f)
    c = {}
    wf = np.empty((L, 2, NJ, 128, SLOT), f)
    for l in range(L):
        for fi, pre in enumerate(("ffn1", "ffn2")):
            wg = g(pre + "_w_gate")[l].reshape(KC, 128, NJ, 128)
            wu = g(pre + "_w_up")[l].reshape(KC, 128, NJ, 128)
            wd = g(pre + "_w_down")[l].reshape(NJ, 128, D)
            wf[l, fi, :, :, 0:1024] = wg.transpose(2, 1, 0, 3).reshape(NJ, 128, 1024)
            wf[l, fi, :, :, 1024:2048] = wu.transpose(2, 1, 0, 3).reshape(NJ, 128, 1024)
            wf[l, fi, :, :, 2048:3072] = wd
    c["wffn"] = wf
    c["w_in"] = np.ascontiguousarray(g("w_in").reshape(L, KC, 128, DIN).transpose(0, 2, 1, 3)).reshape(L, 128, KC * DIN)
    c["w_out"] = np.ascontiguousarray(g("w_out").reshape(L, KC, 128, D).transpose(0, 2, 1, 3)).reshape(L, 128, KC * D)
    nv = [g("ffn1_norm")[0], g("ffn1_norm")[1], g("mix_norm")[0], g("mix_norm")[1], g("ffn2_norm")[0], g("ffn2_norm")[1], g("final_norm")]
    c["norms"] = np.ascontiguousarray(np.stack([v.reshape(KC, 128) for v in nv], 0).transpose(2, 0, 1))
    p = np.arange(128)
    cs = np.zeros((128, 400), f)
    cs[:, 0:128] = np.eye(128)
    cs[:, 128:256] = (p[:, None] // 64 == p[None, :] // 64)
    cs[:, 256:320] = np.arange(1, 65)[None, :]
    for q in range(4):
        for gg in range(2):
            cs[:, 320 + 2 * q + gg] = (p >= 32 * q + 16 * gg) & (p < 32 * q + 16 * gg + 16)
    for gg in range(2):
        cs[:, 328 + gg] = (p // 64 == gg)
    cs[:, 330:394] = (p[:, None] % 64 == np.arange(64)[None, :])
    c["consts"] = cs
    s_ = np.arange(64)
    m4 = np.stack([s_[:, None] < s_[None, :], s_[:, None] <= s_[None, :], s_[:, None] > s_[None, :], s_[:, None] == s_[None, :]], 0).astype(f)
    c["cmask"] = np.ascontiguousarray(m4.transpose(1, 0, 2))
    lr, li, ldt = g("s5_lambda_re"), g("s5_lambda_im"), g("s5_log_dt")
    tri = np.stack([lr, li, np.broadcast_to(ldt[:, :, None], lr.shape)], -1)
    c["s5A"] = np.ascontiguousarray(tri.reshape(L, 12, 2, 64, 3).transpose(0, 2, 3, 1, 4).reshape(L, 128, 12, 3))
    tB = np.broadcast_to(tri.reshape(L, 3, 8, 1, 64, 3), (L, 3, 8, 16, 64, 3))
    c["s5B"] = np.ascontiguousarray(tB.transpose(0, 2, 3, 1, 4, 5).reshape(L, 128, 3, 64, 3))
    bb = np.stack([g("s5_b_re"), g("s5_b_im")], -1)
    c["s5bT"] = np.ascontiguousarray(bb.reshape(L, 3, 8, 64, 16, 2).transpose(0, 2, 4, 1, 3, 5).reshape(L, 128, 3, 64, 2))
    cc = np.stack([g("s5_c_re"), g("s5_c_im")], -1)
    c["s5cT"] = np.ascontiguousarray(cc.reshape(L, 12, 2, 16, 64, 2).transpose(0, 2, 4, 1, 3, 5).reshape(L, 128, 12, 16, 2))
    c["s5v"] = np.ascontiguousarray(np.stack([g("s5_d"), g("s5_glu_b")], -1).reshape(L, 3, 128, 2).transpose(0, 2, 1, 3))
    c["s5glu"] = np.ascontiguousarray(g("s5_glu_w").reshape(L, 3, 128, 384).transpose(0, 2, 1, 3))
    z384 = np.zeros((L, 384), f)
    rw = np.stack([g("rwkv_w0"), g("rwkv_a0"), g("rwkv_k_k"), g("rwkv_k_a"), g("rwkv_r_k").reshape(L, 384), g("rwkv_ln_w"), g("rwkv_ln_b"), z384], -1)
    c["rwp"] = np.ascontiguousarray(rw.reshape(L, 3, 128, 8).transpose(0, 2, 1, 3))
    c["rwmu"] = np.ascontiguousarray(g("rwkv_mu").reshape(L, 11, 128).transpose(0, 2, 1))
    c["rwup"] = np.ascontiguousarray(np.stack([np.concatenate([g("rwkv_w_up"), g("rwkv_a_up")], 1), g("rwkv_g_up")], 2))
    lp_ = np.concatenate([g("lru_conv_w").transpose(0, 2, 1), g("lru_conv_b")[..., None], g("lru_b_a")[..., None], g("lru_b_x")[..., None],
                          g("lru_lambda")[..., None]], -1)
    c["lrp"] = np.ascontiguousarray(lp_.reshape(L, 2, 128, 8).transpose(0, 2, 1, 3))
    lw_ = np.zeros((L, 128, 2, 2, 128), f)
    for wi, nm in enumerate(("lru_w_a", "lru_w_x")):
        w = g(nm)
        for cb in range(2):
            for bq in range(2):
                lw_[:, 64 * bq:64 * bq + 64, cb, wi, 64 * bq:64 * bq + 64] = w[:, 2 * cb + bq]
    c["lrw"] = lw_
    return c


def prep_core(inp, ci, PT=2048):
    f = np.float32
    g = lambda k: np.asarray(inp[k], f)
    meta = g("meta_tokens")
    xs = g("x_sample")[16 * ci:16 * ci + 16, 0]
    xp = g("x_prompt")[ci][:PT]
    cols = np.concatenate([meta, xs, xp], 0)
    m = {"xT": np.ascontiguousarray(cols.T.reshape(KC, 128, 32 + PT).transpose(1, 0, 2))}
    sl = slice(16 * ci, 16 * ci + 16)
    st = np.stack([g("state_s5_re")[:, sl], g("state_s5_im")[:, sl]], -1)
    m["i_s5"] = np.ascontiguousarray(st.reshape(L, 16, 12, 2, 64, 2).transpose(0, 1, 3, 4, 2, 5).reshape(L, 16, 128, 12, 2))
    rs = g("state_rwkv")[:, sl]
    m["i_rwkv"] = np.ascontiguousarray(rs.reshape(L, 16, 3, 2, 64, 64).transpose(0, 1, 3, 5, 2, 4).reshape(L, 16, 128, 3, 64))
    m["i_shift"] = np.ascontiguousarray(g("state_rwkv_shift")[:, sl].reshape(L, 16, 11, 128).transpose(0, 1, 3, 2))
    m["i_lru"] = np.ascontiguousarray(g("state_lru")[:, sl].reshape(L, 16, 2, 128).transpose(0, 1, 3, 2))
    m["i_conv"] = np.ascontiguousarray(g("state_lru_conv")[:, sl].reshape(L, 16, 3, 2, 128).transpose(0, 1, 4, 3, 2))
    return m


def unpack_core(r):
    o = {}
    s5 = r["o_s5"].reshape(L, NSEQ, 2, 64, 12, 2).transpose(0, 1, 4, 2, 3, 5).reshape(L, NSEQ, 24, 64, 2)
    o["s5_re"], o["s5_im"] = s5[..., 0], s5[..., 1]
    o["rwkv"] = r["o_rwkv"].reshape(L, NSEQ, 2, 64, 3, 64).transpose(0, 1, 4, 2, 5, 3).reshape(L, NSEQ, 6, 64, 64)
    o["shift"] = r["o_shift"].transpose(0, 1, 3, 2).reshape(L, NSEQ, 1408)
    o["lru"] = r["o_lru"].transpose(0, 1, 3, 2).reshape(L, NSEQ, 256)
    o["conv"] = r["o_conv"].transpose(0, 1, 4, 3, 2).reshape(L, NSEQ, 3, 256)
    return o


_NC_CACHE = {}


def kernel(**inp):
    if "nc" not in _NC_CACHE:
        _NC_CACHE["nc"] = build()
    nc = _NC_CACHE["nc"]
    common = prep_common(inp)
    in_maps = []
    for ci in range(NCORES):
        m = dict(common)
        m.update(prep_core(inp, ci))
        in_maps.append(m)
    res = run_bass_kernel_spmd(nc, in_maps, core_ids=list(range(NCORES)))
    outs = res.results
    f = np.float32
    y_prompt = np.empty((8, 2048, D), f)
    y_sample = np.empty((128, 1, D), f)
    keys = ("s5_re", "s5_im", "rwkv", "shift", "lru", "conv")
    shp = {"s5_re": (24, 64), "s5_im": (24, 64), "rwkv": (6, 64, 64), "shift": (1408,), "lru": (256,), "conv": (3, 256)}
    P = {k: np.empty((L, 8) + shp[k], f) for k in keys}
    Sx = {k: np.empty((L, 128) + shp[k], f) for k in keys}
    for ci in range(NCORES):
        y = outs[ci]["yT"].transpose(1, 0, 2).reshape(D, 2080).T
        y_prompt[ci] = y[32:]
        y_sample[16 * ci:16 * ci + 16, 0] = y[16:32]
        o = unpack_core(outs[ci])
        for k in keys:
            P[k][:, ci] = o[k][:, 0]
            Sx[k][:, 16 * ci:16 * ci + 16] = o[k][:, 1:]
    return (y_prompt, y_sample) + tuple(P[k] for k in keys) + tuple(Sx[k] for k in keys)
```
